# Optimizing a Trainium2 kernel written in Bass

```python
import math
import jax, jax.numpy as jnp
from jax import lax
import numpy as np

D_MODEL = 1024
BATCH = 16
SEQ = 2048
DEPTH = 4

HEAD_DIM = 64
NSA_HEADS = 6
NSA_KV_GROUPS = 2
NSA_REP = NSA_HEADS // NSA_KV_GROUPS
N_BRANCH = 3
CMP_LEN = 32
CMP_STRIDE = 16
CMP_HIDDEN = 2 * HEAD_DIM
SEL_BLOCK = 64
SEL_TOP_N = 16
WINDOW = 512
NSA_Q_BLOCK = 32
MLA_HEADS = 5
MLA_Q_RANK = 256
MLA_KV_RANK = 128
MLA_NOPE_DIM = 64
MLA_ROPE_DIM = 32
MLA_V_DIM = 64
ATTN_Q_BLOCK = 128
RET_HEADS = 5
RET_CHUNK = 128
ROPE_THETA = 500000.0
PARTIAL_ROPE_DIM = HEAD_DIM // 4
RET_THETA = 10000.0
D_FF = 4 * D_MODEL
NORM_EPS = 1e-6
NEG_INF = -1e30
FORCE_SCORE = 1e9

NSA_Q_W = NSA_HEADS * HEAD_DIM
NSA_KV_W = NSA_KV_GROUPS * HEAD_DIM
NSA_GATE_W = NSA_HEADS * N_BRANCH
MLA_OUT_W = MLA_HEADS * MLA_V_DIM
RET_W = RET_HEADS * HEAD_DIM
MIX_WIDTH = NSA_Q_W + MLA_OUT_W + RET_W
IN_SIZES = (NSA_Q_W, NSA_KV_W, NSA_KV_W, NSA_KV_W, NSA_KV_W, NSA_KV_W, NSA_KV_W, NSA_GATE_W,
            MLA_Q_RANK, MLA_KV_RANK, MLA_ROPE_DIM, RET_W, RET_W, RET_W, RET_W)
N_IN = sum(IN_SIZES)

kernel_name = 'hybrid_nsa_mla_retention'

F32 = jnp.float32


def in_offsets():
    return [int(o) for o in np.cumsum(IN_SIZES)[:-1]]


def rms_norm(x, gain):
    xf = x.astype(F32)
    y = xf * lax.rsqrt(jnp.mean(xf * xf, axis=-1, keepdims=True) + NORM_EPS) * gain.astype(F32)
    return y.astype(x.dtype)


def masked_softmax(s, mask):
    s = jnp.where(mask, s.astype(F32), NEG_INF)
    m = jnp.max(s, axis=-1, keepdims=True)
    p = jnp.exp(s - m) * mask
    return p / jnp.maximum(jnp.sum(p, axis=-1, keepdims=True), 1e-30)


def rope_tables(positions, dim, theta):
    inv = 1.0 / (theta ** (jnp.arange(0, dim, 2, dtype=F32) / dim))
    ang = positions.astype(F32)[..., None] * inv
    return jnp.cos(ang), jnp.sin(ang)


def apply_rope(x, cos, sin):
    half = x.shape[-1] // 2
    x1 = x[..., :half].astype(F32)
    x2 = x[..., half:].astype(F32)
    c = cos[:, :, None, :]
    s = sin[:, :, None, :]
    return jnp.concatenate([x1 * c - x2 * s, x2 * c + x1 * s], axis=-1).astype(x.dtype)


def partial_rope(x, cos, sin):
    return jnp.concatenate([apply_rope(x[..., :PARTIAL_ROPE_DIM], cos, sin), x[..., PARTIAL_ROPE_DIM:]], axis=-1)


def selection_overlap(n_cmp, n_sel):
    cs = np.arange(n_cmp) * CMP_STRIDE
    ce = cs + CMP_LEN
    ss = np.arange(n_sel) * SEL_BLOCK
    se = ss + SEL_BLOCK
    ov = np.clip(np.minimum(ce[:, None], se[None, :]) - np.maximum(cs[:, None], ss[None, :]), 0, None) / CMP_LEN
    return jnp.asarray(ov.astype(np.float32))


def compress_blocks(tok, tok_idx, pos_emb, w1, w2):
    b, _, g, dh = tok.shape
    blk = tok[:, tok_idx] + pos_emb[None, None, :, None, :]
    flat = blk.transpose(0, 1, 3, 2, 4).reshape(b, tok_idx.shape[0], g, CMP_LEN * dh)
    return jax.nn.gelu(flat @ w1) @ w2


def nsa_mixer(q, k_cmp, v_cmp, k_slc, v_slc, k_win, v_win, gates, pos_k, w1_k, w2_k, pos_v, w1_v, w2_v):
    b, s, _, dh = q.shape
    g, r = NSA_KV_GROUPS, NSA_REP
    n_cmp = (s - CMP_LEN) // CMP_STRIDE + 1
    n_sel = s // SEL_BLOCK
    top_n = min(SEL_TOP_N, n_sel)
    scale = dh ** -0.5
    qb_len = NSA_Q_BLOCK

    tok_idx = np.arange(n_cmp)[:, None] * CMP_STRIDE + np.arange(CMP_LEN)[None, :]
    kc = compress_blocks(k_cmp, tok_idx, pos_k, w1_k, w2_k)
    vc = compress_blocks(v_cmp, tok_idx, pos_v, w1_v, w2_v)
    cmp_end = jnp.arange(n_cmp) * CMP_STRIDE + CMP_LEN - 1
    overlap = selection_overlap(n_cmp, n_sel)

    ks_blk = k_slc.reshape(b, n_sel, SEL_BLOCK, g, dh).transpose(0, 3, 1, 2, 4)
    vs_blk = v_slc.reshape(b, n_sel, SEL_BLOCK, g, dh).transpose(0, 3, 1, 2, 4)
    kw_pad = jnp.pad(k_win, ((0, 0), (WINDOW, 0), (0, 0), (0, 0)))
    vw_pad = jnp.pad(v_win, ((0, 0), (WINDOW, 0), (0, 0), (0, 0)))

    n_qb = s // qb_len
    q_blocks = q.reshape(b, n_qb, qb_len, g, r, dh).transpose(1, 0, 2, 3, 4, 5)
    gate_blocks = jax.nn.sigmoid(gates.astype(F32)).reshape(b, n_qb, qb_len, g, r, N_BRANCH).transpose(1, 0, 2, 3, 4, 5)
    b_idx = jnp.arange(b)[:, None, None, None]
    g_idx = jnp.arange(g)[None, :, None, None]
    sel_ids = jnp.arange(n_sel)[None, :]

    def block(args):
        qb, gb, i = args
        q0 = i * qb_len
        t = q0 + jnp.arange(qb_len)
        s_c = jnp.einsum('bqgrd,bngd->bgrqn', qb, kc) * scale
        p_c = masked_softmax(s_c, cmp_end[None, :] <= t[:, None])
        o_c = jnp.einsum('bgrqn,bngd->bqgrd', p_c, vc)
        imp = jnp.einsum('bgrqn,nm->bgqm', p_c, overlap)
        cur = (t // SEL_BLOCK)[:, None]
        valid = sel_ids <= cur
        forced = (sel_ids == 0) | (sel_ids == cur) | (sel_ids == cur - 1)
        imp = jnp.where(valid & forced, FORCE_SCORE, imp)
        imp = jnp.where(valid, imp, NEG_INF)
        _, idx = lax.top_k(imp, top_n)
        k_sel = ks_blk[b_idx, g_idx, idx]
        v_sel = vs_blk[b_idx, g_idx, idx]
        key_pos = idx[..., None] * SEL_BLOCK + jnp.arange(SEL_BLOCK)
        mask_s = (key_pos <= t[None, None, :, None, None]).reshape(b, g, 1, qb_len, top_n * SEL_BLOCK)
        s_s = jnp.einsum('bqgrd,bgqjsd->bgrqjs', qb, k_sel).reshape(b, g, r, qb_len, top_n * SEL_BLOCK) * scale
        p_s = masked_softmax(s_s, mask_s).reshape(b, g, r, qb_len, top_n, SEL_BLOCK)
        o_s = jnp.einsum('bgrqjs,bgqjsd->bqgrd', p_s, v_sel)
        kw = lax.dynamic_slice_in_dim(kw_pad, q0, qb_len + WINDOW, axis=1)
        vw = lax.dynamic_slice_in_dim(vw_pad, q0, qb_len + WINDOW, axis=1)
        kp = (q0 - WINDOW + jnp.arange(qb_len + WINDOW))[None, :]
        mask_w = (kp <= t[:, None]) & (kp > t[:, None] - WINDOW) & (kp >= 0)
        s_w = jnp.einsum('bqgrd,bkgd->bgrqk', qb, kw) * scale
        p_w = masked_softmax(s_w, mask_w)
        o_w = jnp.einsum('bgrqk,bkgd->bqgrd', p_w, vw)
        return gb[..., 0:1] * o_c + gb[..., 1:2] * o_s + gb[..., 2:3] * o_w

    out = lax.map(block, (q_blocks, gate_blocks, jnp.arange(n_qb)))
    return out.transpose(1, 0, 2, 3, 4, 5).reshape(b, s, NSA_HEADS * dh)


def causal_block_attention(q, k, v, scale):
    b, s, h, dq = q.shape
    n_qb = s // ATTN_Q_BLOCK
    q_blocks = q.reshape(b, n_qb, ATTN_Q_BLOCK, h, dq).transpose(1, 0, 2, 3, 4)
    kp = jnp.arange(s)[None, :]

    def block(args):
        qi, i = args
        t = (i * ATTN_Q_BLOCK + jnp.arange(ATTN_Q_BLOCK))[:, None]
        sc = jnp.einsum('bqhd,bkhd->bhqk', qi, k) * scale
        p = masked_softmax(sc, kp <= t)
        return jnp.einsum('bhqk,bkhd->bqhd', p, v)

    out = lax.map(block, (q_blocks, jnp.arange(n_qb)))
    return out.transpose(1, 0, 2, 3, 4).reshape(b, s, h, v.shape[-1])


def mla_mixer(c_q, c_kv, k_pe, q_norm, w_uq, kv_norm, w_ukv, cos, sin):
    b, s, _ = c_q.shape
    q = (rms_norm(c_q, q_norm) @ w_uq).reshape(b, s, MLA_HEADS, MLA_NOPE_DIM + MLA_ROPE_DIM)
    q_nope, q_pe = q[..., :MLA_NOPE_DIM], apply_rope(q[..., MLA_NOPE_DIM:], cos, sin)
    kv = (rms_norm(c_kv, kv_norm) @ w_ukv).reshape(b, s, MLA_HEADS, MLA_NOPE_DIM + MLA_V_DIM)
    k_nope, v = kv[..., :MLA_NOPE_DIM], kv[..., MLA_NOPE_DIM:]
    k_pe = apply_rope(k_pe[:, :, None, :], cos, sin)
    q_full = jnp.concatenate([q_nope, q_pe], axis=-1)
    k_full = jnp.concatenate([k_nope, jnp.broadcast_to(k_pe, (b, s, MLA_HEADS, MLA_ROPE_DIM)).astype(k_nope.dtype)], axis=-1)
    o = causal_block_attention(q_full, k_full, v, (MLA_NOPE_DIM + MLA_ROPE_DIM) ** -0.5)
    return o.reshape(b, s, MLA_OUT_W)


def chunkwise_retention(q, k, v):
    b, s, h, d = q.shape
    n = s // RET_CHUNK
    log_g = jnp.log(1.0 - 2.0 ** (-5.0 - jnp.arange(h, dtype=F32)))
    i = jnp.arange(RET_CHUNK, dtype=F32)
    diff = i[:, None] - i[None, :]
    intra = jnp.where(diff >= 0, jnp.exp(jnp.maximum(diff, 0.0)[None] * log_g[:, None, None]), 0.0)
    read_decay = jnp.exp((i + 1.0)[None, :] * log_g[:, None])[None, :, :, None]
    write_decay = jnp.exp((RET_CHUNK - 1.0 - i)[None, :] * log_g[:, None])[None, :, :, None]
    chunk_decay = jnp.exp(RET_CHUNK * log_g)[None, :, None, None]

    def to_chunks(a):
        return a.reshape(b, n, RET_CHUNK, h, d).transpose(1, 0, 3, 2, 4)

    def step(state, xs):
        qc, kc, vc = xs
        sc = jnp.einsum('bhid,bhjd->bhij', qc, kc) * intra
        o = jnp.einsum('bhij,bhjd->bhid', sc, vc) + jnp.einsum('bhid,bhde->bhie', qc, state) * read_decay
        state = state * chunk_decay + jnp.einsum('bhjd,bhje->bhde', kc * write_decay, vc)
        return state, o

    state0 = jnp.zeros((b, h, d, d), F32)
    _, outs = lax.scan(step, state0, (to_chunks(q), to_chunks(k), to_chunks(v)))
    return outs.transpose(1, 0, 3, 2, 4).reshape(b, s, h, d)


def retention_mixer(q, k, v, gate, gn_gain, cos, sin):
    b, s, _ = q.shape
    h, d = RET_HEADS, HEAD_DIM
    q = apply_rope(q.reshape(b, s, h, d), cos, sin).astype(F32)
    k = apply_rope(k.reshape(b, s, h, d), cos, sin).astype(F32) * (d ** -0.5)
    v = v.reshape(b, s, h, d).astype(F32)
    o = chunkwise_retention(q, k, v)
    mu = jnp.mean(o, axis=-1, keepdims=True)
    var = jnp.mean(jnp.square(o - mu), axis=-1, keepdims=True)
    o = (o - mu) * lax.rsqrt(var + NORM_EPS) * gn_gain.astype(F32)
    return jax.nn.silu(gate.astype(F32)) * o.reshape(b, s, h * d)


def setup_inputs(seed: int = 0) -> dict:
    key = jax.random.key(seed)
    ks = jax.random.split(key, 24)

    def nrm(k, shape, scale):
        return scale * jax.random.normal(k, shape, F32)

    def gain(k, shape):
        return 1.0 + 0.02 * jax.random.normal(k, shape, F32)

    offs = jax.random.randint(ks[1], (BATCH, 1), 0, 4096)
    positions = (jnp.arange(SEQ, dtype=jnp.int32)[None, :] + offs).astype(jnp.int32)
    return {
        'x': nrm(ks[0], (BATCH, SEQ, D_MODEL), 1.0),
        'positions': positions,
        'ln1_gain': gain(ks[2], (DEPTH, D_MODEL)),
        'w_in': nrm(ks[3], (DEPTH, D_MODEL, N_IN), D_MODEL ** -0.5),
        'cmp_pos_k': nrm(ks[4], (DEPTH, CMP_LEN, HEAD_DIM), 0.1),
        'cmp_w1_k': nrm(ks[5], (DEPTH, CMP_LEN * HEAD_DIM, CMP_HIDDEN), (CMP_LEN * HEAD_DIM) ** -0.5),
        'cmp_w2_k': nrm(ks[6], (DEPTH, CMP_HIDDEN, HEAD_DIM), CMP_HIDDEN ** -0.5),
        'cmp_pos_v': nrm(ks[7], (DEPTH, CMP_LEN, HEAD_DIM), 0.1),
        'cmp_w1_v': nrm(ks[8], (DEPTH, CMP_LEN * HEAD_DIM, CMP_HIDDEN), (CMP_LEN * HEAD_DIM) ** -0.5),
        'cmp_w2_v': nrm(ks[9], (DEPTH, CMP_HIDDEN, HEAD_DIM), CMP_HIDDEN ** -0.5),
        'mla_q_norm': gain(ks[10], (DEPTH, MLA_Q_RANK)),
        'mla_w_uq': nrm(ks[11], (DEPTH, MLA_Q_RANK, MLA_HEADS * (MLA_NOPE_DIM + MLA_ROPE_DIM)), MLA_Q_RANK ** -0.5),
        'mla_kv_norm': gain(ks[12], (DEPTH, MLA_KV_RANK)),
        'mla_w_ukv': nrm(ks[13], (DEPTH, MLA_KV_RANK, MLA_HEADS * (MLA_NOPE_DIM + MLA_V_DIM)), MLA_KV_RANK ** -0.5),
        'ret_gn_gain': gain(ks[14], (DEPTH, RET_HEADS, HEAD_DIM)),
        'w_out': nrm(ks[15], (DEPTH, MIX_WIDTH, D_MODEL), MIX_WIDTH ** -0.5),
        'ln2_gain': gain(ks[16], (DEPTH, D_MODEL)),
        'w_up': nrm(ks[17], (DEPTH, D_MODEL, D_FF), D_MODEL ** -0.5),
        'w_down': nrm(ks[18], (DEPTH, D_FF, D_MODEL), D_FF ** -0.5),
        'final_gain': gain(ks[19], (D_MODEL,)),
    }


def reference(x, positions, ln1_gain, w_in, cmp_pos_k, cmp_w1_k, cmp_w2_k, cmp_pos_v, cmp_w1_v, cmp_w2_v,
              mla_q_norm, mla_w_uq, mla_kv_norm, mla_w_ukv, ret_gn_gain, w_out, ln2_gain, w_up, w_down,
              final_gain):
    b, s, _ = x.shape
    cos_p, sin_p = rope_tables(positions, PARTIAL_ROPE_DIM, ROPE_THETA)
    cos_m, sin_m = rope_tables(positions, MLA_ROPE_DIM, ROPE_THETA)
    cos_r, sin_r = rope_tables(positions, HEAD_DIM, RET_THETA)
    offsets = in_offsets()
    for l in range(DEPTH):
        h = rms_norm(x, ln1_gain[l])
        proj = h @ w_in[l]
        (nsa_q, k_cmp, v_cmp, k_slc, v_slc, k_win, v_win, nsa_gate,
         mla_cq, mla_ckv, mla_kpe, ret_q, ret_k, ret_v, ret_gate) = jnp.split(proj, offsets, axis=-1)

        def kv_heads(a):
            return a.reshape(b, s, NSA_KV_GROUPS, HEAD_DIM)

        o_nsa = nsa_mixer(
            partial_rope(nsa_q.reshape(b, s, NSA_HEADS, HEAD_DIM), cos_p, sin_p),
            partial_rope(kv_heads(k_cmp), cos_p, sin_p), kv_heads(v_cmp),
            partial_rope(kv_heads(k_slc), cos_p, sin_p), kv_heads(v_slc),
            partial_rope(kv_heads(k_win), cos_p, sin_p), kv_heads(v_win),
            nsa_gate.reshape(b, s, NSA_HEADS, N_BRANCH),
            cmp_pos_k[l], cmp_w1_k[l], cmp_w2_k[l], cmp_pos_v[l], cmp_w1_v[l], cmp_w2_v[l])
        o_mla = mla_mixer(mla_cq, mla_ckv, mla_kpe, mla_q_norm[l], mla_w_uq[l], mla_kv_norm[l], mla_w_ukv[l],
                          cos_m, sin_m)
        o_ret = retention_mixer(ret_q, ret_k, ret_v, ret_gate, ret_gn_gain[l], cos_r, sin_r)
        mixed = jnp.concatenate([o_nsa.astype(x.dtype), o_mla.astype(x.dtype), o_ret.astype(x.dtype)], axis=-1)
        x = x + mixed @ w_out[l]
        h = rms_norm(x, ln2_gain[l])
        x = x + jnp.square(jax.nn.relu(h @ w_up[l])) @ w_down[l]
    return rms_norm(x, final_gain)
```

```python
from contextlib import ExitStack
import numpy as np
import concourse.bass as bass
import concourse.mybir as mybir
from concourse.bass_utils import run_bass_kernel_spmd

F32 = mybir.dt.float32
BF16 = mybir.dt.bfloat16
I32 = mybir.dt.int32
AF = mybir.ActivationFunctionType
ALU = mybir.AluOpType
AX = mybir.AxisListType

D = 1024
S = 2048
DEPTH = 4
NCH = 4
NT = 16
N_IN = 2866
NEG = -30000.0
EPS = 1e-6
OFF = dict(q=0, kc=384, vc=512, ks=640, vs=768, kw=896, vw=1024, gate=1152, cq=1170, ckv=1426,
           kpe=1554, rq=1586, rk=1906, rv=2226, rg=2546)

ENGS = ("pe", "act", "dve", "pool", "sp")


class T:
    __slots__ = ("ap", "name", "lw", "rd")

    def __init__(self, ap, name=""):
        self.ap = ap
        self.name = name
        self.lw = {}
        self.rd = {}

    def __getitem__(self, idx):
        return self.ap[idx]


class DmaGroup:
    def __init__(self, mk, name):
        self.name = name
        self.total = 0
        self.sem = mk._newsem("g_" + name)


class MK:
    def __init__(self, nc, stack):
        self.nc = nc
        self.stack = stack
        self.q = {e: [] for e in ENGS}
        self.cnt = {e: 0 for e in ENGS}
        self.waited = {e: {} for e in ENGS}
        self.esem = {e: self._newsem("c_" + e) for e in ENGS}
        self.groups = {}

    def _newsem(self, name):
        return self.stack.enter_context(self.nc.semaphore(name))

    def group(self, name):
        g = self.groups.get(name)
        if g is None:
            g = DmaGroup(self, name)
            self.groups[name] = g
        return g

    def sb(self, name, shape, dt):
        return self.stack.enter_context(self.nc.sbuf_tensor(name, list(shape), dt))

    def ps(self, name, shape, dt):
        return self.stack.enter_context(self.nc.psum_tensor(name, list(shape), dt))

    def _collect(self, eng, reads, writes):
        need = {}

        def add(d, same_ok):
            for k, v in d.items():
                if k == eng and not same_ok:
                    continue
                if need.get(k, 0) < v:
                    need[k] = v
        raw_same = eng != "pe"
        for t in reads:
            add(t.lw, raw_same)
        for t in writes:
            add(t.lw, False)
            add(t.rd, False)
        out = []
        w = self.waited[eng]
        for k, v in need.items():
            if isinstance(k, DmaGroup):
                v = k.total
            if w.get(k, 0) >= v:
                continue
            w[k] = v
            out.append((k, v))
        return out

    def _mark(self, key, val, reads, writes):
        for t in reads:
            if t.rd.get(key, 0) < val:
                t.rd[key] = val
        for t in writes:
            t.lw = {key: val}
            t.rd = {}

    def op(self, eng, fn, reads=(), writes=()):
        waits = self._collect(eng, reads, writes)
        self.cnt[eng] += 1
        self.q[eng].append((fn, waits, None))
        self._mark(eng, self.cnt[eng], reads, writes)

    def dma(self, eng, out_ap, in_ap, group, reads=(), writes=(), **kw):
        waits = self._collect(eng, reads, writes)
        g = self.group(group) if isinstance(group, str) else group
        g.total += 16
        self.q[eng].append((lambda e: e.dma_start(out=out_ap, in_=in_ap, **kw), waits, g))
        self._mark(g, g.total, reads, writes)

    def claim(self, tiles, others):
        for t in tiles:
            for u in others:
                if u is t:
                    continue
                for d in (u.lw, u.rd):
                    for k, v in d.items():
                        if t.rd.get(k, 0) < v:
                            t.rd[k] = v

    def wait_all_dma(self, eng="sp"):
        waits = []
        for g in self.groups.values():
            if g.total and self.waited[eng].get(g, 0) < g.total:
                waits.append((g, g.total))
                self.waited[eng][g] = g.total
        self.q[eng].append((None, waits, None))

    def _sem(self, k):
        return k.sem if isinstance(k, DmaGroup) else self.esem[k]

    def emit(self):
        nc = self.nc
        mk = self
        with nc.Block() as block:
            def run(engname, e):
                for fn, waits, g in mk.q[engname]:
                    if fn is None:
                        for k, v in waits:
                            e.wait_ge(mk._sem(k), v)
                        continue
                    for k, v in waits[1:]:
                        e.wait_ge(mk._sem(k), v)
                    ins = fn(e)
                    if waits:
                        k, v = waits[0]
                        ins._wait_ge(mk._sem(k), v)
                    if g is not None:
                        ins.then_inc(g.sem, 16)
                    else:
                        ins.then_inc(mk.esem[engname], 1)

            @block.sync
            def _(e):
                run("sp", e)

            @block.scalar
            def _(e):
                run("act", e)

            @block.vector
            def _(e):
                run("dve", e)

            @block.gpsimd
            def _(e):
                run("pool", e)

            @block.tensor
            def _(e):
                run("pe", e)


def _blob_layout():
    secs = [("ident", 128), ("tri", 128), ("winb", 128), ("cmpb", 2048), ("E", 2048), ("ovl1", 33),
            ("keep", 256), ("addt", 256), ("intraT", 640), ("rd", 640), ("wd", 5), ("cols", 8),
            ("sel9", 576), ("ones", 128), ("sel18", 1152), ("tfull", 640)]
    off = {}
    o = 0
    for n, w in secs:
        off[n] = (o, w)
        o += w
    return off, o


BLOB_OFF, BLOB_W = _blob_layout()


def make_blob():
    b = np.zeros((128, BLOB_W), np.float32)

    def sec(n):
        o, w = BLOB_OFF[n]
        return b[:, o:o + w]
    p = np.arange(128)
    sec("ident")[:] = np.eye(128, dtype=np.float32)
    sec("ones")[:] = 1.0
    sec("tri")[:] = np.where(p[:, None] <= p[None, :], 0.0, NEG)
    sec("winb")[:] = np.where(p[:, None] > p[None, :], 0.0, NEG)
    q = np.arange(S)
    cm = np.where((16 * p[:, None] + 31) <= q[None, :], 0.0, NEG)
    cm[127, :] = NEG
    sec("cmpb")[:] = cm
    e = np.zeros((128, S), np.float32)
    for r in range(32):
        e[64 + r, 64 * r:64 * r + 64] = -NEG
    sec("E")[:] = e
    n_cmp, n_sel = 127, 32
    cs = np.arange(n_cmp) * 16
    ce = cs + 32
    ss = np.arange(n_sel) * 64
    se = ss + 64
    ov = np.clip(np.minimum(ce[:, None], se[None, :]) - np.maximum(cs[:, None], ss[None, :]), 0, None) / 32.0
    o1 = sec("ovl1")
    o1[:127, :32] = ov
    o1[:127, 32] = 1.0
    keep = np.zeros((128, 8, 32), np.float32)
    addt = np.zeros((128, 8, 32), np.float32)
    blk = np.arange(32)
    for qt in range(8, 16):
        t = qt * 128 + p
        cur = t // 64
        valid = blk[None, :] <= cur[:, None]
        forced = (blk[None, :] == 0) | (blk[None, :] == cur[:, None]) | (blk[None, :] == cur[:, None] - 1)
        keep[:, qt - 8, :] = (valid & ~forced)
        addt[:, qt - 8, :] = np.where(valid & forced, 1e9, np.where(valid, 0.0, -1e30))
    sec("keep")[:] = keep.reshape(128, 256)
    sec("addt")[:] = addt.reshape(128, 256)
    lg = np.log(1.0 - 2.0 ** (-5.0 - np.arange(5, dtype=np.float64)))
    i = np.arange(128, dtype=np.float64)
    it = np.zeros((128, 5, 128), np.float64)
    for h in range(5):
        diff = i[None, :] - i[:, None]
        it[:, h, :] = np.where(diff >= 0, np.exp(np.maximum(diff, 0) * lg[h]), 0.0) * 0.125
    sec("intraT")[:] = it.reshape(128, 640)
    tf = np.zeros((128, 5, 128), np.float64)
    for h in range(5):
        tf[:, h, :] = np.exp((i[None, :] - i[:, None]) * lg[h]) * 0.125
    sec("tfull")[:] = tf.reshape(128, 640)
    rd = np.zeros((128, 5, 128), np.float64)
    for h in range(5):
        rd[:, h, :] = np.exp((i + 1.0) * lg[h])[None, :]
    sec("rd")[:] = rd.reshape(128, 640)
    wd = sec("wd")
    for h in range(5):
        wd[:, h] = np.exp((127.0 - i) * lg[h]) * 0.125
    cols = sec("cols")
    inv_p = 1.0 / (500000.0 ** (np.arange(0, 16, 2, dtype=np.float32) / 16)).astype(np.float32)
    inv_m = 1.0 / (500000.0 ** (np.arange(0, 32, 2, dtype=np.float32) / 32)).astype(np.float32)
    inv_r = 1.0 / (10000.0 ** (np.arange(0, 64, 2, dtype=np.float32) / 64)).astype(np.float32)
    tp = 2.0 * np.pi
    cols[:, 1] = tp
    cols[:, 3] = tp
    for r in range(16):
        cols[r, 0] = np.float64(inv_p[r % 8]) / tp
    cols[0:8, 1] = -tp
    for r in range(64, 96):
        cols[r, 0] = np.float64(inv_m[(r - 64) % 16]) / tp
    cols[64:80, 1] = -tp
    for r in range(128):
        cols[r, 2] = np.float64(inv_r[(r % 64) % 32]) / tp
        if (r % 64) < 32:
            cols[r, 3] = -tp
    s9 = np.zeros((128, 9, 64), np.float32)
    for j in range(9):
        s9[j, j, :] = 1.0
    sec("sel9")[:] = s9.reshape(128, 576)
    s18 = np.zeros((128, 18, 64), np.float32)
    for j in range(18):
        s18[j, j, :] = 1.0
    sec("sel18")[:] = s18.reshape(128, 1152)
    return b, lg


WNAMES = ["ln1_gain", "w_in", "cmp_pos_k", "cmp_w1_k", "cmp_w2_k", "cmp_pos_v", "cmp_w1_v", "cmp_w2_v",
          "mla_q_norm", "mla_w_uq", "mla_kv_norm", "mla_w_ukv", "ret_gn_gain", "w_out", "ln2_gain",
          "w_up", "w_down", "final_gain"]
WSHAPES = dict(ln1_gain=[4, 1024], w_in=[4, 1024, 2866], cmp_pos_k=[4, 32, 64], cmp_w1_k=[4, 2048, 128],
               cmp_w2_k=[4, 128, 64], cmp_pos_v=[4, 32, 64], cmp_w1_v=[4, 2048, 128], cmp_w2_v=[4, 128, 64],
               mla_q_norm=[4, 256], mla_w_uq=[4, 256, 480], mla_kv_norm=[4, 128], mla_w_ukv=[4, 128, 640],
               ret_gn_gain=[4, 5, 64], w_out=[4, 1024, 1024], ln2_gain=[4, 1024], w_up=[4, 1024, 4096],
               w_down=[4, 4096, 1024], final_gain=[1024])


class Ctx:
    pass


def build(n_layers=DEPTH, n_seq=2, mixers=("nsa", "mla", "ret"), dbg=False):
    nc = bass.Bass("TRN2", target_bir_lowering=False)
    C = Ctx()
    C.nc = nc
    C.dbg = dbg
    C.dbg_outs = {}
    dr = {}
    dr["x"] = nc.dram_tensor("x", [n_seq, S, D], F32, kind="ExternalInput").ap()
    dr["pos"] = nc.dram_tensor("pos", [n_seq, S], I32, kind="ExternalInput").ap()
    dr["blob"] = nc.dram_tensor("blob", [128, BLOB_W], F32, kind="ExternalInput").ap()
    for n in WNAMES:
        dr[n] = nc.dram_tensor(n, WSHAPES[n], F32, kind="ExternalInput").ap()
    dr["out"] = nc.dram_tensor("out", [n_seq, S, D], F32, kind="ExternalOutput").ap()
    C.dr = dr
    with ExitStack() as st:
        mk = MK(nc, st)
        C.mk = mk
        _setup(C)
        if "ret" in mixers:
            _tok0_prepass(C, n_seq, n_layers)
            dump(C, "s00", C.Ts00, C.s00[:, :], [128, 40])
        for s in range(n_seq):
            C.cur_seq = s
            _load_x(C, s)
            if s == 0:
                dump(C, "xT", C.TX[0], C.xT[:, :, 0:512], [128, 8, 512])
                dump(C, "xT3", C.TX[3], C.xT[:, :, 1536:2048], [128, 8, 512])
            if mixers:
                _rope_tables(C, s)
            for l in range(n_layers):
                if mixers:
                    _rmsnorm_main(C, C.g1, l)
                    _mixer(C, l, mixers)
                _rmsnorm_main(C, C.g2, l)
                if s == 0 and l == 0:
                    dump(C, "h2T", C.TH[0], C.hT[:, :, 0:512], [128, 8, 512], BF16)
                _ffn(C, l)
                if s == 0 and l == 0:
                    dump(C, "aT", C.TA, C.aT[:, :, :], [128, 32, 512], BF16)
                    dump(C, "x2T", C.TX[3], C.xT[:, :, 1536:2048], [128, 8, 512])
            _final(C, s)
        mk.wait_all_dma("sp")
        mk.emit()
    return nc, C


def _bsec(C, name):
    o, w = BLOB_OFF[name]
    return C.dr["blob"][:, o:o + w]


def _setup(C):
    mk = C.mk
    cg = "const"

    def cf32(name, sec, w):
        t = mk.sb(name, [128, w], F32)
        tt = T(t, name)
        mk.dma("sp", t[:], _bsec(C, sec), cg, writes=[tt])
        return tt

    def cbf(name, sec, w):
        t = mk.sb(name, [128, w], BF16)
        tt = T(t, name)
        mk.dma("pool", t[:], _bsec(C, sec), cg, writes=[tt])
        return tt
    C.identF = cf32("identF", "ident", 128)
    C.identB = cbf("identB", "ident", 128)
    C.onesB = cbf("onesB", "ones", 128)
    C.triB = cbf("triB", "tri", 128)
    C.winbB = cbf("winbB", "winb", 128)
    C.cmpbB = cbf("cmpbB", "cmpb", 2048)
    C.ovl1B = cbf("ovl1B", "ovl1", 33)
    C.keepF = cf32("keepF", "keep", 256)
    C.addtF = cf32("addtF", "addt", 256)
    C.intraTF = cf32("intraTF", "intraT", 640)
    C.tfullF = cf32("tfullF", "tfull", 640)
    C.colsF = cf32("colsF", "cols", 8)
    C.sel9B = cbf("sel9B", "sel9", 576)
    C.onesF = cf32("onesF", "ones", 128)
    t0 = mk.sb("s00", [128, 40], F32)
    C.s00 = t0
    C.Ts00 = T(t0, "s00")
    t = mk.sb("bigF", [128, 32], F32)
    C.bigF = T(t, "bigF")
    MEMSET(C, "pool", t[:], 1.0e15, [C.bigF])
    dr = C.dr

    def gain_cols(name, src_ap, shape):
        t = mk.sb(name, shape, F32)
        tt = T(t, name)
        mk.dma("sp", t[:], src_ap, cg, writes=[tt], allow_slow_non_contiguous=True)
        return tt
    C.g1 = gain_cols("g1", dr["ln1_gain"].rearrange("l (c p) -> p l c", p=128), [128, 4, 8])
    C.g2 = gain_cols("g2", dr["ln2_gain"].rearrange("l (c p) -> p l c", p=128), [128, 4, 8])
    C.gf = gain_cols("gf", dr["final_gain"].rearrange("(c p) -> p c", p=128), [128, 8])
    C.gq = gain_cols("gq", dr["mla_q_norm"].rearrange("l (c p) -> p l c", p=128), [128, 4, 2])
    C.gkv = gain_cols("gkv", dr["mla_kv_norm"].rearrange("l p -> p l"), [128, 4])
    C.gnT = gain_cols("gnT", dr["ret_gn_gain"].rearrange("l h d -> d (l h)"), [64, 20])
    C.xT = mk.sb("xT", [128, 8, S], F32)
    C.TX = [T(C.xT, "xT%d" % c) for c in range(NCH)]
    C.hT = mk.sb("hT", [128, 8, S], BF16)
    C.TH = [T(C.hT, "hT%d" % c) for c in range(NCH)]
    AW = 27648
    C.arena = mk.sb("arena", [128, AW], BF16)
    C.arena_tiles = []
    C.mixT = C.arena[:, 0:4 * S].rearrange("p (c t) -> p c t", c=4)
    C.TMIX = [T(C.mixT, "mix%d" % c) for c in range(4)]
    C.aT = C.arena[:, 0:32 * 512].rearrange("p (f t) -> p f t", f=32)
    C.TA = T(C.aT, "aT")
    C.WORK0 = 4 * S
    C.sq = C.arena[:, AW - 4096:AW].rearrange("p (k t) -> p k t", k=8)
    C.TSQ = T(C.sq, "sq")
    stf = C.arena[:, AW - 8192:AW - 4096].bitcast(F32)
    C.stage = [stf[:, 0:1024], stf[:, 1024:2048]]
    C.TST = [T(C.stage[i], "stage%d" % i) for i in range(2)]
    C.arena_tiles += C.TMIX + [C.TA, C.TSQ] + C.TST
    C.NW = 2
    C.wsl = [mk.sb("wsl%d" % i, [128, 4096], BF16) for i in range(C.NW)]
    C.TW = [T(C.wsl[i], "wsl%d" % i) for i in range(C.NW)]
    C.wi = 0
    C.t32 = [mk.sb("t32_%d" % i, [128, 512], F32) for i in range(4)]
    C.TT32 = [T(C.t32[i], "t32_%d" % i) for i in range(4)]
    C.t32i = 0
    C.pb = [mk.ps("pb%d" % i, [128, 512], F32) for i in range(8)]
    C.TP = [T(C.pb[i], "pb%d" % i) for i in range(8)]
    C.pbi = {}


def bank(C, cls, choices):
    i = C.pbi.get(cls, 0)
    C.pbi[cls] = i + 1
    b = choices[i % len(choices)]
    return C.pb[b], C.TP[b]


def tmp32(C):
    i = C.t32i % 4
    C.t32i += 1
    return C.t32[i], C.TT32[i]


def wslot(C):
    i = C.wi % C.NW
    C.wi += 1
    return C.wsl[i], C.TW[i], "wsl%d" % i


def dump(C, name, t, ap, shape, dt=F32):
    if not C.dbg:
        return
    d = C.nc.dram_tensor("dbg_" + name, list(shape), dt, kind="ExternalOutput").ap()
    C.dbg_outs[name] = d
    C.mk.dma("sp", d, ap, "dbg", reads=[t])


def claim(C, tiles):
    C.mk.claim(tiles, C.arena_tiles)
    for t in tiles:
        if t not in C.arena_tiles:
            C.arena_tiles.append(t)


def MM(C, o, l, r, st, sp, reads, writes):
    C.mk.op("pe", lambda e: e.matmul(o, l, r, start=st, stop=sp, skip_group_check=True), reads, writes)


def TR(C, o, i, ident, reads, writes):
    C.mk.op("pe", lambda e: e.transpose(out=o, in_=i, identity=ident), reads, writes)


def ACTF(C, o, i, func, reads, writes, **kw):
    C.mk.op("act", lambda e: e.activation(out=o, in_=i, func=func, **kw), reads, writes)


def TT(C, eng, o, a, b, op, reads, writes):
    C.mk.op(eng, lambda e: e.tensor_tensor(out=o, in0=a, in1=b, op=op), reads, writes)


def TS(C, eng, o, a, s1, s2, op0, op1, reads, writes):
    if s2 is None:
        C.mk.op(eng, lambda e: e.tensor_scalar(out=o, in0=a, scalar1=s1, scalar2=None, op0=op0), reads, writes)
    else:
        C.mk.op(eng, lambda e: e.tensor_scalar(out=o, in0=a, scalar1=s1, scalar2=s2, op0=op0, op1=op1), reads, writes)


def STT(C, eng, o, a, sc, b, op0, op1, reads, writes):
    C.mk.op(eng, lambda e: e.scalar_tensor_tensor(out=o, in0=a, scalar=sc, in1=b, op0=op0, op1=op1), reads, writes)


def CP(C, eng, o, i, reads, writes):
    if eng == "act":
        C.mk.op(eng, lambda e: e.copy(out=o, in_=i), reads, writes)
    else:
        C.mk.op(eng, lambda e: e.tensor_copy(out=o, in_=i), reads, writes)


def RECIP(C, o, i, reads, writes):
    C.mk.op("dve", lambda e: e.reciprocal(out=o, in_=i), reads, writes)


def MEMSET(C, eng, o, v, writes):
    C.mk.op(eng, lambda e: e.memset(o, v), (), writes)


def LDW(C, dst_ap, src_ap, group, writes, eng="pool", **kw):
    C.mk.dma(eng, dst_ap, src_ap, group, writes=writes, **kw)


def claim(C, tiles):
    C.mk.claim(tiles, C.arena_tiles)
    for t in tiles:
        if t not in C.arena_tiles:
            C.arena_tiles.append(t)


def _load_x(C, s):
    mk = C.mk
    x = C.dr["x"]
    claim(C, C.TST)
    for t in range(NT):
        stg, tst = C.stage[t % 2], C.TST[t % 2]
        mk.dma("sp", stg, x[s, 128 * t:128 * t + 128, :], "stage%d" % (t % 2), writes=[tst])
        c = t // 4
        for half in range(2):
            pb, tp = bank(C, "tr", [0, 1])
            for j in range(4):
                dc = 4 * half + j
                MM(C, pb[:, 128 * j:128 * j + 128], stg[:, 128 * dc:128 * dc + 128], C.identF[:], True, True,
                   [tst, C.identF], [tp])
            CP(C, "dve", C.xT[:, 4 * half:4 * half + 4, 128 * t:128 * t + 128],
               pb[:].rearrange("p (a b) -> p a b", a=4), [tp], [C.TX[c]])


def _rms(C, src, srcT, sq, sqT, nk, rows, gain_ap_fn, dst_fn, dstT, inv_n):
    ACTF(C, sq[0:rows, 0:nk, :], src, AF.Square, [srcT], [sqT])
    pb, tp = bank(C, "rms", [2, 3])
    for k in range(nk):
        MM(C, pb[:], C.onesB[0:rows, :], sq[0:rows, k, :], k == 0, k == nk - 1, [sqT, C.onesB], [tp])
    rt, trt = tmp32(C)
    ACTF(C, rt[:], pb[:], AF.Sqrt, [tp], [trt], bias=EPS, scale=inv_n)
    RECIP(C, rt[:], rt[:], [trt], [trt])
    for k in range(nk):
        STT(C, "dve", dst_fn(k), src[:, k, :], gain_ap_fn(k), rt[0:rows, :], ALU.mult, ALU.mult,
            [srcT, trt], [dstT])


def _rmsnorm_main(C, g, l):
    claim(C, [C.TSQ])
    for c in range(NCH):
        cs = slice(512 * c, 512 * c + 512)
        _rms(C, C.xT[:, :, cs], C.TX[c], C.sq, C.TSQ, 8, 128, lambda k, g=g, l=l: g[:, l, k:k + 1],
             lambda k, cs=cs: C.hT[:, k, cs], C.TH[c], 1.0 / D)


def _ffn(C, l):
    w_up = C.dr["w_up"]
    w_dn = C.dr["w_down"]
    if not hasattr(C, "ffn_tiles"):
        o = 0
        C.f_at, C.f_up, C.f_dn = [], [], []
        for nm, n, lst in (("fat", 2048, C.f_at), ("fat", 2048, C.f_at), ("fup", 4096, C.f_up), ("fup", 4096, C.f_up),
                           ("fdn", 4096, C.f_dn), ("fdn", 4096, C.f_dn)):
            ap = C.arena[:, o:o + n]
            o += n
            lst.append((ap, T(ap, "%s%d" % (nm, len(lst)))))
        C.ffn_tiles = [t for lst in (C.f_at, C.f_up, C.f_dn) for (_, t) in lst]
        C.fai = 0
    claim(C, C.ffn_tiles)
    for fb in range(8):
        up, tup = C.f_up[fb % 2]
        dn, tdn = C.f_dn[fb % 2]
        upv = up.rearrange("p (k n) -> p k n", k=8)
        dnv = dn.rearrange("p (k n) -> p k n", k=4)
        LDW(C, upv, w_up[l, :, 512 * fb:512 * fb + 512].rearrange("(k p) n -> p k n", p=128), "fup%d" % (fb % 2), [tup])
        LDW(C, dnv, w_dn[l, 512 * fb:512 * fb + 512, :].rearrange("(k p) n -> p k n", p=128), "fdn%d" % (fb % 2), [tdn])
        for c in range(NCH):
            cs = slice(512 * c, 512 * c + 512)
            at, tat = C.f_at[C.fai % 2]
            C.fai += 1
            atv = at.rearrange("p (k n) -> p k n", k=4)
            for fi in range(4):
                pb, tp = bank(C, "up", [4, 5])
                for k in range(8):
                    MM(C, pb[:], upv[:, k, 128 * fi:128 * fi + 128], C.hT[:, k, cs], k == 0, k == 7, [tup, C.TH[c]], [tp])
                r, tr = tmp32(C)
                ACTF(C, r[:], pb[:], AF.Relu, [tp], [tr])
                TT(C, "pool", atv[:, fi, :], r[:], r[:], ALU.mult, [tr], [tat])
            for d in range(8):
                pb, tp = bank(C, "dn", [0, 1, 2, 3])
                for kk in range(4):
                    MM(C, pb[:], dnv[:, kk, 128 * d:128 * d + 128], atv[:, kk, :], kk == 0, kk == 3, [tdn, tat], [tp])
                TT(C, "dve", C.xT[:, d, cs], pb[:], C.xT[:, d, cs], ALU.add, [tp, C.TX[c]], [C.TX[c]])


def _final(C, s):
    mk = C.mk
    out = C.dr["out"]
    claim(C, [C.TSQ] + C.TST)
    for c in range(NCH):
        cs = slice(512 * c, 512 * c + 512)
        ACTF(C, C.sq[:, :, :], C.xT[:, :, cs], AF.Square, [C.TX[c]], [C.TSQ])
        pb, tp = bank(C, "rms", [2, 3])
        for k in range(8):
            MM(C, pb[:], C.onesB[:, :], C.sq[:, k, :], k == 0, k == 7, [C.TSQ, C.onesB], [tp])
        rt, trt = tmp32(C)
        ACTF(C, rt[:], pb[:], AF.Sqrt, [tp], [trt], bias=EPS, scale=1.0 / D)
        RECIP(C, rt[:], rt[:], [trt], [trt])
        for k in range(8):
            STT(C, "dve", C.xT[:, k, cs], C.xT[:, k, cs], C.gf[:, k:k + 1], rt[:], ALU.mult, ALU.mult,
                [C.TX[c], trt, C.gf], [C.TX[c]])
        for tl in range(4):
            t = 4 * c + tl
            stg, tst = C.stage[t % 2], C.TST[t % 2]
            for half in range(2):
                pb, tp = bank(C, "tr", [0, 1])
                for j in range(4):
                    dc = 4 * half + j
                    MM(C, pb[:, 128 * j:128 * j + 128], C.xT[:, dc, 128 * t:128 * t + 128], C.identF[:], True, True,
                       [C.TX[c], C.identF], [tp])
                CP(C, "act", stg[:, 512 * half:512 * half + 512], pb[:], [tp], [tst])
            mk.dma("sp", out[s, 128 * t:128 * t + 128, :], stg, "stage%d" % (t % 2), reads=[tst])


PI = float(np.pi)


class WA:
    def __init__(self, C, start=None):
        self.C = C
        self.o = C.WORK0 if start is None else start
        self.tiles = []

    def bf(self, n, name):
        ap = self.C.arena[:, self.o:self.o + n]
        self.o += n
        assert self.o <= 27648, (name, self.o)
        t = T(ap, name)
        self.tiles.append(t)
        return ap, t

    def f32(self, n, name):
        ap, t = self.bf(2 * n, name)
        ap = ap.bitcast(F32)
        t.ap = ap
        return ap, t


def _rope_tables(C, s):
    mk = C.mk
    if not hasattr(C, "CA"):
        C.CA = mk.sb("CA", [128, S], BF16)
        C.SA = mk.sb("SA", [128, S], BF16)
        C.CR = mk.sb("CR", [128, S], BF16)
        C.SR = mk.sb("SR", [128, S], BF16)
        C.TCA, C.TSA, C.TCR, C.TSR = T(C.CA, "CA"), T(C.SA, "SA"), T(C.CR, "CR"), T(C.SR, "SR")
    wa = WA(C, 0)
    posI, tpi = wa.f32(S, "posI")
    posI = posI.bitcast(I32)
    posF, tpf = wa.f32(S, "posF")
    ang, tang = wa.f32(S, "ang")
    tt, ttt = wa.f32(S, "ropetmp")
    claim(C, wa.tiles)
    mk.dma("sp", posI, C.dr["pos"][s, :].partition_broadcast(128), "const", writes=[tpi])
    CP(C, "dve", posF, posI, [tpi], [tpf])
    MAGIC = 12582912.0
    for (ic, isg, ctab, tct, stab, tst_) in ((0, 1, C.CA, C.TCA, C.SA, C.TSA), (2, 3, C.CR, C.TCR, C.SR, C.TSR)):
        for which in (0, 1):
            ACTF(C, ang, posF, AF.Identity, [tpf, C.colsF], [tang], scale=C.colsF[:, ic:ic + 1],
                 bias=(0.25 if which == 0 else 0.0))
            TS(C, "dve", tt, ang, MAGIC, None, ALU.add, None, [tang], [ttt])
            TS(C, "dve", tt, tt, -MAGIC, None, ALU.add, None, [ttt], [ttt])
            TT(C, "dve", tt, ang, tt, ALU.subtract, [tang, ttt], [ttt])
            if which == 0:
                ACTF(C, ctab[:], tt, AF.Sin, [ttt], [tct], scale=2 * PI)
            else:
                ACTF(C, stab[:], tt, AF.Sin, [ttt, C.colsF], [tst_], scale=C.colsF[:, isg:isg + 1])


def attn_chunk(C, c, kt_fn, TK, krows, QT, TQ, v_fn, TV, kind, scale, PT, TPT):
    if kind == "causal":
        kts = list(range(0, 4 * c + 4))
    else:
        first = 4 * c - 1 if c >= 1 else 0
        kts = [first] + [j for j in range(max(0, 4 * c - 4), 4 * c + 4) if j != first]
    oa, toa = bank(C, "oa", [6, 7])
    for idx, j in enumerate(kts):
        d = j - 4 * c
        lo = max(d, 0)
        hi = 4 if kind == "causal" else min(j + 4, 4 * c + 3) - 4 * c + 1
        st, tst = bank(C, "st", [4, 5])
        segs = []
        plo, phi = lo, hi
        if d >= 0:
            segs.append((lo, lo + 1, C.triB))
            plo = lo + 1
        if kind == "window" and 0 <= j + 4 - 4 * c <= 3:
            segs.append((hi - 1, hi, C.winbB))
            phi = hi - 1
        if phi > plo:
            segs.append((plo, phi, None))
        for (a, b, bias) in segs:
            cs = slice(128 * a, 128 * b)
            MM(C, st[:, cs], kt_fn(j), QT[0:krows, cs], True, bias is None, [TK, TQ], [tst])
            if bias is not None:
                MM(C, st[:, cs], C.identB[:, :], bias[:, :], False, True, [C.identB, bias], [tst])
        cs = slice(128 * lo, 128 * hi)
        pi = C.pti % len(PT)
        C.pti += 1
        ACTF(C, PT[pi][:, cs], st[:, cs], AF.Exp, [tst], [TPT[pi]], scale=scale)
        MM(C, oa[:, cs], v_fn(j), PT[pi][:, cs], idx == 0, idx == len(kts) - 1, [TV, TPT[pi]], [toa])
    return oa, toa


def _mix_dst(C, feat0):
    ch = (feat0 // 128) % 4
    po = feat0 % 128
    return ch, po


def _mla_pre(C, l):
    mk = C.mk
    wa = WA(C)
    M = Ctx()
    C.M = M
    M.cqn, M.tcqn = wa.bf(2 * S, "cqn")
    M.cqn = M.cqn.rearrange("p (k t) -> p k t", k=2)
    M.ckvn, M.tckvn = wa.bf(S, "ckvn")
    M.kpeT, M.tkpe = wa.bf(S, "kpeT")
    M.KT, M.tKT = wa.bf(S, "mlaKT")
    M.V, M.tV = wa.bf(S, "mlaV")
    M.V = M.V.rearrange("p (t n) -> p t n", t=16)
    M.QT, M.tQT = wa.bf(512, "mlaQT")
    M.PT, M.tPT = [], []
    for i in range(2):
        a, t = wa.bf(512, "mlaPT%d" % i)
        M.PT.append(a)
        M.tPT.append(t)
    M.sq, M.tsq = wa.bf(1024, "mlasq")
    M.sq = M.sq.rearrange("p (k t) -> p k t", k=2)
    M.wkp, M.twkp = wa.bf(768, "wkpeP")
    M.wkp = M.wkp.rearrange("p (k n) -> p k n", k=8)
    M.wuq, M.twuq = wa.bf(960, "wuq")
    M.wuq = M.wuq.rearrange("p (k n) -> p k n", k=2)
    M.wuqP, M.twuqP = wa.bf(960, "wuqP")
    M.wuqP = M.wuqP.rearrange("p (k n) -> p k n", k=2)
    M.wukv, M.twukv = wa.bf(640, "wukv")
    claim(C, wa.tiles)
    C.pti = 0
    ws, tw, gname = wslot(C)
    wv = ws[:, 0:8 * 416].rearrange("p (k n) -> p k n", k=8)
    LDW(C, wv, C.dr["w_in"][l, :, OFF["cq"]:OFF["cq"] + 416].rearrange("(k p) n -> p k n", p=128), gname, [tw])
    LDW(C, M.wuq, C.dr["mla_w_uq"][l].rearrange("(k p) n -> p k n", p=128), "mlaw", [M.twuq])
    LDW(C, M.wukv, C.dr["mla_w_ukv"][l], "mlaw", [M.twukv])
    CP(C, "pool", M.wkp, wv[:, :, 320:416], [tw], [M.twkp])
    CP(C, "pool", M.wkp[:, :, 64:80], wv[:, :, 400:416], [tw], [M.twkp])
    CP(C, "pool", M.wkp[:, :, 80:96], wv[:, :, 384:400], [tw], [M.twkp])
    CP(C, "pool", M.wuqP, M.wuq, [M.twuq], [M.twuqP])
    wq4 = M.wuq.rearrange("p k (h n) -> p k h n", h=5)
    wq4P = M.wuqP.rearrange("p k (h n) -> p k h n", h=5)
    CP(C, "pool", wq4P[:, :, :, 64:80], wq4[:, :, :, 80:96], [M.twuq], [M.twuqP])
    CP(C, "pool", wq4P[:, :, :, 80:96], wq4[:, :, :, 64:80], [M.twuq], [M.twuqP])
    MEMSET(C, "pool", M.V[:, :, 64:128], 1.0, [M.tV])
    for c in range(NCH):
        cs = slice(512 * c, 512 * c + 512)
        r0, tr0 = tmp32(C)
        r1, tr1 = tmp32(C)
        for m, (r, tr) in enumerate(((r0, tr0), (r1, tr1))):
            pb, tp = bank(C, "proj", [0, 1])
            for k in range(8):
                MM(C, pb[:], wv[:, k, 128 * m:128 * m + 128], C.hT[:, k, cs], k == 0, k == 7, [tw, C.TH[c]], [tp])
            CP(C, "act", r[:], pb[:], [tp], [tr])
            ACTF(C, M.sq[:, m, :], pb[:], AF.Square, [tp], [M.tsq])
        pb, tp = bank(C, "rms", [2, 3])
        for m in range(2):
            MM(C, pb[:], C.onesB[:, :], M.sq[:, m, :], m == 0, m == 1, [M.tsq, C.onesB], [tp])
        rt, trt = tmp32(C)
        ACTF(C, rt[:], pb[:], AF.Sqrt, [tp], [trt], bias=EPS, scale=1.0 / 256)
        RECIP(C, rt[:], rt[:], [trt], [trt])
        for m, (r, tr) in enumerate(((r0, tr0), (r1, tr1))):
            STT(C, "dve", M.cqn[:, m, cs], r[:], C.gq[:, l, m:m + 1], rt[:], ALU.mult, ALU.mult,
                [tr, trt, C.gq], [M.tcqn])
        pb, tp = bank(C, "proj", [0, 1])
        for k in range(8):
            MM(C, pb[:], wv[:, k, 256:384], C.hT[:, k, cs], k == 0, k == 7, [tw, C.TH[c]], [tp])
        r, tr = tmp32(C)
        CP(C, "act", r[:], pb[:], [tp], [tr])
        ACTF(C, M.sq[:, 0, :], pb[:], AF.Square, [tp], [M.tsq])
        pb2, tp2 = bank(C, "rms", [2, 3])
        MM(C, pb2[:], C.onesB[:, :], M.sq[:, 0, :], True, True, [M.tsq, C.onesB], [tp2])
        rt, trt = tmp32(C)
        ACTF(C, rt[:], pb2[:], AF.Sqrt, [tp2], [trt], bias=EPS, scale=1.0 / 128)
        RECIP(C, rt[:], rt[:], [trt], [trt])
        STT(C, "dve", M.ckvn[:, cs], r[:], C.gkv[:, l:l + 1], rt[:], ALU.mult, ALU.mult, [tr, trt, C.gkv], [M.tckvn])
        pa, tpa = bank(C, "proj", [0, 1])
        for k in range(8):
            MM(C, pa[0:96, :], wv[:, k, 320:416], C.hT[:, k, cs], k == 0, k == 7, [tw, C.TH[c]], [tpa])
        pbb, tpb = bank(C, "proj", [0, 1])
        for k in range(8):
            MM(C, pbb[0:96, :], M.wkp[:, k, :], C.hT[:, k, cs], k == 0, k == 7, [M.twkp, C.TH[c]], [tpb])
        ra, tra = tmp32(C)
        rb, trb = tmp32(C)
        TT(C, "dve", ra[64:96, :], pa[64:96, :], C.CA[64:96, cs], ALU.mult, [tpa, C.TCA], [tra])
        TT(C, "dve", rb[64:96, :], pbb[64:96, :], C.SA[64:96, cs], ALU.mult, [tpb, C.TSA], [trb])
        TT(C, "dve", M.kpeT[64:96, cs], ra[64:96, :], rb[64:96, :], ALU.add, [tra, trb], [M.tkpe])


def _mla_head(C, l, h):
    M = C.M
    for c in range(NCH):
        cs = slice(512 * c, 512 * c + 512)
        pb, tp = bank(C, "proj", [0, 1])
        MM(C, pb[0:64, :], M.wukv[:, 128 * h:128 * h + 64], M.ckvn[:, cs], True, True, [M.twukv, M.tckvn], [tp])
        CP(C, "act", M.KT[0:64, cs], pb[0:64, :], [tp], [M.tKT])
    CP(C, "pool", M.KT[64:96, :], M.kpeT[64:96, :], [M.tkpe], [M.tKT])
    for t4 in range(4):
        pb, tp = bank(C, "proj", [0, 1])
        for j in range(4):
            t = 4 * t4 + j
            MM(C, pb[:, 64 * j:64 * j + 64], M.ckvn[:, 128 * t:128 * t + 128], M.wukv[:, 128 * h + 64:128 * h + 128],
               True, True, [M.tckvn, M.twukv], [tp])
        CP(C, "dve", M.V[:, 4 * t4:4 * t4 + 4, 0:64], pb[:, 0:256].rearrange("p (a b) -> p a b", a=4), [tp], [M.tV])
    ch, po = _mix_dst(C, 384 + 64 * h)
    for c in range(NCH):
        cs = slice(512 * c, 512 * c + 512)
        pa, tpa = bank(C, "proj", [0, 1])
        for m in range(2):
            MM(C, pa[0:96, :], M.wuq[:, m, 96 * h:96 * h + 96], M.cqn[:, m, cs], m == 0, m == 1,
               [M.twuq, M.tcqn], [tpa])
        pbb, tpb = bank(C, "proj", [0, 1])
        for m in range(2):
            MM(C, pbb[0:96, :], M.wuqP[:, m, 96 * h:96 * h + 96], M.cqn[:, m, cs], m == 0, m == 1,
               [M.twuqP, M.tcqn], [tpb])
        CP(C, "act", M.QT[0:64, :], pa[0:64, :], [tpa], [M.tQT])
        ra, tra = tmp32(C)
        rb, trb = tmp32(C)
        TT(C, "dve", ra[64:96, :], pa[64:96, :], C.CA[64:96, cs], ALU.mult, [tpa, C.TCA], [tra])
        TT(C, "dve", rb[64:96, :], pbb[64:96, :], C.SA[64:96, cs], ALU.mult, [tpb, C.TSA], [trb])
        TT(C, "dve", M.QT[64:96, :], ra[64:96, :], rb[64:96, :], ALU.add, [tra, trb], [M.tQT])
        oa, toa = attn_chunk(C, c, lambda j: M.KT[0:96, 128 * j:128 * j + 128], M.tKT, 96, M.QT, M.tQT,
                             lambda j: M.V[:, j, :], M.tV, "causal", 96 ** -0.5, M.PT, M.tPT)
        rc, trc = tmp32(C)
        RECIP(C, rc[0:64, :], oa[64:128, :], [toa], [trc])
        TT(C, "dve", C.mixT[po:po + 64, ch, cs], oa[0:64, :], rc[0:64, :], ALU.mult, [toa, trc], [C.TMIX[ch]])


def _wout(C, l, phase):
    w_out = C.dr["w_out"]
    for dq in range(4):
        ws, tw, gname = wslot(C)
        wv = ws[:, 0:1024].rearrange("p (k n) -> p k n", k=4)
        LDW(C, wv, w_out[l, 512 * phase:512 * phase + 512, 256 * dq:256 * dq + 256].rearrange(
            "(k p) n -> p k n", p=128), gname, [tw])
        for dd in range(2):
            d = 2 * dq + dd
            for c in range(NCH):
                cs = slice(512 * c, 512 * c + 512)
                pb, tp = bank(C, "proj", [0, 1])
                for k in range(4):
                    MM(C, pb[:], wv[:, k, 128 * dd:128 * dd + 128], C.mixT[:, k, cs], k == 0, k == 3,
                       [tw, C.TMIX[k]], [tp])
                TT(C, "dve", C.xT[:, d, cs], pb[:], C.xT[:, d, cs], ALU.add, [tp, C.TX[c]], [C.TX[c]])


def _tok0_prepass(C, n_seq, n_layers):
    mk = C.mk
    AR = C.arena[:].bitcast(F32)
    st = {"o": 0}
    tiles = []

    def alloc(n, name):
        ap = AR[:, st["o"]:st["o"] + n]
        st["o"] += n
        t = T(ap, name)
        tiles.append(t)
        return ap, t
    slot, tslot = [], []
    for i in range(2):
        a_, t_ = alloc(4096, "pslot%d" % i)
        slot.append(a_)
        tslot.append(t_)
    sel, tsel = alloc(1152, "sel18")
    X0, tX = alloc(16, "X0")
    X0 = X0.rearrange("p (k s) -> p k s", s=2)
    Hh, tH = alloc(16, "H0")
    Hh = Hh.rearrange("p (k s) -> p k s", s=2)
    SQ, tSQ = alloc(64, "SQ0")
    SQ = SQ.rearrange("p (k s) -> p k s", s=2)
    Aa, tA = alloc(64, "A0")
    Aa = Aa.rearrange("p (k s) -> p k s", s=2)
    PRn, tPRn = alloc(8, "PRn")
    PRn = PRn.rearrange("p (k s) -> p k s", s=2)
    SG, tSG = alloc(2, "SG")
    CK, tCK = alloc(2, "CK")
    CKN, tCKN = alloc(2, "CKN")
    R, tR = alloc(40, "R0")
    R = R.rearrange("p (k s) -> p k s", s=2)
    O, tO = alloc(10, "O0")
    O = O.rearrange("p (k s) -> p k s", s=2)
    CEN, tCEN = alloc(10, "CEN")
    CEN = CEN.rearrange("p (k s) -> p k s", s=2)
    RS_, tRS = alloc(10, "RSTD")
    RS_ = RS_.rearrange("p (k s) -> p k s", s=2)
    SGG, tSGG = alloc(10, "SGG")
    SGG = SGG.rearrange("p (k s) -> p k s", s=2)
    MIX, tMIX = alloc(32, "MIX0")
    MIX = MIX.rearrange("p (k s) -> p k s", s=2)
    sm, tsm = alloc(4, "psmall")
    tmp, ttmp = alloc(24, "ptmp")
    tmp3 = tmp.rearrange("p (k s) -> p k s", s=2)
    claim(C, tiles)
    C.wpi = 0
    dr = C.dr

    def wload(src_ap, view_fn):
        i = C.wpi % 2
        C.wpi += 1
        v = view_fn(slot[i])
        mk.dma("sp", v, src_ap, "pslot%d" % i, writes=[tslot[i]])
        return v, tslot[i]

    def mvg(lhs_fn, rhs_fn, nk, out, tout, reads):
        for k in range(nk):
            MM(C, out, lhs_fn(k), rhs_fn(k), k == 0, k == nk - 1, reads, [tout])

    def rms0(X, tXx, nk, rows, gain_fn, out, tout, n):
        TT(C, "dve", SQ[0:rows, 0:nk, :], X[0:rows, 0:nk, :], X[0:rows, 0:nk, :], ALU.mult, [tXx], [tSQ])
        mvg(lambda k: C.onesF[0:rows, 0:rows], lambda k: SQ[0:rows, k, :], nk, C.pb[7][0:rows, 0:2], C.TP[7],
            [C.onesF, tSQ])
        ACTF(C, sm[0:rows, 0:2], C.pb[7][0:rows, 0:2], AF.Sqrt, [C.TP[7]], [tsm], bias=EPS, scale=1.0 / n)
        RECIP(C, sm[0:rows, 0:2], sm[0:rows, 0:2], [tsm], [tsm])
        for k in range(nk):
            STT(C, "dve", out[0:rows, k, :], X[0:rows, k, :], gain_fn(k), sm[0:rows, 0:2], ALU.mult, ALU.mult,
                [tXx, tsm], [tout])

    MEMSET(C, "pool", AR[:, 8192 + 1152:st["o"]], 0.0, tiles[3:])
    o_, w_ = BLOB_OFF["sel18"]
    mk.dma("sp", sel, dr["blob"][:, o_:o_ + w_], "pconst", writes=[tsel])
    for s_ in range(n_seq):
        mk.dma("sp", X0[:, :, s_], dr["x"][s_, 0, :].rearrange("(c p) -> p c", p=128), "pconst", writes=[tX],
               allow_slow_non_contiguous=True)
    ps1, tp1 = C.pb[0], C.TP[0]
    ps2, tp2 = C.pb[1], C.TP[1]
    ps3, tp3 = C.pb[2], C.TP[2]
    psS, tpS = C.pb[3], C.TP[3]
    psD, tpD = C.pb[4], C.TP[4]
    psU, tpU = C.pb[5], C.TP[5]
    psV, tpV = C.pb[6], C.TP[6]
    win = dr["w_in"]
    for l in range(n_layers):
        rms0(X0, tX, 8, 128, lambda k: C.g1[:, l, k:k + 1], Hh, tH, float(D))
        hk = lambda k: Hh[:, k, :]
        wv, tw = wload(win[l, :, OFF["vs"]:OFF["vs"] + 128].rearrange("(k p) n -> p k n", p=128),
                       lambda sl: sl[:, 0:1024].rearrange("p (k n) -> p k n", k=8))
        for g in range(2):
            mvg(lambda k, g=g: wv[:, k, 64 * g:64 * g + 64], hk, 8, ps1[0:64, 2 * g:2 * g + 2], tp1, [tw, tH])
        wv, tw = wload(win[l, :, OFF["vw"]:OFF["vw"] + 146].rearrange("(k p) n -> p k n", p=128),
                       lambda sl: sl[:, 0:8 * 146].rearrange("p (k n) -> p k n", k=8))
        for g in range(2):
            mvg(lambda k, g=g: wv[:, k, 64 * g:64 * g + 64], hk, 8, ps1[0:64, 4 + 2 * g:6 + 2 * g], tp1, [tw, tH])
        mvg(lambda k: wv[:, k, 128:146], hk, 8, ps1[0:18, 8:10], tp1, [tw, tH])
        CP(C, "dve", PRn[0:64, :, :], ps1[0:64, 0:8].rearrange("p (k s) -> p k s", s=2), [tp1], [tPRn])
        CP(C, "dve", SG[0:18, :], ps1[0:18, 8:10], [tp1], [tSG])
        ACTF(C, SG[0:18, :], SG[0:18, :], AF.Exp, [tSG], [tSG], scale=-1.0)
        TS(C, "dve", SG[0:18, :], SG[0:18, :], 1.0, None, ALU.add, None, [tSG], [tSG])
        RECIP(C, SG[0:18, :], SG[0:18, :], [tSG], [tSG])
        for h in range(6):
            for b in (1, 2):
                j = 3 * h + b
                ci = 2 * (2 * h + b - 1)
                MM(C, ps2[0:64, ci:ci + 2], sel[0:18, 64 * j:64 * j + 64], SG[0:18, :], True, True, [tsel, tSG], [tp2])
        for h in range(6):
            g = h // 3
            c1 = 2 * (2 * h)
            TT(C, "dve", tmp3[0:64, 0, :], ps2[0:64, c1:c1 + 2], PRn[0:64, g, :], ALU.mult, [tp2, tPRn], [ttmp])
            TT(C, "dve", tmp3[0:64, 1, :], ps2[0:64, c1 + 2:c1 + 4], PRn[0:64, 2 + g, :], ALU.mult, [tp2, tPRn], [ttmp])
            TT(C, "dve", MIX[0:64, h, :], tmp3[0:64, 0, :], tmp3[0:64, 1, :], ALU.add, [ttmp], [tMIX])
        wv, tw = wload(win[l, :, OFF["ckv"]:OFF["ckv"] + 128].rearrange("(k p) n -> p k n", p=128),
                       lambda sl: sl[:, 0:1024].rearrange("p (k n) -> p k n", k=8))
        mvg(lambda k: wv[:, k, :], hk, 8, ps1[:, 10:12], tp1, [tw, tH])
        CP(C, "dve", CK[:, :], ps1[:, 10:12], [tp1], [tCK])
        CK3 = CK.rearrange("p (k s) -> p k s", s=2)
        CKN3 = CKN.rearrange("p (k s) -> p k s", s=2)
        rms0(CK3, tCK, 1, 128, lambda k: C.gkv[:, l:l + 1], CKN3, tCKN, 128.0)
        wv, tw = wload(dr["mla_w_ukv"][l], lambda sl: sl[:, 0:640])
        for h in range(5):
            MM(C, psV[0:64, 2 * h:2 * h + 2], wv[:, 128 * h + 64:128 * h + 128], CKN[:, 0:2], True, True, [tw, tCKN], [tpV])
        CP(C, "dve", MIX[0:64, 6:11, :], psV[0:64, 0:10].rearrange("p (k s) -> p k s", s=2), [tpV], [tMIX])
        for piece in range(3):
            n = 512 if piece < 2 else 256
            c0 = OFF["rq"] + 512 * piece
            wv, tw = wload(win[l, :, c0:c0 + n].rearrange("(k p) n -> p k n", p=128),
                           lambda sl, n=n: sl[:, 0:8 * n].rearrange("p (k n) -> p k n", k=8))
            for gi in range(8 * piece, 8 * piece + n // 64):
                off = 64 * gi - 512 * piece
                mvg(lambda k, off=off: wv[:, k, off:off + 64], hk, 8, ps3[0:64, 2 * gi:2 * gi + 2], tp3, [tw, tH])
        CP(C, "dve", R[0:64, :, :], ps3[0:64, 0:40].rearrange("p (k s) -> p k s", s=2), [tp3], [tR])
        TT(C, "dve", O[0:64, :, :], R[0:64, 0:5, :], R[0:64, 5:10, :], ALU.mult, [tR], [tO])
        MM(C, psS[0:64, 0:10], C.onesF[0:64, 0:64], O[0:64, :, :].rearrange("p k s -> p (k s)"), True, True,
           [C.onesF, tO], [tpS])
        CP(C, "dve", C.s00[0:64, 10 * l:10 * l + 10], psS[0:64, 0:10], [tpS], [C.Ts00])
        TT(C, "dve", O[0:64, :, :], psS[0:64, 0:10].rearrange("p (k s) -> p k s", s=2), R[0:64, 10:15, :], ALU.mult,
           [tpS, tR], [tO])
        TS(C, "dve", O[0:64, :, :], O[0:64, :, :], 0.125, None, ALU.mult, None, [tO], [tO])
        MM(C, psS[0:64, 10:20], C.onesF[0:64, 0:64], O[0:64, :, :].rearrange("p k s -> p (k s)"), True, True,
           [C.onesF, tO], [tpS])
        STT(C, "dve", CEN[0:64, :, :], psS[0:64, 10:20].rearrange("p (k s) -> p k s", s=2), -1.0 / 64, O[0:64, :, :],
            ALU.mult, ALU.add, [tpS, tO], [tCEN])
        TT(C, "dve", O[0:64, :, :], CEN[0:64, :, :], CEN[0:64, :, :], ALU.mult, [tCEN], [tO])
        MM(C, C.pb[7][0:64, 2:12], C.onesF[0:64, 0:64], O[0:64, :, :].rearrange("p k s -> p (k s)"), True, True,
           [C.onesF, tO], [C.TP[7]])
        ACTF(C, RS_[0:64, :, :], C.pb[7][0:64, 2:12].rearrange("p (k s) -> p k s", s=2), AF.Sqrt, [C.TP[7]], [tRS],
             bias=EPS, scale=1.0 / 64)
        RECIP(C, RS_[0:64, :, :], RS_[0:64, :, :], [tRS], [tRS])
        ACTF(C, SGG[0:64, :, :], R[0:64, 15:20, :], AF.Exp, [tR], [tSGG], scale=-1.0)
        TS(C, "dve", SGG[0:64, :, :], SGG[0:64, :, :], 1.0, None, ALU.add, None, [tSGG], [tSGG])
        RECIP(C, SGG[0:64, :, :], SGG[0:64, :, :], [tSGG], [tSGG])
        TT(C, "dve", SGG[0:64, :, :], SGG[0:64, :, :], R[0:64, 15:20, :], ALU.mult, [tSGG, tR], [tSGG])
        for h in range(5):
            STT(C, "dve", CEN[0:64, h, :], CEN[0:64, h, :], C.gnT[:, 5 * l + h:5 * l + h + 1], RS_[0:64, h, :],
                ALU.mult, ALU.mult, [tCEN, tRS, C.gnT], [tCEN])
        TT(C, "dve", MIX[0:64, 11:16, :], CEN[0:64, :, :], SGG[0:64, :, :], ALU.mult, [tCEN, tSGG], [tMIX])
        for q4 in range(4):
            wv, tw = wload(dr["w_out"][l, :, 256 * q4:256 * q4 + 256].rearrange("(j p) n -> p j n", p=64),
                           lambda sl: sl[0:64, :].rearrange("p (j n) -> p j n", j=16))
            for oo in range(2):
                oc = 2 * q4 + oo
                mvg(lambda j, oo=oo: wv[:, j, 128 * oo:128 * oo + 128], lambda j: MIX[0:64, j, :], 16,
                    psD[:, 2 * oc:2 * oc + 2], tpD, [tw, tMIX])
        TT(C, "dve", X0, X0, psD[:, 0:16].rearrange("p (k s) -> p k s", s=2), ALU.add, [tpD, tX], [tX])
        rms0(X0, tX, 8, 128, lambda k: C.g2[:, l, k:k + 1], Hh, tH, float(D))
        for fb in range(8):
            wv, tw = wload(dr["w_up"][l, :, 512 * fb:512 * fb + 512].rearrange("(k p) n -> p k n", p=128),
                           lambda sl: sl[:, :].rearrange("p (k n) -> p k n", k=8))
            for fi in range(4):
                f = 4 * fb + fi
                mvg(lambda k, fi=fi: wv[:, k, 128 * fi:128 * fi + 128], hk, 8, psU[:, 2 * f:2 * f + 2], tpU, [tw, tH])
        ACTF(C, Aa, psU[:, 0:64].rearrange("p (k s) -> p k s", s=2), AF.Relu, [tpU], [tA])
        TT(C, "dve", Aa, Aa, Aa, ALU.mult, [tA], [tA])
        MEMSET(C, "dve", psD[:, 16:32], 0.0, [tpD])
        for fs in range(8):
            wv, tw = wload(dr["w_down"][l, 512 * fs:512 * fs + 512, :].rearrange("(k p) n -> p k n", p=128),
                           lambda sl: sl[:, :].rearrange("p (k n) -> p k n", k=4))
            for oc in range(8):
                for kk in range(4):
                    MM(C, psD[:, 16 + 2 * oc:18 + 2 * oc], wv[:, kk, 128 * oc:128 * oc + 128], Aa[:, 4 * fs + kk, :],
                       False, False, [tw, tA], [tpD])
        TT(C, "dve", X0, X0, psD[:, 16:32].rearrange("p (k s) -> p k s", s=2), ALU.add, [tpD, tX], [tX])


def _tok0_h(C, l):
    h0, th = C.h0, C.Th0
    x0 = C.xT[:, :, 0]
    ACTF(C, h0[:, 11:19], x0, AF.Square, [C.TX[0]], [th], accum_out=h0[:, 9:10])
    pb, tp = bank(C, "rms", [2, 3])
    MM(C, pb[:, 0:2], C.onesF[:, :], h0[:, 9:11], True, True, [C.onesF, th], [tp])
    ACTF(C, h0[:, 19:20], pb[:, 0:1], AF.Sqrt, [tp], [th], bias=EPS, scale=1.0 / D)
    RECIP(C, h0[:, 19:20], h0[:, 19:20], [th], [th])
    STT(C, "dve", h0[:, 0:8], x0, h0[:, 19:20], C.g1[:, l, :], ALU.mult, ALU.mult, [C.TX[0], th, C.g1], [th])


def _mixer(C, l, mixers):
    claim(C, C.TMIX)
    if "nsa" in mixers:
        _nsa(C, l, 0)
        _nsa(C, l, 1)
    else:
        for ch in range(3):
            MEMSET(C, "pool", C.mixT[:, ch, :], 0.0, [C.TMIX[ch]])
    if "mla" in mixers:
        _mla_pre(C, l)
        _mla_head(C, l, 0)
        _mla_head(C, l, 1)
    else:
        MEMSET(C, "pool", C.mixT[:, 3, :], 0.0, [C.TMIX[3]])
    _wout(C, l, 0)
    if "mla" in mixers:
        for h in (2, 3, 4):
            _mla_head(C, l, h)
    else:
        MEMSET(C, "pool", C.mixT[:, 0, :], 0.0, [C.TMIX[0]])
        MEMSET(C, "pool", C.mixT[0:64, 1, :], 0.0, [C.TMIX[1]])
    if "ret" in mixers:
        _ret(C, l)
    else:
        MEMSET(C, "pool", C.mixT[64:128, 1, :], 0.0, [C.TMIX[1]])
        for ch in (2, 3):
            MEMSET(C, "pool", C.mixT[:, ch, :], 0.0, [C.TMIX[ch]])
    _wout(C, l, 1)


def proj_rope(C, c, wa_fn, wb_fn, M, wts, dst, tdst, tabC, tCt, tabS, tSt, nr=16):
    cs = slice(512 * c, 512 * c + 512)
    pa, tpa = bank(C, "proj", [0, 1])
    for k in range(8):
        MM(C, pa[0:M, :], wa_fn(k), C.hT[:, k, cs], k == 0, k == 7, wts + [C.TH[c]], [tpa])
    pb, tpb = bank(C, "proj", [0, 1])
    for k in range(8):
        MM(C, pb[0:nr, :], wb_fn(k), C.hT[:, k, cs], k == 0, k == 7, wts + [C.TH[c]], [tpb])
    CP(C, "dve", dst[0:M, :], pa[0:M, :], [tpa], [tdst])
    ra, tra = tmp32(C)
    rb, trb = tmp32(C)
    TT(C, "dve", ra[0:nr, :], pa[0:nr, :], tabC[0:nr, cs], ALU.mult, [tpa, tCt], [tra])
    TT(C, "dve", rb[0:nr, :], pb[0:nr, :], tabS[0:nr, cs], ALU.mult, [tpb, tSt], [trb])
    TT(C, "pool", dst[0:nr, :], ra[0:nr, :], rb[0:nr, :], ALU.add, [tra, trb], [tdst])


def _nsa(C, l, g):
    mk = C.mk
    win = C.dr["w_in"]
    wa = WA(C)
    KVc, tKVc = wa.bf(S, "KVc")
    KTs, tKTs = wa.bf(S, "KTs")
    KTw, tKTw = wa.bf(S, "KTw")
    Vs, tVs = wa.bf(S, "Vs")
    Vs = Vs.rearrange("p (t n) -> p t n", t=16)
    Vw, tVw = wa.bf(S, "Vw")
    Vw = Vw.rearrange("p (t n) -> p t n", t=16)
    QT, tQT = [], []
    for r in range(3):
        a_, t_ = wa.bf(512, "nQT%d" % r)
        QT.append(a_)
        tQT.append(t_)
    PT, tPT = [], []
    for i in range(2):
        a_, t_ = wa.bf(512, "nPT%d" % i)
        PT.append(a_)
        tPT.append(t_)
    acc, tacc = [], []
    for r in range(3):
        a_, t_ = wa.f32(512, "nacc%d" % r)
        acc.append(a_)
        tacc.append(t_)
    kcT, tkcT = wa.bf(128, "kcT")
    Vc, tVc = wa.bf(128, "Vcaug")
    hid, thid = wa.bf(128, "hid")
    Wn2, tWn2 = wa.bf(640, "Wn2")
    Wn2 = Wn2.rearrange("p (k n) -> p k n", k=8)
    WP, tWP = wa.bf(768, "nWP")
    WP = WP.rearrange("p (k n) -> p k n", k=8)
    glog, tglog = wa.bf(512, "glog")
    selb, tselb = wa.bf(128, "selb")
    w2, tw2 = wa.bf(128, "w2kv")
    posT, tposT = wa.bf(64, "posT")
    zt, tzt = wa.f32(128, "gz")
    ut, tut = wa.f32(128, "gu")
    sm, tsm = wa.f32(128, "nsmall")
    bsb, tbsb = wa.f32(4, "nbias")
    claim(C, wa.tiles)
    C.pti = 0
    ws, tw, gname = wslot(C)
    Wn = ws[:].rearrange("p (k n) -> p k n", k=8)
    cols = [(OFF["q"] + 192 * g, 192, 0), (OFF["kc"] + 64 * g, 64, 192), (OFF["vc"] + 64 * g, 64, 256),
            (OFF["ks"] + 64 * g, 64, 320), (OFF["kw"] + 64 * g, 64, 384), (OFF["vs"] + 64 * g, 64, 448)]
    for (src, n, dst) in cols:
        LDW(C, Wn[:, :, dst:dst + n], win[l, :, src:src + n].rearrange("(k p) n -> p k n", p=128), gname, [tw])
    LDW(C, Wn2[:, :, 0:64], win[l, :, OFF["vw"] + 64 * g:OFF["vw"] + 64 * g + 64].rearrange("(k p) n -> p k n", p=128),
        "nsaw", [tWn2])
    LDW(C, Wn2[:, :, 64:73], win[l, :, OFF["gate"] + 9 * g:OFF["gate"] + 9 * g + 9].rearrange(
        "(k p) n -> p k n", p=128), "nsaw", [tWn2], allow_slow_non_contiguous=True)
    ws1, tw1, gname1 = wslot(C)
    w1 = ws1[:].rearrange("p (l h) -> p l h", l=32)
    LDW(C, w1[0:64], C.dr["cmp_w1_k"][l].rearrange("(l d) h -> d l h", d=64), gname1, [tw1])
    LDW(C, w1[64:128], C.dr["cmp_w1_v"][l].rearrange("(l d) h -> d l h", d=64), gname1, [tw1])
    LDW(C, w2[:, 0:64], C.dr["cmp_w2_k"][l], "nsaw", [tw2])
    LDW(C, w2[:, 64:128], C.dr["cmp_w2_v"][l], "nsaw", [tw2])
    MEMSET(C, "pool", posT, 0.0, [tposT])
    LDW(C, posT[0:64, 0:32], C.dr["cmp_pos_k"][l].rearrange("l d -> d l"), "nsaw", [tposT],
        allow_slow_non_contiguous=True)
    LDW(C, posT[64:128, 0:32], C.dr["cmp_pos_v"][l].rearrange("l d -> d l"), "nsaw", [tposT],
        allow_slow_non_contiguous=True)
    o, w_ = BLOB_OFF["E"]
    LDW(C, KTs[64:96, :], C.dr["blob"][64:96, o:o + w_], "nsaw", [tKTs])
    for i, off in enumerate((0, 64, 128, 192, 320, 384)):
        CP(C, "pool", WP[:, :, 16 * i:16 * i + 8], Wn[:, :, off + 8:off + 16], [tw], [tWP])
        CP(C, "pool", WP[:, :, 16 * i + 8:16 * i + 16], Wn[:, :, off:off + 8], [tw], [tWP])
    MEMSET(C, "pool", Vs[:, :, 64:128], 1.0, [tVs])
    MEMSET(C, "pool", Vw[:, :, 64:128], 1.0, [tVw])
    MEMSET(C, "pool", Vc[:, 64:128], 1.0, [tVc])
    MEMSET(C, "pool", Vc[:, 0:64], 0.0, [tVc])
    MEMSET(C, "pool", selb, 0.0, [tselb])
    for c in range(NCH):
        cs = slice(512 * c, 512 * c + 512)
        proj_rope(C, c, lambda k: Wn[:, k, 192:320], lambda k: WP[:, k, 48:64], 128, [tw, tWP], KVc[:, cs], tKVc,
                  C.CA, C.TCA, C.SA, C.TSA)
        proj_rope(C, c, lambda k: Wn[:, k, 320:384], lambda k: WP[:, k, 64:80], 64, [tw, tWP], KTs[:, cs], tKTs,
                  C.CA, C.TCA, C.SA, C.TSA)
        proj_rope(C, c, lambda k: Wn[:, k, 384:448], lambda k: WP[:, k, 80:96], 64, [tw, tWP], KTw[:, cs], tKTw,
                  C.CA, C.TCA, C.SA, C.TSA)
        pv, tpv = bank(C, "proj", [0, 1])
        for tl in range(4):
            t = 4 * c + tl
            for k in range(8):
                MM(C, pv[:, 128 * tl:128 * tl + 64], C.hT[:, k, 128 * t:128 * t + 128], Wn[:, k, 448:512], k == 0, k == 7,
                   [tw, C.TH[c]], [tpv])
            for k in range(8):
                MM(C, pv[:, 128 * tl + 64:128 * tl + 128], C.hT[:, k, 128 * t:128 * t + 128], Wn2[:, k, 0:64], k == 0,
                   k == 7, [tWn2, C.TH[c]], [tpv])
        pv3 = pv[:].rearrange("p (a b) -> p a b", a=4)
        CP(C, "act", Vs[:, 4 * c:4 * c + 4, 0:64], pv3[:, :, 0:64], [tpv], [tVs])
        CP(C, "act", Vw[:, 4 * c:4 * c + 4, 0:64], pv3[:, :, 64:128], [tpv], [tVw])
    for kind in (0, 1):
        r0 = 64 * kind
        pbias, tpbias = bank(C, "oa", [6, 7])
        for li in range(32):
            MM(C, pbias[:, 0:2], w1[r0:r0 + 64, li, :], posT[r0:r0 + 64, li:li + 2], li == 0, li == 31,
               [tw1, tposT], [tpbias])
        CP(C, "dve", bsb[:, 0:2], pbias[:, 0:2], [tpbias], [tbsb])
        ph, tph = bank(C, "st", [4, 5])
        for li in range(32):
            MM(C, ph[:, 0:127], w1[r0:r0 + 64, li, :], KVc[r0:r0 + 64, li:li + 2017:16], li == 0, li == 31,
               [tw1, tKVc], [tph])
        ACTF(C, zt[:, 0:127], ph[:, 0:127], AF.Identity, [tph, tbsb], [tzt], bias=bsb[:, 0:1], scale=1.0)
        TT(C, "pool", ut[:, 0:127], zt[:, 0:127], zt[:, 0:127], ALU.mult, [tzt], [tut])
        TS(C, "pool", ut[:, 0:127], ut[:, 0:127], 0.044715, None, ALU.mult, None, [tut], [tut])
        TS(C, "pool", ut[:, 0:127], ut[:, 0:127], 1.0, None, ALU.add, None, [tut], [tut])
        TT(C, "pool", ut[:, 0:127], ut[:, 0:127], zt[:, 0:127], ALU.mult, [tut, tzt], [tut])
        ACTF(C, ut[:, 0:127], ut[:, 0:127], AF.Exp, [tut], [tut], scale=-1.5957691216057308)
        TS(C, "pool", ut[:, 0:127], ut[:, 0:127], 1.0, None, ALU.add, None, [tut], [tut])
        RECIP(C, ut[:, 0:127], ut[:, 0:127], [tut], [tut])
        TT(C, "dve", hid[:, 0:127], zt[:, 0:127], ut[:, 0:127], ALU.mult, [tzt, tut], [thid])
        pk, tpk = bank(C, "oa", [6, 7])
        if kind == 0:
            MM(C, pk[0:64, 0:127], w2[:, 0:64], hid[:, 0:127], True, True, [tw2, thid], [tpk])
            CP(C, "act", kcT[0:64, 0:127], pk[0:64, 0:127], [tpk], [tkcT])
        else:
            MM(C, pk[0:127, 0:64], hid[:, 0:127], w2[:, 64:128], True, True, [tw2, thid], [tpk])
            CP(C, "act", Vc[0:127, 0:64], pk[0:127, 0:64], [tpk], [tVc])

    def combine(oa, toa, r, b, c, mode):
        cs = slice(512 * c, 512 * c + 512)
        rc, trc = tmp32(C)
        if b == 0:
            TS(C, "dve", rc[0:64, :], oa[64:128, :], 1e-30, None, ALU.max, None, [toa], [trc])
            RECIP(C, rc[0:64, :], rc[0:64, :], [trc], [trc])
        else:
            RECIP(C, rc[0:64, :], oa[64:128, :], [toa], [trc])
        j = 3 * r + b
        pg2, tpg2 = bank(C, "g", [2])
        MM(C, pg2[0:64, :], C.sel9B[0:9, 64 * j:64 * j + 64], glog[0:9, :], True, True, [C.sel9B, tglog], [tpg2])
        gs, tgs = tmp32(C)
        ACTF(C, gs[0:64, :], pg2[0:64, :], AF.Exp, [tpg2], [tgs], scale=-1.0)
        TS(C, "pool", gs[0:64, :], gs[0:64, :], 1.0, None, ALU.add, None, [tgs], [tgs])
        RECIP(C, gs[0:64, :], gs[0:64, :], [tgs], [tgs])
        TT(C, "pool", rc[0:64, :], rc[0:64, :], gs[0:64, :], ALU.mult, [trc, tgs], [trc])
        if mode == 0:
            TT(C, "dve", acc[r][0:64, :], oa[0:64, :], rc[0:64, :], ALU.mult, [toa, trc], [tacc[r]])
        elif mode == 1:
            TT(C, "dve", gs[0:64, :], oa[0:64, :], rc[0:64, :], ALU.mult, [toa, trc, tgs], [tgs])
            TT(C, "pool", acc[r][0:64, :], acc[r][0:64, :], gs[0:64, :], ALU.add, [tacc[r], tgs], [tacc[r]])
        else:
            ch, po = _mix_dst(C, 64 * (3 * g + r))
            TT(C, "dve", gs[0:64, :], oa[0:64, :], rc[0:64, :], ALU.mult, [toa, trc, tgs], [tgs])
            TT(C, "dve", C.mixT[po:po + 64, ch, cs], acc[r][0:64, :], gs[0:64, :], ALU.add, [tacc[r], tgs],
               [C.TMIX[ch]])

    for c in range(NCH):
        cs = slice(512 * c, 512 * c + 512)
        for r in range(3):
            proj_rope(C, c, lambda k, r=r: Wn[:, k, 64 * r:64 * r + 64], lambda k, r=r: WP[:, k, 16 * r:16 * r + 16], 64,
                      [tw, tWP], QT[r], tQT[r], C.CA, C.TCA, C.SA, C.TSA)
            if c < 2:
                MEMSET(C, "pool", QT[r][64:96, :], 0.0, [tQT[r]])
        pg, tpg = bank(C, "g", [2])
        for k in range(8):
            MM(C, pg[0:9, :], Wn2[:, k, 64:73], C.hT[:, k, cs], k == 0, k == 7, [tWn2, C.TH[c]], [tpg])
        CP(C, "act", glog[0:9, :], pg[0:9, :], [tpg], [tglog])
        impb, timp = C.pb[3], C.TP[3]
        for r in range(3):
            st, tst = bank(C, "st", [4, 5])
            MM(C, st[0:127, :], kcT[0:64, 0:127], QT[r][0:64, :], True, False, [tkcT, tQT[r]], [tst])
            MM(C, st[0:127, :], C.identB[0:127, 0:127], C.cmpbB[0:127, cs], False, True, [C.identB, C.cmpbB], [tst])
            pi = C.pti % 2
            C.pti += 1
            ACTF(C, PT[pi][0:127, :], st[0:127, :], AF.Exp, [tst], [tPT[pi]], scale=0.125)
            oa, toa = bank(C, "oa", [6, 7])
            MM(C, oa[:, :], Vc[0:127, :], PT[pi][0:127, :], True, True, [tVc, tPT[pi]], [toa])
            if c >= 2:
                for tl in range(4):
                    o0 = (tl * 3 + r) * 33
                    MM(C, impb[:, o0:o0 + 33], PT[pi][0:127, 128 * tl:128 * tl + 128], C.ovl1B[0:127, 0:33], True, True,
                       [tPT[pi], C.ovl1B], [timp])
            combine(oa, toa, r, 0, c, 0)
        if c >= 2:
            for tl in range(4):
                kq = 4 * c + tl - 8
                bb = sm[:, 0:32]
                t2 = sm[:, 32:64]
                m1 = sm[:, 64:72]
                m2 = sm[:, 72:80]
                dn = sm[:, 80:83]
                keep = C.keepF[:, 32 * kq:32 * kq + 32]
                addt = C.addtF[:, 32 * kq:32 * kq + 32]
                TS(C, "dve", dn, impb[:, tl * 99 + 32:tl * 99 + 99:33], 1e-30, None, ALU.max, None, [timp], [tsm])
                RECIP(C, dn, dn, [tsm], [tsm])
                for r in range(3):
                    o0 = (tl * 3 + r) * 33
                    if r == 0:
                        STT(C, "dve", bb, impb[:, o0:o0 + 32], dn[:, 0:1], keep, ALU.mult, ALU.mult,
                            [timp, tsm, C.keepF], [tsm])
                    else:
                        STT(C, "dve", t2, impb[:, o0:o0 + 32], dn[:, r:r + 1], keep, ALU.mult, ALU.mult,
                            [timp, tsm, C.keepF], [tsm])
                        TT(C, "dve", bb, bb, t2, ALU.add, [tsm], [tsm])
                TT(C, "dve", bb, bb, addt, ALU.add, [tsm, C.addtF], [tsm])
                mk.op("dve", lambda e, m1=m1, bb=bb: e.max(out=m1, in_=bb), [tsm], [tsm])
                mk.op("dve", lambda e, m1=m1, bb=bb, t2=t2: e.match_replace(out=t2, in_to_replace=m1, in_values=bb,
                                                                        imm_value=-3.0e38), [tsm], [tsm])
                mk.op("dve", lambda e, m2=m2, t2=t2: e.max(out=m2, in_=t2), [tsm], [tsm])
                STT(C, "dve", t2, bb, m2[:, 7:8], C.bigF[:, 0:32], ALU.subtract, ALU.mult, [tsm, C.bigF], [tsm])
                TS(C, "dve", t2, t2, 0.0, None, ALU.min, None, [tsm], [tsm])
                TS(C, "dve", selb[:, 64:96], t2, -1.0, None, ALU.max, None, [tsm], [tselb])
                ptr, tptr = bank(C, "proj", [0, 1])
                ptb = ptr[:].bitcast(BF16)
                TR(C, ptb[0:96, 0:128], selb[:, 0:96], C.identB[:, :], [tselb, C.identB], [tptr])
                for r in range(3):
                    CP(C, "act", QT[r][64:96, 128 * tl:128 * tl + 128], ptb[64:96, 0:128], [tptr], [tQT[r]])
        for r in range(3):
            oa, toa = attn_chunk(C, c, lambda j: KTs[0:96, 128 * j:128 * j + 128], tKTs, 96, QT[r], tQT[r],
                                 lambda j: Vs[:, j, :], tVs, "causal", 0.125, PT, tPT)
            combine(oa, toa, r, 1, c, 1)
            oa, toa = attn_chunk(C, c, lambda j: KTw[0:64, 128 * j:128 * j + 128], tKTw, 64, QT[r], tQT[r],
                                 lambda j: Vw[:, j, :], tVw, "window", 0.125, PT, tPT)
            combine(oa, toa, r, 2, c, 2)


def _ret(C, l):
    mk = C.mk
    lg = np.log(1.0 - 2.0 ** (-5.0 - np.arange(5, dtype=np.float64)))
    win = C.dr["w_in"]
    C.pti = 0
    for h in range(5):
        wa = WA(C)
        qT, tq = wa.bf(S, "rqT")
        kT, tk = wa.bf(S, "rkT")
        v, tv = wa.bf(1024, "rv")
        v = v.rearrange("p (t n) -> p t n", t=16)
        W, tW = wa.bf(8 * 256, "rW")
        W = W.rearrange("p (i k n) -> p i k n", i=4, k=8)
        WP, tWP = wa.bf(8 * 128, "rWP")
        WP = WP.rearrange("p (k n) -> p k n", k=8)
        PT, tPT = [], []
        for i in range(2):
            a_, t_ = wa.bf(512, "rPT%d" % i)
            PT.append(a_)
            tPT.append(t_)
        o32, to32 = wa.f32(512, "ro32")
        cen, tcen = wa.f32(512, "rcen")
        sq, tsq = wa.f32(512, "rsq")
        gsb, tgsb = wa.f32(512, "rgsb")
        ge, tge = wa.f32(512, "rge")
        claim(C, wa.tiles)
        for i, nm in enumerate(("rq", "rk", "rv", "rg")):
            LDW(C, W[:, i, :, :], win[l, :, OFF[nm] + 64 * h:OFF[nm] + 64 * h + 64].rearrange(
                "(k p) n -> p k n", p=128), "retw", [tW])
        for i in range(2):
            CP(C, "pool", WP[:, :, 64 * i:64 * i + 32], W[:, i, :, 32:64], [tW], [tWP])
            CP(C, "pool", WP[:, :, 64 * i + 32:64 * i + 64], W[:, i, :, 0:32], [tW], [tWP])
        ch, po = _mix_dst(C, 704 + 64 * h)
        for c in range(NCH):
            cs = slice(512 * c, 512 * c + 512)
            for i, (dst, tdst) in enumerate(((qT, tq), (kT, tk))):
                pa, tpa = bank(C, "proj", [0, 1])
                for k in range(8):
                    MM(C, pa[0:64, :], W[:, i, k, :], C.hT[:, k, cs], k == 0, k == 7, [tW, C.TH[c]], [tpa])
                pb, tpb = bank(C, "proj", [0, 1])
                for k in range(8):
                    MM(C, pb[0:64, :], WP[:, k, 64 * i:64 * i + 64], C.hT[:, k, cs], k == 0, k == 7, [tWP, C.TH[c]], [tpb])
                ra, tra = tmp32(C)
                rb, trb = tmp32(C)
                TT(C, "dve", ra[0:64, :], pa[0:64, :], C.CR[0:64, cs], ALU.mult, [tpa, C.TCR], [tra])
                TT(C, "dve", rb[0:64, :], pb[0:64, :], C.SR[0:64, cs], ALU.mult, [tpb, C.TSR], [trb])
                TT(C, "pool", dst[0:64, cs], ra[0:64, :], rb[0:64, :], ALU.add, [tra, trb], [tdst])
            pv, tpv = bank(C, "proj", [0, 1])
            for tl in range(4):
                t = 4 * c + tl
                for k in range(8):
                    MM(C, pv[:, 64 * tl:64 * tl + 64], C.hT[:, k, 128 * t:128 * t + 128], W[:, 2, k, :], k == 0,
                       k == 7, [tW, C.TH[c]], [tpv])
            CP(C, "act", v[:, 4 * c:4 * c + 4, :], pv[:, 0:256].rearrange("p (a b) -> p a b", a=4), [tpv], [tv])
        cdl = float(np.exp(128.0 * lg[h]))
        for c in range(NCH):
            cs = slice(512 * c, 512 * c + 512)
            pg, tpg = bank(C, "g", [2])
            for k in range(8):
                MM(C, pg[0:64, :], W[:, 3, k, :], C.hT[:, k, cs], k == 0, k == 7, [tW, C.TH[c]], [tpg])
            CP(C, "act", gsb[0:64, :], pg[0:64, :], [tpg], [tgsb])
            ACTF(C, ge[0:64, :], gsb[0:64, :], AF.Exp, [tgsb], [tge], scale=-1.0)
            TS(C, "pool", ge[0:64, :], ge[0:64, :], 1.0, None, ALU.add, None, [tge], [tge])
            RECIP(C, ge[0:64, :], ge[0:64, :], [tge], [tge])
            TT(C, "pool", gsb[0:64, :], gsb[0:64, :], ge[0:64, :], ALU.mult, [tgsb, tge], [tgsb])
            oT, toT = bank(C, "oa", [6, 7])
            nk = 4 * c + 4
            for m in range(nk):
                d = m - 4 * c
                lo = max(d, 0)
                st, tst = bank(C, "st", [4, 5])
                wcs = slice(128 * lo, 512)
                MM(C, st[:, wcs], kT[0:64, 128 * m:128 * m + 128], qT[0:64, 512 * c + 128 * lo:512 * c + 512], True, True,
                   [tk, tq], [tst])
                pi = C.pti % 2
                C.pti += 1
                for tl in range(lo, 4):
                    n = 4 * c + tl
                    bs = slice(128 * tl, 128 * tl + 128)
                    if n == m:
                        TT(C, "dve", PT[pi][:, bs], st[:, bs], C.intraTF[:, 128 * h:128 * h + 128], ALU.mult,
                           [tst, C.intraTF], [tPT[pi]])
                        if n == 0:
                            ix = (l * 5 + h) * 2 + C.cur_seq
                            TS(C, "dve", PT[pi][0:1, 0:1], C.s00[0:1, ix:ix + 1], 0.125, None, ALU.mult, None,
                               [C.Ts00], [tPT[pi]])
                    else:
                        STT(C, "dve", PT[pi][:, bs], st[:, bs], float(cdl ** (n - m)),
                            C.tfullF[:, 128 * h:128 * h + 128], ALU.mult, ALU.mult, [tst, C.tfullF], [tPT[pi]])
                MM(C, oT[0:64, wcs], v[:, m, :], PT[pi][:, wcs], m == 0, m == nk - 1, [tv, tPT[pi]], [toT])
            CP(C, "act", o32[0:64, :], oT[0:64, :], [toT], [to32])
            pm, tpm = bank(C, "rms", [3])
            MM(C, pm[0:64, :], C.onesF[0:64, 0:64], o32[0:64, :], True, True, [C.onesF, to32], [tpm])
            STT(C, "dve", cen[0:64, :], pm[0:64, :], -1.0 / 64, o32[0:64, :], ALU.mult, ALU.add, [tpm, to32], [tcen])
            TT(C, "pool", sq[0:64, :], cen[0:64, :], cen[0:64, :], ALU.mult, [tcen], [tsq])
            pv2, tpv2 = bank(C, "g", [2])
            MM(C, pv2[0:64, :], C.onesF[0:64, 0:64], sq[0:64, :], True, True, [C.onesF, tsq], [tpv2])
            ACTF(C, sq[0:64, :], pv2[0:64, :], AF.Sqrt, [tpv2], [tsq], bias=EPS, scale=1.0 / 64)
            RECIP(C, sq[0:64, :], sq[0:64, :], [tsq], [tsq])
            TT(C, "pool", cen[0:64, :], cen[0:64, :], sq[0:64, :], ALU.mult, [tcen, tsq], [tcen])
            STT(C, "dve", C.mixT[po:po + 64, ch, cs], cen[0:64, :], C.gnT[:, 5 * l + h:5 * l + h + 1], gsb[0:64, :],
                ALU.mult, ALU.mult, [tcen, C.gnT, tgsb], [C.TMIX[ch]])


_CACHE = {}


def kernel(**inputs):
    n_cores = 8
    x = np.ascontiguousarray(np.asarray(inputs["x"], dtype=np.float32))
    pos = np.ascontiguousarray(np.asarray(inputs["positions"], dtype=np.int32))
    if "nc" not in _CACHE:
        _CACHE["nc"] = build()[0]
        _CACHE["blob"] = make_blob()[0]
    nc = _CACHE["nc"]
    w = {n: np.ascontiguousarray(np.asarray(inputs[n], dtype=np.float32)) for n in WNAMES}
    in_maps = []
    for i in range(n_cores):
        m = {"x": x[2 * i:2 * i + 2], "pos": pos[2 * i:2 * i + 2], "blob": _CACHE["blob"]}
        m.update(w)
        in_maps.append(m)
    res = run_bass_kernel_spmd(nc, in_maps, core_ids=list(range(n_cores)))
    return np.concatenate([np.asarray(r["out"]) for r in res.results], axis=0).astype(np.float32)
```

```python
from contextlib import ExitStack
import numpy as np
import concourse.bass as bass
import concourse.mybir as mybir
from concourse.bass_utils import run_bass_kernel_spmd

F32 = mybir.dt.float32
BF16 = mybir.dt.bfloat16
I32 = mybir.dt.int32
AF = mybir.ActivationFunctionType
ALU = mybir.AluOpType
AX = mybir.AxisListType

D = 1024
S = 2048
DEPTH = 4
NCH = 4
NT = 16
N_IN = 2866
NEG = -30000.0
EPS = 1e-6
OFF = dict(q=0, kc=384, vc=512, ks=640, vs=768, kw=896, vw=1024, gate=1152, cq=1170, ckv=1426,
           kpe=1554, rq=1586, rk=1906, rv=2226, rg=2546)

ENGS = ("pe", "act", "dve", "pool", "sp")


class T:
    __slots__ = ("ap", "name", "lw", "rd")

    def __init__(self, ap, name=""):
        self.ap = ap
        self.name = name
        self.lw = {}
        self.rd = {}

    def __getitem__(self, idx):
        return self.ap[idx]


class DmaGroup:
    def __init__(self, mk, name):
        self.name = name
        self.total = 0
        self.sem = mk._newsem("g_" + name)


class MK:
    def __init__(self, nc, stack):
        self.nc = nc
        self.stack = stack
        self.q = {e: [] for e in ENGS}
        self.cnt = {e: 0 for e in ENGS}
        self.waited = {e: {} for e in ENGS}
        self.esem = {e: self._newsem("c_" + e) for e in ENGS}
        self.groups = {}

    def _newsem(self, name):
        return self.stack.enter_context(self.nc.semaphore(name))

    def group(self, name):
        g = self.groups.get(name)
        if g is None:
            g = DmaGroup(self, name)
            self.groups[name] = g
        return g

    def sb(self, name, shape, dt):
        return self.stack.enter_context(self.nc.sbuf_tensor(name, list(shape), dt))

    def ps(self, name, shape, dt):
        return self.stack.enter_context(self.nc.psum_tensor(name, list(shape), dt))

    def _collect(self, eng, reads, writes):
        need = {}

        def add(d, same_ok):
            for k, v in d.items():
                if k == eng and not same_ok:
                    continue
                if need.get(k, 0) < v:
                    need[k] = v
        raw_same = eng != "pe"
        for t in reads:
            add(t.lw, raw_same)
        for t in writes:
            add(t.lw, False)
            add(t.rd, False)
        out = []
        w = self.waited[eng]
        for k, v in need.items():
            if isinstance(k, DmaGroup):
                v = k.total
            if w.get(k, 0) >= v:
                continue
            w[k] = v
            out.append((k, v))
        return out

    def _mark(self, key, val, reads, writes):
        for t in reads:
            if t.rd.get(key, 0) < val:
                t.rd[key] = val
        for t in writes:
            t.lw = {key: val}
            t.rd = {}

    def op(self, eng, fn, reads=(), writes=()):
        waits = self._collect(eng, reads, writes)
        self.cnt[eng] += 1
        self.q[eng].append((fn, waits, None))
        self._mark(eng, self.cnt[eng], reads, writes)

    def dma(self, eng, out_ap, in_ap, group, reads=(), writes=(), **kw):
        waits = self._collect(eng, reads, writes)
        g = self.group(group) if isinstance(group, str) else group
        g.total += 16
        self.q[eng].append((lambda e: e.dma_start(out=out_ap, in_=in_ap, **kw), waits, g))
        self._mark(g, g.total, reads, writes)

    def claim(self, tiles, others):
        for t in tiles:
            for u in others:
                if u is t:
                    continue
                for d in (u.lw, u.rd):
                    for k, v in d.items():
                        if t.rd.get(k, 0) < v:
                            t.rd[k] = v

    def wait_all_dma(self, eng="sp"):
        waits = []
        for g in self.groups.values():
            if g.total and self.waited[eng].get(g, 0) < g.total:
                waits.append((g, g.total))
                self.waited[eng][g] = g.total
        self.q[eng].append((None, waits, None))

    def _sem(self, k):
        return k.sem if isinstance(k, DmaGroup) else self.esem[k]

    def emit(self):
        nc = self.nc
        mk = self
        with nc.Block() as block:
            def run(engname, e):
                for fn, waits, g in mk.q[engname]:
                    if fn is None:
                        for k, v in waits:
                            e.wait_ge(mk._sem(k), v)
                        continue
                    for k, v in waits[1:]:
                        e.wait_ge(mk._sem(k), v)
                    ins = fn(e)
                    if waits:
                        k, v = waits[0]
                        ins._wait_ge(mk._sem(k), v)
                    if g is not None:
                        ins.then_inc(g.sem, 16)
                    else:
                        ins.then_inc(mk.esem[engname], 1)

            @block.sync
            def _(e):
                run("sp", e)

            @block.scalar
            def _(e):
                run("act", e)

            @block.vector
            def _(e):
                run("dve", e)

            @block.gpsimd
            def _(e):
                run("pool", e)

            @block.tensor
            def _(e):
                run("pe", e)


def _blob_layout():
    secs = [("ident", 128), ("tri", 128), ("winb", 128), ("cmpb", 2048), ("E", 2048), ("ovl1", 33),
            ("keep", 256), ("addt", 256), ("intraT", 640), ("rd", 640), ("wd", 5), ("cols", 8),
            ("sel9", 576), ("ones", 128), ("sel18", 1152)]
    off = {}
    o = 0
    for n, w in secs:
        off[n] = (o, w)
        o += w
    return off, o


BLOB_OFF, BLOB_W = _blob_layout()


def make_blob():
    b = np.zeros((128, BLOB_W), np.float32)

    def sec(n):
        o, w = BLOB_OFF[n]
        return b[:, o:o + w]
    p = np.arange(128)
    sec("ident")[:] = np.eye(128, dtype=np.float32)
    sec("ones")[:] = 1.0
    sec("tri")[:] = np.where(p[:, None] <= p[None, :], 0.0, NEG)
    sec("winb")[:] = np.where(p[:, None] > p[None, :], 0.0, NEG)
    q = np.arange(S)
    cm = np.where((16 * p[:, None] + 31) <= q[None, :], 0.0, NEG)
    cm[127, :] = NEG
    sec("cmpb")[:] = cm
    e = np.zeros((128, S), np.float32)
    for r in range(32):
        e[64 + r, 64 * r:64 * r + 64] = -NEG
    sec("E")[:] = e
    n_cmp, n_sel = 127, 32
    cs = np.arange(n_cmp) * 16
    ce = cs + 32
    ss = np.arange(n_sel) * 64
    se = ss + 64
    ov = np.clip(np.minimum(ce[:, None], se[None, :]) - np.maximum(cs[:, None], ss[None, :]), 0, None) / 32.0
    o1 = sec("ovl1")
    o1[:127, :32] = ov
    o1[:127, 32] = 1.0
    keep = np.zeros((128, 8, 32), np.float32)
    addt = np.zeros((128, 8, 32), np.float32)
    blk = np.arange(32)
    for qt in range(8, 16):
        t = qt * 128 + p
        cur = t // 64
        valid = blk[None, :] <= cur[:, None]
        forced = (blk[None, :] == 0) | (blk[None, :] == cur[:, None]) | (blk[None, :] == cur[:, None] - 1)
        keep[:, qt - 8, :] = (valid & ~forced)
        addt[:, qt - 8, :] = np.where(valid & forced, 1e9, np.where(valid, 0.0, -1e30))
    sec("keep")[:] = keep.reshape(128, 256)
    sec("addt")[:] = addt.reshape(128, 256)
    lg = np.log(1.0 - 2.0 ** (-5.0 - np.arange(5, dtype=np.float64)))
    i = np.arange(128, dtype=np.float64)
    it = np.zeros((128, 5, 128), np.float64)
    for h in range(5):
        diff = i[None, :] - i[:, None]
        it[:, h, :] = np.where(diff >= 0, np.exp(np.maximum(diff, 0) * lg[h]), 0.0) * 0.125
    sec("intraT")[:] = it.reshape(128, 640)
    rd = np.zeros((128, 5, 128), np.float64)
    for h in range(5):
        rd[:, h, :] = np.exp((i + 1.0) * lg[h])[None, :]
    sec("rd")[:] = rd.reshape(128, 640)
    wd = sec("wd")
    for h in range(5):
        wd[:, h] = np.exp((127.0 - i) * lg[h]) * 0.125
    cols = sec("cols")
    inv_p = 1.0 / (500000.0 ** (np.arange(0, 16, 2, dtype=np.float32) / 16)).astype(np.float32)
    inv_m = 1.0 / (500000.0 ** (np.arange(0, 32, 2, dtype=np.float32) / 32)).astype(np.float32)
    inv_r = 1.0 / (10000.0 ** (np.arange(0, 64, 2, dtype=np.float32) / 64)).astype(np.float32)
    tp = 2.0 * np.pi
    cols[:, 1] = tp
    cols[:, 3] = tp
    for r in range(16):
        cols[r, 0] = np.float64(inv_p[r % 8]) / tp
    cols[0:8, 1] = -tp
    for r in range(64, 96):
        cols[r, 0] = np.float64(inv_m[(r - 64) % 16]) / tp
    cols[64:80, 1] = -tp
    for r in range(128):
        cols[r, 2] = np.float64(inv_r[(r % 64) % 32]) / tp
        if (r % 64) < 32:
            cols[r, 3] = -tp
    s9 = np.zeros((128, 9, 64), np.float32)
    for j in range(9):
        s9[j, j, :] = 1.0
    sec("sel9")[:] = s9.reshape(128, 576)
    s18 = np.zeros((128, 18, 64), np.float32)
    for j in range(18):
        s18[j, j, :] = 1.0
    sec("sel18")[:] = s18.reshape(128, 1152)
    return b, lg


WNAMES = ["ln1_gain", "w_in", "cmp_pos_k", "cmp_w1_k", "cmp_w2_k", "cmp_pos_v", "cmp_w1_v", "cmp_w2_v",
          "mla_q_norm", "mla_w_uq", "mla_kv_norm", "mla_w_ukv", "ret_gn_gain", "w_out", "ln2_gain",
          "w_up", "w_down", "final_gain"]
WSHAPES = dict(ln1_gain=[4, 1024], w_in=[4, 1024, 2866], cmp_pos_k=[4, 32, 64], cmp_w1_k=[4, 2048, 128],
               cmp_w2_k=[4, 128, 64], cmp_pos_v=[4, 32, 64], cmp_w1_v=[4, 2048, 128], cmp_w2_v=[4, 128, 64],
               mla_q_norm=[4, 256], mla_w_uq=[4, 256, 480], mla_kv_norm=[4, 128], mla_w_ukv=[4, 128, 640],
               ret_gn_gain=[4, 5, 64], w_out=[4, 1024, 1024], ln2_gain=[4, 1024], w_up=[4, 1024, 4096],
               w_down=[4, 4096, 1024], final_gain=[1024])


class Ctx:
    pass


def build(n_layers=DEPTH, n_seq=2, mixers=("nsa", "mla", "ret"), dbg=False):
    nc = bass.Bass("TRN2", target_bir_lowering=False)
    C = Ctx()
    C.nc = nc
    C.dbg = dbg
    C.dbg_outs = {}
    dr = {}
    dr["x"] = nc.dram_tensor("x", [n_seq, S, D], F32, kind="ExternalInput").ap()
    dr["pos"] = nc.dram_tensor("pos", [n_seq, S], I32, kind="ExternalInput").ap()
    dr["blob"] = nc.dram_tensor("blob", [128, BLOB_W], F32, kind="ExternalInput").ap()
    for n in WNAMES:
        dr[n] = nc.dram_tensor(n, WSHAPES[n], F32, kind="ExternalInput").ap()
    dr["out"] = nc.dram_tensor("out", [n_seq, S, D], F32, kind="ExternalOutput").ap()
    C.dr = dr
    with ExitStack() as st:
        mk = MK(nc, st)
        C.mk = mk
        _setup(C)
        if "ret" in mixers:
            _tok0_prepass(C, n_seq, n_layers)
            dump(C, "s00", C.Ts00, C.s00[:, :], [128, 40])
        for s in range(n_seq):
            C.cur_seq = s
            _load_x(C, s)
            if s == 0:
                dump(C, "xT", C.TX[0], C.xT[:, :, 0:512], [128, 8, 512])
                dump(C, "xT3", C.TX[3], C.xT[:, :, 1536:2048], [128, 8, 512])
            if mixers:
                _rope_tables(C, s)
            for l in range(n_layers):
                if mixers:
                    _rmsnorm_main(C, C.g1, l)
                    _mixer(C, l, mixers)
                _rmsnorm_main(C, C.g2, l)
                if s == 0 and l == 0:
                    dump(C, "h2T", C.TH[0], C.hT[:, :, 0:512], [128, 8, 512], BF16)
                _ffn(C, l)
                if s == 0 and l == 0:
                    dump(C, "aT", C.TA, C.aT[:, :, :], [128, 32, 512], BF16)
                    dump(C, "x2T", C.TX[3], C.xT[:, :, 1536:2048], [128, 8, 512])
            _final(C, s)
        mk.wait_all_dma("sp")
        mk.emit()
    return nc, C


def _bsec(C, name):
    o, w = BLOB_OFF[name]
    return C.dr["blob"][:, o:o + w]


def _setup(C):
    mk = C.mk
    cg = "const"

    def cf32(name, sec, w):
        t = mk.sb(name, [128, w], F32)
        tt = T(t, name)
        mk.dma("sp", t[:], _bsec(C, sec), cg, writes=[tt])
        return tt

    def cbf(name, sec, w):
        t = mk.sb(name, [128, w], BF16)
        tt = T(t, name)
        mk.dma("pool", t[:], _bsec(C, sec), cg, writes=[tt])
        return tt
    C.identF = cf32("identF", "ident", 128)
    C.identB = cbf("identB", "ident", 128)
    C.onesB = cbf("onesB", "ones", 128)
    C.triB = cbf("triB", "tri", 128)
    C.winbB = cbf("winbB", "winb", 128)
    C.cmpbB = cbf("cmpbB", "cmpb", 2048)
    C.ovl1B = cbf("ovl1B", "ovl1", 33)
    C.keepF = cf32("keepF", "keep", 256)
    C.addtF = cf32("addtF", "addt", 256)
    C.intraTF = cf32("intraTF", "intraT", 640)
    C.rdB = cbf("rdB", "rd", 640)
    C.wdF = cf32("wdF", "wd", 5)
    C.colsF = cf32("colsF", "cols", 8)
    C.sel9B = cbf("sel9B", "sel9", 576)
    C.onesF = cf32("onesF", "ones", 128)
    t0 = mk.sb("s00", [128, 40], F32)
    C.s00 = t0
    C.Ts00 = T(t0, "s00")
    t = mk.sb("bigF", [128, 32], F32)
    C.bigF = T(t, "bigF")
    MEMSET(C, "pool", t[:], 1.0e15, [C.bigF])
    dr = C.dr

    def gain_cols(name, src_ap, shape):
        t = mk.sb(name, shape, F32)
        tt = T(t, name)
        mk.dma("sp", t[:], src_ap, cg, writes=[tt], allow_slow_non_contiguous=True)
        return tt
    C.g1 = gain_cols("g1", dr["ln1_gain"].rearrange("l (c p) -> p l c", p=128), [128, 4, 8])
    C.g2 = gain_cols("g2", dr["ln2_gain"].rearrange("l (c p) -> p l c", p=128), [128, 4, 8])
    C.gf = gain_cols("gf", dr["final_gain"].rearrange("(c p) -> p c", p=128), [128, 8])
    C.gq = gain_cols("gq", dr["mla_q_norm"].rearrange("l (c p) -> p l c", p=128), [128, 4, 2])
    C.gkv = gain_cols("gkv", dr["mla_kv_norm"].rearrange("l p -> p l"), [128, 4])
    C.gnT = gain_cols("gnT", dr["ret_gn_gain"].rearrange("l h d -> d (l h)"), [64, 20])
    C.xT = mk.sb("xT", [128, 8, S], F32)
    C.TX = [T(C.xT, "xT%d" % c) for c in range(NCH)]
    C.hT = mk.sb("hT", [128, 8, S], BF16)
    C.TH = [T(C.hT, "hT%d" % c) for c in range(NCH)]
    AW = 27648
    C.arena = mk.sb("arena", [128, AW], BF16)
    C.arena_tiles = []
    C.mixT = C.arena[:, 0:4 * S].rearrange("p (c t) -> p c t", c=4)
    C.TMIX = [T(C.mixT, "mix%d" % c) for c in range(4)]
    C.aT = C.arena[:, 0:32 * 512].rearrange("p (f t) -> p f t", f=32)
    C.TA = T(C.aT, "aT")
    C.WORK0 = 4 * S
    C.sq = C.arena[:, AW - 4096:AW].rearrange("p (k t) -> p k t", k=8)
    C.TSQ = T(C.sq, "sq")
    stf = C.arena[:, AW - 8192:AW - 4096].bitcast(F32)
    C.stage = [stf[:, 0:1024], stf[:, 1024:2048]]
    C.TST = [T(C.stage[i], "stage%d" % i) for i in range(2)]
    C.arena_tiles += C.TMIX + [C.TA, C.TSQ] + C.TST
    C.NW = 2
    C.wsl = [mk.sb("wsl%d" % i, [128, 4096], BF16) for i in range(C.NW)]
    C.TW = [T(C.wsl[i], "wsl%d" % i) for i in range(C.NW)]
    C.wi = 0
    C.t32 = [mk.sb("t32_%d" % i, [128, 512], F32) for i in range(4)]
    C.TT32 = [T(C.t32[i], "t32_%d" % i) for i in range(4)]
    C.t32i = 0
    C.pb = [mk.ps("pb%d" % i, [128, 512], F32) for i in range(8)]
    C.TP = [T(C.pb[i], "pb%d" % i) for i in range(8)]
    C.pbi = {}


def bank(C, cls, choices):
    i = C.pbi.get(cls, 0)
    C.pbi[cls] = i + 1
    b = choices[i % len(choices)]
    return C.pb[b], C.TP[b]


def tmp32(C):
    i = C.t32i % 4
    C.t32i += 1
    return C.t32[i], C.TT32[i]


def wslot(C):
    i = C.wi % C.NW
    C.wi += 1
    return C.wsl[i], C.TW[i], "wsl%d" % i


def dump(C, name, t, ap, shape, dt=F32):
    if not C.dbg:
        return
    d = C.nc.dram_tensor("dbg_" + name, list(shape), dt, kind="ExternalOutput").ap()
    C.dbg_outs[name] = d
    C.mk.dma("sp", d, ap, "dbg", reads=[t])


def claim(C, tiles):
    C.mk.claim(tiles, C.arena_tiles)
    for t in tiles:
        if t not in C.arena_tiles:
            C.arena_tiles.append(t)


def MM(C, o, l, r, st, sp, reads, writes):
    C.mk.op("pe", lambda e: e.matmul(o, l, r, start=st, stop=sp, skip_group_check=True), reads, writes)


def TR(C, o, i, ident, reads, writes):
    C.mk.op("pe", lambda e: e.transpose(out=o, in_=i, identity=ident), reads, writes)


def ACTF(C, o, i, func, reads, writes, **kw):
    C.mk.op("act", lambda e: e.activation(out=o, in_=i, func=func, **kw), reads, writes)


def TT(C, eng, o, a, b, op, reads, writes):
    C.mk.op(eng, lambda e: e.tensor_tensor(out=o, in0=a, in1=b, op=op), reads, writes)


def TS(C, eng, o, a, s1, s2, op0, op1, reads, writes):
    if s2 is None:
        C.mk.op(eng, lambda e: e.tensor_scalar(out=o, in0=a, scalar1=s1, scalar2=None, op0=op0), reads, writes)
    else:
        C.mk.op(eng, lambda e: e.tensor_scalar(out=o, in0=a, scalar1=s1, scalar2=s2, op0=op0, op1=op1), reads, writes)


def STT(C, eng, o, a, sc, b, op0, op1, reads, writes):
    C.mk.op(eng, lambda e: e.scalar_tensor_tensor(out=o, in0=a, scalar=sc, in1=b, op0=op0, op1=op1), reads, writes)


def CP(C, eng, o, i, reads, writes):
    if eng == "act":
        C.mk.op(eng, lambda e: e.copy(out=o, in_=i), reads, writes)
    else:
        C.mk.op(eng, lambda e: e.tensor_copy(out=o, in_=i), reads, writes)


def RECIP(C, o, i, reads, writes):
    C.mk.op("dve", lambda e: e.reciprocal(out=o, in_=i), reads, writes)


def MEMSET(C, eng, o, v, writes):
    C.mk.op(eng, lambda e: e.memset(o, v), (), writes)


def LDW(C, dst_ap, src_ap, group, writes, eng="pool", **kw):
    C.mk.dma(eng, dst_ap, src_ap, group, writes=writes, **kw)


def claim(C, tiles):
    C.mk.claim(tiles, C.arena_tiles)
    for t in tiles:
        if t not in C.arena_tiles:
            C.arena_tiles.append(t)


def _load_x(C, s):
    mk = C.mk
    x = C.dr["x"]
    claim(C, C.TST)
    for t in range(NT):
        stg, tst = C.stage[t % 2], C.TST[t % 2]
        mk.dma("sp", stg, x[s, 128 * t:128 * t + 128, :], "stage%d" % (t % 2), writes=[tst])
        c = t // 4
        for half in range(2):
            pb, tp = bank(C, "tr", [0, 1])
            for j in range(4):
                dc = 4 * half + j
                MM(C, pb[:, 128 * j:128 * j + 128], stg[:, 128 * dc:128 * dc + 128], C.identF[:], True, True,
                   [tst, C.identF], [tp])
            CP(C, "dve", C.xT[:, 4 * half:4 * half + 4, 128 * t:128 * t + 128],
               pb[:].rearrange("p (a b) -> p a b", a=4), [tp], [C.TX[c]])


def _rms(C, src, srcT, sq, sqT, nk, rows, gain_ap_fn, dst_fn, dstT, inv_n):
    ACTF(C, sq[0:rows, 0:nk, :], src, AF.Square, [srcT], [sqT])
    pb, tp = bank(C, "rms", [2, 3])
    for k in range(nk):
        MM(C, pb[:], C.onesB[0:rows, :], sq[0:rows, k, :], k == 0, k == nk - 1, [sqT, C.onesB], [tp])
    rt, trt = tmp32(C)
    ACTF(C, rt[:], pb[:], AF.Ln, [tp], [trt], bias=EPS, scale=inv_n)
    ACTF(C, rt[:], rt[:], AF.Exp, [trt], [trt], scale=-0.5)
    for k in range(nk):
        STT(C, "dve", dst_fn(k), src[:, k, :], gain_ap_fn(k), rt[0:rows, :], ALU.mult, ALU.mult,
            [srcT, trt], [dstT])


def _rmsnorm_main(C, g, l):
    claim(C, [C.TSQ])
    for c in range(NCH):
        cs = slice(512 * c, 512 * c + 512)
        _rms(C, C.xT[:, :, cs], C.TX[c], C.sq, C.TSQ, 8, 128, lambda k, g=g, l=l: g[:, l, k:k + 1],
             lambda k, cs=cs: C.hT[:, k, cs], C.TH[c], 1.0 / D)


def _ffn(C, l):
    w_up = C.dr["w_up"]
    w_dn = C.dr["w_down"]
    if not hasattr(C, "ffn_tiles"):
        o = 0
        C.f_at, C.f_up, C.f_dn = [], [], []
        for nm, n, lst in (("fat", 2048, C.f_at), ("fat", 2048, C.f_at), ("fup", 4096, C.f_up), ("fup", 4096, C.f_up),
                           ("fdn", 4096, C.f_dn), ("fdn", 4096, C.f_dn)):
            ap = C.arena[:, o:o + n]
            o += n
            lst.append((ap, T(ap, "%s%d" % (nm, len(lst)))))
        C.ffn_tiles = [t for lst in (C.f_at, C.f_up, C.f_dn) for (_, t) in lst]
        C.fai = 0
    claim(C, C.ffn_tiles)
    for fb in range(8):
        up, tup = C.f_up[fb % 2]
        dn, tdn = C.f_dn[fb % 2]
        upv = up.rearrange("p (k n) -> p k n", k=8)
        dnv = dn.rearrange("p (k n) -> p k n", k=4)
        LDW(C, upv, w_up[l, :, 512 * fb:512 * fb + 512].rearrange("(k p) n -> p k n", p=128), "fup%d" % (fb % 2), [tup])
        LDW(C, dnv, w_dn[l, 512 * fb:512 * fb + 512, :].rearrange("(k p) n -> p k n", p=128), "fdn%d" % (fb % 2), [tdn])
        for c in range(NCH):
            cs = slice(512 * c, 512 * c + 512)
            at, tat = C.f_at[C.fai % 2]
            C.fai += 1
            atv = at.rearrange("p (k n) -> p k n", k=4)
            for fi in range(4):
                pb, tp = bank(C, "up", [4, 5])
                for k in range(8):
                    MM(C, pb[:], upv[:, k, 128 * fi:128 * fi + 128], C.hT[:, k, cs], k == 0, k == 7, [tup, C.TH[c]], [tp])
                r, tr = tmp32(C)
                ACTF(C, r[:], pb[:], AF.Relu, [tp], [tr])
                TT(C, "pool", atv[:, fi, :], r[:], r[:], ALU.mult, [tr], [tat])
            for d in range(8):
                pb, tp = bank(C, "dn", [0, 1, 2, 3])
                for kk in range(4):
                    MM(C, pb[:], dnv[:, kk, 128 * d:128 * d + 128], atv[:, kk, :], kk == 0, kk == 3, [tdn, tat], [tp])
                TT(C, "dve", C.xT[:, d, cs], pb[:], C.xT[:, d, cs], ALU.add, [tp, C.TX[c]], [C.TX[c]])


def _final(C, s):
    mk = C.mk
    out = C.dr["out"]
    claim(C, [C.TSQ] + C.TST)
    for c in range(NCH):
        cs = slice(512 * c, 512 * c + 512)
        ACTF(C, C.sq[:, :, :], C.xT[:, :, cs], AF.Square, [C.TX[c]], [C.TSQ])
        pb, tp = bank(C, "rms", [2, 3])
        for k in range(8):
            MM(C, pb[:], C.onesB[:, :], C.sq[:, k, :], k == 0, k == 7, [C.TSQ, C.onesB], [tp])
        rt, trt = tmp32(C)
        ACTF(C, rt[:], pb[:], AF.Ln, [tp], [trt], bias=EPS, scale=1.0 / D)
        ACTF(C, rt[:], rt[:], AF.Exp, [trt], [trt], scale=-0.5)
        for k in range(8):
            STT(C, "dve", C.xT[:, k, cs], C.xT[:, k, cs], C.gf[:, k:k + 1], rt[:], ALU.mult, ALU.mult,
                [C.TX[c], trt, C.gf], [C.TX[c]])
        for tl in range(4):
            t = 4 * c + tl
            stg, tst = C.stage[t % 2], C.TST[t % 2]
            for half in range(2):
                pb, tp = bank(C, "tr", [0, 1])
                for j in range(4):
                    dc = 4 * half + j
                    MM(C, pb[:, 128 * j:128 * j + 128], C.xT[:, dc, 128 * t:128 * t + 128], C.identF[:], True, True,
                       [C.TX[c], C.identF], [tp])
                CP(C, "act", stg[:, 512 * half:512 * half + 512], pb[:], [tp], [tst])
            mk.dma("sp", out[s, 128 * t:128 * t + 128, :], stg, "stage%d" % (t % 2), reads=[tst])


PI = float(np.pi)


class WA:
    def __init__(self, C, start=None):
        self.C = C
        self.o = C.WORK0 if start is None else start
        self.tiles = []

    def bf(self, n, name):
        ap = self.C.arena[:, self.o:self.o + n]
        self.o += n
        assert self.o <= 27648, (name, self.o)
        t = T(ap, name)
        self.tiles.append(t)
        return ap, t

    def f32(self, n, name):
        ap, t = self.bf(2 * n, name)
        ap = ap.bitcast(F32)
        t.ap = ap
        return ap, t


def _rope_tables(C, s):
    mk = C.mk
    if not hasattr(C, "CA"):
        C.CA = mk.sb("CA", [128, S], BF16)
        C.SA = mk.sb("SA", [128, S], BF16)
        C.CR = mk.sb("CR", [128, S], BF16)
        C.SR = mk.sb("SR", [128, S], BF16)
        C.TCA, C.TSA, C.TCR, C.TSR = T(C.CA, "CA"), T(C.SA, "SA"), T(C.CR, "CR"), T(C.SR, "SR")
    wa = WA(C, 0)
    posI, tpi = wa.f32(S, "posI")
    posI = posI.bitcast(I32)
    posF, tpf = wa.f32(S, "posF")
    ang, tang = wa.f32(S, "ang")
    tt, ttt = wa.f32(S, "ropetmp")
    claim(C, wa.tiles)
    mk.dma("sp", posI, C.dr["pos"][s, :].partition_broadcast(128), "const", writes=[tpi])
    CP(C, "dve", posF, posI, [tpi], [tpf])
    MAGIC = 12582912.0
    for (ic, isg, ctab, tct, stab, tst_) in ((0, 1, C.CA, C.TCA, C.SA, C.TSA), (2, 3, C.CR, C.TCR, C.SR, C.TSR)):
        for which in (0, 1):
            ACTF(C, ang, posF, AF.Identity, [tpf, C.colsF], [tang], scale=C.colsF[:, ic:ic + 1],
                 bias=(0.25 if which == 0 else 0.0))
            TS(C, "dve", tt, ang, MAGIC, None, ALU.add, None, [tang], [ttt])
            TS(C, "dve", tt, tt, -MAGIC, None, ALU.add, None, [ttt], [ttt])
            TT(C, "dve", tt, ang, tt, ALU.subtract, [tang, ttt], [ttt])
            if which == 0:
                ACTF(C, ctab[:], tt, AF.Sin, [ttt], [tct], scale=2 * PI)
            else:
                ACTF(C, stab[:], tt, AF.Sin, [ttt, C.colsF], [tst_], scale=C.colsF[:, isg:isg + 1])


def attn_chunk(C, c, kt_fn, TK, krows, QT, TQ, v_fn, TV, kind, scale, PT, TPT):
    if kind == "causal":
        kts = list(range(0, 4 * c + 4))
    else:
        first = 4 * c - 1 if c >= 1 else 0
        kts = [first] + [j for j in range(max(0, 4 * c - 4), 4 * c + 4) if j != first]
    oa, toa = bank(C, "oa", [6, 7])
    for idx, j in enumerate(kts):
        d = j - 4 * c
        lo = max(d, 0)
        hi = 4 if kind == "causal" else min(j + 4, 4 * c + 3) - 4 * c + 1
        st, tst = bank(C, "st", [4, 5])
        segs = []
        plo, phi = lo, hi
        if d >= 0:
            segs.append((lo, lo + 1, C.triB))
            plo = lo + 1
        if kind == "window" and 0 <= j + 4 - 4 * c <= 3:
            segs.append((hi - 1, hi, C.winbB))
            phi = hi - 1
        if phi > plo:
            segs.append((plo, phi, None))
        for (a, b, bias) in segs:
            cs = slice(128 * a, 128 * b)
            MM(C, st[:, cs], kt_fn(j), QT[0:krows, cs], True, bias is None, [TK, TQ], [tst])
            if bias is not None:
                MM(C, st[:, cs], C.identB[:, :], bias[:, :], False, True, [C.identB, bias], [tst])
        cs = slice(128 * lo, 128 * hi)
        pi = C.pti % len(PT)
        C.pti += 1
        ACTF(C, PT[pi][:, cs], st[:, cs], AF.Exp, [tst], [TPT[pi]], scale=scale)
        MM(C, oa[:, cs], v_fn(j), PT[pi][:, cs], idx == 0, idx == len(kts) - 1, [TV, TPT[pi]], [toa])
    return oa, toa


def _mix_dst(C, feat0):
    ch = (feat0 // 128) % 4
    po = feat0 % 128
    return ch, po


def _mla_pre(C, l):
    mk = C.mk
    wa = WA(C)
    M = Ctx()
    C.M = M
    M.cqn, M.tcqn = wa.bf(2 * S, "cqn")
    M.cqn = M.cqn.rearrange("p (k t) -> p k t", k=2)
    M.ckvn, M.tckvn = wa.bf(S, "ckvn")
    M.kpeT, M.tkpe = wa.bf(S, "kpeT")
    M.KT, M.tKT = wa.bf(S, "mlaKT")
    M.V, M.tV = wa.bf(S, "mlaV")
    M.V = M.V.rearrange("p (t n) -> p t n", t=16)
    M.QT, M.tQT = wa.bf(512, "mlaQT")
    M.PT, M.tPT = [], []
    for i in range(2):
        a, t = wa.bf(512, "mlaPT%d" % i)
        M.PT.append(a)
        M.tPT.append(t)
    M.sq, M.tsq = wa.bf(1024, "mlasq")
    M.sq = M.sq.rearrange("p (k t) -> p k t", k=2)
    M.wkp, M.twkp = wa.bf(768, "wkpeP")
    M.wkp = M.wkp.rearrange("p (k n) -> p k n", k=8)
    M.wuq, M.twuq = wa.bf(960, "wuq")
    M.wuq = M.wuq.rearrange("p (k n) -> p k n", k=2)
    M.wuqP, M.twuqP = wa.bf(960, "wuqP")
    M.wuqP = M.wuqP.rearrange("p (k n) -> p k n", k=2)
    M.wukv, M.twukv = wa.bf(640, "wukv")
    claim(C, wa.tiles)
    C.pti = 0
    ws, tw, gname = wslot(C)
    wv = ws[:, 0:8 * 416].rearrange("p (k n) -> p k n", k=8)
    LDW(C, wv, C.dr["w_in"][l, :, OFF["cq"]:OFF["cq"] + 416].rearrange("(k p) n -> p k n", p=128), gname, [tw])
    LDW(C, M.wuq, C.dr["mla_w_uq"][l].rearrange("(k p) n -> p k n", p=128), "mlaw", [M.twuq])
    LDW(C, M.wukv, C.dr["mla_w_ukv"][l], "mlaw", [M.twukv])
    CP(C, "pool", M.wkp, wv[:, :, 320:416], [tw], [M.twkp])
    CP(C, "pool", M.wkp[:, :, 64:80], wv[:, :, 400:416], [tw], [M.twkp])
    CP(C, "pool", M.wkp[:, :, 80:96], wv[:, :, 384:400], [tw], [M.twkp])
    CP(C, "pool", M.wuqP, M.wuq, [M.twuq], [M.twuqP])
    wq4 = M.wuq.rearrange("p k (h n) -> p k h n", h=5)
    wq4P = M.wuqP.rearrange("p k (h n) -> p k h n", h=5)
    CP(C, "pool", wq4P[:, :, :, 64:80], wq4[:, :, :, 80:96], [M.twuq], [M.twuqP])
    CP(C, "pool", wq4P[:, :, :, 80:96], wq4[:, :, :, 64:80], [M.twuq], [M.twuqP])
    MEMSET(C, "pool", M.V[:, :, 64:128], 1.0, [M.tV])
    for c in range(NCH):
        cs = slice(512 * c, 512 * c + 512)
        r0, tr0 = tmp32(C)
        r1, tr1 = tmp32(C)
        for m, (r, tr) in enumerate(((r0, tr0), (r1, tr1))):
            pb, tp = bank(C, "proj", [0, 1])
            for k in range(8):
                MM(C, pb[:], wv[:, k, 128 * m:128 * m + 128], C.hT[:, k, cs], k == 0, k == 7, [tw, C.TH[c]], [tp])
            CP(C, "act", r[:], pb[:], [tp], [tr])
            ACTF(C, M.sq[:, m, :], pb[:], AF.Square, [tp], [M.tsq])
        pb, tp = bank(C, "rms", [2, 3])
        for m in range(2):
            MM(C, pb[:], C.onesB[:, :], M.sq[:, m, :], m == 0, m == 1, [M.tsq, C.onesB], [tp])
        rt, trt = tmp32(C)
        ACTF(C, rt[:], pb[:], AF.Ln, [tp], [trt], bias=EPS, scale=1.0 / 256)
        ACTF(C, rt[:], rt[:], AF.Exp, [trt], [trt], scale=-0.5)
        for m, (r, tr) in enumerate(((r0, tr0), (r1, tr1))):
            STT(C, "dve", M.cqn[:, m, cs], r[:], C.gq[:, l, m:m + 1], rt[:], ALU.mult, ALU.mult,
                [tr, trt, C.gq], [M.tcqn])
        pb, tp = bank(C, "proj", [0, 1])
        for k in range(8):
            MM(C, pb[:], wv[:, k, 256:384], C.hT[:, k, cs], k == 0, k == 7, [tw, C.TH[c]], [tp])
        r, tr = tmp32(C)
        CP(C, "act", r[:], pb[:], [tp], [tr])
        ACTF(C, M.sq[:, 0, :], pb[:], AF.Square, [tp], [M.tsq])
        pb2, tp2 = bank(C, "rms", [2, 3])
        MM(C, pb2[:], C.onesB[:, :], M.sq[:, 0, :], True, True, [M.tsq, C.onesB], [tp2])
        rt, trt = tmp32(C)
        ACTF(C, rt[:], pb2[:], AF.Ln, [tp2], [trt], bias=EPS, scale=1.0 / 128)
        ACTF(C, rt[:], rt[:], AF.Exp, [trt], [trt], scale=-0.5)
        STT(C, "dve", M.ckvn[:, cs], r[:], C.gkv[:, l:l + 1], rt[:], ALU.mult, ALU.mult, [tr, trt, C.gkv], [M.tckvn])
        pa, tpa = bank(C, "proj", [0, 1])
        for k in range(8):
            MM(C, pa[0:96, :], wv[:, k, 320:416], C.hT[:, k, cs], k == 0, k == 7, [tw, C.TH[c]], [tpa])
        pbb, tpb = bank(C, "proj", [0, 1])
        for k in range(8):
            MM(C, pbb[0:96, :], M.wkp[:, k, :], C.hT[:, k, cs], k == 0, k == 7, [M.twkp, C.TH[c]], [tpb])
        ra, tra = tmp32(C)
        rb, trb = tmp32(C)
        TT(C, "dve", ra[64:96, :], pa[64:96, :], C.CA[64:96, cs], ALU.mult, [tpa, C.TCA], [tra])
        TT(C, "dve", rb[64:96, :], pbb[64:96, :], C.SA[64:96, cs], ALU.mult, [tpb, C.TSA], [trb])
        TT(C, "dve", M.kpeT[64:96, cs], ra[64:96, :], rb[64:96, :], ALU.add, [tra, trb], [M.tkpe])


def _mla_head(C, l, h):
    M = C.M
    for c in range(NCH):
        cs = slice(512 * c, 512 * c + 512)
        pb, tp = bank(C, "proj", [0, 1])
        MM(C, pb[0:64, :], M.wukv[:, 128 * h:128 * h + 64], M.ckvn[:, cs], True, True, [M.twukv, M.tckvn], [tp])
        CP(C, "act", M.KT[0:64, cs], pb[0:64, :], [tp], [M.tKT])
    CP(C, "pool", M.KT[64:96, :], M.kpeT[64:96, :], [M.tkpe], [M.tKT])
    for t4 in range(4):
        pb, tp = bank(C, "proj", [0, 1])
        for j in range(4):
            t = 4 * t4 + j
            MM(C, pb[:, 64 * j:64 * j + 64], M.ckvn[:, 128 * t:128 * t + 128], M.wukv[:, 128 * h + 64:128 * h + 128],
               True, True, [M.tckvn, M.twukv], [tp])
        CP(C, "dve", M.V[:, 4 * t4:4 * t4 + 4, 0:64], pb[:, 0:256].rearrange("p (a b) -> p a b", a=4), [tp], [M.tV])
    ch, po = _mix_dst(C, 384 + 64 * h)
    for c in range(NCH):
        cs = slice(512 * c, 512 * c + 512)
        pa, tpa = bank(C, "proj", [0, 1])
        for m in range(2):
            MM(C, pa[0:96, :], M.wuq[:, m, 96 * h:96 * h + 96], M.cqn[:, m, cs], m == 0, m == 1,
               [M.twuq, M.tcqn], [tpa])
        pbb, tpb = bank(C, "proj", [0, 1])
        for m in range(2):
            MM(C, pbb[0:96, :], M.wuqP[:, m, 96 * h:96 * h + 96], M.cqn[:, m, cs], m == 0, m == 1,
               [M.twuqP, M.tcqn], [tpb])
        CP(C, "act", M.QT[0:64, :], pa[0:64, :], [tpa], [M.tQT])
        ra, tra = tmp32(C)
        rb, trb = tmp32(C)
        TT(C, "dve", ra[64:96, :], pa[64:96, :], C.CA[64:96, cs], ALU.mult, [tpa, C.TCA], [tra])
        TT(C, "dve", rb[64:96, :], pbb[64:96, :], C.SA[64:96, cs], ALU.mult, [tpb, C.TSA], [trb])
        TT(C, "dve", M.QT[64:96, :], ra[64:96, :], rb[64:96, :], ALU.add, [tra, trb], [M.tQT])
        oa, toa = attn_chunk(C, c, lambda j: M.KT[0:96, 128 * j:128 * j + 128], M.tKT, 96, M.QT, M.tQT,
                             lambda j: M.V[:, j, :], M.tV, "causal", 96 ** -0.5, M.PT, M.tPT)
        rc, trc = tmp32(C)
        ACTF(C, rc[0:64, :], oa[64:128, :], AF.Ln, [toa], [trc], bias=1e-30, scale=1.0)
        ACTF(C, rc[0:64, :], rc[0:64, :], AF.Exp, [trc], [trc], scale=-1.0)
        TT(C, "dve", C.mixT[po:po + 64, ch, cs], oa[0:64, :], rc[0:64, :], ALU.mult, [toa, trc], [C.TMIX[ch]])


def _wout(C, l, phase):
    w_out = C.dr["w_out"]
    for dq in range(4):
        ws, tw, gname = wslot(C)
        wv = ws[:, 0:1024].rearrange("p (k n) -> p k n", k=4)
        LDW(C, wv, w_out[l, 512 * phase:512 * phase + 512, 256 * dq:256 * dq + 256].rearrange(
            "(k p) n -> p k n", p=128), gname, [tw])
        for dd in range(2):
            d = 2 * dq + dd
            for c in range(NCH):
                cs = slice(512 * c, 512 * c + 512)
                pb, tp = bank(C, "proj", [0, 1])
                for k in range(4):
                    MM(C, pb[:], wv[:, k, 128 * dd:128 * dd + 128], C.mixT[:, k, cs], k == 0, k == 3,
                       [tw, C.TMIX[k]], [tp])
                TT(C, "dve", C.xT[:, d, cs], pb[:], C.xT[:, d, cs], ALU.add, [tp, C.TX[c]], [C.TX[c]])


def _tok0_prepass(C, n_seq, n_layers):
    mk = C.mk
    AR = C.arena[:].bitcast(F32)
    st = {"o": 0}
    tiles = []

    def alloc(n, name):
        ap = AR[:, st["o"]:st["o"] + n]
        st["o"] += n
        t = T(ap, name)
        tiles.append(t)
        return ap, t
    slot, tslot = [], []
    for i in range(2):
        a_, t_ = alloc(4096, "pslot%d" % i)
        slot.append(a_)
        tslot.append(t_)
    sel, tsel = alloc(1152, "sel18")
    X0, tX = alloc(16, "X0")
    X0 = X0.rearrange("p (k s) -> p k s", s=2)
    Hh, tH = alloc(16, "H0")
    Hh = Hh.rearrange("p (k s) -> p k s", s=2)
    SQ, tSQ = alloc(64, "SQ0")
    SQ = SQ.rearrange("p (k s) -> p k s", s=2)
    Aa, tA = alloc(64, "A0")
    Aa = Aa.rearrange("p (k s) -> p k s", s=2)
    PRn, tPRn = alloc(8, "PRn")
    PRn = PRn.rearrange("p (k s) -> p k s", s=2)
    SG, tSG = alloc(2, "SG")
    CK, tCK = alloc(2, "CK")
    CKN, tCKN = alloc(2, "CKN")
    R, tR = alloc(40, "R0")
    R = R.rearrange("p (k s) -> p k s", s=2)
    O, tO = alloc(10, "O0")
    O = O.rearrange("p (k s) -> p k s", s=2)
    CEN, tCEN = alloc(10, "CEN")
    CEN = CEN.rearrange("p (k s) -> p k s", s=2)
    RS_, tRS = alloc(10, "RSTD")
    RS_ = RS_.rearrange("p (k s) -> p k s", s=2)
    SGG, tSGG = alloc(10, "SGG")
    SGG = SGG.rearrange("p (k s) -> p k s", s=2)
    MIX, tMIX = alloc(32, "MIX0")
    MIX = MIX.rearrange("p (k s) -> p k s", s=2)
    sm, tsm = alloc(4, "psmall")
    tmp, ttmp = alloc(24, "ptmp")
    tmp3 = tmp.rearrange("p (k s) -> p k s", s=2)
    claim(C, tiles)
    C.wpi = 0
    dr = C.dr

    def wload(src_ap, view_fn):
        i = C.wpi % 2
        C.wpi += 1
        v = view_fn(slot[i])
        mk.dma("sp", v, src_ap, "pslot%d" % i, writes=[tslot[i]])
        return v, tslot[i]

    def mvg(lhs_fn, rhs_fn, nk, out, tout, reads):
        for k in range(nk):
            MM(C, out, lhs_fn(k), rhs_fn(k), k == 0, k == nk - 1, reads, [tout])

    def rms0(X, tXx, nk, rows, gain_fn, out, tout, n):
        TT(C, "dve", SQ[0:rows, 0:nk, :], X[0:rows, 0:nk, :], X[0:rows, 0:nk, :], ALU.mult, [tXx], [tSQ])
        mvg(lambda k: C.onesF[0:rows, 0:rows], lambda k: SQ[0:rows, k, :], nk, C.pb[7][0:rows, 0:2], C.TP[7],
            [C.onesF, tSQ])
        ACTF(C, sm[0:rows, 0:2], C.pb[7][0:rows, 0:2], AF.Sqrt, [C.TP[7]], [tsm], bias=EPS, scale=1.0 / n)
        RECIP(C, sm[0:rows, 0:2], sm[0:rows, 0:2], [tsm], [tsm])
        for k in range(nk):
            STT(C, "dve", out[0:rows, k, :], X[0:rows, k, :], gain_fn(k), sm[0:rows, 0:2], ALU.mult, ALU.mult,
                [tXx, tsm], [tout])

    MEMSET(C, "pool", AR[:, 8192 + 1152:st["o"]], 0.0, tiles[3:])
    o_, w_ = BLOB_OFF["sel18"]
    mk.dma("sp", sel, dr["blob"][:, o_:o_ + w_], "pconst", writes=[tsel])
    for s_ in range(n_seq):
        mk.dma("sp", X0[:, :, s_], dr["x"][s_, 0, :].rearrange("(c p) -> p c", p=128), "pconst", writes=[tX],
               allow_slow_non_contiguous=True)
    ps1, tp1 = C.pb[0], C.TP[0]
    ps2, tp2 = C.pb[1], C.TP[1]
    ps3, tp3 = C.pb[2], C.TP[2]
    psS, tpS = C.pb[3], C.TP[3]
    psD, tpD = C.pb[4], C.TP[4]
    psU, tpU = C.pb[5], C.TP[5]
    psV, tpV = C.pb[6], C.TP[6]
    win = dr["w_in"]
    for l in range(n_layers):
        rms0(X0, tX, 8, 128, lambda k: C.g1[:, l, k:k + 1], Hh, tH, float(D))
        hk = lambda k: Hh[:, k, :]
        wv, tw = wload(win[l, :, OFF["vs"]:OFF["vs"] + 128].rearrange("(k p) n -> p k n", p=128),
                       lambda sl: sl[:, 0:1024].rearrange("p (k n) -> p k n", k=8))
        for g in range(2):
            mvg(lambda k, g=g: wv[:, k, 64 * g:64 * g + 64], hk, 8, ps1[0:64, 2 * g:2 * g + 2], tp1, [tw, tH])
        wv, tw = wload(win[l, :, OFF["vw"]:OFF["vw"] + 146].rearrange("(k p) n -> p k n", p=128),
                       lambda sl: sl[:, 0:8 * 146].rearrange("p (k n) -> p k n", k=8))
        for g in range(2):
            mvg(lambda k, g=g: wv[:, k, 64 * g:64 * g + 64], hk, 8, ps1[0:64, 4 + 2 * g:6 + 2 * g], tp1, [tw, tH])
        mvg(lambda k: wv[:, k, 128:146], hk, 8, ps1[0:18, 8:10], tp1, [tw, tH])
        CP(C, "dve", PRn[0:64, :, :], ps1[0:64, 0:8].rearrange("p (k s) -> p k s", s=2), [tp1], [tPRn])
        CP(C, "dve", SG[0:18, :], ps1[0:18, 8:10], [tp1], [tSG])
        ACTF(C, SG[0:18, :], SG[0:18, :], AF.Exp, [tSG], [tSG], scale=-1.0)
        TS(C, "dve", SG[0:18, :], SG[0:18, :], 1.0, None, ALU.add, None, [tSG], [tSG])
        RECIP(C, SG[0:18, :], SG[0:18, :], [tSG], [tSG])
        for h in range(6):
            for b in (1, 2):
                j = 3 * h + b
                ci = 2 * (2 * h + b - 1)
                MM(C, ps2[0:64, ci:ci + 2], sel[0:18, 64 * j:64 * j + 64], SG[0:18, :], True, True, [tsel, tSG], [tp2])
        for h in range(6):
            g = h // 3
            c1 = 2 * (2 * h)
            TT(C, "dve", tmp3[0:64, 0, :], ps2[0:64, c1:c1 + 2], PRn[0:64, g, :], ALU.mult, [tp2, tPRn], [ttmp])
            TT(C, "dve", tmp3[0:64, 1, :], ps2[0:64, c1 + 2:c1 + 4], PRn[0:64, 2 + g, :], ALU.mult, [tp2, tPRn], [ttmp])
            TT(C, "dve", MIX[0:64, h, :], tmp3[0:64, 0, :], tmp3[0:64, 1, :], ALU.add, [ttmp], [tMIX])
        wv, tw = wload(win[l, :, OFF["ckv"]:OFF["ckv"] + 128].rearrange("(k p) n -> p k n", p=128),
                       lambda sl: sl[:, 0:1024].rearrange("p (k n) -> p k n", k=8))
        mvg(lambda k: wv[:, k, :], hk, 8, ps1[:, 10:12], tp1, [tw, tH])
        CP(C, "dve", CK[:, :], ps1[:, 10:12], [tp1], [tCK])
        CK3 = CK.rearrange("p (k s) -> p k s", s=2)
        CKN3 = CKN.rearrange("p (k s) -> p k s", s=2)
        rms0(CK3, tCK, 1, 128, lambda k: C.gkv[:, l:l + 1], CKN3, tCKN, 128.0)
        wv, tw = wload(dr["mla_w_ukv"][l], lambda sl: sl[:, 0:640])
        for h in range(5):
            MM(C, psV[0:64, 2 * h:2 * h + 2], wv[:, 128 * h + 64:128 * h + 128], CKN[:, 0:2], True, True, [tw, tCKN], [tpV])
        CP(C, "dve", MIX[0:64, 6:11, :], psV[0:64, 0:10].rearrange("p (k s) -> p k s", s=2), [tpV], [tMIX])
        for piece in range(3):
            n = 512 if piece < 2 else 256
            c0 = OFF["rq"] + 512 * piece
            wv, tw = wload(win[l, :, c0:c0 + n].rearrange("(k p) n -> p k n", p=128),
                           lambda sl, n=n: sl[:, 0:8 * n].rearrange("p (k n) -> p k n", k=8))
            for gi in range(8 * piece, 8 * piece + n // 64):
                off = 64 * gi - 512 * piece
                mvg(lambda k, off=off: wv[:, k, off:off + 64], hk, 8, ps3[0:64, 2 * gi:2 * gi + 2], tp3, [tw, tH])
        CP(C, "dve", R[0:64, :, :], ps3[0:64, 0:40].rearrange("p (k s) -> p k s", s=2), [tp3], [tR])
        TT(C, "dve", O[0:64, :, :], R[0:64, 0:5, :], R[0:64, 5:10, :], ALU.mult, [tR], [tO])
        MM(C, psS[0:64, 0:10], C.onesF[0:64, 0:64], O[0:64, :, :].rearrange("p k s -> p (k s)"), True, True,
           [C.onesF, tO], [tpS])
        CP(C, "dve", C.s00[0:64, 10 * l:10 * l + 10], psS[0:64, 0:10], [tpS], [C.Ts00])
        TT(C, "dve", O[0:64, :, :], psS[0:64, 0:10].rearrange("p (k s) -> p k s", s=2), R[0:64, 10:15, :], ALU.mult,
           [tpS, tR], [tO])
        TS(C, "dve", O[0:64, :, :], O[0:64, :, :], 0.125, None, ALU.mult, None, [tO], [tO])
        MM(C, psS[0:64, 10:20], C.onesF[0:64, 0:64], O[0:64, :, :].rearrange("p k s -> p (k s)"), True, True,
           [C.onesF, tO], [tpS])
        STT(C, "dve", CEN[0:64, :, :], psS[0:64, 10:20].rearrange("p (k s) -> p k s", s=2), -1.0 / 64, O[0:64, :, :],
            ALU.mult, ALU.add, [tpS, tO], [tCEN])
        TT(C, "dve", O[0:64, :, :], CEN[0:64, :, :], CEN[0:64, :, :], ALU.mult, [tCEN], [tO])
        MM(C, C.pb[7][0:64, 2:12], C.onesF[0:64, 0:64], O[0:64, :, :].rearrange("p k s -> p (k s)"), True, True,
           [C.onesF, tO], [C.TP[7]])
        ACTF(C, RS_[0:64, :, :], C.pb[7][0:64, 2:12].rearrange("p (k s) -> p k s", s=2), AF.Sqrt, [C.TP[7]], [tRS],
             bias=EPS, scale=1.0 / 64)
        RECIP(C, RS_[0:64, :, :], RS_[0:64, :, :], [tRS], [tRS])
        ACTF(C, SGG[0:64, :, :], R[0:64, 15:20, :], AF.Exp, [tR], [tSGG], scale=-1.0)
        TS(C, "dve", SGG[0:64, :, :], SGG[0:64, :, :], 1.0, None, ALU.add, None, [tSGG], [tSGG])
        RECIP(C, SGG[0:64, :, :], SGG[0:64, :, :], [tSGG], [tSGG])
        TT(C, "dve", SGG[0:64, :, :], SGG[0:64, :, :], R[0:64, 15:20, :], ALU.mult, [tSGG, tR], [tSGG])
        for h in range(5):
            STT(C, "dve", CEN[0:64, h, :], CEN[0:64, h, :], C.gnT[:, 5 * l + h:5 * l + h + 1], RS_[0:64, h, :],
                ALU.mult, ALU.mult, [tCEN, tRS, C.gnT], [tCEN])
        TT(C, "dve", MIX[0:64, 11:16, :], CEN[0:64, :, :], SGG[0:64, :, :], ALU.mult, [tCEN, tSGG], [tMIX])
        for q4 in range(4):
            wv, tw = wload(dr["w_out"][l, :, 256 * q4:256 * q4 + 256].rearrange("(j p) n -> p j n", p=64),
                           lambda sl: sl[0:64, :].rearrange("p (j n) -> p j n", j=16))
            for oo in range(2):
                oc = 2 * q4 + oo
                mvg(lambda j, oo=oo: wv[:, j, 128 * oo:128 * oo + 128], lambda j: MIX[0:64, j, :], 16,
                    psD[:, 2 * oc:2 * oc + 2], tpD, [tw, tMIX])
        TT(C, "dve", X0, X0, psD[:, 0:16].rearrange("p (k s) -> p k s", s=2), ALU.add, [tpD, tX], [tX])
        rms0(X0, tX, 8, 128, lambda k: C.g2[:, l, k:k + 1], Hh, tH, float(D))
        for fb in range(8):
            wv, tw = wload(dr["w_up"][l, :, 512 * fb:512 * fb + 512].rearrange("(k p) n -> p k n", p=128),
                           lambda sl: sl[:, :].rearrange("p (k n) -> p k n", k=8))
            for fi in range(4):
                f = 4 * fb + fi
                mvg(lambda k, fi=fi: wv[:, k, 128 * fi:128 * fi + 128], hk, 8, psU[:, 2 * f:2 * f + 2], tpU, [tw, tH])
        ACTF(C, Aa, psU[:, 0:64].rearrange("p (k s) -> p k s", s=2), AF.Relu, [tpU], [tA])
        TT(C, "dve", Aa, Aa, Aa, ALU.mult, [tA], [tA])
        MEMSET(C, "dve", psD[:, 16:32], 0.0, [tpD])
        for fs in range(8):
            wv, tw = wload(dr["w_down"][l, 512 * fs:512 * fs + 512, :].rearrange("(k p) n -> p k n", p=128),
                           lambda sl: sl[:, :].rearrange("p (k n) -> p k n", k=4))
            for oc in range(8):
                for kk in range(4):
                    MM(C, psD[:, 16 + 2 * oc:18 + 2 * oc], wv[:, kk, 128 * oc:128 * oc + 128], Aa[:, 4 * fs + kk, :],
                       False, False, [tw, tA], [tpD])
        TT(C, "dve", X0, X0, psD[:, 16:32].rearrange("p (k s) -> p k s", s=2), ALU.add, [tpD, tX], [tX])


def _tok0_h(C, l):
    h0, th = C.h0, C.Th0
    x0 = C.xT[:, :, 0]
    ACTF(C, h0[:, 11:19], x0, AF.Square, [C.TX[0]], [th], accum_out=h0[:, 9:10])
    pb, tp = bank(C, "rms", [2, 3])
    MM(C, pb[:, 0:2], C.onesF[:, :], h0[:, 9:11], True, True, [C.onesF, th], [tp])
    ACTF(C, h0[:, 19:20], pb[:, 0:1], AF.Sqrt, [tp], [th], bias=EPS, scale=1.0 / D)
    RECIP(C, h0[:, 19:20], h0[:, 19:20], [th], [th])
    STT(C, "dve", h0[:, 0:8], x0, h0[:, 19:20], C.g1[:, l, :], ALU.mult, ALU.mult, [C.TX[0], th, C.g1], [th])


def _mixer(C, l, mixers):
    claim(C, C.TMIX)
    if "nsa" in mixers:
        _nsa(C, l, 0)
        _nsa(C, l, 1)
    else:
        for ch in range(3):
            MEMSET(C, "pool", C.mixT[:, ch, :], 0.0, [C.TMIX[ch]])
    if "mla" in mixers:
        _mla_pre(C, l)
        _mla_head(C, l, 0)
        _mla_head(C, l, 1)
    else:
        MEMSET(C, "pool", C.mixT[:, 3, :], 0.0, [C.TMIX[3]])
    _wout(C, l, 0)
    if "mla" in mixers:
        for h in (2, 3, 4):
            _mla_head(C, l, h)
    else:
        MEMSET(C, "pool", C.mixT[:, 0, :], 0.0, [C.TMIX[0]])
        MEMSET(C, "pool", C.mixT[0:64, 1, :], 0.0, [C.TMIX[1]])
    if "ret" in mixers:
        _ret(C, l)
    else:
        MEMSET(C, "pool", C.mixT[64:128, 1, :], 0.0, [C.TMIX[1]])
        for ch in (2, 3):
            MEMSET(C, "pool", C.mixT[:, ch, :], 0.0, [C.TMIX[ch]])
    _wout(C, l, 1)


def proj_rope(C, c, wa_fn, wb_fn, M, wts, dst, tdst, tabC, tCt, tabS, tSt, nr=16):
    cs = slice(512 * c, 512 * c + 512)
    pa, tpa = bank(C, "proj", [0, 1])
    for k in range(8):
        MM(C, pa[0:M, :], wa_fn(k), C.hT[:, k, cs], k == 0, k == 7, wts + [C.TH[c]], [tpa])
    pb, tpb = bank(C, "proj", [0, 1])
    for k in range(8):
        MM(C, pb[0:nr, :], wb_fn(k), C.hT[:, k, cs], k == 0, k == 7, wts + [C.TH[c]], [tpb])
    CP(C, "dve", dst[0:M, :], pa[0:M, :], [tpa], [tdst])
    ra, tra = tmp32(C)
    rb, trb = tmp32(C)
    TT(C, "dve", ra[0:nr, :], pa[0:nr, :], tabC[0:nr, cs], ALU.mult, [tpa, tCt], [tra])
    TT(C, "dve", rb[0:nr, :], pb[0:nr, :], tabS[0:nr, cs], ALU.mult, [tpb, tSt], [trb])
    TT(C, "pool", dst[0:nr, :], ra[0:nr, :], rb[0:nr, :], ALU.add, [tra, trb], [tdst])


def _nsa(C, l, g):
    mk = C.mk
    win = C.dr["w_in"]
    wa = WA(C)
    KVc, tKVc = wa.bf(S, "KVc")
    KTs, tKTs = wa.bf(S, "KTs")
    KTw, tKTw = wa.bf(S, "KTw")
    Vs, tVs = wa.bf(S, "Vs")
    Vs = Vs.rearrange("p (t n) -> p t n", t=16)
    Vw, tVw = wa.bf(S, "Vw")
    Vw = Vw.rearrange("p (t n) -> p t n", t=16)
    QT, tQT = [], []
    for r in range(3):
        a_, t_ = wa.bf(512, "nQT%d" % r)
        QT.append(a_)
        tQT.append(t_)
    PT, tPT = [], []
    for i in range(2):
        a_, t_ = wa.bf(512, "nPT%d" % i)
        PT.append(a_)
        tPT.append(t_)
    acc, tacc = [], []
    for r in range(3):
        a_, t_ = wa.f32(512, "nacc%d" % r)
        acc.append(a_)
        tacc.append(t_)
    kcT, tkcT = wa.bf(128, "kcT")
    Vc, tVc = wa.bf(128, "Vcaug")
    hid, thid = wa.bf(128, "hid")
    Wn2, tWn2 = wa.bf(640, "Wn2")
    Wn2 = Wn2.rearrange("p (k n) -> p k n", k=8)
    WP, tWP = wa.bf(768, "nWP")
    WP = WP.rearrange("p (k n) -> p k n", k=8)
    glog, tglog = wa.bf(512, "glog")
    selb, tselb = wa.bf(128, "selb")
    w2, tw2 = wa.bf(128, "w2kv")
    posT, tposT = wa.bf(64, "posT")
    zt, tzt = wa.f32(128, "gz")
    ut, tut = wa.f32(128, "gu")
    sm, tsm = wa.f32(128, "nsmall")
    bsb, tbsb = wa.f32(4, "nbias")
    claim(C, wa.tiles)
    C.pti = 0
    ws, tw, gname = wslot(C)
    Wn = ws[:].rearrange("p (k n) -> p k n", k=8)
    cols = [(OFF["q"] + 192 * g, 192, 0), (OFF["kc"] + 64 * g, 64, 192), (OFF["vc"] + 64 * g, 64, 256),
            (OFF["ks"] + 64 * g, 64, 320), (OFF["kw"] + 64 * g, 64, 384), (OFF["vs"] + 64 * g, 64, 448)]
    for (src, n, dst) in cols:
        LDW(C, Wn[:, :, dst:dst + n], win[l, :, src:src + n].rearrange("(k p) n -> p k n", p=128), gname, [tw])
    LDW(C, Wn2[:, :, 0:64], win[l, :, OFF["vw"] + 64 * g:OFF["vw"] + 64 * g + 64].rearrange("(k p) n -> p k n", p=128),
        "nsaw", [tWn2])
    LDW(C, Wn2[:, :, 64:73], win[l, :, OFF["gate"] + 9 * g:OFF["gate"] + 9 * g + 9].rearrange(
        "(k p) n -> p k n", p=128), "nsaw", [tWn2], allow_slow_non_contiguous=True)
    ws1, tw1, gname1 = wslot(C)
    w1 = ws1[:].rearrange("p (l h) -> p l h", l=32)
    LDW(C, w1[0:64], C.dr["cmp_w1_k"][l].rearrange("(l d) h -> d l h", d=64), gname1, [tw1])
    LDW(C, w1[64:128], C.dr["cmp_w1_v"][l].rearrange("(l d) h -> d l h", d=64), gname1, [tw1])
    LDW(C, w2[:, 0:64], C.dr["cmp_w2_k"][l], "nsaw", [tw2])
    LDW(C, w2[:, 64:128], C.dr["cmp_w2_v"][l], "nsaw", [tw2])
    MEMSET(C, "pool", posT, 0.0, [tposT])
    LDW(C, posT[0:64, 0:32], C.dr["cmp_pos_k"][l].rearrange("l d -> d l"), "nsaw", [tposT],
        allow_slow_non_contiguous=True)
    LDW(C, posT[64:128, 0:32], C.dr["cmp_pos_v"][l].rearrange("l d -> d l"), "nsaw", [tposT],
        allow_slow_non_contiguous=True)
    o, w_ = BLOB_OFF["E"]
    LDW(C, KTs[64:96, :], C.dr["blob"][64:96, o:o + w_], "nsaw", [tKTs])
    for i, off in enumerate((0, 64, 128, 192, 320, 384)):
        CP(C, "pool", WP[:, :, 16 * i:16 * i + 8], Wn[:, :, off + 8:off + 16], [tw], [tWP])
        CP(C, "pool", WP[:, :, 16 * i + 8:16 * i + 16], Wn[:, :, off:off + 8], [tw], [tWP])
    MEMSET(C, "pool", Vs[:, :, 64:128], 1.0, [tVs])
    MEMSET(C, "pool", Vw[:, :, 64:128], 1.0, [tVw])
    MEMSET(C, "pool", Vc[:, 64:128], 1.0, [tVc])
    MEMSET(C, "pool", Vc[:, 0:64], 0.0, [tVc])
    MEMSET(C, "pool", selb, 0.0, [tselb])
    for c in range(NCH):
        cs = slice(512 * c, 512 * c + 512)
        proj_rope(C, c, lambda k: Wn[:, k, 192:320], lambda k: WP[:, k, 48:64], 128, [tw, tWP], KVc[:, cs], tKVc,
                  C.CA, C.TCA, C.SA, C.TSA)
        proj_rope(C, c, lambda k: Wn[:, k, 320:384], lambda k: WP[:, k, 64:80], 64, [tw, tWP], KTs[:, cs], tKTs,
                  C.CA, C.TCA, C.SA, C.TSA)
        proj_rope(C, c, lambda k: Wn[:, k, 384:448], lambda k: WP[:, k, 80:96], 64, [tw, tWP], KTw[:, cs], tKTw,
                  C.CA, C.TCA, C.SA, C.TSA)
        pv, tpv = bank(C, "proj", [0, 1])
        for tl in range(4):
            t = 4 * c + tl
            for k in range(8):
                MM(C, pv[:, 128 * tl:128 * tl + 64], C.hT[:, k, 128 * t:128 * t + 128], Wn[:, k, 448:512], k == 0, k == 7,
                   [tw, C.TH[c]], [tpv])
            for k in range(8):
                MM(C, pv[:, 128 * tl + 64:128 * tl + 128], C.hT[:, k, 128 * t:128 * t + 128], Wn2[:, k, 0:64], k == 0,
                   k == 7, [tWn2, C.TH[c]], [tpv])
        pv3 = pv[:].rearrange("p (a b) -> p a b", a=4)
        CP(C, "act", Vs[:, 4 * c:4 * c + 4, 0:64], pv3[:, :, 0:64], [tpv], [tVs])
        CP(C, "act", Vw[:, 4 * c:4 * c + 4, 0:64], pv3[:, :, 64:128], [tpv], [tVw])
    for kind in (0, 1):
        r0 = 64 * kind
        pbias, tpbias = bank(C, "oa", [6, 7])
        for li in range(32):
            MM(C, pbias[:, 0:2], w1[r0:r0 + 64, li, :], posT[r0:r0 + 64, li:li + 2], li == 0, li == 31,
               [tw1, tposT], [tpbias])
        CP(C, "dve", bsb[:, 0:2], pbias[:, 0:2], [tpbias], [tbsb])
        ph, tph = bank(C, "st", [4, 5])
        for li in range(32):
            MM(C, ph[:, 0:127], w1[r0:r0 + 64, li, :], KVc[r0:r0 + 64, li:li + 2017:16], li == 0, li == 31,
               [tw1, tKVc], [tph])
        ACTF(C, zt[:, 0:127], ph[:, 0:127], AF.Identity, [tph, tbsb], [tzt], bias=bsb[:, 0:1], scale=1.0)
        TT(C, "pool", ut[:, 0:127], zt[:, 0:127], zt[:, 0:127], ALU.mult, [tzt], [tut])
        TS(C, "pool", ut[:, 0:127], ut[:, 0:127], 0.044715, None, ALU.mult, None, [tut], [tut])
        TS(C, "pool", ut[:, 0:127], ut[:, 0:127], 1.0, None, ALU.add, None, [tut], [tut])
        TT(C, "pool", ut[:, 0:127], ut[:, 0:127], zt[:, 0:127], ALU.mult, [tut, tzt], [tut])
        ACTF(C, ut[:, 0:127], ut[:, 0:127], AF.Exp, [tut], [tut], scale=-1.5957691216057308)
        TS(C, "pool", ut[:, 0:127], ut[:, 0:127], 1.0, None, ALU.add, None, [tut], [tut])
        RECIP(C, ut[:, 0:127], ut[:, 0:127], [tut], [tut])
        TT(C, "dve", hid[:, 0:127], zt[:, 0:127], ut[:, 0:127], ALU.mult, [tzt, tut], [thid])
        pk, tpk = bank(C, "oa", [6, 7])
        if kind == 0:
            MM(C, pk[0:64, 0:127], w2[:, 0:64], hid[:, 0:127], True, True, [tw2, thid], [tpk])
            CP(C, "act", kcT[0:64, 0:127], pk[0:64, 0:127], [tpk], [tkcT])
        else:
            MM(C, pk[0:127, 0:64], hid[:, 0:127], w2[:, 64:128], True, True, [tw2, thid], [tpk])
            CP(C, "act", Vc[0:127, 0:64], pk[0:127, 0:64], [tpk], [tVc])

    def combine(oa, toa, r, b, c, mode):
        cs = slice(512 * c, 512 * c + 512)
        rc, trc = tmp32(C)
        ACTF(C, rc[0:64, :], oa[64:128, :], AF.Ln, [toa], [trc], bias=1e-30, scale=1.0)
        j = 3 * r + b
        pg2, tpg2 = bank(C, "g", [2])
        MM(C, pg2[0:64, :], C.sel9B[0:9, 64 * j:64 * j + 64], glog[0:9, :], True, True, [C.sel9B, tglog], [tpg2])
        gs, tgs = tmp32(C)
        ACTF(C, gs[0:64, :], pg2[0:64, :], AF.Exp, [tpg2], [tgs], scale=-1.0)
        ACTF(C, gs[0:64, :], gs[0:64, :], AF.Ln, [tgs], [tgs], bias=1.0, scale=1.0)
        TT(C, "pool", rc[0:64, :], rc[0:64, :], gs[0:64, :], ALU.add, [trc, tgs], [trc])
        ACTF(C, rc[0:64, :], rc[0:64, :], AF.Exp, [trc], [trc], scale=-1.0)
        if mode == 0:
            TT(C, "dve", acc[r][0:64, :], oa[0:64, :], rc[0:64, :], ALU.mult, [toa, trc], [tacc[r]])
        elif mode == 1:
            TT(C, "dve", gs[0:64, :], oa[0:64, :], rc[0:64, :], ALU.mult, [toa, trc, tgs], [tgs])
            TT(C, "pool", acc[r][0:64, :], acc[r][0:64, :], gs[0:64, :], ALU.add, [tacc[r], tgs], [tacc[r]])
        else:
            ch, po = _mix_dst(C, 64 * (3 * g + r))
            TT(C, "dve", gs[0:64, :], oa[0:64, :], rc[0:64, :], ALU.mult, [toa, trc, tgs], [tgs])
            TT(C, "dve", C.mixT[po:po + 64, ch, cs], acc[r][0:64, :], gs[0:64, :], ALU.add, [tacc[r], tgs],
               [C.TMIX[ch]])

    for c in range(NCH):
        cs = slice(512 * c, 512 * c + 512)
        for r in range(3):
            proj_rope(C, c, lambda k, r=r: Wn[:, k, 64 * r:64 * r + 64], lambda k, r=r: WP[:, k, 16 * r:16 * r + 16], 64,
                      [tw, tWP], QT[r], tQT[r], C.CA, C.TCA, C.SA, C.TSA)
            if c < 2:
                MEMSET(C, "pool", QT[r][64:96, :], 0.0, [tQT[r]])
        pg, tpg = bank(C, "g", [2])
        for k in range(8):
            MM(C, pg[0:9, :], Wn2[:, k, 64:73], C.hT[:, k, cs], k == 0, k == 7, [tWn2, C.TH[c]], [tpg])
        CP(C, "act", glog[0:9, :], pg[0:9, :], [tpg], [tglog])
        impb, timp = C.pb[3], C.TP[3]
        for r in range(3):
            st, tst = bank(C, "st", [4, 5])
            MM(C, st[0:127, :], kcT[0:64, 0:127], QT[r][0:64, :], True, False, [tkcT, tQT[r]], [tst])
            MM(C, st[0:127, :], C.identB[0:127, 0:127], C.cmpbB[0:127, cs], False, True, [C.identB, C.cmpbB], [tst])
            pi = C.pti % 2
            C.pti += 1
            ACTF(C, PT[pi][0:127, :], st[0:127, :], AF.Exp, [tst], [tPT[pi]], scale=0.125)
            oa, toa = bank(C, "oa", [6, 7])
            MM(C, oa[:, :], Vc[0:127, :], PT[pi][0:127, :], True, True, [tVc, tPT[pi]], [toa])
            if c >= 2:
                for tl in range(4):
                    o0 = (tl * 3 + r) * 33
                    MM(C, impb[:, o0:o0 + 33], PT[pi][0:127, 128 * tl:128 * tl + 128], C.ovl1B[0:127, 0:33], True, True,
                       [tPT[pi], C.ovl1B], [timp])
            combine(oa, toa, r, 0, c, 0)
        if c >= 2:
            for tl in range(4):
                kq = 4 * c + tl - 8
                bb = sm[:, 0:32]
                t2 = sm[:, 32:64]
                m1 = sm[:, 64:72]
                m2 = sm[:, 72:80]
                dn = sm[:, 80:83]
                keep = C.keepF[:, 32 * kq:32 * kq + 32]
                addt = C.addtF[:, 32 * kq:32 * kq + 32]
                TS(C, "dve", dn, impb[:, tl * 99 + 32:tl * 99 + 99:33], 1e-30, None, ALU.max, None, [timp], [tsm])
                RECIP(C, dn, dn, [tsm], [tsm])
                for r in range(3):
                    o0 = (tl * 3 + r) * 33
                    if r == 0:
                        STT(C, "dve", bb, impb[:, o0:o0 + 32], dn[:, 0:1], keep, ALU.mult, ALU.mult,
                            [timp, tsm, C.keepF], [tsm])
                    else:
                        STT(C, "dve", t2, impb[:, o0:o0 + 32], dn[:, r:r + 1], keep, ALU.mult, ALU.mult,
                            [timp, tsm, C.keepF], [tsm])
                        TT(C, "dve", bb, bb, t2, ALU.add, [tsm], [tsm])
                TT(C, "dve", bb, bb, addt, ALU.add, [tsm, C.addtF], [tsm])
                mk.op("dve", lambda e, m1=m1, bb=bb: e.max(out=m1, in_=bb), [tsm], [tsm])
                mk.op("dve", lambda e, m1=m1, bb=bb, t2=t2: e.match_replace(out=t2, in_to_replace=m1, in_values=bb,
                                                                        imm_value=-3.0e38), [tsm], [tsm])
                mk.op("dve", lambda e, m2=m2, t2=t2: e.max(out=m2, in_=t2), [tsm], [tsm])
                STT(C, "dve", t2, bb, m2[:, 7:8], C.bigF[:, 0:32], ALU.subtract, ALU.mult, [tsm, C.bigF], [tsm])
                TS(C, "dve", t2, t2, 0.0, None, ALU.min, None, [tsm], [tsm])
                TS(C, "dve", selb[:, 64:96], t2, -1.0, None, ALU.max, None, [tsm], [tselb])
                ptr, tptr = bank(C, "proj", [0, 1])
                ptb = ptr[:].bitcast(BF16)
                TR(C, ptb[0:96, 0:128], selb[:, 0:96], C.identB[:, :], [tselb, C.identB], [tptr])
                for r in range(3):
                    CP(C, "act", QT[r][64:96, 128 * tl:128 * tl + 128], ptb[64:96, 0:128], [tptr], [tQT[r]])
        for r in range(3):
            oa, toa = attn_chunk(C, c, lambda j: KTs[0:96, 128 * j:128 * j + 128], tKTs, 96, QT[r], tQT[r],
                                 lambda j: Vs[:, j, :], tVs, "causal", 0.125, PT, tPT)
            combine(oa, toa, r, 1, c, 1)
            oa, toa = attn_chunk(C, c, lambda j: KTw[0:64, 128 * j:128 * j + 128], tKTw, 64, QT[r], tQT[r],
                                 lambda j: Vw[:, j, :], tVw, "window", 0.125, PT, tPT)
            combine(oa, toa, r, 2, c, 2)


def _ret(C, l):
    mk = C.mk
    RS = 3
    lg = np.log(1.0 - 2.0 ** (-5.0 - np.arange(5, dtype=np.float64)))
    for h in range(5):
        wa = WA(C)
        qT, tq = wa.bf(S, "rqT")
        kT, tk = wa.bf(S, "rkT")
        qd, tqd = wa.bf(S, "rqd")
        v, tv = wa.bf(1024, "rv")
        v = v.rearrange("p (t n) -> p t n", t=16)
        gt, tg = wa.bf(1024, "rg")
        gt = gt.rearrange("p (t n) -> p t n", t=16)
        W, tW = wa.bf(8 * 256, "rW")
        W = W.rearrange("p (i k n) -> p i k n", i=4, k=8)
        WP, tWP = wa.bf(8 * 128, "rWP")
        WP = WP.rearrange("p (k n) -> p k n", k=8)
        scm, tscm = wa.bf(128, "rscm")
        kw, tkw = wa.bf(64, "rkw")
        sbf, tsbf = wa.bf(64, "rsbf")
        sf, tsf = wa.f32(64, "rsf")
        yt, tyt = wa.bf(128, "ryt")
        osb, tosb = wa.f32(64, "rosb")
        junk, tjunk = wa.f32(64, "rjunk")
        stt, tstt = wa.f32(16, "rstat")
        gn, tgn = wa.f32(64, "rgn")
        claim(C, wa.tiles)
        win = C.dr["w_in"]
        ch, po = _mix_dst(C, 704 + 64 * h)
        if RS <= -2:
            MEMSET(C, "pool", C.mixT[po:po + 64, ch, :], 0.0, [C.TMIX[ch]])
            continue
        for i, nm in enumerate(("rq", "rk", "rv", "rg")):
            LDW(C, W[:, i, :, :], win[l, :, OFF[nm] + 64 * h:OFF[nm] + 64 * h + 64].rearrange(
                "(k p) n -> p k n", p=128), "retw", [tW])
        mk.dma("sp", gn, C.dr["ret_gn_gain"][l, h, :].partition_broadcast(128), "const", writes=[tgn])
        for i in range(2):
            CP(C, "pool", WP[:, :, 64 * i:64 * i + 32], W[:, i, :, 32:64], [tW], [tWP])
            CP(C, "pool", WP[:, :, 64 * i + 32:64 * i + 64], W[:, i, :, 0:32], [tW], [tWP])
        MEMSET(C, "pool", yt, 0.0, [tyt])
        if RS == -1:
            MEMSET(C, "pool", C.mixT[po:po + 64, ch, :], 0.0, [C.TMIX[ch]])
            continue
        for c in range(NCH):
            cs = slice(512 * c, 512 * c + 512)
            for i, (dst, tdst) in enumerate(((qT, tq), (kT, tk))):
                pa, tpa = bank(C, "proj", [0, 1])
                for k in range(8):
                    MM(C, pa[0:64, :], W[:, i, k, :], C.hT[:, k, cs], k == 0, k == 7, [tW, C.TH[c]], [tpa])
                pb, tpb = bank(C, "proj", [0, 1])
                for k in range(8):
                    MM(C, pb[0:64, :], WP[:, k, 64 * i:64 * i + 64], C.hT[:, k, cs], k == 0, k == 7, [tWP, C.TH[c]], [tpb])
                ra, tra = tmp32(C)
                rb, trb = tmp32(C)
                TT(C, "dve", ra[0:64, :], pa[0:64, :], C.CR[0:64, cs], ALU.mult, [tpa, C.TCR], [tra])
                TT(C, "dve", rb[0:64, :], pb[0:64, :], C.SR[0:64, cs], ALU.mult, [tpb, C.TSR], [trb])
                TT(C, "pool", dst[0:64, cs], ra[0:64, :], rb[0:64, :], ALU.add, [tra, trb], [tdst])
            if RS != 0:
                pv, tpv = bank(C, "proj", [0, 1])
                for tl in range(4):
                    t = 4 * c + tl
                    for k in range(8):
                        MM(C, pv[:, 128 * tl:128 * tl + 64], C.hT[:, k, 128 * t:128 * t + 128], W[:, 2, k, :], k == 0,
                           k == 7, [tW, C.TH[c]], [tpv])
                    for k in range(8):
                        MM(C, pv[:, 128 * tl + 64:128 * tl + 128], C.hT[:, k, 128 * t:128 * t + 128], W[:, 3, k, :],
                           k == 0, k == 7, [tW, C.TH[c]], [tpv])
                pv3 = pv[:].rearrange("p (a b) -> p a b", a=4)
                CP(C, "act", v[:, 4 * c:4 * c + 4, :], pv3[:, :, 0:64], [tpv], [tv])
                ge, tge = tmp32(C)
                ge3 = ge[:, 0:256].rearrange("p (a b) -> p a b", a=4)
                ACTF(C, ge3, pv3[:, :, 64:128], AF.Exp, [tpv], [tge], scale=-1.0)
                ACTF(C, ge[:, 0:256], ge[:, 0:256], AF.Ln, [tge], [tge], bias=1.0, scale=1.0)
                ACTF(C, ge[:, 0:256], ge[:, 0:256], AF.Exp, [tge], [tge], scale=-1.0)
                if True:
                    TT(C, "dve", gt[:, 4 * c:4 * c + 4, :], pv3[:, :, 64:128], ge3, ALU.mult, [tpv, tge], [tg])
        if RS in (0, 5, 6, 7, 8, 9):
            MEMSET(C, "pool", C.mixT[po:po + 64, ch, :], 0.0, [C.TMIX[ch]])
            continue
        for n in range(16):
            ns = slice(128 * n, 128 * n + 128)
            TT(C, "pool", qd[0:64, ns], qT[0:64, ns], C.rdB[0:64, 128 * h:128 * h + 128], ALU.mult, [tq, C.rdB], [tqd])
        cd = float(np.exp(128.0 * lg[h]))
        if RS == 1:
            MEMSET(C, "pool", C.mixT[po:po + 64, ch, :], 0.0, [C.TMIX[ch]])
            continue
        for n in range(16):
            ns = slice(128 * n, 128 * n + 128)
            ps, tps = bank(C, "st", [4, 5])
            MM(C, ps[:, 0:128], kT[0:64, ns], qT[0:64, ns], True, True, [tk, tq], [tps])
            TT(C, "dve", scm, ps[:, 0:128], C.intraTF[:, 128 * h:128 * h + 128], ALU.mult, [tps, C.intraTF], [tscm])
            if n == 0:
                ix = (l * 5 + h) * 2 + C.cur_seq
                TS(C, "dve", scm[0:1, 0:1], C.s00[0:1, ix:ix + 1], 0.125, None, ALU.mult, None, [C.Ts00], [tscm])
            po_, tpo = bank(C, "oa", [6, 7])
            MM(C, po_[:, 0:64], scm, v[:, n, :], True, n == 0, [tscm, tv], [tpo])
            if n > 0:
                MM(C, po_[:, 0:64], qd[0:64, ns], sbf[0:64, :], False, True, [tqd, tsbf], [tpo])
            if n < 15:
                pk, tpk = bank(C, "st", [4, 5])
                MM(C, pk[:, 0:64], kT[0:64, ns], C.identB[0:64, 0:64], True, True, [tk, C.identB], [tpk])
                ACTF(C, kw, pk[:, 0:64], AF.Identity, [tpk, C.wdF], [tkw], scale=C.wdF[:, h:h + 1])
                pst, tpst = bank(C, "rms", [2, 3])
                MM(C, pst[0:64, 0:64], kw, v[:, n, :], True, True, [tkw, tv], [tpst])
                if n == 0:
                    CP(C, "dve", sf[0:64, :], pst[0:64, 0:64], [tpst], [tsf])
                else:
                    STT(C, "dve", sf[0:64, :], sf[0:64, :], cd, pst[0:64, 0:64], ALU.mult, ALU.add, [tsf, tpst], [tsf])
                CP(C, "pool", sbf[0:64, :], sf[0:64, :], [tsf], [tsbf])
            if RS == 2:
                CP(C, "act", yt[:, po:po + 64], po_[:, 0:64], [tpo], [tyt])
                ptr, tptr = bank(C, "proj", [0, 1])
                ptb = ptr[:].bitcast(BF16)
                TR(C, ptb[:, 0:128], yt, C.identB[:, :], [tyt, C.identB], [tptr])
                CP(C, "act", C.mixT[po:po + 64, ch, ns], ptb[po:po + 64, 0:128], [tptr], [C.TMIX[ch]])
                continue
            ACTF(C, osb, po_[:, 0:64], AF.Identity, [tpo], [tosb, tstt], accum_out=stt[:, 0:1])
            ACTF(C, junk, po_[:, 0:64], AF.Square, [tpo], [tjunk, tstt], accum_out=stt[:, 1:2])
            TS(C, "dve", stt[:, 2:3], stt[:, 0:1], 1.0 / 64, None, ALU.mult, None, [tstt], [tstt])
            TT(C, "dve", stt[:, 3:4], stt[:, 2:3], stt[:, 2:3], ALU.mult, [tstt], [tstt])
            STT(C, "dve", stt[:, 4:5], stt[:, 1:2], 1.0 / 64, stt[:, 3:4], ALU.mult, ALU.subtract, [tstt], [tstt])
            ACTF(C, stt[:, 5:6], stt[:, 4:5], AF.Ln, [tstt], [tstt], bias=EPS, scale=1.0)
            ACTF(C, stt[:, 6:7], stt[:, 5:6], AF.Exp, [tstt], [tstt], scale=-0.5)
            TT(C, "dve", stt[:, 7:8], stt[:, 2:3], stt[:, 6:7], ALU.mult, [tstt], [tstt])
            STT(C, "dve", osb, osb, stt[:, 6:7], stt[:, 7:8].to_broadcast([128, 64]), ALU.mult, ALU.subtract,
                [tosb, tstt], [tosb])
            TT(C, "pool", osb, osb, gn, ALU.mult, [tosb, tgn], [tosb])
            TT(C, "pool", yt[:, po:po + 64], osb, gt[:, n, :], ALU.mult, [tosb, tg], [tyt])
            ptr, tptr = bank(C, "proj", [0, 1])
            ptb = ptr[:].bitcast(BF16)
            TR(C, ptb[:, 0:128], yt, C.identB[:, :], [tyt, C.identB], [tptr])
            CP(C, "act", C.mixT[po:po + 64, ch, ns], ptb[po:po + 64, 0:128], [tptr], [C.TMIX[ch]])


_CACHE = {}


def kernel(**inputs):
    n_cores = 8
    x = np.ascontiguousarray(np.asarray(inputs["x"], dtype=np.float32))
    pos = np.ascontiguousarray(np.asarray(inputs["positions"], dtype=np.int32))
    if "nc" not in _CACHE:
        _CACHE["nc"] = build()[0]
        _CACHE["blob"] = make_blob()[0]
    nc = _CACHE["nc"]
    w = {n: np.ascontiguousarray(np.asarray(inputs[n], dtype=np.float32)) for n in WNAMES}
    in_maps = []
    for i in range(n_cores):
        m = {"x": x[2 * i:2 * i + 2], "pos": pos[2 * i:2 * i + 2], "blob": _CACHE["blob"]}
        m.update(w)
        in_maps.append(m)
    res = run_bass_kernel_spmd(nc, in_maps, core_ids=list(range(n_cores)))
    return np.concatenate([np.asarray(r["out"]) for r in res.results], axis=0).astype(np.float32)
```

```python
from contextlib import ExitStack
import numpy as np
import concourse.bass as bass
import concourse.mybir as mybir
from concourse.bass_utils import run_bass_kernel_spmd

F32 = mybir.dt.float32
BF16 = mybir.dt.bfloat16
I32 = mybir.dt.int32
AF = mybir.ActivationFunctionType
ALU = mybir.AluOpType
AX = mybir.AxisListType

D = 1024
S = 2048
DEPTH = 4
NCH = 4
NT = 16
N_IN = 2866
NEG = -30000.0
EPS = 1e-6
OFF = dict(q=0, kc=384, vc=512, ks=640, vs=768, kw=896, vw=1024, gate=1152, cq=1170, ckv=1426,
           kpe=1554, rq=1586, rk=1906, rv=2226, rg=2546)

ENGS = ("pe", "act", "dve", "pool", "sp")


class T:
    __slots__ = ("ap", "name", "lw", "rd")

    def __init__(self, ap, name=""):
        self.ap = ap
        self.name = name
        self.lw = {}
        self.rd = {}

    def __getitem__(self, idx):
        return self.ap[idx]


class DmaGroup:
    def __init__(self, mk, name):
        self.name = name
        self.total = 0
        self.sem = mk._newsem("g_" + name)


class MK:
    def __init__(self, nc, stack):
        self.nc = nc
        self.stack = stack
        self.q = {e: [] for e in ENGS}
        self.cnt = {e: 0 for e in ENGS}
        self.waited = {e: {} for e in ENGS}
        self.esem = {e: self._newsem("c_" + e) for e in ENGS}
        self.groups = {}

    def _newsem(self, name):
        return self.stack.enter_context(self.nc.semaphore(name))

    def group(self, name):
        g = self.groups.get(name)
        if g is None:
            g = DmaGroup(self, name)
            self.groups[name] = g
        return g

    def sb(self, name, shape, dt):
        return self.stack.enter_context(self.nc.sbuf_tensor(name, list(shape), dt))

    def ps(self, name, shape, dt):
        return self.stack.enter_context(self.nc.psum_tensor(name, list(shape), dt))

    def _collect(self, eng, reads, writes):
        need = {}

        def add(d, same_ok):
            for k, v in d.items():
                if k == eng and not same_ok:
                    continue
                if need.get(k, 0) < v:
                    need[k] = v
        raw_same = eng != "pe"
        for t in reads:
            add(t.lw, raw_same)
        for t in writes:
            add(t.lw, False)
            add(t.rd, False)
        out = []
        w = self.waited[eng]
        for k, v in need.items():
            if isinstance(k, DmaGroup):
                v = k.total
            if w.get(k, 0) >= v:
                continue
            w[k] = v
            out.append((k, v))
        return out

    def _mark(self, key, val, reads, writes):
        for t in reads:
            if t.rd.get(key, 0) < val:
                t.rd[key] = val
        for t in writes:
            t.lw = {key: val}
            t.rd = {}

    def op(self, eng, fn, reads=(), writes=()):
        waits = self._collect(eng, reads, writes)
        self.cnt[eng] += 1
        self.q[eng].append((fn, waits, None))
        self._mark(eng, self.cnt[eng], reads, writes)

    def dma(self, eng, out_ap, in_ap, group, reads=(), writes=(), **kw):
        waits = self._collect(eng, reads, writes)
        g = self.group(group) if isinstance(group, str) else group
        g.total += 16
        self.q[eng].append((lambda e: e.dma_start(out=out_ap, in_=in_ap, **kw), waits, g))
        self._mark(g, g.total, reads, writes)

    def claim(self, tiles, others):
        for t in tiles:
            for u in others:
                if u is t:
                    continue
                for d in (u.lw, u.rd):
                    for k, v in d.items():
                        if t.rd.get(k, 0) < v:
                            t.rd[k] = v

    def wait_all_dma(self, eng="sp"):
        waits = []
        for g in self.groups.values():
            if g.total and self.waited[eng].get(g, 0) < g.total:
                waits.append((g, g.total))
                self.waited[eng][g] = g.total
        self.q[eng].append((None, waits, None))

    def _sem(self, k):
        return k.sem if isinstance(k, DmaGroup) else self.esem[k]

    def emit(self):
        nc = self.nc
        mk = self
        with nc.Block() as block:
            def run(engname, e):
                for fn, waits, g in mk.q[engname]:
                    if fn is None:
                        for k, v in waits:
                            e.wait_ge(mk._sem(k), v)
                        continue
                    for k, v in waits[1:]:
                        e.wait_ge(mk._sem(k), v)
                    ins = fn(e)
                    if waits:
                        k, v = waits[0]
                        ins._wait_ge(mk._sem(k), v)
                    if g is not None:
                        ins.then_inc(g.sem, 16)
                    else:
                        ins.then_inc(mk.esem[engname], 1)

            @block.sync
            def _(e):
                run("sp", e)

            @block.scalar
            def _(e):
                run("act", e)

            @block.vector
            def _(e):
                run("dve", e)

            @block.gpsimd
            def _(e):
                run("pool", e)

            @block.tensor
            def _(e):
                run("pe", e)


def _blob_layout():
    secs = [("ident", 128), ("tri", 128), ("winb", 128), ("cmpb", 2048), ("E", 2048), ("ovl1", 33),
            ("keep", 256), ("addt", 256), ("intraT", 640), ("rd", 640), ("wd", 5), ("cols", 8),
            ("sel9", 576), ("ones", 128), ("sel18", 1152)]
    off = {}
    o = 0
    for n, w in secs:
        off[n] = (o, w)
        o += w
    return off, o


BLOB_OFF, BLOB_W = _blob_layout()


def make_blob():
    b = np.zeros((128, BLOB_W), np.float32)

    def sec(n):
        o, w = BLOB_OFF[n]
        return b[:, o:o + w]
    p = np.arange(128)
    sec("ident")[:] = np.eye(128, dtype=np.float32)
    sec("ones")[:] = 1.0
    sec("tri")[:] = np.where(p[:, None] <= p[None, :], 0.0, NEG)
    sec("winb")[:] = np.where(p[:, None] > p[None, :], 0.0, NEG)
    q = np.arange(S)
    cm = np.where((16 * p[:, None] + 31) <= q[None, :], 0.0, NEG)
    cm[127, :] = NEG
    sec("cmpb")[:] = cm
    e = np.zeros((128, S), np.float32)
    for r in range(32):
        e[64 + r, 64 * r:64 * r + 64] = -NEG
    sec("E")[:] = e
    n_cmp, n_sel = 127, 32
    cs = np.arange(n_cmp) * 16
    ce = cs + 32
    ss = np.arange(n_sel) * 64
    se = ss + 64
    ov = np.clip(np.minimum(ce[:, None], se[None, :]) - np.maximum(cs[:, None], ss[None, :]), 0, None) / 32.0
    o1 = sec("ovl1")
    o1[:127, :32] = ov
    o1[:127, 32] = 1.0
    keep = np.zeros((128, 8, 32), np.float32)
    addt = np.zeros((128, 8, 32), np.float32)
    blk = np.arange(32)
    for qt in range(8, 16):
        t = qt * 128 + p
        cur = t // 64
        valid = blk[None, :] <= cur[:, None]
        forced = (blk[None, :] == 0) | (blk[None, :] == cur[:, None]) | (blk[None, :] == cur[:, None] - 1)
        keep[:, qt - 8, :] = (valid & ~forced)
        addt[:, qt - 8, :] = np.where(valid & forced, 1e9, np.where(valid, 0.0, -1e30))
    sec("keep")[:] = keep.reshape(128, 256)
    sec("addt")[:] = addt.reshape(128, 256)
    lg = np.log(1.0 - 2.0 ** (-5.0 - np.arange(5, dtype=np.float64)))
    i = np.arange(128, dtype=np.float64)
    it = np.zeros((128, 5, 128), np.float64)
    for h in range(5):
        diff = i[None, :] - i[:, None]
        it[:, h, :] = np.where(diff >= 0, np.exp(np.maximum(diff, 0) * lg[h]), 0.0) * 0.125
    sec("intraT")[:] = it.reshape(128, 640)
    rd = np.zeros((128, 5, 128), np.float64)
    for h in range(5):
        rd[:, h, :] = np.exp((i + 1.0) * lg[h])[None, :]
    sec("rd")[:] = rd.reshape(128, 640)
    wd = sec("wd")
    for h in range(5):
        wd[:, h] = np.exp((127.0 - i) * lg[h]) * 0.125
    cols = sec("cols")
    inv_p = 1.0 / (500000.0 ** (np.arange(0, 16, 2, dtype=np.float32) / 16)).astype(np.float32)
    inv_m = 1.0 / (500000.0 ** (np.arange(0, 32, 2, dtype=np.float32) / 32)).astype(np.float32)
    inv_r = 1.0 / (10000.0 ** (np.arange(0, 64, 2, dtype=np.float32) / 64)).astype(np.float32)
    tp = 2.0 * np.pi
    cols[:, 1] = tp
    cols[:, 3] = tp
    for r in range(16):
        cols[r, 0] = np.float64(inv_p[r % 8]) / tp
    cols[0:8, 1] = -tp
    for r in range(64, 96):
        cols[r, 0] = np.float64(inv_m[(r - 64) % 16]) / tp
    cols[64:80, 1] = -tp
    for r in range(128):
        cols[r, 2] = np.float64(inv_r[(r % 64) % 32]) / tp
        if (r % 64) < 32:
            cols[r, 3] = -tp
    s9 = np.zeros((128, 9, 64), np.float32)
    for j in range(9):
        s9[j, j, :] = 1.0
    sec("sel9")[:] = s9.reshape(128, 576)
    s18 = np.zeros((128, 18, 64), np.float32)
    for j in range(18):
        s18[j, j, :] = 1.0
    sec("sel18")[:] = s18.reshape(128, 1152)
    return b, lg


WNAMES = ["ln1_gain", "w_in", "cmp_pos_k", "cmp_w1_k", "cmp_w2_k", "cmp_pos_v", "cmp_w1_v", "cmp_w2_v",
          "mla_q_norm", "mla_w_uq", "mla_kv_norm", "mla_w_ukv", "ret_gn_gain", "w_out", "ln2_gain",
          "w_up", "w_down", "final_gain"]
WSHAPES = dict(ln1_gain=[4, 1024], w_in=[4, 1024, 2866], cmp_pos_k=[4, 32, 64], cmp_w1_k=[4, 2048, 128],
               cmp_w2_k=[4, 128, 64], cmp_pos_v=[4, 32, 64], cmp_w1_v=[4, 2048, 128], cmp_w2_v=[4, 128, 64],
               mla_q_norm=[4, 256], mla_w_uq=[4, 256, 480], mla_kv_norm=[4, 128], mla_w_ukv=[4, 128, 640],
               ret_gn_gain=[4, 5, 64], w_out=[4, 1024, 1024], ln2_gain=[4, 1024], w_up=[4, 1024, 4096],
               w_down=[4, 4096, 1024], final_gain=[1024])


class Ctx:
    pass


def build(n_layers=DEPTH, n_seq=2, mixers=("nsa", "mla", "ret"), dbg=False):
    nc = bass.Bass("TRN2", target_bir_lowering=False)
    C = Ctx()
    C.nc = nc
    C.dbg = dbg
    C.dbg_outs = {}
    dr = {}
    dr["x"] = nc.dram_tensor("x", [n_seq, S, D], F32, kind="ExternalInput").ap()
    dr["pos"] = nc.dram_tensor("pos", [n_seq, S], I32, kind="ExternalInput").ap()
    dr["blob"] = nc.dram_tensor("blob", [128, BLOB_W], F32, kind="ExternalInput").ap()
    for n in WNAMES:
        dr[n] = nc.dram_tensor(n, WSHAPES[n], F32, kind="ExternalInput").ap()
    dr["out"] = nc.dram_tensor("out", [n_seq, S, D], F32, kind="ExternalOutput").ap()
    C.dr = dr
    with ExitStack() as st:
        mk = MK(nc, st)
        C.mk = mk
        _setup(C)
        if "ret" in mixers:
            _tok0_prepass(C, n_seq, n_layers)
            dump(C, "s00", C.Ts00, C.s00[:, :], [128, 40])
        for s in range(n_seq):
            C.cur_seq = s
            _load_x(C, s)
            if s == 0:
                dump(C, "xT", C.TX[0], C.xT[:, :, 0:512], [128, 8, 512])
                dump(C, "xT3", C.TX[3], C.xT[:, :, 1536:2048], [128, 8, 512])
            if mixers:
                _rope_tables(C, s)
            for l in range(n_layers):
                if mixers:
                    _rmsnorm_main(C, C.g1, l)
                    _mixer(C, l, mixers)
                _rmsnorm_main(C, C.g2, l)
                if s == 0 and l == 0:
                    dump(C, "h2T", C.TH[0], C.hT[:, :, 0:512], [128, 8, 512], BF16)
                _ffn(C, l)
                if s == 0 and l == 0:
                    dump(C, "aT", C.TA, C.aT[:, :, :], [128, 32, 512], BF16)
                    dump(C, "x2T", C.TX[3], C.xT[:, :, 1536:2048], [128, 8, 512])
            _final(C, s)
        mk.wait_all_dma("sp")
        mk.emit()
    return nc, C


def _bsec(C, name):
    o, w = BLOB_OFF[name]
    return C.dr["blob"][:, o:o + w]


def _setup(C):
    mk = C.mk
    cg = "const"

    def cf32(name, sec, w):
        t = mk.sb(name, [128, w], F32)
        tt = T(t, name)
        mk.dma("sp", t[:], _bsec(C, sec), cg, writes=[tt])
        return tt

    def cbf(name, sec, w):
        t = mk.sb(name, [128, w], BF16)
        tt = T(t, name)
        mk.dma("pool", t[:], _bsec(C, sec), cg, writes=[tt])
        return tt
    C.identF = cf32("identF", "ident", 128)
    C.identB = cbf("identB", "ident", 128)
    C.onesB = cbf("onesB", "ones", 128)
    C.triB = cbf("triB", "tri", 128)
    C.winbB = cbf("winbB", "winb", 128)
    C.cmpbB = cbf("cmpbB", "cmpb", 2048)
    C.ovl1B = cbf("ovl1B", "ovl1", 33)
    C.keepF = cf32("keepF", "keep", 256)
    C.addtF = cf32("addtF", "addt", 256)
    C.intraTF = cf32("intraTF", "intraT", 640)
    C.rdB = cbf("rdB", "rd", 640)
    C.wdF = cf32("wdF", "wd", 5)
    C.colsF = cf32("colsF", "cols", 8)
    C.sel9B = cbf("sel9B", "sel9", 576)
    C.onesF = cf32("onesF", "ones", 128)
    t0 = mk.sb("s00", [128, 40], F32)
    C.s00 = t0
    C.Ts00 = T(t0, "s00")
    t = mk.sb("bigF", [128, 32], F32)
    C.bigF = T(t, "bigF")
    MEMSET(C, "pool", t[:], 1.0e15, [C.bigF])
    dr = C.dr

    def gain_cols(name, src_ap, shape):
        t = mk.sb(name, shape, F32)
        tt = T(t, name)
        mk.dma("sp", t[:], src_ap, cg, writes=[tt], allow_slow_non_contiguous=True)
        return tt
    C.g1 = gain_cols("g1", dr["ln1_gain"].rearrange("l (c p) -> p l c", p=128), [128, 4, 8])
    C.g2 = gain_cols("g2", dr["ln2_gain"].rearrange("l (c p) -> p l c", p=128), [128, 4, 8])
    C.gf = gain_cols("gf", dr["final_gain"].rearrange("(c p) -> p c", p=128), [128, 8])
    C.gq = gain_cols("gq", dr["mla_q_norm"].rearrange("l (c p) -> p l c", p=128), [128, 4, 2])
    C.gkv = gain_cols("gkv", dr["mla_kv_norm"].rearrange("l p -> p l"), [128, 4])
    C.gnT = gain_cols("gnT", dr["ret_gn_gain"].rearrange("l h d -> d (l h)"), [64, 20])
    C.xT = mk.sb("xT", [128, 8, S], F32)
    C.TX = [T(C.xT, "xT%d" % c) for c in range(NCH)]
    C.hT = mk.sb("hT", [128, 8, S], BF16)
    C.TH = [T(C.hT, "hT%d" % c) for c in range(NCH)]
    AW = 27648
    C.arena = mk.sb("arena", [128, AW], BF16)
    C.arena_tiles = []
    C.mixT = C.arena[:, 0:4 * S].rearrange("p (c t) -> p c t", c=4)
    C.TMIX = [T(C.mixT, "mix%d" % c) for c in range(4)]
    C.aT = C.arena[:, 0:32 * 512].rearrange("p (f t) -> p f t", f=32)
    C.TA = T(C.aT, "aT")
    C.WORK0 = 4 * S
    C.sq = C.arena[:, AW - 4096:AW].rearrange("p (k t) -> p k t", k=8)
    C.TSQ = T(C.sq, "sq")
    stf = C.arena[:, AW - 8192:AW - 4096].bitcast(F32)
    C.stage = [stf[:, 0:1024], stf[:, 1024:2048]]
    C.TST = [T(C.stage[i], "stage%d" % i) for i in range(2)]
    C.arena_tiles += C.TMIX + [C.TA, C.TSQ] + C.TST
    C.NW = 2
    C.wsl = [mk.sb("wsl%d" % i, [128, 4096], BF16) for i in range(C.NW)]
    C.TW = [T(C.wsl[i], "wsl%d" % i) for i in range(C.NW)]
    C.wi = 0
    C.t32 = [mk.sb("t32_%d" % i, [128, 512], F32) for i in range(4)]
    C.TT32 = [T(C.t32[i], "t32_%d" % i) for i in range(4)]
    C.t32i = 0
    C.pb = [mk.ps("pb%d" % i, [128, 512], F32) for i in range(8)]
    C.TP = [T(C.pb[i], "pb%d" % i) for i in range(8)]
    C.pbi = {}


def bank(C, cls, choices):
    i = C.pbi.get(cls, 0)
    C.pbi[cls] = i + 1
    b = choices[i % len(choices)]
    return C.pb[b], C.TP[b]


def tmp32(C):
    i = C.t32i % 4
    C.t32i += 1
    return C.t32[i], C.TT32[i]


def wslot(C):
    i = C.wi % C.NW
    C.wi += 1
    return C.wsl[i], C.TW[i], "wsl%d" % i


def dump(C, name, t, ap, shape, dt=F32):
    if not C.dbg:
        return
    d = C.nc.dram_tensor("dbg_" + name, list(shape), dt, kind="ExternalOutput").ap()
    C.dbg_outs[name] = d
    C.mk.dma("sp", d, ap, "dbg", reads=[t])


def claim(C, tiles):
    C.mk.claim(tiles, C.arena_tiles)
    for t in tiles:
        if t not in C.arena_tiles:
            C.arena_tiles.append(t)


def MM(C, o, l, r, st, sp, reads, writes):
    C.mk.op("pe", lambda e: e.matmul(o, l, r, start=st, stop=sp, skip_group_check=True), reads, writes)


def TR(C, o, i, ident, reads, writes):
    C.mk.op("pe", lambda e: e.transpose(out=o, in_=i, identity=ident), reads, writes)


def ACTF(C, o, i, func, reads, writes, **kw):
    C.mk.op("act", lambda e: e.activation(out=o, in_=i, func=func, **kw), reads, writes)


def TT(C, eng, o, a, b, op, reads, writes):
    C.mk.op(eng, lambda e: e.tensor_tensor(out=o, in0=a, in1=b, op=op), reads, writes)


def TS(C, eng, o, a, s1, s2, op0, op1, reads, writes):
    if s2 is None:
        C.mk.op(eng, lambda e: e.tensor_scalar(out=o, in0=a, scalar1=s1, scalar2=None, op0=op0), reads, writes)
    else:
        C.mk.op(eng, lambda e: e.tensor_scalar(out=o, in0=a, scalar1=s1, scalar2=s2, op0=op0, op1=op1), reads, writes)


def STT(C, eng, o, a, sc, b, op0, op1, reads, writes):
    C.mk.op(eng, lambda e: e.scalar_tensor_tensor(out=o, in0=a, scalar=sc, in1=b, op0=op0, op1=op1), reads, writes)


def CP(C, eng, o, i, reads, writes):
    if eng == "act":
        C.mk.op(eng, lambda e: e.copy(out=o, in_=i), reads, writes)
    else:
        C.mk.op(eng, lambda e: e.tensor_copy(out=o, in_=i), reads, writes)


def RECIP(C, o, i, reads, writes):
    C.mk.op("dve", lambda e: e.reciprocal(out=o, in_=i), reads, writes)


def MEMSET(C, eng, o, v, writes):
    C.mk.op(eng, lambda e: e.memset(o, v), (), writes)


def LDW(C, dst_ap, src_ap, group, writes, eng="pool", **kw):
    C.mk.dma(eng, dst_ap, src_ap, group, writes=writes, **kw)


def claim(C, tiles):
    C.mk.claim(tiles, C.arena_tiles)
    for t in tiles:
        if t not in C.arena_tiles:
            C.arena_tiles.append(t)


def _load_x(C, s):
    mk = C.mk
    x = C.dr["x"]
    claim(C, C.TST)
    for t in range(NT):
        stg, tst = C.stage[t % 2], C.TST[t % 2]
        mk.dma("sp", stg, x[s, 128 * t:128 * t + 128, :], "stage%d" % (t % 2), writes=[tst])
        c = t // 4
        for half in range(2):
            pb, tp = bank(C, "tr", [0, 1])
            for j in range(4):
                dc = 4 * half + j
                MM(C, pb[:, 128 * j:128 * j + 128], stg[:, 128 * dc:128 * dc + 128], C.identF[:], True, True,
                   [tst, C.identF], [tp])
            CP(C, "dve", C.xT[:, 4 * half:4 * half + 4, 128 * t:128 * t + 128],
               pb[:].rearrange("p (a b) -> p a b", a=4), [tp], [C.TX[c]])


def _rms(C, src, srcT, sq, sqT, nk, rows, gain_ap_fn, dst_fn, dstT, inv_n):
    ACTF(C, sq[0:rows, 0:nk, :], src, AF.Square, [srcT], [sqT])
    pb, tp = bank(C, "rms", [2, 3])
    for k in range(nk):
        MM(C, pb[:], C.onesB[0:rows, :], sq[0:rows, k, :], k == 0, k == nk - 1, [sqT, C.onesB], [tp])
    rt, trt = tmp32(C)
    ACTF(C, rt[:], pb[:], AF.Ln, [tp], [trt], bias=EPS, scale=inv_n)
    ACTF(C, rt[:], rt[:], AF.Exp, [trt], [trt], scale=-0.5)
    for k in range(nk):
        STT(C, "dve", dst_fn(k), src[:, k, :], gain_ap_fn(k), rt[0:rows, :], ALU.mult, ALU.mult,
            [srcT, trt], [dstT])


def _rmsnorm_main(C, g, l):
    claim(C, [C.TSQ])
    for c in range(NCH):
        cs = slice(512 * c, 512 * c + 512)
        _rms(C, C.xT[:, :, cs], C.TX[c], C.sq, C.TSQ, 8, 128, lambda k, g=g, l=l: g[:, l, k:k + 1],
             lambda k, cs=cs: C.hT[:, k, cs], C.TH[c], 1.0 / D)


def _ffn(C, l):
    w_up = C.dr["w_up"]
    w_dn = C.dr["w_down"]
    if not hasattr(C, "ffn_tiles"):
        o = 0
        C.f_at, C.f_up, C.f_dn = [], [], []
        for nm, n, lst in (("fat", 2048, C.f_at), ("fat", 2048, C.f_at), ("fup", 4096, C.f_up), ("fup", 4096, C.f_up),
                           ("fdn", 4096, C.f_dn), ("fdn", 4096, C.f_dn)):
            ap = C.arena[:, o:o + n]
            o += n
            lst.append((ap, T(ap, "%s%d" % (nm, len(lst)))))
        C.ffn_tiles = [t for lst in (C.f_at, C.f_up, C.f_dn) for (_, t) in lst]
        C.fai = 0
    claim(C, C.ffn_tiles)
    pend = None

    def down(p):
        (c_, atv_, tat_, dnv_, tdn_) = p
        cs_ = slice(512 * c_, 512 * c_ + 512)
        for d in range(8):
            pb, tp = bank(C, "dn", [0, 1, 2, 3])
            for kk in range(4):
                MM(C, pb[:], dnv_[:, kk, 128 * d:128 * d + 128], atv_[:, kk, :], kk == 0, kk == 3, [tdn_, tat_], [tp])
            TT(C, "dve", C.xT[:, d, cs_], pb[:], C.xT[:, d, cs_], ALU.add, [tp, C.TX[c_]], [C.TX[c_]])

    for fb in range(8):
        up, tup = C.f_up[fb % 2]
        dn, tdn = C.f_dn[fb % 2]
        upv = up.rearrange("p (k n) -> p k n", k=8)
        dnv = dn.rearrange("p (k n) -> p k n", k=4)
        LDW(C, upv, w_up[l, :, 512 * fb:512 * fb + 512].rearrange("(k p) n -> p k n", p=128), "fup%d" % (fb % 2), [tup])
        LDW(C, dnv, w_dn[l, 512 * fb:512 * fb + 512, :].rearrange("(k p) n -> p k n", p=128), "fdn%d" % (fb % 2), [tdn])
        for c in range(NCH):
            cs = slice(512 * c, 512 * c + 512)
            at, tat = C.f_at[C.fai % 2]
            C.fai += 1
            atv = at.rearrange("p (k n) -> p k n", k=4)
            for fi in range(4):
                pb, tp = bank(C, "up", [4, 5])
                for k in range(8):
                    MM(C, pb[:], upv[:, k, 128 * fi:128 * fi + 128], C.hT[:, k, cs], k == 0, k == 7, [tup, C.TH[c]], [tp])
                r, tr = tmp32(C)
                ACTF(C, r[:], pb[:], AF.Relu, [tp], [tr])
                TT(C, "pool", atv[:, fi, :], r[:], r[:], ALU.mult, [tr], [tat])
            if pend is not None:
                down(pend)
            pend = (c, atv, tat, dnv, tdn)
    down(pend)


def _final(C, s):
    mk = C.mk
    out = C.dr["out"]
    claim(C, [C.TSQ] + C.TST)
    for c in range(NCH):
        cs = slice(512 * c, 512 * c + 512)
        ACTF(C, C.sq[:, :, :], C.xT[:, :, cs], AF.Square, [C.TX[c]], [C.TSQ])
        pb, tp = bank(C, "rms", [2, 3])
        for k in range(8):
            MM(C, pb[:], C.onesB[:, :], C.sq[:, k, :], k == 0, k == 7, [C.TSQ, C.onesB], [tp])
        rt, trt = tmp32(C)
        ACTF(C, rt[:], pb[:], AF.Ln, [tp], [trt], bias=EPS, scale=1.0 / D)
        ACTF(C, rt[:], rt[:], AF.Exp, [trt], [trt], scale=-0.5)
        for k in range(8):
            STT(C, "dve", C.xT[:, k, cs], C.xT[:, k, cs], C.gf[:, k:k + 1], rt[:], ALU.mult, ALU.mult,
                [C.TX[c], trt, C.gf], [C.TX[c]])
        for tl in range(4):
            t = 4 * c + tl
            stg, tst = C.stage[t % 2], C.TST[t % 2]
            for half in range(2):
                pb, tp = bank(C, "tr", [0, 1])
                for j in range(4):
                    dc = 4 * half + j
                    MM(C, pb[:, 128 * j:128 * j + 128], C.xT[:, dc, 128 * t:128 * t + 128], C.identF[:], True, True,
                       [C.TX[c], C.identF], [tp])
                CP(C, "act", stg[:, 512 * half:512 * half + 512], pb[:], [tp], [tst])
            mk.dma("sp", out[s, 128 * t:128 * t + 128, :], stg, "stage%d" % (t % 2), reads=[tst])


PI = float(np.pi)


class WA:
    def __init__(self, C, start=None):
        self.C = C
        self.o = C.WORK0 if start is None else start
        self.tiles = []

    def bf(self, n, name):
        ap = self.C.arena[:, self.o:self.o + n]
        self.o += n
        assert self.o <= 27648, (name, self.o)
        t = T(ap, name)
        self.tiles.append(t)
        return ap, t

    def f32(self, n, name):
        ap, t = self.bf(2 * n, name)
        ap = ap.bitcast(F32)
        t.ap = ap
        return ap, t


def _rope_tables(C, s):
    mk = C.mk
    if not hasattr(C, "CA"):
        C.CA = mk.sb("CA", [128, S], BF16)
        C.SA = mk.sb("SA", [128, S], BF16)
        C.CR = mk.sb("CR", [128, S], BF16)
        C.SR = mk.sb("SR", [128, S], BF16)
        C.TCA, C.TSA, C.TCR, C.TSR = T(C.CA, "CA"), T(C.SA, "SA"), T(C.CR, "CR"), T(C.SR, "SR")
    wa = WA(C, 0)
    posI, tpi = wa.f32(S, "posI")
    posI = posI.bitcast(I32)
    posF, tpf = wa.f32(S, "posF")
    ang, tang = wa.f32(S, "ang")
    tt, ttt = wa.f32(S, "ropetmp")
    claim(C, wa.tiles)
    mk.dma("sp", posI, C.dr["pos"][s, :].partition_broadcast(128), "const", writes=[tpi])
    CP(C, "dve", posF, posI, [tpi], [tpf])
    MAGIC = 12582912.0
    for (ic, isg, ctab, tct, stab, tst_) in ((0, 1, C.CA, C.TCA, C.SA, C.TSA), (2, 3, C.CR, C.TCR, C.SR, C.TSR)):
        for which in (0, 1):
            ACTF(C, ang, posF, AF.Identity, [tpf, C.colsF], [tang], scale=C.colsF[:, ic:ic + 1],
                 bias=(0.25 if which == 0 else 0.0))
            TS(C, "dve", tt, ang, MAGIC, None, ALU.add, None, [tang], [ttt])
            TS(C, "dve", tt, tt, -MAGIC, None, ALU.add, None, [ttt], [ttt])
            TT(C, "dve", tt, ang, tt, ALU.subtract, [tang, ttt], [ttt])
            if which == 0:
                ACTF(C, ctab[:], tt, AF.Sin, [ttt], [tct], scale=2 * PI)
            else:
                ACTF(C, stab[:], tt, AF.Sin, [ttt, C.colsF], [tst_], scale=C.colsF[:, isg:isg + 1])


def attn_chunk(C, c, kt_fn, TK, krows, QT, TQ, v_fn, TV, kind, scale, PT, TPT):
    if kind == "causal":
        kts = list(range(0, 4 * c + 4))
    else:
        first = 4 * c - 1 if c >= 1 else 0
        kts = [first] + [j for j in range(max(0, 4 * c - 4), 4 * c + 4) if j != first]
    oa, toa = bank(C, "oa", [6, 7])
    pend = None

    def pv(p):
        (j_, pi_, cs_, idx_) = p
        MM(C, oa[:, cs_], v_fn(j_), PT[pi_][:, cs_], idx_ == 0, idx_ == len(kts) - 1, [TV, TPT[pi_]], [toa])

    for idx, j in enumerate(kts):
        d = j - 4 * c
        lo = max(d, 0)
        hi = 4 if kind == "causal" else min(j + 4, 4 * c + 3) - 4 * c + 1
        st, tst = bank(C, "st", [4, 5])
        segs = []
        plo, phi = lo, hi
        if d >= 0:
            segs.append((lo, lo + 1, C.triB))
            plo = lo + 1
        if kind == "window" and 0 <= j + 4 - 4 * c <= 3:
            segs.append((hi - 1, hi, C.winbB))
            phi = hi - 1
        if phi > plo:
            segs.append((plo, phi, None))
        for (a_, b_, bias) in segs:
            cs = slice(128 * a_, 128 * b_)
            MM(C, st[:, cs], kt_fn(j), QT[0:krows, cs], True, bias is None, [TK, TQ], [tst])
            if bias is not None:
                MM(C, st[:, cs], C.identB[:, :], bias[:, :], False, True, [C.identB, bias], [tst])
        cs = slice(128 * lo, 128 * hi)
        pi = C.pti % len(PT)
        C.pti += 1
        ACTF(C, PT[pi][:, cs], st[:, cs], AF.Exp, [tst], [TPT[pi]], scale=scale)
        if pend is not None:
            pv(pend)
        pend = (j, pi, cs, idx)
    pv(pend)
    return oa, toa


def _mix_dst(C, feat0):
    ch = (feat0 // 128) % 4
    po = feat0 % 128
    return ch, po


def _mla_pre(C, l):
    mk = C.mk
    wa = WA(C)
    M = Ctx()
    C.M = M
    M.cqn, M.tcqn = wa.bf(2 * S, "cqn")
    M.cqn = M.cqn.rearrange("p (k t) -> p k t", k=2)
    M.ckvn, M.tckvn = wa.bf(S, "ckvn")
    M.kpeT, M.tkpe = wa.bf(S, "kpeT")
    M.KT, M.tKT = wa.bf(S, "mlaKT")
    M.V, M.tV = wa.bf(S, "mlaV")
    M.V = M.V.rearrange("p (t n) -> p t n", t=16)
    M.QT, M.tQT = wa.bf(512, "mlaQT")
    M.PT, M.tPT = [], []
    for i in range(2):
        a, t = wa.bf(512, "mlaPT%d" % i)
        M.PT.append(a)
        M.tPT.append(t)
    M.sq, M.tsq = wa.bf(1024, "mlasq")
    M.sq = M.sq.rearrange("p (k t) -> p k t", k=2)
    M.wkp, M.twkp = wa.bf(768, "wkpeP")
    M.wkp = M.wkp.rearrange("p (k n) -> p k n", k=8)
    M.wuq, M.twuq = wa.bf(960, "wuq")
    M.wuq = M.wuq.rearrange("p (k n) -> p k n", k=2)
    M.wuqP, M.twuqP = wa.bf(960, "wuqP")
    M.wuqP = M.wuqP.rearrange("p (k n) -> p k n", k=2)
    M.wukv, M.twukv = wa.bf(640, "wukv")
    claim(C, wa.tiles)
    C.pti = 0
    ws, tw, gname = wslot(C)
    wv = ws[:, 0:8 * 416].rearrange("p (k n) -> p k n", k=8)
    LDW(C, wv, C.dr["w_in"][l, :, OFF["cq"]:OFF["cq"] + 416].rearrange("(k p) n -> p k n", p=128), gname, [tw])
    LDW(C, M.wuq, C.dr["mla_w_uq"][l].rearrange("(k p) n -> p k n", p=128), "mlaw", [M.twuq])
    LDW(C, M.wukv, C.dr["mla_w_ukv"][l], "mlaw", [M.twukv])
    CP(C, "pool", M.wkp, wv[:, :, 320:416], [tw], [M.twkp])
    CP(C, "pool", M.wkp[:, :, 64:80], wv[:, :, 400:416], [tw], [M.twkp])
    CP(C, "pool", M.wkp[:, :, 80:96], wv[:, :, 384:400], [tw], [M.twkp])
    CP(C, "pool", M.wuqP, M.wuq, [M.twuq], [M.twuqP])
    wq4 = M.wuq.rearrange("p k (h n) -> p k h n", h=5)
    wq4P = M.wuqP.rearrange("p k (h n) -> p k h n", h=5)
    CP(C, "pool", wq4P[:, :, :, 64:80], wq4[:, :, :, 80:96], [M.twuq], [M.twuqP])
    CP(C, "pool", wq4P[:, :, :, 80:96], wq4[:, :, :, 64:80], [M.twuq], [M.twuqP])
    MEMSET(C, "pool", M.V[:, :, 64:128], 1.0, [M.tV])
    for c in range(NCH):
        cs = slice(512 * c, 512 * c + 512)
        r0, tr0 = tmp32(C)
        r1, tr1 = tmp32(C)
        for m, (r, tr) in enumerate(((r0, tr0), (r1, tr1))):
            pb, tp = bank(C, "proj", [0, 1])
            for k in range(8):
                MM(C, pb[:], wv[:, k, 128 * m:128 * m + 128], C.hT[:, k, cs], k == 0, k == 7, [tw, C.TH[c]], [tp])
            CP(C, "act", r[:], pb[:], [tp], [tr])
            ACTF(C, M.sq[:, m, :], pb[:], AF.Square, [tp], [M.tsq])
        pb, tp = bank(C, "rms", [2, 3])
        for m in range(2):
            MM(C, pb[:], C.onesB[:, :], M.sq[:, m, :], m == 0, m == 1, [M.tsq, C.onesB], [tp])
        rt, trt = tmp32(C)
        ACTF(C, rt[:], pb[:], AF.Ln, [tp], [trt], bias=EPS, scale=1.0 / 256)
        ACTF(C, rt[:], rt[:], AF.Exp, [trt], [trt], scale=-0.5)
        for m, (r, tr) in enumerate(((r0, tr0), (r1, tr1))):
            STT(C, "dve", M.cqn[:, m, cs], r[:], C.gq[:, l, m:m + 1], rt[:], ALU.mult, ALU.mult,
                [tr, trt, C.gq], [M.tcqn])
        pb, tp = bank(C, "proj", [0, 1])
        for k in range(8):
            MM(C, pb[:], wv[:, k, 256:384], C.hT[:, k, cs], k == 0, k == 7, [tw, C.TH[c]], [tp])
        r, tr = tmp32(C)
        CP(C, "act", r[:], pb[:], [tp], [tr])
        ACTF(C, M.sq[:, 0, :], pb[:], AF.Square, [tp], [M.tsq])
        pb2, tp2 = bank(C, "rms", [2, 3])
        MM(C, pb2[:], C.onesB[:, :], M.sq[:, 0, :], True, True, [M.tsq, C.onesB], [tp2])
        rt, trt = tmp32(C)
        ACTF(C, rt[:], pb2[:], AF.Ln, [tp2], [trt], bias=EPS, scale=1.0 / 128)
        ACTF(C, rt[:], rt[:], AF.Exp, [trt], [trt], scale=-0.5)
        STT(C, "dve", M.ckvn[:, cs], r[:], C.gkv[:, l:l + 1], rt[:], ALU.mult, ALU.mult, [tr, trt, C.gkv], [M.tckvn])
        pa, tpa = bank(C, "proj", [0, 1])
        for k in range(8):
            MM(C, pa[0:96, :], wv[:, k, 320:416], C.hT[:, k, cs], k == 0, k == 7, [tw, C.TH[c]], [tpa])
        pbb, tpb = bank(C, "proj", [0, 1])
        for k in range(8):
            MM(C, pbb[0:96, :], M.wkp[:, k, :], C.hT[:, k, cs], k == 0, k == 7, [M.twkp, C.TH[c]], [tpb])
        ra, tra = tmp32(C)
        rb, trb = tmp32(C)
        TT(C, "dve", ra[64:96, :], pa[64:96, :], C.CA[64:96, cs], ALU.mult, [tpa, C.TCA], [tra])
        TT(C, "dve", rb[64:96, :], pbb[64:96, :], C.SA[64:96, cs], ALU.mult, [tpb, C.TSA], [trb])
        TT(C, "dve", M.kpeT[64:96, cs], ra[64:96, :], rb[64:96, :], ALU.add, [tra, trb], [M.tkpe])


def _mla_head(C, l, h):
    M = C.M
    for c in range(NCH):
        cs = slice(512 * c, 512 * c + 512)
        pb, tp = bank(C, "proj", [0, 1])
        MM(C, pb[0:64, :], M.wukv[:, 128 * h:128 * h + 64], M.ckvn[:, cs], True, True, [M.twukv, M.tckvn], [tp])
        CP(C, "act", M.KT[0:64, cs], pb[0:64, :], [tp], [M.tKT])
    CP(C, "pool", M.KT[64:96, :], M.kpeT[64:96, :], [M.tkpe], [M.tKT])
    for t4 in range(4):
        pb, tp = bank(C, "proj", [0, 1])
        for j in range(4):
            t = 4 * t4 + j
            MM(C, pb[:, 64 * j:64 * j + 64], M.ckvn[:, 128 * t:128 * t + 128], M.wukv[:, 128 * h + 64:128 * h + 128],
               True, True, [M.tckvn, M.twukv], [tp])
        CP(C, "dve", M.V[:, 4 * t4:4 * t4 + 4, 0:64], pb[:, 0:256].rearrange("p (a b) -> p a b", a=4), [tp], [M.tV])
    ch, po = _mix_dst(C, 384 + 64 * h)
    for c in range(NCH):
        cs = slice(512 * c, 512 * c + 512)
        pa, tpa = bank(C, "proj", [0, 1])
        for m in range(2):
            MM(C, pa[0:96, :], M.wuq[:, m, 96 * h:96 * h + 96], M.cqn[:, m, cs], m == 0, m == 1,
               [M.twuq, M.tcqn], [tpa])
        pbb, tpb = bank(C, "proj", [0, 1])
        for m in range(2):
            MM(C, pbb[0:96, :], M.wuqP[:, m, 96 * h:96 * h + 96], M.cqn[:, m, cs], m == 0, m == 1,
               [M.twuqP, M.tcqn], [tpb])
        CP(C, "act", M.QT[0:64, :], pa[0:64, :], [tpa], [M.tQT])
        ra, tra = tmp32(C)
        rb, trb = tmp32(C)
        TT(C, "dve", ra[64:96, :], pa[64:96, :], C.CA[64:96, cs], ALU.mult, [tpa, C.TCA], [tra])
        TT(C, "dve", rb[64:96, :], pbb[64:96, :], C.SA[64:96, cs], ALU.mult, [tpb, C.TSA], [trb])
        TT(C, "dve", M.QT[64:96, :], ra[64:96, :], rb[64:96, :], ALU.add, [tra, trb], [M.tQT])
        oa, toa = attn_chunk(C, c, lambda j: M.KT[0:96, 128 * j:128 * j + 128], M.tKT, 96, M.QT, M.tQT,
                             lambda j: M.V[:, j, :], M.tV, "causal", 96 ** -0.5, M.PT, M.tPT)
        rc, trc = tmp32(C)
        ACTF(C, rc[0:64, :], oa[64:128, :], AF.Ln, [toa], [trc], bias=1e-30, scale=1.0)
        ACTF(C, rc[0:64, :], rc[0:64, :], AF.Exp, [trc], [trc], scale=-1.0)
        TT(C, "dve", C.mixT[po:po + 64, ch, cs], oa[0:64, :], rc[0:64, :], ALU.mult, [toa, trc], [C.TMIX[ch]])


def _wout(C, l, phase):
    w_out = C.dr["w_out"]
    for dq in range(4):
        ws, tw, gname = wslot(C)
        wv = ws[:, 0:1024].rearrange("p (k n) -> p k n", k=4)
        LDW(C, wv, w_out[l, 512 * phase:512 * phase + 512, 256 * dq:256 * dq + 256].rearrange(
            "(k p) n -> p k n", p=128), gname, [tw])
        for dd in range(2):
            d = 2 * dq + dd
            for c in range(NCH):
                cs = slice(512 * c, 512 * c + 512)
                pb, tp = bank(C, "proj", [0, 1])
                for k in range(4):
                    MM(C, pb[:], wv[:, k, 128 * dd:128 * dd + 128], C.mixT[:, k, cs], k == 0, k == 3,
                       [tw, C.TMIX[k]], [tp])
                TT(C, "dve", C.xT[:, d, cs], pb[:], C.xT[:, d, cs], ALU.add, [tp, C.TX[c]], [C.TX[c]])


def _tok0_prepass(C, n_seq, n_layers):
    mk = C.mk
    AR = C.arena[:].bitcast(F32)
    st = {"o": 0}
    tiles = []

    def alloc(n, name):
        ap = AR[:, st["o"]:st["o"] + n]
        st["o"] += n
        t = T(ap, name)
        tiles.append(t)
        return ap, t
    slot, tslot = [], []
    for i in range(2):
        a_, t_ = alloc(4096, "pslot%d" % i)
        slot.append(a_)
        tslot.append(t_)
    sel, tsel = alloc(1152, "sel18")
    X0, tX = alloc(16, "X0")
    X0 = X0.rearrange("p (k s) -> p k s", s=2)
    Hh, tH = alloc(16, "H0")
    Hh = Hh.rearrange("p (k s) -> p k s", s=2)
    SQ, tSQ = alloc(64, "SQ0")
    SQ = SQ.rearrange("p (k s) -> p k s", s=2)
    Aa, tA = alloc(64, "A0")
    Aa = Aa.rearrange("p (k s) -> p k s", s=2)
    PRn, tPRn = alloc(8, "PRn")
    PRn = PRn.rearrange("p (k s) -> p k s", s=2)
    SG, tSG = alloc(2, "SG")
    CK, tCK = alloc(2, "CK")
    CKN, tCKN = alloc(2, "CKN")
    R, tR = alloc(40, "R0")
    R = R.rearrange("p (k s) -> p k s", s=2)
    O, tO = alloc(10, "O0")
    O = O.rearrange("p (k s) -> p k s", s=2)
    CEN, tCEN = alloc(10, "CEN")
    CEN = CEN.rearrange("p (k s) -> p k s", s=2)
    RS_, tRS = alloc(10, "RSTD")
    RS_ = RS_.rearrange("p (k s) -> p k s", s=2)
    SGG, tSGG = alloc(10, "SGG")
    SGG = SGG.rearrange("p (k s) -> p k s", s=2)
    MIX, tMIX = alloc(32, "MIX0")
    MIX = MIX.rearrange("p (k s) -> p k s", s=2)
    sm, tsm = alloc(4, "psmall")
    tmp, ttmp = alloc(24, "ptmp")
    tmp3 = tmp.rearrange("p (k s) -> p k s", s=2)
    claim(C, tiles)
    C.wpi = 0
    dr = C.dr

    def wload(src_ap, view_fn):
        i = C.wpi % 2
        C.wpi += 1
        v = view_fn(slot[i])
        mk.dma("sp", v, src_ap, "pslot%d" % i, writes=[tslot[i]])
        return v, tslot[i]

    def mvg(lhs_fn, rhs_fn, nk, out, tout, reads):
        for k in range(nk):
            MM(C, out, lhs_fn(k), rhs_fn(k), k == 0, k == nk - 1, reads, [tout])

    def rms0(X, tXx, nk, rows, gain_fn, out, tout, n):
        TT(C, "dve", SQ[0:rows, 0:nk, :], X[0:rows, 0:nk, :], X[0:rows, 0:nk, :], ALU.mult, [tXx], [tSQ])
        mvg(lambda k: C.onesF[0:rows, 0:rows], lambda k: SQ[0:rows, k, :], nk, C.pb[7][0:rows, 0:2], C.TP[7],
            [C.onesF, tSQ])
        ACTF(C, sm[0:rows, 0:2], C.pb[7][0:rows, 0:2], AF.Sqrt, [C.TP[7]], [tsm], bias=EPS, scale=1.0 / n)
        RECIP(C, sm[0:rows, 0:2], sm[0:rows, 0:2], [tsm], [tsm])
        for k in range(nk):
            STT(C, "dve", out[0:rows, k, :], X[0:rows, k, :], gain_fn(k), sm[0:rows, 0:2], ALU.mult, ALU.mult,
                [tXx, tsm], [tout])

    MEMSET(C, "pool", AR[:, 8192 + 1152:st["o"]], 0.0, tiles[3:])
    o_, w_ = BLOB_OFF["sel18"]
    mk.dma("sp", sel, dr["blob"][:, o_:o_ + w_], "pconst", writes=[tsel])
    for s_ in range(n_seq):
        mk.dma("sp", X0[:, :, s_], dr["x"][s_, 0, :].rearrange("(c p) -> p c", p=128), "pconst", writes=[tX],
               allow_slow_non_contiguous=True)
    ps1, tp1 = C.pb[0], C.TP[0]
    ps2, tp2 = C.pb[1], C.TP[1]
    ps3, tp3 = C.pb[2], C.TP[2]
    psS, tpS = C.pb[3], C.TP[3]
    psD, tpD = C.pb[4], C.TP[4]
    psU, tpU = C.pb[5], C.TP[5]
    psV, tpV = C.pb[6], C.TP[6]
    win = dr["w_in"]
    for l in range(n_layers):
        rms0(X0, tX, 8, 128, lambda k: C.g1[:, l, k:k + 1], Hh, tH, float(D))
        hk = lambda k: Hh[:, k, :]
        wv, tw = wload(win[l, :, OFF["vs"]:OFF["vs"] + 128].rearrange("(k p) n -> p k n", p=128),
                       lambda sl: sl[:, 0:1024].rearrange("p (k n) -> p k n", k=8))
        for g in range(2):
            mvg(lambda k, g=g: wv[:, k, 64 * g:64 * g + 64], hk, 8, ps1[0:64, 2 * g:2 * g + 2], tp1, [tw, tH])
        wv, tw = wload(win[l, :, OFF["vw"]:OFF["vw"] + 146].rearrange("(k p) n -> p k n", p=128),
                       lambda sl: sl[:, 0:8 * 146].rearrange("p (k n) -> p k n", k=8))
        for g in range(2):
            mvg(lambda k, g=g: wv[:, k, 64 * g:64 * g + 64], hk, 8, ps1[0:64, 4 + 2 * g:6 + 2 * g], tp1, [tw, tH])
        mvg(lambda k: wv[:, k, 128:146], hk, 8, ps1[0:18, 8:10], tp1, [tw, tH])
        CP(C, "dve", PRn[0:64, :, :], ps1[0:64, 0:8].rearrange("p (k s) -> p k s", s=2), [tp1], [tPRn])
        CP(C, "dve", SG[0:18, :], ps1[0:18, 8:10], [tp1], [tSG])
        ACTF(C, SG[0:18, :], SG[0:18, :], AF.Exp, [tSG], [tSG], scale=-1.0)
        TS(C, "dve", SG[0:18, :], SG[0:18, :], 1.0, None, ALU.add, None, [tSG], [tSG])
        RECIP(C, SG[0:18, :], SG[0:18, :], [tSG], [tSG])
        for h in range(6):
            for b in (1, 2):
                j = 3 * h + b
                ci = 2 * (2 * h + b - 1)
                MM(C, ps2[0:64, ci:ci + 2], sel[0:18, 64 * j:64 * j + 64], SG[0:18, :], True, True, [tsel, tSG], [tp2])
        for h in range(6):
            g = h // 3
            c1 = 2 * (2 * h)
            TT(C, "dve", tmp3[0:64, 0, :], ps2[0:64, c1:c1 + 2], PRn[0:64, g, :], ALU.mult, [tp2, tPRn], [ttmp])
            TT(C, "dve", tmp3[0:64, 1, :], ps2[0:64, c1 + 2:c1 + 4], PRn[0:64, 2 + g, :], ALU.mult, [tp2, tPRn], [ttmp])
            TT(C, "dve", MIX[0:64, h, :], tmp3[0:64, 0, :], tmp3[0:64, 1, :], ALU.add, [ttmp], [tMIX])
        wv, tw = wload(win[l, :, OFF["ckv"]:OFF["ckv"] + 128].rearrange("(k p) n -> p k n", p=128),
                       lambda sl: sl[:, 0:1024].rearrange("p (k n) -> p k n", k=8))
        mvg(lambda k: wv[:, k, :], hk, 8, ps1[:, 10:12], tp1, [tw, tH])
        CP(C, "dve", CK[:, :], ps1[:, 10:12], [tp1], [tCK])
        CK3 = CK.rearrange("p (k s) -> p k s", s=2)
        CKN3 = CKN.rearrange("p (k s) -> p k s", s=2)
        rms0(CK3, tCK, 1, 128, lambda k: C.gkv[:, l:l + 1], CKN3, tCKN, 128.0)
        wv, tw = wload(dr["mla_w_ukv"][l], lambda sl: sl[:, 0:640])
        for h in range(5):
            MM(C, psV[0:64, 2 * h:2 * h + 2], wv[:, 128 * h + 64:128 * h + 128], CKN[:, 0:2], True, True, [tw, tCKN], [tpV])
        CP(C, "dve", MIX[0:64, 6:11, :], psV[0:64, 0:10].rearrange("p (k s) -> p k s", s=2), [tpV], [tMIX])
        for piece in range(3):
            n = 512 if piece < 2 else 256
            c0 = OFF["rq"] + 512 * piece
            wv, tw = wload(win[l, :, c0:c0 + n].rearrange("(k p) n -> p k n", p=128),
                           lambda sl, n=n: sl[:, 0:8 * n].rearrange("p (k n) -> p k n", k=8))
            for gi in range(8 * piece, 8 * piece + n // 64):
                off = 64 * gi - 512 * piece
                mvg(lambda k, off=off: wv[:, k, off:off + 64], hk, 8, ps3[0:64, 2 * gi:2 * gi + 2], tp3, [tw, tH])
        CP(C, "dve", R[0:64, :, :], ps3[0:64, 0:40].rearrange("p (k s) -> p k s", s=2), [tp3], [tR])
        TT(C, "dve", O[0:64, :, :], R[0:64, 0:5, :], R[0:64, 5:10, :], ALU.mult, [tR], [tO])
        MM(C, psS[0:64, 0:10], C.onesF[0:64, 0:64], O[0:64, :, :].rearrange("p k s -> p (k s)"), True, True,
           [C.onesF, tO], [tpS])
        CP(C, "dve", C.s00[0:64, 10 * l:10 * l + 10], psS[0:64, 0:10], [tpS], [C.Ts00])
        TT(C, "dve", O[0:64, :, :], psS[0:64, 0:10].rearrange("p (k s) -> p k s", s=2), R[0:64, 10:15, :], ALU.mult,
           [tpS, tR], [tO])
        TS(C, "dve", O[0:64, :, :], O[0:64, :, :], 0.125, None, ALU.mult, None, [tO], [tO])
        MM(C, psS[0:64, 10:20], C.onesF[0:64, 0:64], O[0:64, :, :].rearrange("p k s -> p (k s)"), True, True,
           [C.onesF, tO], [tpS])
        STT(C, "dve", CEN[0:64, :, :], psS[0:64, 10:20].rearrange("p (k s) -> p k s", s=2), -1.0 / 64, O[0:64, :, :],
            ALU.mult, ALU.add, [tpS, tO], [tCEN])
        TT(C, "dve", O[0:64, :, :], CEN[0:64, :, :], CEN[0:64, :, :], ALU.mult, [tCEN], [tO])
        MM(C, C.pb[7][0:64, 2:12], C.onesF[0:64, 0:64], O[0:64, :, :].rearrange("p k s -> p (k s)"), True, True,
           [C.onesF, tO], [C.TP[7]])
        ACTF(C, RS_[0:64, :, :], C.pb[7][0:64, 2:12].rearrange("p (k s) -> p k s", s=2), AF.Sqrt, [C.TP[7]], [tRS],
             bias=EPS, scale=1.0 / 64)
        RECIP(C, RS_[0:64, :, :], RS_[0:64, :, :], [tRS], [tRS])
        ACTF(C, SGG[0:64, :, :], R[0:64, 15:20, :], AF.Exp, [tR], [tSGG], scale=-1.0)
        TS(C, "dve", SGG[0:64, :, :], SGG[0:64, :, :], 1.0, None, ALU.add, None, [tSGG], [tSGG])
        RECIP(C, SGG[0:64, :, :], SGG[0:64, :, :], [tSGG], [tSGG])
        TT(C, "dve", SGG[0:64, :, :], SGG[0:64, :, :], R[0:64, 15:20, :], ALU.mult, [tSGG, tR], [tSGG])
        for h in range(5):
            STT(C, "dve", CEN[0:64, h, :], CEN[0:64, h, :], C.gnT[:, 5 * l + h:5 * l + h + 1], RS_[0:64, h, :],
                ALU.mult, ALU.mult, [tCEN, tRS, C.gnT], [tCEN])
        TT(C, "dve", MIX[0:64, 11:16, :], CEN[0:64, :, :], SGG[0:64, :, :], ALU.mult, [tCEN, tSGG], [tMIX])
        for q4 in range(4):
            wv, tw = wload(dr["w_out"][l, :, 256 * q4:256 * q4 + 256].rearrange("(j p) n -> p j n", p=64),
                           lambda sl: sl[0:64, :].rearrange("p (j n) -> p j n", j=16))
            for oo in range(2):
                oc = 2 * q4 + oo
                mvg(lambda j, oo=oo: wv[:, j, 128 * oo:128 * oo + 128], lambda j: MIX[0:64, j, :], 16,
                    psD[:, 2 * oc:2 * oc + 2], tpD, [tw, tMIX])
        TT(C, "dve", X0, X0, psD[:, 0:16].rearrange("p (k s) -> p k s", s=2), ALU.add, [tpD, tX], [tX])
        rms0(X0, tX, 8, 128, lambda k: C.g2[:, l, k:k + 1], Hh, tH, float(D))
        for fb in range(8):
            wv, tw = wload(dr["w_up"][l, :, 512 * fb:512 * fb + 512].rearrange("(k p) n -> p k n", p=128),
                           lambda sl: sl[:, :].rearrange("p (k n) -> p k n", k=8))
            for fi in range(4):
                f = 4 * fb + fi
                mvg(lambda k, fi=fi: wv[:, k, 128 * fi:128 * fi + 128], hk, 8, psU[:, 2 * f:2 * f + 2], tpU, [tw, tH])
        ACTF(C, Aa, psU[:, 0:64].rearrange("p (k s) -> p k s", s=2), AF.Relu, [tpU], [tA])
        TT(C, "dve", Aa, Aa, Aa, ALU.mult, [tA], [tA])
        MEMSET(C, "dve", psD[:, 16:32], 0.0, [tpD])
        for fs in range(8):
            wv, tw = wload(dr["w_down"][l, 512 * fs:512 * fs + 512, :].rearrange("(k p) n -> p k n", p=128),
                           lambda sl: sl[:, :].rearrange("p (k n) -> p k n", k=4))
            for oc in range(8):
                for kk in range(4):
                    MM(C, psD[:, 16 + 2 * oc:18 + 2 * oc], wv[:, kk, 128 * oc:128 * oc + 128], Aa[:, 4 * fs + kk, :],
                       False, False, [tw, tA], [tpD])
        TT(C, "dve", X0, X0, psD[:, 16:32].rearrange("p (k s) -> p k s", s=2), ALU.add, [tpD, tX], [tX])


def _tok0_h(C, l):
    h0, th = C.h0, C.Th0
    x0 = C.xT[:, :, 0]
    ACTF(C, h0[:, 11:19], x0, AF.Square, [C.TX[0]], [th], accum_out=h0[:, 9:10])
    pb, tp = bank(C, "rms", [2, 3])
    MM(C, pb[:, 0:2], C.onesF[:, :], h0[:, 9:11], True, True, [C.onesF, th], [tp])
    ACTF(C, h0[:, 19:20], pb[:, 0:1], AF.Sqrt, [tp], [th], bias=EPS, scale=1.0 / D)
    RECIP(C, h0[:, 19:20], h0[:, 19:20], [th], [th])
    STT(C, "dve", h0[:, 0:8], x0, h0[:, 19:20], C.g1[:, l, :], ALU.mult, ALU.mult, [C.TX[0], th, C.g1], [th])


def _mixer(C, l, mixers):
    claim(C, C.TMIX)
    if "nsa" in mixers:
        _nsa(C, l, 0)
        _nsa(C, l, 1)
    else:
        for ch in range(3):
            MEMSET(C, "pool", C.mixT[:, ch, :], 0.0, [C.TMIX[ch]])
    if "mla" in mixers:
        _mla_pre(C, l)
        _mla_head(C, l, 0)
        _mla_head(C, l, 1)
    else:
        MEMSET(C, "pool", C.mixT[:, 3, :], 0.0, [C.TMIX[3]])
    _wout(C, l, 0)
    if "mla" in mixers:
        for h in (2, 3, 4):
            _mla_head(C, l, h)
    else:
        MEMSET(C, "pool", C.mixT[:, 0, :], 0.0, [C.TMIX[0]])
        MEMSET(C, "pool", C.mixT[0:64, 1, :], 0.0, [C.TMIX[1]])
    if "ret" in mixers:
        _ret(C, l)
    else:
        MEMSET(C, "pool", C.mixT[64:128, 1, :], 0.0, [C.TMIX[1]])
        for ch in (2, 3):
            MEMSET(C, "pool", C.mixT[:, ch, :], 0.0, [C.TMIX[ch]])
    _wout(C, l, 1)


def proj_rope(C, c, wa_fn, wb_fn, M, wts, dst, tdst, tabC, tCt, tabS, tSt, nr=16):
    cs = slice(512 * c, 512 * c + 512)
    pa, tpa = bank(C, "proj", [0, 1])
    for k in range(8):
        MM(C, pa[0:M, :], wa_fn(k), C.hT[:, k, cs], k == 0, k == 7, wts + [C.TH[c]], [tpa])
    pb, tpb = bank(C, "proj", [0, 1])
    for k in range(8):
        MM(C, pb[0:nr, :], wb_fn(k), C.hT[:, k, cs], k == 0, k == 7, wts + [C.TH[c]], [tpb])
    CP(C, "dve", dst[0:M, :], pa[0:M, :], [tpa], [tdst])
    ra, tra = tmp32(C)
    rb, trb = tmp32(C)
    TT(C, "dve", ra[0:nr, :], pa[0:nr, :], tabC[0:nr, cs], ALU.mult, [tpa, tCt], [tra])
    TT(C, "dve", rb[0:nr, :], pb[0:nr, :], tabS[0:nr, cs], ALU.mult, [tpb, tSt], [trb])
    TT(C, "pool", dst[0:nr, :], ra[0:nr, :], rb[0:nr, :], ALU.add, [tra, trb], [tdst])


def _nsa(C, l, g):
    mk = C.mk
    win = C.dr["w_in"]
    wa = WA(C)
    KVc, tKVc = wa.bf(S, "KVc")
    KTs, tKTs = wa.bf(S, "KTs")
    KTw, tKTw = wa.bf(S, "KTw")
    Vs, tVs = wa.bf(S, "Vs")
    Vs = Vs.rearrange("p (t n) -> p t n", t=16)
    Vw, tVw = wa.bf(S, "Vw")
    Vw = Vw.rearrange("p (t n) -> p t n", t=16)
    QT, tQT = [], []
    for r in range(3):
        a_, t_ = wa.bf(512, "nQT%d" % r)
        QT.append(a_)
        tQT.append(t_)
    PT, tPT = [], []
    for i in range(2):
        a_, t_ = wa.bf(512, "nPT%d" % i)
        PT.append(a_)
        tPT.append(t_)
    acc, tacc = [], []
    for r in range(3):
        a_, t_ = wa.f32(512, "nacc%d" % r)
        acc.append(a_)
        tacc.append(t_)
    kcT, tkcT = wa.bf(128, "kcT")
    Vc, tVc = wa.bf(128, "Vcaug")
    hid, thid = wa.bf(128, "hid")
    Wn2, tWn2 = wa.bf(640, "Wn2")
    Wn2 = Wn2.rearrange("p (k n) -> p k n", k=8)
    WP, tWP = wa.bf(768, "nWP")
    WP = WP.rearrange("p (k n) -> p k n", k=8)
    glog, tglog = wa.bf(512, "glog")
    selb, tselb = wa.bf(128, "selb")
    w2, tw2 = wa.bf(128, "w2kv")
    posT, tposT = wa.bf(64, "posT")
    zt, tzt = wa.f32(128, "gz")
    ut, tut = wa.f32(128, "gu")
    sm, tsm = wa.f32(128, "nsmall")
    bsb, tbsb = wa.f32(4, "nbias")
    claim(C, wa.tiles)
    C.pti = 0
    ws, tw, gname = wslot(C)
    Wn = ws[:].rearrange("p (k n) -> p k n", k=8)
    cols = [(OFF["q"] + 192 * g, 192, 0), (OFF["kc"] + 64 * g, 64, 192), (OFF["vc"] + 64 * g, 64, 256),
            (OFF["ks"] + 64 * g, 64, 320), (OFF["kw"] + 64 * g, 64, 384), (OFF["vs"] + 64 * g, 64, 448)]
    for (src, n, dst) in cols:
        LDW(C, Wn[:, :, dst:dst + n], win[l, :, src:src + n].rearrange("(k p) n -> p k n", p=128), gname, [tw])
    LDW(C, Wn2[:, :, 0:64], win[l, :, OFF["vw"] + 64 * g:OFF["vw"] + 64 * g + 64].rearrange("(k p) n -> p k n", p=128),
        "nsaw", [tWn2])
    LDW(C, Wn2[:, :, 64:73], win[l, :, OFF["gate"] + 9 * g:OFF["gate"] + 9 * g + 9].rearrange(
        "(k p) n -> p k n", p=128), "nsaw", [tWn2], allow_slow_non_contiguous=True)
    ws1, tw1, gname1 = wslot(C)
    w1 = ws1[:].rearrange("p (l h) -> p l h", l=32)
    LDW(C, w1[0:64], C.dr["cmp_w1_k"][l].rearrange("(l d) h -> d l h", d=64), gname1, [tw1])
    LDW(C, w1[64:128], C.dr["cmp_w1_v"][l].rearrange("(l d) h -> d l h", d=64), gname1, [tw1])
    LDW(C, w2[:, 0:64], C.dr["cmp_w2_k"][l], "nsaw", [tw2])
    LDW(C, w2[:, 64:128], C.dr["cmp_w2_v"][l], "nsaw", [tw2])
    MEMSET(C, "pool", posT, 0.0, [tposT])
    LDW(C, posT[0:64, 0:32], C.dr["cmp_pos_k"][l].rearrange("l d -> d l"), "nsaw", [tposT],
        allow_slow_non_contiguous=True)
    LDW(C, posT[64:128, 0:32], C.dr["cmp_pos_v"][l].rearrange("l d -> d l"), "nsaw", [tposT],
        allow_slow_non_contiguous=True)
    o, w_ = BLOB_OFF["E"]
    LDW(C, KTs[64:96, :], C.dr["blob"][64:96, o:o + w_], "nsaw", [tKTs])
    for i, off in enumerate((0, 64, 128, 192, 320, 384)):
        CP(C, "pool", WP[:, :, 16 * i:16 * i + 8], Wn[:, :, off + 8:off + 16], [tw], [tWP])
        CP(C, "pool", WP[:, :, 16 * i + 8:16 * i + 16], Wn[:, :, off:off + 8], [tw], [tWP])
    MEMSET(C, "pool", Vs[:, :, 64:128], 1.0, [tVs])
    MEMSET(C, "pool", Vw[:, :, 64:128], 1.0, [tVw])
    MEMSET(C, "pool", Vc[:, 64:128], 1.0, [tVc])
    MEMSET(C, "pool", Vc[:, 0:64], 0.0, [tVc])
    MEMSET(C, "pool", selb, 0.0, [tselb])
    for c in range(NCH):
        cs = slice(512 * c, 512 * c + 512)
        proj_rope(C, c, lambda k: Wn[:, k, 192:320], lambda k: WP[:, k, 48:64], 128, [tw, tWP], KVc[:, cs], tKVc,
                  C.CA, C.TCA, C.SA, C.TSA)
        proj_rope(C, c, lambda k: Wn[:, k, 320:384], lambda k: WP[:, k, 64:80], 64, [tw, tWP], KTs[:, cs], tKTs,
                  C.CA, C.TCA, C.SA, C.TSA)
        proj_rope(C, c, lambda k: Wn[:, k, 384:448], lambda k: WP[:, k, 80:96], 64, [tw, tWP], KTw[:, cs], tKTw,
                  C.CA, C.TCA, C.SA, C.TSA)
        pv, tpv = bank(C, "proj", [0, 1])
        for tl in range(4):
            t = 4 * c + tl
            for k in range(8):
                MM(C, pv[:, 128 * tl:128 * tl + 64], C.hT[:, k, 128 * t:128 * t + 128], Wn[:, k, 448:512], k == 0, k == 7,
                   [tw, C.TH[c]], [tpv])
            for k in range(8):
                MM(C, pv[:, 128 * tl + 64:128 * tl + 128], C.hT[:, k, 128 * t:128 * t + 128], Wn2[:, k, 0:64], k == 0,
                   k == 7, [tWn2, C.TH[c]], [tpv])
        pv3 = pv[:].rearrange("p (a b) -> p a b", a=4)
        CP(C, "act", Vs[:, 4 * c:4 * c + 4, 0:64], pv3[:, :, 0:64], [tpv], [tVs])
        CP(C, "act", Vw[:, 4 * c:4 * c + 4, 0:64], pv3[:, :, 64:128], [tpv], [tVw])
    for kind in (0, 1):
        r0 = 64 * kind
        pbias, tpbias = bank(C, "oa", [6, 7])
        for li in range(32):
            MM(C, pbias[:, 0:2], w1[r0:r0 + 64, li, :], posT[r0:r0 + 64, li:li + 2], li == 0, li == 31,
               [tw1, tposT], [tpbias])
        CP(C, "dve", bsb[:, 0:2], pbias[:, 0:2], [tpbias], [tbsb])
        ph, tph = bank(C, "st", [4, 5])
        for li in range(32):
            MM(C, ph[:, 0:127], w1[r0:r0 + 64, li, :], KVc[r0:r0 + 64, li:li + 2017:16], li == 0, li == 31,
               [tw1, tKVc], [tph])
        ACTF(C, zt[:, 0:127], ph[:, 0:127], AF.Identity, [tph, tbsb], [tzt], bias=bsb[:, 0:1], scale=1.0)
        TT(C, "pool", ut[:, 0:127], zt[:, 0:127], zt[:, 0:127], ALU.mult, [tzt], [tut])
        TS(C, "pool", ut[:, 0:127], ut[:, 0:127], 0.044715, None, ALU.mult, None, [tut], [tut])
        TS(C, "pool", ut[:, 0:127], ut[:, 0:127], 1.0, None, ALU.add, None, [tut], [tut])
        TT(C, "pool", ut[:, 0:127], ut[:, 0:127], zt[:, 0:127], ALU.mult, [tut, tzt], [tut])
        ACTF(C, ut[:, 0:127], ut[:, 0:127], AF.Exp, [tut], [tut], scale=-1.5957691216057308)
        TS(C, "pool", ut[:, 0:127], ut[:, 0:127], 1.0, None, ALU.add, None, [tut], [tut])
        RECIP(C, ut[:, 0:127], ut[:, 0:127], [tut], [tut])
        TT(C, "dve", hid[:, 0:127], zt[:, 0:127], ut[:, 0:127], ALU.mult, [tzt, tut], [thid])
        pk, tpk = bank(C, "oa", [6, 7])
        if kind == 0:
            MM(C, pk[0:64, 0:127], w2[:, 0:64], hid[:, 0:127], True, True, [tw2, thid], [tpk])
            CP(C, "act", kcT[0:64, 0:127], pk[0:64, 0:127], [tpk], [tkcT])
        else:
            MM(C, pk[0:127, 0:64], hid[:, 0:127], w2[:, 64:128], True, True, [tw2, thid], [tpk])
            CP(C, "act", Vc[0:127, 0:64], pk[0:127, 0:64], [tpk], [tVc])

    def combine(oa, toa, r, b, c, mode):
        cs = slice(512 * c, 512 * c + 512)
        rc, trc = tmp32(C)
        ACTF(C, rc[0:64, :], oa[64:128, :], AF.Ln, [toa], [trc], bias=1e-30, scale=1.0)
        j = 3 * r + b
        pg2, tpg2 = bank(C, "g", [2])
        MM(C, pg2[0:64, :], C.sel9B[0:9, 64 * j:64 * j + 64], glog[0:9, :], True, True, [C.sel9B, tglog], [tpg2])
        gs, tgs = tmp32(C)
        ACTF(C, gs[0:64, :], pg2[0:64, :], AF.Exp, [tpg2], [tgs], scale=-1.0)
        ACTF(C, gs[0:64, :], gs[0:64, :], AF.Ln, [tgs], [tgs], bias=1.0, scale=1.0)
        TT(C, "pool", rc[0:64, :], rc[0:64, :], gs[0:64, :], ALU.add, [trc, tgs], [trc])
        ACTF(C, rc[0:64, :], rc[0:64, :], AF.Exp, [trc], [trc], scale=-1.0)
        if mode == 0:
            TT(C, "dve", acc[r][0:64, :], oa[0:64, :], rc[0:64, :], ALU.mult, [toa, trc], [tacc[r]])
        elif mode == 1:
            TT(C, "dve", gs[0:64, :], oa[0:64, :], rc[0:64, :], ALU.mult, [toa, trc, tgs], [tgs])
            TT(C, "pool", acc[r][0:64, :], acc[r][0:64, :], gs[0:64, :], ALU.add, [tacc[r], tgs], [tacc[r]])
        else:
            ch, po = _mix_dst(C, 64 * (3 * g + r))
            TT(C, "dve", gs[0:64, :], oa[0:64, :], rc[0:64, :], ALU.mult, [toa, trc, tgs], [tgs])
            TT(C, "dve", C.mixT[po:po + 64, ch, cs], acc[r][0:64, :], gs[0:64, :], ALU.add, [tacc[r], tgs],
               [C.TMIX[ch]])

    for c in range(NCH):
        cs = slice(512 * c, 512 * c + 512)
        for r in range(3):
            proj_rope(C, c, lambda k, r=r: Wn[:, k, 64 * r:64 * r + 64], lambda k, r=r: WP[:, k, 16 * r:16 * r + 16], 64,
                      [tw, tWP], QT[r], tQT[r], C.CA, C.TCA, C.SA, C.TSA)
            if c < 2:
                MEMSET(C, "pool", QT[r][64:96, :], 0.0, [tQT[r]])
        pg, tpg = bank(C, "g", [2])
        for k in range(8):
            MM(C, pg[0:9, :], Wn2[:, k, 64:73], C.hT[:, k, cs], k == 0, k == 7, [tWn2, C.TH[c]], [tpg])
        CP(C, "act", glog[0:9, :], pg[0:9, :], [tpg], [tglog])
        impb, timp = C.pb[3], C.TP[3]
        for r in range(3):
            st, tst = bank(C, "st", [4, 5])
            MM(C, st[0:127, :], kcT[0:64, 0:127], QT[r][0:64, :], True, False, [tkcT, tQT[r]], [tst])
            MM(C, st[0:127, :], C.identB[0:127, 0:127], C.cmpbB[0:127, cs], False, True, [C.identB, C.cmpbB], [tst])
            pi = C.pti % 2
            C.pti += 1
            ACTF(C, PT[pi][0:127, :], st[0:127, :], AF.Exp, [tst], [tPT[pi]], scale=0.125)
            oa, toa = bank(C, "oa", [6, 7])
            MM(C, oa[:, :], Vc[0:127, :], PT[pi][0:127, :], True, True, [tVc, tPT[pi]], [toa])
            if c >= 2:
                for tl in range(4):
                    o0 = (tl * 3 + r) * 33
                    MM(C, impb[:, o0:o0 + 33], PT[pi][0:127, 128 * tl:128 * tl + 128], C.ovl1B[0:127, 0:33], True, True,
                       [tPT[pi], C.ovl1B], [timp])
            combine(oa, toa, r, 0, c, 0)
        if c >= 2:
            for tl in range(4):
                kq = 4 * c + tl - 8
                bb = sm[:, 0:32]
                t2 = sm[:, 32:64]
                m1 = sm[:, 64:72]
                m2 = sm[:, 72:80]
                dn = sm[:, 80:83]
                keep = C.keepF[:, 32 * kq:32 * kq + 32]
                addt = C.addtF[:, 32 * kq:32 * kq + 32]
                TS(C, "dve", dn, impb[:, tl * 99 + 32:tl * 99 + 99:33], 1e-30, None, ALU.max, None, [timp], [tsm])
                RECIP(C, dn, dn, [tsm], [tsm])
                for r in range(3):
                    o0 = (tl * 3 + r) * 33
                    if r == 0:
                        STT(C, "dve", bb, impb[:, o0:o0 + 32], dn[:, 0:1], keep, ALU.mult, ALU.mult,
                            [timp, tsm, C.keepF], [tsm])
                    else:
                        STT(C, "dve", t2, impb[:, o0:o0 + 32], dn[:, r:r + 1], keep, ALU.mult, ALU.mult,
                            [timp, tsm, C.keepF], [tsm])
                        TT(C, "dve", bb, bb, t2, ALU.add, [tsm], [tsm])
                TT(C, "dve", bb, bb, addt, ALU.add, [tsm, C.addtF], [tsm])
                mk.op("dve", lambda e, m1=m1, bb=bb: e.max(out=m1, in_=bb), [tsm], [tsm])
                mk.op("dve", lambda e, m1=m1, bb=bb, t2=t2: e.match_replace(out=t2, in_to_replace=m1, in_values=bb,
                                                                        imm_value=-3.0e38), [tsm], [tsm])
                mk.op("dve", lambda e, m2=m2, t2=t2: e.max(out=m2, in_=t2), [tsm], [tsm])
                STT(C, "dve", t2, bb, m2[:, 7:8], C.bigF[:, 0:32], ALU.subtract, ALU.mult, [tsm, C.bigF], [tsm])
                TS(C, "dve", t2, t2, 0.0, None, ALU.min, None, [tsm], [tsm])
                TS(C, "dve", selb[:, 64:96], t2, -1.0, None, ALU.max, None, [tsm], [tselb])
                ptr, tptr = bank(C, "proj", [0, 1])
                ptb = ptr[:].bitcast(BF16)
                TR(C, ptb[0:96, 0:128], selb[:, 0:96], C.identB[:, :], [tselb, C.identB], [tptr])
                for r in range(3):
                    CP(C, "act", QT[r][64:96, 128 * tl:128 * tl + 128], ptb[64:96, 0:128], [tptr], [tQT[r]])
        for r in range(3):
            oa, toa = attn_chunk(C, c, lambda j: KTs[0:96, 128 * j:128 * j + 128], tKTs, 96, QT[r], tQT[r],
                                 lambda j: Vs[:, j, :], tVs, "causal", 0.125, PT, tPT)
            combine(oa, toa, r, 1, c, 1)
            oa, toa = attn_chunk(C, c, lambda j: KTw[0:64, 128 * j:128 * j + 128], tKTw, 64, QT[r], tQT[r],
                                 lambda j: Vw[:, j, :], tVw, "window", 0.125, PT, tPT)
            combine(oa, toa, r, 2, c, 2)


def _ret(C, l):
    mk = C.mk
    RS = 3
    lg = np.log(1.0 - 2.0 ** (-5.0 - np.arange(5, dtype=np.float64)))
    for h in range(5):
        wa = WA(C)
        qT, tq = wa.bf(S, "rqT")
        kT, tk = wa.bf(S, "rkT")
        qd, tqd = wa.bf(S, "rqd")
        v, tv = wa.bf(1024, "rv")
        v = v.rearrange("p (t n) -> p t n", t=16)
        gt, tg = wa.bf(1024, "rg")
        gt = gt.rearrange("p (t n) -> p t n", t=16)
        W, tW = wa.bf(8 * 256, "rW")
        W = W.rearrange("p (i k n) -> p i k n", i=4, k=8)
        WP, tWP = wa.bf(8 * 128, "rWP")
        WP = WP.rearrange("p (k n) -> p k n", k=8)
        scm, tscm = wa.bf(128, "rscm")
        kw, tkw = wa.bf(64, "rkw")
        sbf, tsbf = wa.bf(64, "rsbf")
        sf, tsf = wa.f32(64, "rsf")
        yt, tyt = wa.bf(128, "ryt")
        osb, tosb = wa.f32(64, "rosb")
        junk, tjunk = wa.f32(64, "rjunk")
        stt, tstt = wa.f32(16, "rstat")
        gn, tgn = wa.f32(64, "rgn")
        claim(C, wa.tiles)
        win = C.dr["w_in"]
        ch, po = _mix_dst(C, 704 + 64 * h)
        if RS <= -2:
            MEMSET(C, "pool", C.mixT[po:po + 64, ch, :], 0.0, [C.TMIX[ch]])
            continue
        for i, nm in enumerate(("rq", "rk", "rv", "rg")):
            LDW(C, W[:, i, :, :], win[l, :, OFF[nm] + 64 * h:OFF[nm] + 64 * h + 64].rearrange(
                "(k p) n -> p k n", p=128), "retw", [tW])
        mk.dma("sp", gn, C.dr["ret_gn_gain"][l, h, :].partition_broadcast(128), "const", writes=[tgn])
        for i in range(2):
            CP(C, "pool", WP[:, :, 64 * i:64 * i + 32], W[:, i, :, 32:64], [tW], [tWP])
            CP(C, "pool", WP[:, :, 64 * i + 32:64 * i + 64], W[:, i, :, 0:32], [tW], [tWP])
        MEMSET(C, "pool", yt, 0.0, [tyt])
        if RS == -1:
            MEMSET(C, "pool", C.mixT[po:po + 64, ch, :], 0.0, [C.TMIX[ch]])
            continue
        for c in range(NCH):
            cs = slice(512 * c, 512 * c + 512)
            for i, (dst, tdst) in enumerate(((qT, tq), (kT, tk))):
                pa, tpa = bank(C, "proj", [0, 1])
                for k in range(8):
                    MM(C, pa[0:64, :], W[:, i, k, :], C.hT[:, k, cs], k == 0, k == 7, [tW, C.TH[c]], [tpa])
                pb, tpb = bank(C, "proj", [0, 1])
                for k in range(8):
                    MM(C, pb[0:64, :], WP[:, k, 64 * i:64 * i + 64], C.hT[:, k, cs], k == 0, k == 7, [tWP, C.TH[c]], [tpb])
                ra, tra = tmp32(C)
                rb, trb = tmp32(C)
                TT(C, "dve", ra[0:64, :], pa[0:64, :], C.CR[0:64, cs], ALU.mult, [tpa, C.TCR], [tra])
                TT(C, "dve", rb[0:64, :], pb[0:64, :], C.SR[0:64, cs], ALU.mult, [tpb, C.TSR], [trb])
                TT(C, "pool", dst[0:64, cs], ra[0:64, :], rb[0:64, :], ALU.add, [tra, trb], [tdst])
            if RS != 0:
                pv, tpv = bank(C, "proj", [0, 1])
                for tl in range(4):
                    t = 4 * c + tl
                    for k in range(8):
                        MM(C, pv[:, 128 * tl:128 * tl + 64], C.hT[:, k, 128 * t:128 * t + 128], W[:, 2, k, :], k == 0,
                           k == 7, [tW, C.TH[c]], [tpv])
                    for k in range(8):
                        MM(C, pv[:, 128 * tl + 64:128 * tl + 128], C.hT[:, k, 128 * t:128 * t + 128], W[:, 3, k, :],
                           k == 0, k == 7, [tW, C.TH[c]], [tpv])
                pv3 = pv[:].rearrange("p (a b) -> p a b", a=4)
                CP(C, "act", v[:, 4 * c:4 * c + 4, :], pv3[:, :, 0:64], [tpv], [tv])
                ge, tge = tmp32(C)
                ge3 = ge[:, 0:256].rearrange("p (a b) -> p a b", a=4)
                ACTF(C, ge3, pv3[:, :, 64:128], AF.Exp, [tpv], [tge], scale=-1.0)
                ACTF(C, ge[:, 0:256], ge[:, 0:256], AF.Ln, [tge], [tge], bias=1.0, scale=1.0)
                ACTF(C, ge[:, 0:256], ge[:, 0:256], AF.Exp, [tge], [tge], scale=-1.0)
                if True:
                    TT(C, "dve", gt[:, 4 * c:4 * c + 4, :], pv3[:, :, 64:128], ge3, ALU.mult, [tpv, tge], [tg])
        if RS in (0, 5, 6, 7, 8, 9):
            MEMSET(C, "pool", C.mixT[po:po + 64, ch, :], 0.0, [C.TMIX[ch]])
            continue
        for n in range(16):
            ns = slice(128 * n, 128 * n + 128)
            TT(C, "pool", qd[0:64, ns], qT[0:64, ns], C.rdB[0:64, 128 * h:128 * h + 128], ALU.mult, [tq, C.rdB], [tqd])
        cd = float(np.exp(128.0 * lg[h]))
        if RS == 1:
            MEMSET(C, "pool", C.mixT[po:po + 64, ch, :], 0.0, [C.TMIX[ch]])
            continue
        for n in range(16):
            ns = slice(128 * n, 128 * n + 128)
            ps, tps = bank(C, "st", [4, 5])
            MM(C, ps[:, 0:128], kT[0:64, ns], qT[0:64, ns], True, True, [tk, tq], [tps])
            TT(C, "dve", scm, ps[:, 0:128], C.intraTF[:, 128 * h:128 * h + 128], ALU.mult, [tps, C.intraTF], [tscm])
            if n == 0:
                ix = (l * 5 + h) * 2 + C.cur_seq
                TS(C, "dve", scm[0:1, 0:1], C.s00[0:1, ix:ix + 1], 0.125, None, ALU.mult, None, [C.Ts00], [tscm])
            po_, tpo = bank(C, "oa", [6, 7])
            MM(C, po_[:, 0:64], scm, v[:, n, :], True, n == 0, [tscm, tv], [tpo])
            if n > 0:
                MM(C, po_[:, 0:64], qd[0:64, ns], sbf[0:64, :], False, True, [tqd, tsbf], [tpo])
            if n < 15:
                pk, tpk = bank(C, "st", [4, 5])
                MM(C, pk[:, 0:64], kT[0:64, ns], C.identB[0:64, 0:64], True, True, [tk, C.identB], [tpk])
                ACTF(C, kw, pk[:, 0:64], AF.Identity, [tpk, C.wdF], [tkw], scale=C.wdF[:, h:h + 1])
                pst, tpst = bank(C, "rms", [2, 3])
                MM(C, pst[0:64, 0:64], kw, v[:, n, :], True, True, [tkw, tv], [tpst])
                if n == 0:
                    CP(C, "dve", sf[0:64, :], pst[0:64, 0:64], [tpst], [tsf])
                else:
                    STT(C, "dve", sf[0:64, :], sf[0:64, :], cd, pst[0:64, 0:64], ALU.mult, ALU.add, [tsf, tpst], [tsf])
                CP(C, "pool", sbf[0:64, :], sf[0:64, :], [tsf], [tsbf])
            if RS == 2:
                CP(C, "act", yt[:, po:po + 64], po_[:, 0:64], [tpo], [tyt])
                ptr, tptr = bank(C, "proj", [0, 1])
                ptb = ptr[:].bitcast(BF16)
                TR(C, ptb[:, 0:128], yt, C.identB[:, :], [tyt, C.identB], [tptr])
                CP(C, "act", C.mixT[po:po + 64, ch, ns], ptb[po:po + 64, 0:128], [tptr], [C.TMIX[ch]])
                continue
            ACTF(C, osb, po_[:, 0:64], AF.Identity, [tpo], [tosb, tstt], accum_out=stt[:, 0:1])
            ACTF(C, junk, po_[:, 0:64], AF.Square, [tpo], [tjunk, tstt], accum_out=stt[:, 1:2])
            TS(C, "dve", stt[:, 2:3], stt[:, 0:1], 1.0 / 64, None, ALU.mult, None, [tstt], [tstt])
            TT(C, "dve", stt[:, 3:4], stt[:, 2:3], stt[:, 2:3], ALU.mult, [tstt], [tstt])
            STT(C, "dve", stt[:, 4:5], stt[:, 1:2], 1.0 / 64, stt[:, 3:4], ALU.mult, ALU.subtract, [tstt], [tstt])
            ACTF(C, stt[:, 5:6], stt[:, 4:5], AF.Ln, [tstt], [tstt], bias=EPS, scale=1.0)
            ACTF(C, stt[:, 6:7], stt[:, 5:6], AF.Exp, [tstt], [tstt], scale=-0.5)
            TT(C, "dve", stt[:, 7:8], stt[:, 2:3], stt[:, 6:7], ALU.mult, [tstt], [tstt])
            STT(C, "dve", osb, osb, stt[:, 6:7], stt[:, 7:8].to_broadcast([128, 64]), ALU.mult, ALU.subtract,
                [tosb, tstt], [tosb])
            TT(C, "pool", osb, osb, gn, ALU.mult, [tosb, tgn], [tosb])
            TT(C, "pool", yt[:, po:po + 64], osb, gt[:, n, :], ALU.mult, [tosb, tg], [tyt])
            ptr, tptr = bank(C, "proj", [0, 1])
            ptb = ptr[:].bitcast(BF16)
            TR(C, ptb[:, 0:128], yt, C.identB[:, :], [tyt, C.identB], [tptr])
            CP(C, "act", C.mixT[po:po + 64, ch, ns], ptb[po:po + 64, 0:128], [tptr], [C.TMIX[ch]])


_CACHE = {}


def kernel(**inputs):
    n_cores = 8
    x = np.ascontiguousarray(np.asarray(inputs["x"], dtype=np.float32))
    pos = np.ascontiguousarray(np.asarray(inputs["positions"], dtype=np.int32))
    if "nc" not in _CACHE:
        _CACHE["nc"] = build()[0]
        _CACHE["blob"] = make_blob()[0]
    nc = _CACHE["nc"]
    w = {n: np.ascontiguousarray(np.asarray(inputs[n], dtype=np.float32)) for n in WNAMES}
    in_maps = []
    for i in range(n_cores):
        m = {"x": x[2 * i:2 * i + 2], "pos": pos[2 * i:2 * i + 2], "blob": _CACHE["blob"]}
        m.update(w)
        in_maps.append(m)
    res = run_bass_kernel_spmd(nc, in_maps, core_ids=list(range(n_cores)))
    return np.concatenate([np.asarray(r["out"]) for r in res.results], axis=0).astype(np.float32)
```

```python
from contextlib import ExitStack
import numpy as np
import concourse.bass as bass
import concourse.mybir as mybir
from concourse.bass_utils import run_bass_kernel_spmd

F32 = mybir.dt.float32
BF16 = mybir.dt.bfloat16
I32 = mybir.dt.int32
AF = mybir.ActivationFunctionType
ALU = mybir.AluOpType
AX = mybir.AxisListType

D = 1024
S = 2048
DEPTH = 4
NCH = 4
NT = 16
N_IN = 2866
NEG = -30000.0
EPS = 1e-6
OFF = dict(q=0, kc=384, vc=512, ks=640, vs=768, kw=896, vw=1024, gate=1152, cq=1170, ckv=1426,
           kpe=1554, rq=1586, rk=1906, rv=2226, rg=2546)

ENGS = ("pe", "act", "dve", "pool", "sp")


class T:
    __slots__ = ("ap", "name", "lw", "rd")

    def __init__(self, ap, name=""):
        self.ap = ap
        self.name = name
        self.lw = {}
        self.rd = {}

    def __getitem__(self, idx):
        return self.ap[idx]


class DmaGroup:
    def __init__(self, mk, name):
        self.name = name
        self.total = 0
        self.sem = mk._newsem("g_" + name)


class MK:
    def __init__(self, nc, stack):
        self.nc = nc
        self.stack = stack
        self.q = {e: [] for e in ENGS}
        self.cnt = {e: 0 for e in ENGS}
        self.waited = {e: {} for e in ENGS}
        self.esem = {e: self._newsem("c_" + e) for e in ENGS}
        self.groups = {}

    def _newsem(self, name):
        return self.stack.enter_context(self.nc.semaphore(name))

    def group(self, name):
        g = self.groups.get(name)
        if g is None:
            g = DmaGroup(self, name)
            self.groups[name] = g
        return g

    def sb(self, name, shape, dt):
        return self.stack.enter_context(self.nc.sbuf_tensor(name, list(shape), dt))

    def ps(self, name, shape, dt):
        return self.stack.enter_context(self.nc.psum_tensor(name, list(shape), dt))

    def _collect(self, eng, reads, writes):
        need = {}

        def add(d, same_ok):
            for k, v in d.items():
                if k == eng and not same_ok:
                    continue
                if need.get(k, 0) < v:
                    need[k] = v
        raw_same = eng != "pe"
        for t in reads:
            add(t.lw, raw_same)
        for t in writes:
            add(t.lw, False)
            add(t.rd, False)
        out = []
        w = self.waited[eng]
        for k, v in need.items():
            if isinstance(k, DmaGroup):
                v = k.total
            if w.get(k, 0) >= v:
                continue
            w[k] = v
            out.append((k, v))
        return out

    def _mark(self, key, val, reads, writes):
        for t in reads:
            if t.rd.get(key, 0) < val:
                t.rd[key] = val
        for t in writes:
            t.lw = {key: val}
            t.rd = {}

    def op(self, eng, fn, reads=(), writes=()):
        waits = self._collect(eng, reads, writes)
        self.cnt[eng] += 1
        self.q[eng].append((fn, waits, None))
        self._mark(eng, self.cnt[eng], reads, writes)

    def dma(self, eng, out_ap, in_ap, group, reads=(), writes=(), **kw):
        waits = self._collect(eng, reads, writes)
        g = self.group(group) if isinstance(group, str) else group
        g.total += 16
        self.q[eng].append((lambda e: e.dma_start(out=out_ap, in_=in_ap, **kw), waits, g))
        self._mark(g, g.total, reads, writes)

    def claim(self, tiles, others):
        for t in tiles:
            for u in others:
                if u is t:
                    continue
                for d in (u.lw, u.rd):
                    for k, v in d.items():
                        if t.rd.get(k, 0) < v:
                            t.rd[k] = v

    def wait_all_dma(self, eng="sp"):
        waits = []
        for g in self.groups.values():
            if g.total and self.waited[eng].get(g, 0) < g.total:
                waits.append((g, g.total))
                self.waited[eng][g] = g.total
        self.q[eng].append((None, waits, None))

    def _sem(self, k):
        return k.sem if isinstance(k, DmaGroup) else self.esem[k]

    def emit(self):
        nc = self.nc
        mk = self
        with nc.Block() as block:
            def run(engname, e):
                for fn, waits, g in mk.q[engname]:
                    if fn is None:
                        for k, v in waits:
                            e.wait_ge(mk._sem(k), v)
                        continue
                    for k, v in waits[1:]:
                        e.wait_ge(mk._sem(k), v)
                    ins = fn(e)
                    if waits:
                        k, v = waits[0]
                        ins._wait_ge(mk._sem(k), v)
                    if g is not None:
                        ins.then_inc(g.sem, 16)
                    else:
                        ins.then_inc(mk.esem[engname], 1)

            @block.sync
            def _(e):
                run("sp", e)

            @block.scalar
            def _(e):
                run("act", e)

            @block.vector
            def _(e):
                run("dve", e)

            @block.gpsimd
            def _(e):
                run("pool", e)

            @block.tensor
            def _(e):
                run("pe", e)


def _blob_layout():
    secs = [("ident", 128), ("tri", 128), ("winb", 128), ("cmpb", 2048), ("E", 2048), ("ovl1", 33),
            ("keep", 256), ("addt", 256), ("intraT", 640), ("rd", 640), ("wd", 5), ("cols", 8),
            ("sel9", 576), ("ones", 128), ("sel18", 1152)]
    off = {}
    o = 0
    for n, w in secs:
        off[n] = (o, w)
        o += w
    return off, o


BLOB_OFF, BLOB_W = _blob_layout()


def make_blob():
    b = np.zeros((128, BLOB_W), np.float32)

    def sec(n):
        o, w = BLOB_OFF[n]
        return b[:, o:o + w]
    p = np.arange(128)
    sec("ident")[:] = np.eye(128, dtype=np.float32)
    sec("ones")[:] = 1.0
    sec("tri")[:] = np.where(p[:, None] <= p[None, :], 0.0, NEG)
    sec("winb")[:] = np.where(p[:, None] > p[None, :], 0.0, NEG)
    q = np.arange(S)
    cm = np.where((16 * p[:, None] + 31) <= q[None, :], 0.0, NEG)
    cm[127, :] = NEG
    sec("cmpb")[:] = cm
    e = np.zeros((128, S), np.float32)
    for r in range(32):
        e[64 + r, 64 * r:64 * r + 64] = -NEG
    sec("E")[:] = e
    n_cmp, n_sel = 127, 32
    cs = np.arange(n_cmp) * 16
    ce = cs + 32
    ss = np.arange(n_sel) * 64
    se = ss + 64
    ov = np.clip(np.minimum(ce[:, None], se[None, :]) - np.maximum(cs[:, None], ss[None, :]), 0, None) / 32.0
    o1 = sec("ovl1")
    o1[:127, :32] = ov
    o1[:127, 32] = 1.0
    keep = np.zeros((128, 8, 32), np.float32)
    addt = np.zeros((128, 8, 32), np.float32)
    blk = np.arange(32)
    for qt in range(8, 16):
        t = qt * 128 + p
        cur = t // 64
        valid = blk[None, :] <= cur[:, None]
        forced = (blk[None, :] == 0) | (blk[None, :] == cur[:, None]) | (blk[None, :] == cur[:, None] - 1)
        keep[:, qt - 8, :] = (valid & ~forced)
        addt[:, qt - 8, :] = np.where(valid & forced, 1e9, np.where(valid, 0.0, -1e30))
    sec("keep")[:] = keep.reshape(128, 256)
    sec("addt")[:] = addt.reshape(128, 256)
    lg = np.log(1.0 - 2.0 ** (-5.0 - np.arange(5, dtype=np.float64)))
    i = np.arange(128, dtype=np.float64)
    it = np.zeros((128, 5, 128), np.float64)
    for h in range(5):
        diff = i[None, :] - i[:, None]
        it[:, h, :] = np.where(diff >= 0, np.exp(np.maximum(diff, 0) * lg[h]), 0.0) * 0.125
    sec("intraT")[:] = it.reshape(128, 640)
    rd = np.zeros((128, 5, 128), np.float64)
    for h in range(5):
        rd[:, h, :] = np.exp((i + 1.0) * lg[h])[None, :]
    sec("rd")[:] = rd.reshape(128, 640)
    wd = sec("wd")
    for h in range(5):
        wd[:, h] = np.exp((127.0 - i) * lg[h]) * 0.125
    cols = sec("cols")
    inv_p = 1.0 / (500000.0 ** (np.arange(0, 16, 2, dtype=np.float32) / 16)).astype(np.float32)
    inv_m = 1.0 / (500000.0 ** (np.arange(0, 32, 2, dtype=np.float32) / 32)).astype(np.float32)
    inv_r = 1.0 / (10000.0 ** (np.arange(0, 64, 2, dtype=np.float32) / 64)).astype(np.float32)
    tp = 2.0 * np.pi
    cols[:, 1] = tp
    cols[:, 3] = tp
    for r in range(16):
        cols[r, 0] = np.float64(inv_p[r % 8]) / tp
    cols[0:8, 1] = -tp
    for r in range(64, 96):
        cols[r, 0] = np.float64(inv_m[(r - 64) % 16]) / tp
    cols[64:80, 1] = -tp
    for r in range(128):
        cols[r, 2] = np.float64(inv_r[(r % 64) % 32]) / tp
        if (r % 64) < 32:
            cols[r, 3] = -tp
    s9 = np.zeros((128, 9, 64), np.float32)
    for j in range(9):
        s9[j, j, :] = 1.0
    sec("sel9")[:] = s9.reshape(128, 576)
    s18 = np.zeros((128, 18, 64), np.float32)
    for j in range(18):
        s18[j, j, :] = 1.0
    sec("sel18")[:] = s18.reshape(128, 1152)
    return b, lg


WNAMES = ["ln1_gain", "w_in", "cmp_pos_k", "cmp_w1_k", "cmp_w2_k", "cmp_pos_v", "cmp_w1_v", "cmp_w2_v",
          "mla_q_norm", "mla_w_uq", "mla_kv_norm", "mla_w_ukv", "ret_gn_gain", "w_out", "ln2_gain",
          "w_up", "w_down", "final_gain"]
WSHAPES = dict(ln1_gain=[4, 1024], w_in=[4, 1024, 2866], cmp_pos_k=[4, 32, 64], cmp_w1_k=[4, 2048, 128],
               cmp_w2_k=[4, 128, 64], cmp_pos_v=[4, 32, 64], cmp_w1_v=[4, 2048, 128], cmp_w2_v=[4, 128, 64],
               mla_q_norm=[4, 256], mla_w_uq=[4, 256, 480], mla_kv_norm=[4, 128], mla_w_ukv=[4, 128, 640],
               ret_gn_gain=[4, 5, 64], w_out=[4, 1024, 1024], ln2_gain=[4, 1024], w_up=[4, 1024, 4096],
               w_down=[4, 4096, 1024], final_gain=[1024])


class Ctx:
    pass


def build(n_layers=DEPTH, n_seq=2, mixers=("nsa", "mla", "ret"), dbg=False):
    nc = bass.Bass("TRN2", target_bir_lowering=False)
    C = Ctx()
    C.nc = nc
    C.dbg = dbg
    C.dbg_outs = {}
    dr = {}
    dr["x"] = nc.dram_tensor("x", [n_seq, S, D], F32, kind="ExternalInput").ap()
    dr["pos"] = nc.dram_tensor("pos", [n_seq, S], I32, kind="ExternalInput").ap()
    dr["blob"] = nc.dram_tensor("blob", [128, BLOB_W], F32, kind="ExternalInput").ap()
    for n in WNAMES:
        dr[n] = nc.dram_tensor(n, WSHAPES[n], F32, kind="ExternalInput").ap()
    dr["out"] = nc.dram_tensor("out", [n_seq, S, D], F32, kind="ExternalOutput").ap()
    C.dr = dr
    with ExitStack() as st:
        mk = MK(nc, st)
        C.mk = mk
        _setup(C)
        if "ret" in mixers:
            _tok0_prepass(C, n_seq, n_layers)
            dump(C, "s00", C.Ts00, C.s00[:, :], [128, 40])
        for s in range(n_seq):
            C.cur_seq = s
            _load_x(C, s)
            if s == 0:
                dump(C, "xT", C.TX[0], C.xT[:, :, 0:512], [128, 8, 512])
                dump(C, "xT3", C.TX[3], C.xT[:, :, 1536:2048], [128, 8, 512])
            if mixers:
                _rope_tables(C, s)
            for l in range(n_layers):
                if mixers:
                    _rmsnorm_main(C, C.g1, l)
                    _mixer(C, l, mixers)
                _rmsnorm_main(C, C.g2, l)
                if s == 0 and l == 0:
                    dump(C, "h2T", C.TH[0], C.hT[:, :, 0:512], [128, 8, 512], BF16)
                _ffn(C, l)
                if s == 0 and l == 0:
                    dump(C, "aT", C.TA, C.aT[:, :, :], [128, 32, 512], BF16)
                    dump(C, "x2T", C.TX[3], C.xT[:, :, 1536:2048], [128, 8, 512])
            _final(C, s)
        mk.wait_all_dma("sp")
        mk.emit()
    return nc, C


def _bsec(C, name):
    o, w = BLOB_OFF[name]
    return C.dr["blob"][:, o:o + w]


def _setup(C):
    mk = C.mk
    cg = "const"

    def cf32(name, sec, w):
        t = mk.sb(name, [128, w], F32)
        tt = T(t, name)
        mk.dma("sp", t[:], _bsec(C, sec), cg, writes=[tt])
        return tt

    def cbf(name, sec, w):
        t = mk.sb(name, [128, w], BF16)
        tt = T(t, name)
        mk.dma("pool", t[:], _bsec(C, sec), cg, writes=[tt])
        return tt
    C.identF = cf32("identF", "ident", 128)
    C.identB = cbf("identB", "ident", 128)
    C.onesB = cbf("onesB", "ones", 128)
    C.triB = cbf("triB", "tri", 128)
    C.winbB = cbf("winbB", "winb", 128)
    C.cmpbB = cbf("cmpbB", "cmpb", 2048)
    C.ovl1B = cbf("ovl1B", "ovl1", 33)
    C.keepF = cf32("keepF", "keep", 256)
    C.addtF = cf32("addtF", "addt", 256)
    C.intraTF = cf32("intraTF", "intraT", 640)
    C.rdB = cbf("rdB", "rd", 640)
    C.wdF = cf32("wdF", "wd", 5)
    C.colsF = cf32("colsF", "cols", 8)
    C.sel9B = cbf("sel9B", "sel9", 576)
    C.onesF = cf32("onesF", "ones", 128)
    t0 = mk.sb("s00", [128, 40], F32)
    C.s00 = t0
    C.Ts00 = T(t0, "s00")
    t = mk.sb("bigF", [128, 32], F32)
    C.bigF = T(t, "bigF")
    MEMSET(C, "pool", t[:], 1.0e15, [C.bigF])
    dr = C.dr

    def gain_cols(name, src_ap, shape):
        t = mk.sb(name, shape, F32)
        tt = T(t, name)
        mk.dma("sp", t[:], src_ap, cg, writes=[tt], allow_slow_non_contiguous=True)
        return tt
    C.g1 = gain_cols("g1", dr["ln1_gain"].rearrange("l (c p) -> p l c", p=128), [128, 4, 8])
    C.g2 = gain_cols("g2", dr["ln2_gain"].rearrange("l (c p) -> p l c", p=128), [128, 4, 8])
    C.gf = gain_cols("gf", dr["final_gain"].rearrange("(c p) -> p c", p=128), [128, 8])
    C.gq = gain_cols("gq", dr["mla_q_norm"].rearrange("l (c p) -> p l c", p=128), [128, 4, 2])
    C.gkv = gain_cols("gkv", dr["mla_kv_norm"].rearrange("l p -> p l"), [128, 4])
    C.gnT = gain_cols("gnT", dr["ret_gn_gain"].rearrange("l h d -> d (l h)"), [64, 20])
    C.xT = mk.sb("xT", [128, 8, S], F32)
    C.TX = [T(C.xT, "xT%d" % c) for c in range(NCH)]
    C.hT = mk.sb("hT", [128, 8, S], BF16)
    C.TH = [T(C.hT, "hT%d" % c) for c in range(NCH)]
    AW = 27648
    C.arena = mk.sb("arena", [128, AW], BF16)
    C.arena_tiles = []
    C.mixT = C.arena[:, 0:4 * S].rearrange("p (c t) -> p c t", c=4)
    C.TMIX = [T(C.mixT, "mix%d" % c) for c in range(4)]
    C.aT = C.arena[:, 0:32 * 512].rearrange("p (f t) -> p f t", f=32)
    C.TA = T(C.aT, "aT")
    C.WORK0 = 4 * S
    C.sq = C.arena[:, AW - 4096:AW].rearrange("p (k t) -> p k t", k=8)
    C.TSQ = T(C.sq, "sq")
    stf = C.arena[:, AW - 8192:AW - 4096].bitcast(F32)
    C.stage = [stf[:, 0:1024], stf[:, 1024:2048]]
    C.TST = [T(C.stage[i], "stage%d" % i) for i in range(2)]
    C.arena_tiles += C.TMIX + [C.TA, C.TSQ] + C.TST
    C.NW = 2
    C.wsl = [mk.sb("wsl%d" % i, [128, 4096], BF16) for i in range(C.NW)]
    C.TW = [T(C.wsl[i], "wsl%d" % i) for i in range(C.NW)]
    C.wi = 0
    C.t32 = [mk.sb("t32_%d" % i, [128, 512], F32) for i in range(4)]
    C.TT32 = [T(C.t32[i], "t32_%d" % i) for i in range(4)]
    C.t32i = 0
    C.pb = [mk.ps("pb%d" % i, [128, 512], F32) for i in range(8)]
    C.TP = [T(C.pb[i], "pb%d" % i) for i in range(8)]
    C.pbi = {}


def bank(C, cls, choices):
    i = C.pbi.get(cls, 0)
    C.pbi[cls] = i + 1
    b = choices[i % len(choices)]
    return C.pb[b], C.TP[b]


def tmp32(C):
    i = C.t32i % 4
    C.t32i += 1
    return C.t32[i], C.TT32[i]


def wslot(C):
    i = C.wi % C.NW
    C.wi += 1
    return C.wsl[i], C.TW[i], "wsl%d" % i


def dump(C, name, t, ap, shape, dt=F32):
    if not C.dbg:
        return
    d = C.nc.dram_tensor("dbg_" + name, list(shape), dt, kind="ExternalOutput").ap()
    C.dbg_outs[name] = d
    C.mk.dma("sp", d, ap, "dbg", reads=[t])


def claim(C, tiles):
    C.mk.claim(tiles, C.arena_tiles)
    for t in tiles:
        if t not in C.arena_tiles:
            C.arena_tiles.append(t)


def MM(C, o, l, r, st, sp, reads, writes):
    C.mk.op("pe", lambda e: e.matmul(o, l, r, start=st, stop=sp, skip_group_check=True), reads, writes)


def TR(C, o, i, ident, reads, writes):
    C.mk.op("pe", lambda e: e.transpose(out=o, in_=i, identity=ident), reads, writes)


def ACTF(C, o, i, func, reads, writes, **kw):
    C.mk.op("act", lambda e: e.activation(out=o, in_=i, func=func, **kw), reads, writes)


def TT(C, eng, o, a, b, op, reads, writes):
    C.mk.op(eng, lambda e: e.tensor_tensor(out=o, in0=a, in1=b, op=op), reads, writes)


def TS(C, eng, o, a, s1, s2, op0, op1, reads, writes):
    if s2 is None:
        C.mk.op(eng, lambda e: e.tensor_scalar(out=o, in0=a, scalar1=s1, scalar2=None, op0=op0), reads, writes)
    else:
        C.mk.op(eng, lambda e: e.tensor_scalar(out=o, in0=a, scalar1=s1, scalar2=s2, op0=op0, op1=op1), reads, writes)


def STT(C, eng, o, a, sc, b, op0, op1, reads, writes):
    C.mk.op(eng, lambda e: e.scalar_tensor_tensor(out=o, in0=a, scalar=sc, in1=b, op0=op0, op1=op1), reads, writes)


def CP(C, eng, o, i, reads, writes):
    if eng == "act":
        C.mk.op(eng, lambda e: e.copy(out=o, in_=i), reads, writes)
    else:
        C.mk.op(eng, lambda e: e.tensor_copy(out=o, in_=i), reads, writes)


def RECIP(C, o, i, reads, writes):
    C.mk.op("dve", lambda e: e.reciprocal(out=o, in_=i), reads, writes)


def MEMSET(C, eng, o, v, writes):
    C.mk.op(eng, lambda e: e.memset(o, v), (), writes)


def LDW(C, dst_ap, src_ap, group, writes, eng="pool", **kw):
    C.mk.dma(eng, dst_ap, src_ap, group, writes=writes, **kw)


def claim(C, tiles):
    C.mk.claim(tiles, C.arena_tiles)
    for t in tiles:
        if t not in C.arena_tiles:
            C.arena_tiles.append(t)


def _load_x(C, s):
    mk = C.mk
    x = C.dr["x"]
    claim(C, C.TST)
    for t in range(NT):
        stg, tst = C.stage[t % 2], C.TST[t % 2]
        mk.dma("sp", stg, x[s, 128 * t:128 * t + 128, :], "stage%d" % (t % 2), writes=[tst])
        c = t // 4
        for half in range(2):
            pb, tp = bank(C, "tr", [0, 1])
            for j in range(4):
                dc = 4 * half + j
                MM(C, pb[:, 128 * j:128 * j + 128], stg[:, 128 * dc:128 * dc + 128], C.identF[:], True, True,
                   [tst, C.identF], [tp])
            CP(C, "dve", C.xT[:, 4 * half:4 * half + 4, 128 * t:128 * t + 128],
               pb[:].rearrange("p (a b) -> p a b", a=4), [tp], [C.TX[c]])


def _rms(C, src, srcT, sq, sqT, nk, rows, gain_ap_fn, dst_fn, dstT, inv_n):
    ACTF(C, sq[0:rows, 0:nk, :], src, AF.Square, [srcT], [sqT])
    pb, tp = bank(C, "rms", [2, 3])
    for k in range(nk):
        MM(C, pb[:], C.onesB[0:rows, :], sq[0:rows, k, :], k == 0, k == nk - 1, [sqT, C.onesB], [tp])
    rt, trt = tmp32(C)
    ACTF(C, rt[:], pb[:], AF.Ln, [tp], [trt], bias=EPS, scale=inv_n)
    ACTF(C, rt[:], rt[:], AF.Exp, [trt], [trt], scale=-0.5)
    for k in range(nk):
        STT(C, "dve", dst_fn(k), src[:, k, :], gain_ap_fn(k), rt[0:rows, :], ALU.mult, ALU.mult,
            [srcT, trt], [dstT])


def _rmsnorm_main(C, g, l):
    claim(C, [C.TSQ])
    for c in range(NCH):
        cs = slice(512 * c, 512 * c + 512)
        _rms(C, C.xT[:, :, cs], C.TX[c], C.sq, C.TSQ, 8, 128, lambda k, g=g, l=l: g[:, l, k:k + 1],
             lambda k, cs=cs: C.hT[:, k, cs], C.TH[c], 1.0 / D)


def _ffn(C, l):
    w_up = C.dr["w_up"]
    w_dn = C.dr["w_down"]
    if not hasattr(C, "ffn_tiles"):
        o = 0
        C.f_at, C.f_up, C.f_dn = [], [], []
        for nm, n, lst in (("fat", 2048, C.f_at), ("fat", 2048, C.f_at), ("fup", 4096, C.f_up), ("fup", 4096, C.f_up),
                           ("fdn", 4096, C.f_dn), ("fdn", 4096, C.f_dn)):
            ap = C.arena[:, o:o + n]
            o += n
            lst.append((ap, T(ap, "%s%d" % (nm, len(lst)))))
        C.ffn_tiles = [t for lst in (C.f_at, C.f_up, C.f_dn) for (_, t) in lst]
        C.fai = 0
    claim(C, C.ffn_tiles)
    pend = None

    def down(p):
        (c_, atv_, tat_, dnv_, tdn_) = p
        cs_ = slice(512 * c_, 512 * c_ + 512)
        for d in range(8):
            pb, tp = bank(C, "dn", [0, 1, 2, 3])
            for kk in range(4):
                MM(C, pb[:], dnv_[:, kk, 128 * d:128 * d + 128], atv_[:, kk, :], kk == 0, kk == 3, [tdn_, tat_], [tp])
            TT(C, "dve", C.xT[:, d, cs_], pb[:], C.xT[:, d, cs_], ALU.add, [tp, C.TX[c_]], [C.TX[c_]])

    for fb in range(8):
        up, tup = C.f_up[fb % 2]
        dn, tdn = C.f_dn[fb % 2]
        upv = up.rearrange("p (k n) -> p k n", k=8)
        dnv = dn.rearrange("p (k n) -> p k n", k=4)
        LDW(C, upv, w_up[l, :, 512 * fb:512 * fb + 512].rearrange("(k p) n -> p k n", p=128), "fup%d" % (fb % 2), [tup])
        LDW(C, dnv, w_dn[l, 512 * fb:512 * fb + 512, :].rearrange("(k p) n -> p k n", p=128), "fdn%d" % (fb % 2), [tdn])
        for c in range(NCH):
            cs = slice(512 * c, 512 * c + 512)
            at, tat = C.f_at[C.fai % 2]
            C.fai += 1
            atv = at.rearrange("p (k n) -> p k n", k=4)
            for fi in range(4):
                pb, tp = bank(C, "up", [4, 5])
                for k in range(8):
                    MM(C, pb[:], upv[:, k, 128 * fi:128 * fi + 128], C.hT[:, k, cs], k == 0, k == 7, [tup, C.TH[c]], [tp])
                r, tr = tmp32(C)
                ACTF(C, r[:], pb[:], AF.Relu, [tp], [tr])
                TT(C, "pool", atv[:, fi, :], r[:], r[:], ALU.mult, [tr], [tat])
            if pend is not None:
                down(pend)
            pend = (c, atv, tat, dnv, tdn)
    down(pend)


def _final(C, s):
    mk = C.mk
    out = C.dr["out"]
    claim(C, [C.TSQ] + C.TST)
    for c in range(NCH):
        cs = slice(512 * c, 512 * c + 512)
        ACTF(C, C.sq[:, :, :], C.xT[:, :, cs], AF.Square, [C.TX[c]], [C.TSQ])
        pb, tp = bank(C, "rms", [2, 3])
        for k in range(8):
            MM(C, pb[:], C.onesB[:, :], C.sq[:, k, :], k == 0, k == 7, [C.TSQ, C.onesB], [tp])
        rt, trt = tmp32(C)
        ACTF(C, rt[:], pb[:], AF.Ln, [tp], [trt], bias=EPS, scale=1.0 / D)
        ACTF(C, rt[:], rt[:], AF.Exp, [trt], [trt], scale=-0.5)
        for k in range(8):
            STT(C, "dve", C.xT[:, k, cs], C.xT[:, k, cs], C.gf[:, k:k + 1], rt[:], ALU.mult, ALU.mult,
                [C.TX[c], trt, C.gf], [C.TX[c]])
        for tl in range(4):
            t = 4 * c + tl
            stg, tst = C.stage[t % 2], C.TST[t % 2]
            for half in range(2):
                pb, tp = bank(C, "tr", [0, 1])
                for j in range(4):
                    dc = 4 * half + j
                    MM(C, pb[:, 128 * j:128 * j + 128], C.xT[:, dc, 128 * t:128 * t + 128], C.identF[:], True, True,
                       [C.TX[c], C.identF], [tp])
                CP(C, "act", stg[:, 512 * half:512 * half + 512], pb[:], [tp], [tst])
            mk.dma("sp", out[s, 128 * t:128 * t + 128, :], stg, "stage%d" % (t % 2), reads=[tst])


PI = float(np.pi)


class WA:
    def __init__(self, C, start=None):
        self.C = C
        self.o = C.WORK0 if start is None else start
        self.tiles = []

    def bf(self, n, name):
        ap = self.C.arena[:, self.o:self.o + n]
        self.o += n
        assert self.o <= 27648, (name, self.o)
        t = T(ap, name)
        self.tiles.append(t)
        return ap, t

    def f32(self, n, name):
        ap, t = self.bf(2 * n, name)
        ap = ap.bitcast(F32)
        t.ap = ap
        return ap, t


def _rope_tables(C, s):
    mk = C.mk
    if not hasattr(C, "CA"):
        C.CA = mk.sb("CA", [128, S], BF16)
        C.SA = mk.sb("SA", [128, S], BF16)
        C.CR = mk.sb("CR", [128, S], BF16)
        C.SR = mk.sb("SR", [128, S], BF16)
        C.TCA, C.TSA, C.TCR, C.TSR = T(C.CA, "CA"), T(C.SA, "SA"), T(C.CR, "CR"), T(C.SR, "SR")
    wa = WA(C, 0)
    posI, tpi = wa.f32(S, "posI")
    posI = posI.bitcast(I32)
    posF, tpf = wa.f32(S, "posF")
    ang, tang = wa.f32(S, "ang")
    tt, ttt = wa.f32(S, "ropetmp")
    claim(C, wa.tiles)
    mk.dma("sp", posI, C.dr["pos"][s, :].partition_broadcast(128), "const", writes=[tpi])
    CP(C, "dve", posF, posI, [tpi], [tpf])
    MAGIC = 12582912.0
    for (ic, isg, ctab, tct, stab, tst_) in ((0, 1, C.CA, C.TCA, C.SA, C.TSA), (2, 3, C.CR, C.TCR, C.SR, C.TSR)):
        for which in (0, 1):
            ACTF(C, ang, posF, AF.Identity, [tpf, C.colsF], [tang], scale=C.colsF[:, ic:ic + 1],
                 bias=(0.25 if which == 0 else 0.0))
            TS(C, "dve", tt, ang, MAGIC, None, ALU.add, None, [tang], [ttt])
            TS(C, "dve", tt, tt, -MAGIC, None, ALU.add, None, [ttt], [ttt])
            TT(C, "dve", tt, ang, tt, ALU.subtract, [tang, ttt], [ttt])
            if which == 0:
                ACTF(C, ctab[:], tt, AF.Sin, [ttt], [tct], scale=2 * PI)
            else:
                ACTF(C, stab[:], tt, AF.Sin, [ttt, C.colsF], [tst_], scale=C.colsF[:, isg:isg + 1])


def attn_chunk(C, c, kt_fn, TK, krows, QT, TQ, v_fn, TV, kind, scale, PT, TPT):
    if kind == "causal":
        kts = list(range(0, 4 * c + 4))
    else:
        first = 4 * c - 1 if c >= 1 else 0
        kts = [first] + [j for j in range(max(0, 4 * c - 4), 4 * c + 4) if j != first]
    oa, toa = bank(C, "oa", [6, 7])
    pend = None

    def pv(p):
        (j_, pi_, cs_, idx_) = p
        MM(C, oa[:, cs_], v_fn(j_), PT[pi_][:, cs_], idx_ == 0, idx_ == len(kts) - 1, [TV, TPT[pi_]], [toa])

    for idx, j in enumerate(kts):
        d = j - 4 * c
        lo = max(d, 0)
        hi = 4 if kind == "causal" else min(j + 4, 4 * c + 3) - 4 * c + 1
        st, tst = bank(C, "st", [4, 5])
        segs = []
        plo, phi = lo, hi
        if d >= 0:
            segs.append((lo, lo + 1, C.triB))
            plo = lo + 1
        if kind == "window" and 0 <= j + 4 - 4 * c <= 3:
            segs.append((hi - 1, hi, C.winbB))
            phi = hi - 1
        if phi > plo:
            segs.append((plo, phi, None))
        for (a_, b_, bias) in segs:
            cs = slice(128 * a_, 128 * b_)
            MM(C, st[:, cs], kt_fn(j), QT[0:krows, cs], True, bias is None, [TK, TQ], [tst])
            if bias is not None:
                MM(C, st[:, cs], C.identB[:, :], bias[:, :], False, True, [C.identB, bias], [tst])
        cs = slice(128 * lo, 128 * hi)
        pi = C.pti % len(PT)
        C.pti += 1
        ACTF(C, PT[pi][:, cs], st[:, cs], AF.Exp, [tst], [TPT[pi]], scale=scale)
        if pend is not None:
            pv(pend)
        pend = (j, pi, cs, idx)
    pv(pend)
    return oa, toa


def _mix_dst(C, feat0):
    ch = (feat0 // 128) % 4
    po = feat0 % 128
    return ch, po


def _mla_pre(C, l):
    mk = C.mk
    wa = WA(C)
    M = Ctx()
    C.M = M
    M.cqn, M.tcqn = wa.bf(2 * S, "cqn")
    M.cqn = M.cqn.rearrange("p (k t) -> p k t", k=2)
    M.ckvn, M.tckvn = wa.bf(S, "ckvn")
    M.kpeT, M.tkpe = wa.bf(S, "kpeT")
    M.KT, M.tKT = wa.bf(S, "mlaKT")
    M.V, M.tV = wa.bf(S, "mlaV")
    M.V = M.V.rearrange("p (t n) -> p t n", t=16)
    M.QT, M.tQT = wa.bf(512, "mlaQT")
    M.PT, M.tPT = [], []
    for i in range(2):
        a, t = wa.bf(512, "mlaPT%d" % i)
        M.PT.append(a)
        M.tPT.append(t)
    M.sq, M.tsq = wa.bf(1024, "mlasq")
    M.sq = M.sq.rearrange("p (k t) -> p k t", k=2)
    M.wkp, M.twkp = wa.bf(768, "wkpeP")
    M.wkp = M.wkp.rearrange("p (k n) -> p k n", k=8)
    M.wuq, M.twuq = wa.bf(960, "wuq")
    M.wuq = M.wuq.rearrange("p (k n) -> p k n", k=2)
    M.wuqP, M.twuqP = wa.bf(960, "wuqP")
    M.wuqP = M.wuqP.rearrange("p (k n) -> p k n", k=2)
    M.wukv, M.twukv = wa.bf(640, "wukv")
    claim(C, wa.tiles)
    C.pti = 0
    ws, tw, gname = wslot(C)
    wv = ws[:, 0:8 * 416].rearrange("p (k n) -> p k n", k=8)
    LDW(C, wv, C.dr["w_in"][l, :, OFF["cq"]:OFF["cq"] + 416].rearrange("(k p) n -> p k n", p=128), gname, [tw])
    LDW(C, M.wuq, C.dr["mla_w_uq"][l].rearrange("(k p) n -> p k n", p=128), "mlaw", [M.twuq])
    LDW(C, M.wukv, C.dr["mla_w_ukv"][l], "mlaw", [M.twukv])
    CP(C, "pool", M.wkp, wv[:, :, 320:416], [tw], [M.twkp])
    CP(C, "pool", M.wkp[:, :, 64:80], wv[:, :, 400:416], [tw], [M.twkp])
    CP(C, "pool", M.wkp[:, :, 80:96], wv[:, :, 384:400], [tw], [M.twkp])
    CP(C, "pool", M.wuqP, M.wuq, [M.twuq], [M.twuqP])
    wq4 = M.wuq.rearrange("p k (h n) -> p k h n", h=5)
    wq4P = M.wuqP.rearrange("p k (h n) -> p k h n", h=5)
    CP(C, "pool", wq4P[:, :, :, 64:80], wq4[:, :, :, 80:96], [M.twuq], [M.twuqP])
    CP(C, "pool", wq4P[:, :, :, 80:96], wq4[:, :, :, 64:80], [M.twuq], [M.twuqP])
    MEMSET(C, "pool", M.V[:, :, 64:128], 1.0, [M.tV])
    for c in range(NCH):
        cs = slice(512 * c, 512 * c + 512)
        r0, tr0 = tmp32(C)
        r1, tr1 = tmp32(C)
        for m, (r, tr) in enumerate(((r0, tr0), (r1, tr1))):
            pb, tp = bank(C, "proj", [0, 1])
            for k in range(8):
                MM(C, pb[:], wv[:, k, 128 * m:128 * m + 128], C.hT[:, k, cs], k == 0, k == 7, [tw, C.TH[c]], [tp])
            CP(C, "act", r[:], pb[:], [tp], [tr])
            ACTF(C, M.sq[:, m, :], pb[:], AF.Square, [tp], [M.tsq])
        pb, tp = bank(C, "rms", [2, 3])
        for m in range(2):
            MM(C, pb[:], C.onesB[:, :], M.sq[:, m, :], m == 0, m == 1, [M.tsq, C.onesB], [tp])
        rt, trt = tmp32(C)
        ACTF(C, rt[:], pb[:], AF.Ln, [tp], [trt], bias=EPS, scale=1.0 / 256)
        ACTF(C, rt[:], rt[:], AF.Exp, [trt], [trt], scale=-0.5)
        for m, (r, tr) in enumerate(((r0, tr0), (r1, tr1))):
            STT(C, "dve", M.cqn[:, m, cs], r[:], C.gq[:, l, m:m + 1], rt[:], ALU.mult, ALU.mult,
                [tr, trt, C.gq], [M.tcqn])
        pb, tp = bank(C, "proj", [0, 1])
        for k in range(8):
            MM(C, pb[:], wv[:, k, 256:384], C.hT[:, k, cs], k == 0, k == 7, [tw, C.TH[c]], [tp])
        r, tr = tmp32(C)
        CP(C, "act", r[:], pb[:], [tp], [tr])
        ACTF(C, M.sq[:, 0, :], pb[:], AF.Square, [tp], [M.tsq])
        pb2, tp2 = bank(C, "rms", [2, 3])
        MM(C, pb2[:], C.onesB[:, :], M.sq[:, 0, :], True, True, [M.tsq, C.onesB], [tp2])
        rt, trt = tmp32(C)
        ACTF(C, rt[:], pb2[:], AF.Ln, [tp2], [trt], bias=EPS, scale=1.0 / 128)
        ACTF(C, rt[:], rt[:], AF.Exp, [trt], [trt], scale=-0.5)
        STT(C, "dve", M.ckvn[:, cs], r[:], C.gkv[:, l:l + 1], rt[:], ALU.mult, ALU.mult, [tr, trt, C.gkv], [M.tckvn])
        pa, tpa = bank(C, "proj", [0, 1])
        for k in range(8):
            MM(C, pa[0:96, :], wv[:, k, 320:416], C.hT[:, k, cs], k == 0, k == 7, [tw, C.TH[c]], [tpa])
        pbb, tpb = bank(C, "proj", [0, 1])
        for k in range(8):
            MM(C, pbb[0:96, :], M.wkp[:, k, :], C.hT[:, k, cs], k == 0, k == 7, [M.twkp, C.TH[c]], [tpb])
        ra, tra = tmp32(C)
        rb, trb = tmp32(C)
        TT(C, "dve", ra[64:96, :], pa[64:96, :], C.CA[64:96, cs], ALU.mult, [tpa, C.TCA], [tra])
        TT(C, "dve", rb[64:96, :], pbb[64:96, :], C.SA[64:96, cs], ALU.mult, [tpb, C.TSA], [trb])
        TT(C, "dve", M.kpeT[64:96, cs], ra[64:96, :], rb[64:96, :], ALU.add, [tra, trb], [M.tkpe])


def _mla_head(C, l, h):
    M = C.M
    for c in range(NCH):
        cs = slice(512 * c, 512 * c + 512)
        pb, tp = bank(C, "proj", [0, 1])
        MM(C, pb[0:64, :], M.wukv[:, 128 * h:128 * h + 64], M.ckvn[:, cs], True, True, [M.twukv, M.tckvn], [tp])
        CP(C, "act", M.KT[0:64, cs], pb[0:64, :], [tp], [M.tKT])
    CP(C, "pool", M.KT[64:96, :], M.kpeT[64:96, :], [M.tkpe], [M.tKT])
    for t4 in range(4):
        pb, tp = bank(C, "proj", [0, 1])
        for j in range(4):
            t = 4 * t4 + j
            MM(C, pb[:, 64 * j:64 * j + 64], M.ckvn[:, 128 * t:128 * t + 128], M.wukv[:, 128 * h + 64:128 * h + 128],
               True, True, [M.tckvn, M.twukv], [tp])
        CP(C, "dve", M.V[:, 4 * t4:4 * t4 + 4, 0:64], pb[:, 0:256].rearrange("p (a b) -> p a b", a=4), [tp], [M.tV])
    ch, po = _mix_dst(C, 384 + 64 * h)
    for c in range(NCH):
        cs = slice(512 * c, 512 * c + 512)
        pa, tpa = bank(C, "proj", [0, 1])
        for m in range(2):
            MM(C, pa[0:96, :], M.wuq[:, m, 96 * h:96 * h + 96], M.cqn[:, m, cs], m == 0, m == 1,
               [M.twuq, M.tcqn], [tpa])
        pbb, tpb = bank(C, "proj", [0, 1])
        for m in range(2):
            MM(C, pbb[0:96, :], M.wuqP[:, m, 96 * h:96 * h + 96], M.cqn[:, m, cs], m == 0, m == 1,
               [M.twuqP, M.tcqn], [tpb])
        CP(C, "act", M.QT[0:64, :], pa[0:64, :], [tpa], [M.tQT])
        ra, tra = tmp32(C)
        rb, trb = tmp32(C)
        TT(C, "dve", ra[64:96, :], pa[64:96, :], C.CA[64:96, cs], ALU.mult, [tpa, C.TCA], [tra])
        TT(C, "dve", rb[64:96, :], pbb[64:96, :], C.SA[64:96, cs], ALU.mult, [tpb, C.TSA], [trb])
        TT(C, "dve", M.QT[64:96, :], ra[64:96, :], rb[64:96, :], ALU.add, [tra, trb], [M.tQT])
        oa, toa = attn_chunk(C, c, lambda j: M.KT[0:96, 128 * j:128 * j + 128], M.tKT, 96, M.QT, M.tQT,
                             lambda j: M.V[:, j, :], M.tV, "causal", 96 ** -0.5, M.PT, M.tPT)
        rc, trc = tmp32(C)
        ACTF(C, rc[0:64, :], oa[64:128, :], AF.Ln, [toa], [trc], bias=1e-30, scale=1.0)
        ACTF(C, rc[0:64, :], rc[0:64, :], AF.Exp, [trc], [trc], scale=-1.0)
        TT(C, "dve", C.mixT[po:po + 64, ch, cs], oa[0:64, :], rc[0:64, :], ALU.mult, [toa, trc], [C.TMIX[ch]])


def _wout(C, l, phase):
    w_out = C.dr["w_out"]
    for dq in range(4):
        ws, tw, gname = wslot(C)
        wv = ws[:, 0:1024].rearrange("p (k n) -> p k n", k=4)
        LDW(C, wv, w_out[l, 512 * phase:512 * phase + 512, 256 * dq:256 * dq + 256].rearrange(
            "(k p) n -> p k n", p=128), gname, [tw])
        for dd in range(2):
            d = 2 * dq + dd
            for c in range(NCH):
                cs = slice(512 * c, 512 * c + 512)
                pb, tp = bank(C, "proj", [0, 1])
                for k in range(4):
                    MM(C, pb[:], wv[:, k, 128 * dd:128 * dd + 128], C.mixT[:, k, cs], k == 0, k == 3,
                       [tw, C.TMIX[k]], [tp])
                TT(C, "dve", C.xT[:, d, cs], pb[:], C.xT[:, d, cs], ALU.add, [tp, C.TX[c]], [C.TX[c]])


def _tok0_prepass(C, n_seq, n_layers):
    mk = C.mk
    AR = C.arena[:].bitcast(F32)
    st = {"o": 0}
    tiles = []

    def alloc(n, name):
        ap = AR[:, st["o"]:st["o"] + n]
        st["o"] += n
        t = T(ap, name)
        tiles.append(t)
        return ap, t
    slot, tslot = [], []
    for i in range(2):
        a_, t_ = alloc(4096, "pslot%d" % i)
        slot.append(a_)
        tslot.append(t_)
    sel, tsel = alloc(1152, "sel18")
    X0, tX = alloc(16, "X0")
    X0 = X0.rearrange("p (k s) -> p k s", s=2)
    Hh, tH = alloc(16, "H0")
    Hh = Hh.rearrange("p (k s) -> p k s", s=2)
    SQ, tSQ = alloc(64, "SQ0")
    SQ = SQ.rearrange("p (k s) -> p k s", s=2)
    Aa, tA = alloc(64, "A0")
    Aa = Aa.rearrange("p (k s) -> p k s", s=2)
    PRn, tPRn = alloc(8, "PRn")
    PRn = PRn.rearrange("p (k s) -> p k s", s=2)
    SG, tSG = alloc(2, "SG")
    CK, tCK = alloc(2, "CK")
    CKN, tCKN = alloc(2, "CKN")
    R, tR = alloc(40, "R0")
    R = R.rearrange("p (k s) -> p k s", s=2)
    O, tO = alloc(10, "O0")
    O = O.rearrange("p (k s) -> p k s", s=2)
    CEN, tCEN = alloc(10, "CEN")
    CEN = CEN.rearrange("p (k s) -> p k s", s=2)
    RS_, tRS = alloc(10, "RSTD")
    RS_ = RS_.rearrange("p (k s) -> p k s", s=2)
    SGG, tSGG = alloc(10, "SGG")
    SGG = SGG.rearrange("p (k s) -> p k s", s=2)
    MIX, tMIX = alloc(32, "MIX0")
    MIX = MIX.rearrange("p (k s) -> p k s", s=2)
    sm, tsm = alloc(4, "psmall")
    tmp, ttmp = alloc(24, "ptmp")
    tmp3 = tmp.rearrange("p (k s) -> p k s", s=2)
    claim(C, tiles)
    C.wpi = 0
    dr = C.dr

    def wload(src_ap, view_fn):
        i = C.wpi % 2
        C.wpi += 1
        v = view_fn(slot[i])
        mk.dma("sp", v, src_ap, "pslot%d" % i, writes=[tslot[i]])
        return v, tslot[i]

    def mvg(lhs_fn, rhs_fn, nk, out, tout, reads):
        for k in range(nk):
            MM(C, out, lhs_fn(k), rhs_fn(k), k == 0, k == nk - 1, reads, [tout])

    def rms0(X, tXx, nk, rows, gain_fn, out, tout, n):
        TT(C, "dve", SQ[0:rows, 0:nk, :], X[0:rows, 0:nk, :], X[0:rows, 0:nk, :], ALU.mult, [tXx], [tSQ])
        mvg(lambda k: C.onesF[0:rows, 0:rows], lambda k: SQ[0:rows, k, :], nk, C.pb[7][0:rows, 0:2], C.TP[7],
            [C.onesF, tSQ])
        ACTF(C, sm[0:rows, 0:2], C.pb[7][0:rows, 0:2], AF.Sqrt, [C.TP[7]], [tsm], bias=EPS, scale=1.0 / n)
        RECIP(C, sm[0:rows, 0:2], sm[0:rows, 0:2], [tsm], [tsm])
        for k in range(nk):
            STT(C, "dve", out[0:rows, k, :], X[0:rows, k, :], gain_fn(k), sm[0:rows, 0:2], ALU.mult, ALU.mult,
                [tXx, tsm], [tout])

    MEMSET(C, "pool", AR[:, 8192 + 1152:st["o"]], 0.0, tiles[3:])
    o_, w_ = BLOB_OFF["sel18"]
    mk.dma("sp", sel, dr["blob"][:, o_:o_ + w_], "pconst", writes=[tsel])
    for s_ in range(n_seq):
        mk.dma("sp", X0[:, :, s_], dr["x"][s_, 0, :].rearrange("(c p) -> p c", p=128), "pconst", writes=[tX],
               allow_slow_non_contiguous=True)
    ps1, tp1 = C.pb[0], C.TP[0]
    ps2, tp2 = C.pb[1], C.TP[1]
    ps3, tp3 = C.pb[2], C.TP[2]
    psS, tpS = C.pb[3], C.TP[3]
    psD, tpD = C.pb[4], C.TP[4]
    psU, tpU = C.pb[5], C.TP[5]
    psV, tpV = C.pb[6], C.TP[6]
    win = dr["w_in"]
    for l in range(n_layers):
        rms0(X0, tX, 8, 128, lambda k: C.g1[:, l, k:k + 1], Hh, tH, float(D))
        hk = lambda k: Hh[:, k, :]
        wv, tw = wload(win[l, :, OFF["vs"]:OFF["vs"] + 128].rearrange("(k p) n -> p k n", p=128),
                       lambda sl: sl[:, 0:1024].rearrange("p (k n) -> p k n", k=8))
        for g in range(2):
            mvg(lambda k, g=g: wv[:, k, 64 * g:64 * g + 64], hk, 8, ps1[0:64, 2 * g:2 * g + 2], tp1, [tw, tH])
        wv, tw = wload(win[l, :, OFF["vw"]:OFF["vw"] + 146].rearrange("(k p) n -> p k n", p=128),
                       lambda sl: sl[:, 0:8 * 146].rearrange("p (k n) -> p k n", k=8))
        for g in range(2):
            mvg(lambda k, g=g: wv[:, k, 64 * g:64 * g + 64], hk, 8, ps1[0:64, 4 + 2 * g:6 + 2 * g], tp1, [tw, tH])
        mvg(lambda k: wv[:, k, 128:146], hk, 8, ps1[0:18, 8:10], tp1, [tw, tH])
        CP(C, "dve", PRn[0:64, :, :], ps1[0:64, 0:8].rearrange("p (k s) -> p k s", s=2), [tp1], [tPRn])
        CP(C, "dve", SG[0:18, :], ps1[0:18, 8:10], [tp1], [tSG])
        ACTF(C, SG[0:18, :], SG[0:18, :], AF.Exp, [tSG], [tSG], scale=-1.0)
        TS(C, "dve", SG[0:18, :], SG[0:18, :], 1.0, None, ALU.add, None, [tSG], [tSG])
        RECIP(C, SG[0:18, :], SG[0:18, :], [tSG], [tSG])
        for h in range(6):
            for b in (1, 2):
                j = 3 * h + b
                ci = 2 * (2 * h + b - 1)
                MM(C, ps2[0:64, ci:ci + 2], sel[0:18, 64 * j:64 * j + 64], SG[0:18, :], True, True, [tsel, tSG], [tp2])
        for h in range(6):
            g = h // 3
            c1 = 2 * (2 * h)
            TT(C, "dve", tmp3[0:64, 0, :], ps2[0:64, c1:c1 + 2], PRn[0:64, g, :], ALU.mult, [tp2, tPRn], [ttmp])
            TT(C, "dve", tmp3[0:64, 1, :], ps2[0:64, c1 + 2:c1 + 4], PRn[0:64, 2 + g, :], ALU.mult, [tp2, tPRn], [ttmp])
            TT(C, "dve", MIX[0:64, h, :], tmp3[0:64, 0, :], tmp3[0:64, 1, :], ALU.add, [ttmp], [tMIX])
        wv, tw = wload(win[l, :, OFF["ckv"]:OFF["ckv"] + 128].rearrange("(k p) n -> p k n", p=128),
                       lambda sl: sl[:, 0:1024].rearrange("p (k n) -> p k n", k=8))
        mvg(lambda k: wv[:, k, :], hk, 8, ps1[:, 10:12], tp1, [tw, tH])
        CP(C, "dve", CK[:, :], ps1[:, 10:12], [tp1], [tCK])
        CK3 = CK.rearrange("p (k s) -> p k s", s=2)
        CKN3 = CKN.rearrange("p (k s) -> p k s", s=2)
        rms0(CK3, tCK, 1, 128, lambda k: C.gkv[:, l:l + 1], CKN3, tCKN, 128.0)
        wv, tw = wload(dr["mla_w_ukv"][l], lambda sl: sl[:, 0:640])
        for h in range(5):
            MM(C, psV[0:64, 2 * h:2 * h + 2], wv[:, 128 * h + 64:128 * h + 128], CKN[:, 0:2], True, True, [tw, tCKN], [tpV])
        CP(C, "dve", MIX[0:64, 6:11, :], psV[0:64, 0:10].rearrange("p (k s) -> p k s", s=2), [tpV], [tMIX])
        for piece in range(3):
            n = 512 if piece < 2 else 256
            c0 = OFF["rq"] + 512 * piece
            wv, tw = wload(win[l, :, c0:c0 + n].rearrange("(k p) n -> p k n", p=128),
                           lambda sl, n=n: sl[:, 0:8 * n].rearrange("p (k n) -> p k n", k=8))
            for gi in range(8 * piece, 8 * piece + n // 64):
                off = 64 * gi - 512 * piece
                mvg(lambda k, off=off: wv[:, k, off:off + 64], hk, 8, ps3[0:64, 2 * gi:2 * gi + 2], tp3, [tw, tH])
        CP(C, "dve", R[0:64, :, :], ps3[0:64, 0:40].rearrange("p (k s) -> p k s", s=2), [tp3], [tR])
        TT(C, "dve", O[0:64, :, :], R[0:64, 0:5, :], R[0:64, 5:10, :], ALU.mult, [tR], [tO])
        MM(C, psS[0:64, 0:10], C.onesF[0:64, 0:64], O[0:64, :, :].rearrange("p k s -> p (k s)"), True, True,
           [C.onesF, tO], [tpS])
        CP(C, "dve", C.s00[0:64, 10 * l:10 * l + 10], psS[0:64, 0:10], [tpS], [C.Ts00])
        TT(C, "dve", O[0:64, :, :], psS[0:64, 0:10].rearrange("p (k s) -> p k s", s=2), R[0:64, 10:15, :], ALU.mult,
           [tpS, tR], [tO])
        TS(C, "dve", O[0:64, :, :], O[0:64, :, :], 0.125, None, ALU.mult, None, [tO], [tO])
        MM(C, psS[0:64, 10:20], C.onesF[0:64, 0:64], O[0:64, :, :].rearrange("p k s -> p (k s)"), True, True,
           [C.onesF, tO], [tpS])
        STT(C, "dve", CEN[0:64, :, :], psS[0:64, 10:20].rearrange("p (k s) -> p k s", s=2), -1.0 / 64, O[0:64, :, :],
            ALU.mult, ALU.add, [tpS, tO], [tCEN])
        TT(C, "dve", O[0:64, :, :], CEN[0:64, :, :], CEN[0:64, :, :], ALU.mult, [tCEN], [tO])
        MM(C, C.pb[7][0:64, 2:12], C.onesF[0:64, 0:64], O[0:64, :, :].rearrange("p k s -> p (k s)"), True, True,
           [C.onesF, tO], [C.TP[7]])
        ACTF(C, RS_[0:64, :, :], C.pb[7][0:64, 2:12].rearrange("p (k s) -> p k s", s=2), AF.Sqrt, [C.TP[7]], [tRS],
             bias=EPS, scale=1.0 / 64)
        RECIP(C, RS_[0:64, :, :], RS_[0:64, :, :], [tRS], [tRS])
        ACTF(C, SGG[0:64, :, :], R[0:64, 15:20, :], AF.Exp, [tR], [tSGG], scale=-1.0)
        TS(C, "dve", SGG[0:64, :, :], SGG[0:64, :, :], 1.0, None, ALU.add, None, [tSGG], [tSGG])
        RECIP(C, SGG[0:64, :, :], SGG[0:64, :, :], [tSGG], [tSGG])
        TT(C, "dve", SGG[0:64, :, :], SGG[0:64, :, :], R[0:64, 15:20, :], ALU.mult, [tSGG, tR], [tSGG])
        for h in range(5):
            STT(C, "dve", CEN[0:64, h, :], CEN[0:64, h, :], C.gnT[:, 5 * l + h:5 * l + h + 1], RS_[0:64, h, :],
                ALU.mult, ALU.mult, [tCEN, tRS, C.gnT], [tCEN])
        TT(C, "dve", MIX[0:64, 11:16, :], CEN[0:64, :, :], SGG[0:64, :, :], ALU.mult, [tCEN, tSGG], [tMIX])
        for q4 in range(4):
            wv, tw = wload(dr["w_out"][l, :, 256 * q4:256 * q4 + 256].rearrange("(j p) n -> p j n", p=64),
                           lambda sl: sl[0:64, :].rearrange("p (j n) -> p j n", j=16))
            for oo in range(2):
                oc = 2 * q4 + oo
                mvg(lambda j, oo=oo: wv[:, j, 128 * oo:128 * oo + 128], lambda j: MIX[0:64, j, :], 16,
                    psD[:, 2 * oc:2 * oc + 2], tpD, [tw, tMIX])
        TT(C, "dve", X0, X0, psD[:, 0:16].rearrange("p (k s) -> p k s", s=2), ALU.add, [tpD, tX], [tX])
        rms0(X0, tX, 8, 128, lambda k: C.g2[:, l, k:k + 1], Hh, tH, float(D))
        for fb in range(8):
            wv, tw = wload(dr["w_up"][l, :, 512 * fb:512 * fb + 512].rearrange("(k p) n -> p k n", p=128),
                           lambda sl: sl[:, :].rearrange("p (k n) -> p k n", k=8))
            for fi in range(4):
                f = 4 * fb + fi
                mvg(lambda k, fi=fi: wv[:, k, 128 * fi:128 * fi + 128], hk, 8, psU[:, 2 * f:2 * f + 2], tpU, [tw, tH])
        ACTF(C, Aa, psU[:, 0:64].rearrange("p (k s) -> p k s", s=2), AF.Relu, [tpU], [tA])
        TT(C, "dve", Aa, Aa, Aa, ALU.mult, [tA], [tA])
        MEMSET(C, "dve", psD[:, 16:32], 0.0, [tpD])
        for fs in range(8):
            wv, tw = wload(dr["w_down"][l, 512 * fs:512 * fs + 512, :].rearrange("(k p) n -> p k n", p=128),
                           lambda sl: sl[:, :].rearrange("p (k n) -> p k n", k=4))
            for oc in range(8):
                for kk in range(4):
                    MM(C, psD[:, 16 + 2 * oc:18 + 2 * oc], wv[:, kk, 128 * oc:128 * oc + 128], Aa[:, 4 * fs + kk, :],
                       False, False, [tw, tA], [tpD])
        TT(C, "dve", X0, X0, psD[:, 16:32].rearrange("p (k s) -> p k s", s=2), ALU.add, [tpD, tX], [tX])


def _tok0_h(C, l):
    h0, th = C.h0, C.Th0
    x0 = C.xT[:, :, 0]
    ACTF(C, h0[:, 11:19], x0, AF.Square, [C.TX[0]], [th], accum_out=h0[:, 9:10])
    pb, tp = bank(C, "rms", [2, 3])
    MM(C, pb[:, 0:2], C.onesF[:, :], h0[:, 9:11], True, True, [C.onesF, th], [tp])
    ACTF(C, h0[:, 19:20], pb[:, 0:1], AF.Sqrt, [tp], [th], bias=EPS, scale=1.0 / D)
    RECIP(C, h0[:, 19:20], h0[:, 19:20], [th], [th])
    STT(C, "dve", h0[:, 0:8], x0, h0[:, 19:20], C.g1[:, l, :], ALU.mult, ALU.mult, [C.TX[0], th, C.g1], [th])


def _mixer(C, l, mixers):
    claim(C, C.TMIX)
    if "nsa" in mixers:
        _nsa(C, l, 0)
        _nsa(C, l, 1)
    else:
        for ch in range(3):
            MEMSET(C, "pool", C.mixT[:, ch, :], 0.0, [C.TMIX[ch]])
    if "mla" in mixers:
        _mla_pre(C, l)
        _mla_head(C, l, 0)
        _mla_head(C, l, 1)
    else:
        MEMSET(C, "pool", C.mixT[:, 3, :], 0.0, [C.TMIX[3]])
    _wout(C, l, 0)
    if "mla" in mixers:
        for h in (2, 3, 4):
            _mla_head(C, l, h)
    else:
        MEMSET(C, "pool", C.mixT[:, 0, :], 0.0, [C.TMIX[0]])
        MEMSET(C, "pool", C.mixT[0:64, 1, :], 0.0, [C.TMIX[1]])
    if "ret" in mixers:
        _ret(C, l)
    else:
        MEMSET(C, "pool", C.mixT[64:128, 1, :], 0.0, [C.TMIX[1]])
        for ch in (2, 3):
            MEMSET(C, "pool", C.mixT[:, ch, :], 0.0, [C.TMIX[ch]])
    _wout(C, l, 1)


def proj_rope(C, c, wa_fn, wb_fn, M, wts, dst, tdst, tabC, tCt, tabS, tSt, nr=16):
    cs = slice(512 * c, 512 * c + 512)
    pa, tpa = bank(C, "proj", [0, 1])
    for k in range(8):
        MM(C, pa[0:M, :], wa_fn(k), C.hT[:, k, cs], k == 0, k == 7, wts + [C.TH[c]], [tpa])
    pb, tpb = bank(C, "proj", [0, 1])
    for k in range(8):
        MM(C, pb[0:nr, :], wb_fn(k), C.hT[:, k, cs], k == 0, k == 7, wts + [C.TH[c]], [tpb])
    CP(C, "dve", dst[0:M, :], pa[0:M, :], [tpa], [tdst])
    ra, tra = tmp32(C)
    rb, trb = tmp32(C)
    TT(C, "dve", ra[0:nr, :], pa[0:nr, :], tabC[0:nr, cs], ALU.mult, [tpa, tCt], [tra])
    TT(C, "dve", rb[0:nr, :], pb[0:nr, :], tabS[0:nr, cs], ALU.mult, [tpb, tSt], [trb])
    TT(C, "pool", dst[0:nr, :], ra[0:nr, :], rb[0:nr, :], ALU.add, [tra, trb], [tdst])


def _nsa(C, l, g):
    mk = C.mk
    win = C.dr["w_in"]
    wa = WA(C)
    KVc, tKVc = wa.bf(S, "KVc")
    KTs, tKTs = wa.bf(S, "KTs")
    KTw, tKTw = wa.bf(S, "KTw")
    Vs, tVs = wa.bf(S, "Vs")
    Vs = Vs.rearrange("p (t n) -> p t n", t=16)
    Vw, tVw = wa.bf(S, "Vw")
    Vw = Vw.rearrange("p (t n) -> p t n", t=16)
    QT, tQT = [], []
    for r in range(3):
        a_, t_ = wa.bf(512, "nQT%d" % r)
        QT.append(a_)
        tQT.append(t_)
    PT, tPT = [], []
    for i in range(2):
        a_, t_ = wa.bf(512, "nPT%d" % i)
        PT.append(a_)
        tPT.append(t_)
    acc, tacc = [], []
    for r in range(3):
        a_, t_ = wa.f32(512, "nacc%d" % r)
        acc.append(a_)
        tacc.append(t_)
    kcT, tkcT = wa.bf(128, "kcT")
    Vc, tVc = wa.bf(128, "Vcaug")
    hid, thid = wa.bf(128, "hid")
    Wn2, tWn2 = wa.bf(640, "Wn2")
    Wn2 = Wn2.rearrange("p (k n) -> p k n", k=8)
    WP, tWP = wa.bf(768, "nWP")
    WP = WP.rearrange("p (k n) -> p k n", k=8)
    glog, tglog = wa.bf(512, "glog")
    w2, tw2 = wa.bf(128, "w2kv")
    posT, tposT = wa.bf(64, "posT")
    off_z = wa.o
    zt, tzt = wa.f32(128, "gz")
    ut, tut = wa.f32(128, "gu")
    selb4, tselb4 = [], []
    for i in range(4):
        ap_ = C.arena[:, off_z + 128 * i:off_z + 128 * i + 128]
        selb4.append(ap_)
        t_ = T(ap_, "selb%d" % i)
        tselb4.append(t_)
        wa.tiles.append(t_)
    sm, tsm = wa.f32(128, "nsmall")
    bsb, tbsb = wa.f32(4, "nbias")
    claim(C, wa.tiles)
    C.pti = 0
    ws, tw, gname = wslot(C)
    Wn = ws[:].rearrange("p (k n) -> p k n", k=8)
    cols = [(OFF["q"] + 192 * g, 192, 0), (OFF["kc"] + 64 * g, 64, 192), (OFF["vc"] + 64 * g, 64, 256),
            (OFF["ks"] + 64 * g, 64, 320), (OFF["kw"] + 64 * g, 64, 384), (OFF["vs"] + 64 * g, 64, 448)]
    for (src, n, dst) in cols:
        LDW(C, Wn[:, :, dst:dst + n], win[l, :, src:src + n].rearrange("(k p) n -> p k n", p=128), gname, [tw])
    LDW(C, Wn2[:, :, 0:64], win[l, :, OFF["vw"] + 64 * g:OFF["vw"] + 64 * g + 64].rearrange("(k p) n -> p k n", p=128),
        "nsaw", [tWn2])
    LDW(C, Wn2[:, :, 64:73], win[l, :, OFF["gate"] + 9 * g:OFF["gate"] + 9 * g + 9].rearrange(
        "(k p) n -> p k n", p=128), "nsaw", [tWn2], allow_slow_non_contiguous=True)
    ws1, tw1, gname1 = wslot(C)
    w1 = ws1[:].rearrange("p (l h) -> p l h", l=32)
    LDW(C, w1[0:64], C.dr["cmp_w1_k"][l].rearrange("(l d) h -> d l h", d=64), gname1, [tw1])
    LDW(C, w1[64:128], C.dr["cmp_w1_v"][l].rearrange("(l d) h -> d l h", d=64), gname1, [tw1])
    LDW(C, w2[:, 0:64], C.dr["cmp_w2_k"][l], "nsaw", [tw2])
    LDW(C, w2[:, 64:128], C.dr["cmp_w2_v"][l], "nsaw", [tw2])
    MEMSET(C, "pool", posT, 0.0, [tposT])
    LDW(C, posT[0:64, 0:32], C.dr["cmp_pos_k"][l].rearrange("l d -> d l"), "nsaw", [tposT],
        allow_slow_non_contiguous=True)
    LDW(C, posT[64:128, 0:32], C.dr["cmp_pos_v"][l].rearrange("l d -> d l"), "nsaw", [tposT],
        allow_slow_non_contiguous=True)
    o, w_ = BLOB_OFF["E"]
    LDW(C, KTs[64:96, :], C.dr["blob"][64:96, o:o + w_], "nsaw", [tKTs])
    for i, off in enumerate((0, 64, 128, 192, 320, 384)):
        CP(C, "pool", WP[:, :, 16 * i:16 * i + 8], Wn[:, :, off + 8:off + 16], [tw], [tWP])
        CP(C, "pool", WP[:, :, 16 * i + 8:16 * i + 16], Wn[:, :, off:off + 8], [tw], [tWP])
    MEMSET(C, "pool", Vs[:, :, 64:128], 1.0, [tVs])
    MEMSET(C, "pool", Vw[:, :, 64:128], 1.0, [tVw])
    MEMSET(C, "pool", Vc[:, 64:128], 1.0, [tVc])
    MEMSET(C, "pool", Vc[:, 0:64], 0.0, [tVc])
    for c in range(NCH):
        cs = slice(512 * c, 512 * c + 512)
        proj_rope(C, c, lambda k: Wn[:, k, 192:320], lambda k: WP[:, k, 48:64], 128, [tw, tWP], KVc[:, cs], tKVc,
                  C.CA, C.TCA, C.SA, C.TSA)
        proj_rope(C, c, lambda k: Wn[:, k, 320:384], lambda k: WP[:, k, 64:80], 64, [tw, tWP], KTs[:, cs], tKTs,
                  C.CA, C.TCA, C.SA, C.TSA)
        proj_rope(C, c, lambda k: Wn[:, k, 384:448], lambda k: WP[:, k, 80:96], 64, [tw, tWP], KTw[:, cs], tKTw,
                  C.CA, C.TCA, C.SA, C.TSA)
        pv, tpv = bank(C, "proj", [0, 1])
        for tl in range(4):
            t = 4 * c + tl
            for k in range(8):
                MM(C, pv[:, 128 * tl:128 * tl + 64], C.hT[:, k, 128 * t:128 * t + 128], Wn[:, k, 448:512], k == 0, k == 7,
                   [tw, C.TH[c]], [tpv])
            for k in range(8):
                MM(C, pv[:, 128 * tl + 64:128 * tl + 128], C.hT[:, k, 128 * t:128 * t + 128], Wn2[:, k, 0:64], k == 0,
                   k == 7, [tWn2, C.TH[c]], [tpv])
        pv3 = pv[:].rearrange("p (a b) -> p a b", a=4)
        CP(C, "act", Vs[:, 4 * c:4 * c + 4, 0:64], pv3[:, :, 0:64], [tpv], [tVs])
        CP(C, "act", Vw[:, 4 * c:4 * c + 4, 0:64], pv3[:, :, 64:128], [tpv], [tVw])
    for kind in (0, 1):
        r0 = 64 * kind
        pbias, tpbias = bank(C, "oa", [6, 7])
        for li in range(32):
            MM(C, pbias[:, 0:2], w1[r0:r0 + 64, li, :], posT[r0:r0 + 64, li:li + 2], li == 0, li == 31,
               [tw1, tposT], [tpbias])
        CP(C, "dve", bsb[:, 0:2], pbias[:, 0:2], [tpbias], [tbsb])
        ph, tph = bank(C, "st", [4, 5])
        for li in range(32):
            MM(C, ph[:, 0:127], w1[r0:r0 + 64, li, :], KVc[r0:r0 + 64, li:li + 2017:16], li == 0, li == 31,
               [tw1, tKVc], [tph])
        ACTF(C, zt[:, 0:127], ph[:, 0:127], AF.Identity, [tph, tbsb], [tzt], bias=bsb[:, 0:1], scale=1.0)
        TT(C, "pool", ut[:, 0:127], zt[:, 0:127], zt[:, 0:127], ALU.mult, [tzt], [tut])
        TS(C, "pool", ut[:, 0:127], ut[:, 0:127], 0.044715, None, ALU.mult, None, [tut], [tut])
        TS(C, "pool", ut[:, 0:127], ut[:, 0:127], 1.0, None, ALU.add, None, [tut], [tut])
        TT(C, "pool", ut[:, 0:127], ut[:, 0:127], zt[:, 0:127], ALU.mult, [tut, tzt], [tut])
        ACTF(C, ut[:, 0:127], ut[:, 0:127], AF.Exp, [tut], [tut], scale=-1.5957691216057308)
        TS(C, "pool", ut[:, 0:127], ut[:, 0:127], 1.0, None, ALU.add, None, [tut], [tut])
        RECIP(C, ut[:, 0:127], ut[:, 0:127], [tut], [tut])
        TT(C, "dve", hid[:, 0:127], zt[:, 0:127], ut[:, 0:127], ALU.mult, [tzt, tut], [thid])
        pk, tpk = bank(C, "oa", [6, 7])
        if kind == 0:
            MM(C, pk[0:64, 0:127], w2[:, 0:64], hid[:, 0:127], True, True, [tw2, thid], [tpk])
            CP(C, "act", kcT[0:64, 0:127], pk[0:64, 0:127], [tpk], [tkcT])
        else:
            MM(C, pk[0:127, 0:64], hid[:, 0:127], w2[:, 64:128], True, True, [tw2, thid], [tpk])
            CP(C, "act", Vc[0:127, 0:64], pk[0:127, 0:64], [tpk], [tVc])

    for i in range(4):
        mk.op("pool", lambda e, ap_=selb4[i]: e.memset(ap_, 0.0), [thid, tkcT, tVc], [tselb4[i]])

    def combine(oa, toa, r, b, c, mode):
        cs = slice(512 * c, 512 * c + 512)
        rc, trc = tmp32(C)
        ACTF(C, rc[0:64, :], oa[64:128, :], AF.Ln, [toa], [trc], bias=1e-30, scale=1.0)
        j = 3 * r + b
        pg2, tpg2 = bank(C, "g", [2])
        MM(C, pg2[0:64, :], C.sel9B[0:9, 64 * j:64 * j + 64], glog[0:9, :], True, True, [C.sel9B, tglog], [tpg2])
        gs, tgs = tmp32(C)
        ACTF(C, gs[0:64, :], pg2[0:64, :], AF.Exp, [tpg2], [tgs], scale=-1.0)
        ACTF(C, gs[0:64, :], gs[0:64, :], AF.Ln, [tgs], [tgs], bias=1.0, scale=1.0)
        TT(C, "pool", rc[0:64, :], rc[0:64, :], gs[0:64, :], ALU.add, [trc, tgs], [trc])
        ACTF(C, rc[0:64, :], rc[0:64, :], AF.Exp, [trc], [trc], scale=-1.0)
        if mode == 0:
            TT(C, "dve", acc[r][0:64, :], oa[0:64, :], rc[0:64, :], ALU.mult, [toa, trc], [tacc[r]])
        elif mode == 1:
            TT(C, "dve", gs[0:64, :], oa[0:64, :], rc[0:64, :], ALU.mult, [toa, trc, tgs], [tgs])
            TT(C, "pool", acc[r][0:64, :], acc[r][0:64, :], gs[0:64, :], ALU.add, [tacc[r], tgs], [tacc[r]])
        else:
            ch, po = _mix_dst(C, 64 * (3 * g + r))
            TT(C, "dve", gs[0:64, :], oa[0:64, :], rc[0:64, :], ALU.mult, [toa, trc, tgs], [tgs])
            TT(C, "dve", C.mixT[po:po + 64, ch, cs], acc[r][0:64, :], gs[0:64, :], ALU.add, [tacc[r], tgs],
               [C.TMIX[ch]])

    for c in range(NCH):
        cs = slice(512 * c, 512 * c + 512)
        for r in range(3):
            proj_rope(C, c, lambda k, r=r: Wn[:, k, 64 * r:64 * r + 64], lambda k, r=r: WP[:, k, 16 * r:16 * r + 16], 64,
                      [tw, tWP], QT[r], tQT[r], C.CA, C.TCA, C.SA, C.TSA)
            if c < 2:
                MEMSET(C, "pool", QT[r][64:96, :], 0.0, [tQT[r]])
        pg, tpg = bank(C, "g", [2])
        for k in range(8):
            MM(C, pg[0:9, :], Wn2[:, k, 64:73], C.hT[:, k, cs], k == 0, k == 7, [tWn2, C.TH[c]], [tpg])
        CP(C, "act", glog[0:9, :], pg[0:9, :], [tpg], [tglog])
        impb, timp = C.pb[3], C.TP[3]
        for r in range(3):
            st, tst = bank(C, "st", [4, 5])
            MM(C, st[0:127, :], kcT[0:64, 0:127], QT[r][0:64, :], True, False, [tkcT, tQT[r]], [tst])
            MM(C, st[0:127, :], C.identB[0:127, 0:127], C.cmpbB[0:127, cs], False, True, [C.identB, C.cmpbB], [tst])
            pi = C.pti % 2
            C.pti += 1
            ACTF(C, PT[pi][0:127, :], st[0:127, :], AF.Exp, [tst], [tPT[pi]], scale=0.125)
            oa, toa = bank(C, "oa", [6, 7])
            MM(C, oa[:, :], Vc[0:127, :], PT[pi][0:127, :], True, True, [tVc, tPT[pi]], [toa])
            if c >= 2:
                for tl in range(4):
                    o0 = (tl * 3 + r) * 33
                    MM(C, impb[:, o0:o0 + 33], PT[pi][0:127, 128 * tl:128 * tl + 128], C.ovl1B[0:127, 0:33], True, True,
                       [tPT[pi], C.ovl1B], [timp])
            combine(oa, toa, r, 0, c, 0)
        if c >= 2:
            for tl in range(4):
                kq = 4 * c + tl - 8
                bb = sm[:, 0:32]
                t2 = sm[:, 32:64]
                m1 = sm[:, 64:72]
                m2 = sm[:, 72:80]
                dn = sm[:, 80:83]
                keep = C.keepF[:, 32 * kq:32 * kq + 32]
                addt = C.addtF[:, 32 * kq:32 * kq + 32]
                TS(C, "dve", dn, impb[:, tl * 99 + 32:tl * 99 + 99:33], 1e-30, None, ALU.max, None, [timp], [tsm])
                RECIP(C, dn, dn, [tsm], [tsm])
                for r in range(3):
                    o0 = (tl * 3 + r) * 33
                    if r == 0:
                        STT(C, "dve", bb, impb[:, o0:o0 + 32], dn[:, 0:1], keep, ALU.mult, ALU.mult,
                            [timp, tsm, C.keepF], [tsm])
                    else:
                        STT(C, "dve", t2, impb[:, o0:o0 + 32], dn[:, r:r + 1], keep, ALU.mult, ALU.mult,
                            [timp, tsm, C.keepF], [tsm])
                        TT(C, "dve", bb, bb, t2, ALU.add, [tsm], [tsm])
                TT(C, "dve", bb, bb, addt, ALU.add, [tsm, C.addtF], [tsm])
                mk.op("dve", lambda e, m1=m1, bb=bb: e.max(out=m1, in_=bb), [tsm], [tsm])
                mk.op("dve", lambda e, m1=m1, bb=bb, t2=t2: e.match_replace(out=t2, in_to_replace=m1, in_values=bb,
                                                                        imm_value=-3.0e38), [tsm], [tsm])
                mk.op("dve", lambda e, m2=m2, t2=t2: e.max(out=m2, in_=t2), [tsm], [tsm])
                STT(C, "dve", t2, bb, m2[:, 7:8], C.bigF[:, 0:32], ALU.subtract, ALU.mult, [tsm, C.bigF], [tsm])
                TS(C, "dve", t2, t2, 0.0, None, ALU.min, None, [tsm], [tsm])
                TS(C, "dve", selb4[tl][:, 64:96], t2, -1.0, None, ALU.max, None, [tsm], [tselb4[tl]])
        for r in range(3):
            oa, toa = attn_chunk(C, c, lambda j: KTw[0:64, 128 * j:128 * j + 128], tKTw, 64, QT[r], tQT[r],
                                 lambda j: Vw[:, j, :], tVw, "window", 0.125, PT, tPT)
            combine(oa, toa, r, 2, c, 1)
        if c >= 2:
            for tl in range(4):
                ptr, tptr = bank(C, "proj", [0, 1])
                ptb = ptr[:].bitcast(BF16)
                TR(C, ptb[0:96, 0:128], selb4[tl][:, 0:96], C.identB[:, :], [tselb4[tl], C.identB], [tptr])
                for r in range(3):
                    CP(C, "act", QT[r][64:96, 128 * tl:128 * tl + 128], ptb[64:96, 0:128], [tptr], [tQT[r]])
        for r in range(3):
            oa, toa = attn_chunk(C, c, lambda j: KTs[0:96, 128 * j:128 * j + 128], tKTs, 96, QT[r], tQT[r],
                                 lambda j: Vs[:, j, :], tVs, "causal", 0.125, PT, tPT)
            combine(oa, toa, r, 1, c, 2)


def _ret(C, l):
    mk = C.mk
    RS = 3
    lg = np.log(1.0 - 2.0 ** (-5.0 - np.arange(5, dtype=np.float64)))
    for h in range(5):
        wa = WA(C)
        qT, tq = wa.bf(S, "rqT")
        kT, tk = wa.bf(S, "rkT")
        qd, tqd = wa.bf(S, "rqd")
        v, tv = wa.bf(1024, "rv")
        v = v.rearrange("p (t n) -> p t n", t=16)
        gt, tg = wa.bf(1024, "rg")
        gt = gt.rearrange("p (t n) -> p t n", t=16)
        W, tW = wa.bf(8 * 256, "rW")
        W = W.rearrange("p (i k n) -> p i k n", i=4, k=8)
        WP, tWP = wa.bf(8 * 128, "rWP")
        WP = WP.rearrange("p (k n) -> p k n", k=8)
        scm, tscm = wa.bf(128, "rscm")
        kw, tkw = wa.bf(64, "rkw")
        sbf, tsbf = wa.bf(64, "rsbf")
        sf, tsf = wa.f32(64, "rsf")
        yt, tyt = wa.bf(128, "ryt")
        osb, tosb = wa.f32(64, "rosb")
        junk, tjunk = wa.f32(64, "rjunk")
        stt, tstt = wa.f32(16, "rstat")
        gn, tgn = wa.f32(64, "rgn")
        claim(C, wa.tiles)
        win = C.dr["w_in"]
        ch, po = _mix_dst(C, 704 + 64 * h)
        if RS <= -2:
            MEMSET(C, "pool", C.mixT[po:po + 64, ch, :], 0.0, [C.TMIX[ch]])
            continue
        for i, nm in enumerate(("rq", "rk", "rv", "rg")):
            LDW(C, W[:, i, :, :], win[l, :, OFF[nm] + 64 * h:OFF[nm] + 64 * h + 64].rearrange(
                "(k p) n -> p k n", p=128), "retw", [tW])
        mk.dma("sp", gn, C.dr["ret_gn_gain"][l, h, :].partition_broadcast(128), "const", writes=[tgn])
        for i in range(2):
            CP(C, "pool", WP[:, :, 64 * i:64 * i + 32], W[:, i, :, 32:64], [tW], [tWP])
            CP(C, "pool", WP[:, :, 64 * i + 32:64 * i + 64], W[:, i, :, 0:32], [tW], [tWP])
        MEMSET(C, "pool", yt, 0.0, [tyt])
        if RS == -1:
            MEMSET(C, "pool", C.mixT[po:po + 64, ch, :], 0.0, [C.TMIX[ch]])
            continue
        for c in range(NCH):
            cs = slice(512 * c, 512 * c + 512)
            for i, (dst, tdst) in enumerate(((qT, tq), (kT, tk))):
                pa, tpa = bank(C, "proj", [0, 1])
                for k in range(8):
                    MM(C, pa[0:64, :], W[:, i, k, :], C.hT[:, k, cs], k == 0, k == 7, [tW, C.TH[c]], [tpa])
                pb, tpb = bank(C, "proj", [0, 1])
                for k in range(8):
                    MM(C, pb[0:64, :], WP[:, k, 64 * i:64 * i + 64], C.hT[:, k, cs], k == 0, k == 7, [tWP, C.TH[c]], [tpb])
                ra, tra = tmp32(C)
                rb, trb = tmp32(C)
                TT(C, "dve", ra[0:64, :], pa[0:64, :], C.CR[0:64, cs], ALU.mult, [tpa, C.TCR], [tra])
                TT(C, "dve", rb[0:64, :], pb[0:64, :], C.SR[0:64, cs], ALU.mult, [tpb, C.TSR], [trb])
                TT(C, "pool", dst[0:64, cs], ra[0:64, :], rb[0:64, :], ALU.add, [tra, trb], [tdst])
            if RS != 0:
                pv, tpv = bank(C, "proj", [0, 1])
                for tl in range(4):
                    t = 4 * c + tl
                    for k in range(8):
                        MM(C, pv[:, 128 * tl:128 * tl + 64], C.hT[:, k, 128 * t:128 * t + 128], W[:, 2, k, :], k == 0,
                           k == 7, [tW, C.TH[c]], [tpv])
                    for k in range(8):
                        MM(C, pv[:, 128 * tl + 64:128 * tl + 128], C.hT[:, k, 128 * t:128 * t + 128], W[:, 3, k, :],
                           k == 0, k == 7, [tW, C.TH[c]], [tpv])
                pv3 = pv[:].rearrange("p (a b) -> p a b", a=4)
                CP(C, "act", v[:, 4 * c:4 * c + 4, :], pv3[:, :, 0:64], [tpv], [tv])
                ge, tge = tmp32(C)
                ge3 = ge[:, 0:256].rearrange("p (a b) -> p a b", a=4)
                ACTF(C, ge3, pv3[:, :, 64:128], AF.Exp, [tpv], [tge], scale=-1.0)
                ACTF(C, ge[:, 0:256], ge[:, 0:256], AF.Ln, [tge], [tge], bias=1.0, scale=1.0)
                ACTF(C, ge[:, 0:256], ge[:, 0:256], AF.Exp, [tge], [tge], scale=-1.0)
                if True:
                    TT(C, "dve", gt[:, 4 * c:4 * c + 4, :], pv3[:, :, 64:128], ge3, ALU.mult, [tpv, tge], [tg])
        if RS in (0, 5, 6, 7, 8, 9):
            MEMSET(C, "pool", C.mixT[po:po + 64, ch, :], 0.0, [C.TMIX[ch]])
            continue
        for n in range(16):
            ns = slice(128 * n, 128 * n + 128)
            TT(C, "pool", qd[0:64, ns], qT[0:64, ns], C.rdB[0:64, 128 * h:128 * h + 128], ALU.mult, [tq, C.rdB], [tqd])
        cd = float(np.exp(128.0 * lg[h]))
        if RS == 1:
            MEMSET(C, "pool", C.mixT[po:po + 64, ch, :], 0.0, [C.TMIX[ch]])
            continue
        for n in range(16):
            ns = slice(128 * n, 128 * n + 128)
            ps, tps = bank(C, "st", [4, 5])
            MM(C, ps[:, 0:128], kT[0:64, ns], qT[0:64, ns], True, True, [tk, tq], [tps])
            TT(C, "dve", scm, ps[:, 0:128], C.intraTF[:, 128 * h:128 * h + 128], ALU.mult, [tps, C.intraTF], [tscm])
            if n == 0:
                ix = (l * 5 + h) * 2 + C.cur_seq
                TS(C, "dve", scm[0:1, 0:1], C.s00[0:1, ix:ix + 1], 0.125, None, ALU.mult, None, [C.Ts00], [tscm])
            po_, tpo = bank(C, "oa", [6, 7])
            MM(C, po_[:, 0:64], scm, v[:, n, :], True, n == 0, [tscm, tv], [tpo])
            if n > 0:
                MM(C, po_[:, 0:64], qd[0:64, ns], sbf[0:64, :], False, True, [tqd, tsbf], [tpo])
            if n < 15:
                pk, tpk = bank(C, "st", [4, 5])
                MM(C, pk[:, 0:64], kT[0:64, ns], C.identB[0:64, 0:64], True, True, [tk, C.identB], [tpk])
                ACTF(C, kw, pk[:, 0:64], AF.Identity, [tpk, C.wdF], [tkw], scale=C.wdF[:, h:h + 1])
                pst, tpst = bank(C, "rms", [2, 3])
                MM(C, pst[0:64, 0:64], kw, v[:, n, :], True, True, [tkw, tv], [tpst])
                if n == 0:
                    CP(C, "dve", sf[0:64, :], pst[0:64, 0:64], [tpst], [tsf])
                else:
                    STT(C, "dve", sf[0:64, :], sf[0:64, :], cd, pst[0:64, 0:64], ALU.mult, ALU.add, [tsf, tpst], [tsf])
                CP(C, "pool", sbf[0:64, :], sf[0:64, :], [tsf], [tsbf])
            if RS == 2:
                CP(C, "act", yt[:, po:po + 64], po_[:, 0:64], [tpo], [tyt])
                ptr, tptr = bank(C, "proj", [0, 1])
                ptb = ptr[:].bitcast(BF16)
                TR(C, ptb[:, 0:128], yt, C.identB[:, :], [tyt, C.identB], [tptr])
                CP(C, "act", C.mixT[po:po + 64, ch, ns], ptb[po:po + 64, 0:128], [tptr], [C.TMIX[ch]])
                continue
            ACTF(C, osb, po_[:, 0:64], AF.Identity, [tpo], [tosb, tstt], accum_out=stt[:, 0:1])
            ACTF(C, junk, po_[:, 0:64], AF.Square, [tpo], [tjunk, tstt], accum_out=stt[:, 1:2])
            TS(C, "dve", stt[:, 2:3], stt[:, 0:1], 1.0 / 64, None, ALU.mult, None, [tstt], [tstt])
            TT(C, "dve", stt[:, 3:4], stt[:, 2:3], stt[:, 2:3], ALU.mult, [tstt], [tstt])
            STT(C, "dve", stt[:, 4:5], stt[:, 1:2], 1.0 / 64, stt[:, 3:4], ALU.mult, ALU.subtract, [tstt], [tstt])
            ACTF(C, stt[:, 5:6], stt[:, 4:5], AF.Ln, [tstt], [tstt], bias=EPS, scale=1.0)
            ACTF(C, stt[:, 6:7], stt[:, 5:6], AF.Exp, [tstt], [tstt], scale=-0.5)
            TT(C, "dve", stt[:, 7:8], stt[:, 2:3], stt[:, 6:7], ALU.mult, [tstt], [tstt])
            STT(C, "dve", osb, osb, stt[:, 6:7], stt[:, 7:8].to_broadcast([128, 64]), ALU.mult, ALU.subtract,
                [tosb, tstt], [tosb])
            TT(C, "pool", osb, osb, gn, ALU.mult, [tosb, tgn], [tosb])
            TT(C, "pool", yt[:, po:po + 64], osb, gt[:, n, :], ALU.mult, [tosb, tg], [tyt])
            ptr, tptr = bank(C, "proj", [0, 1])
            ptb = ptr[:].bitcast(BF16)
            TR(C, ptb[:, 0:128], yt, C.identB[:, :], [tyt, C.identB], [tptr])
            CP(C, "act", C.mixT[po:po + 64, ch, ns], ptb[po:po + 64, 0:128], [tptr], [C.TMIX[ch]])


_CACHE = {}


def kernel(**inputs):
    n_cores = 8
    x = np.ascontiguousarray(np.asarray(inputs["x"], dtype=np.float32))
    pos = np.ascontiguousarray(np.asarray(inputs["positions"], dtype=np.int32))
    if "nc" not in _CACHE:
        _CACHE["nc"] = build()[0]
        _CACHE["blob"] = make_blob()[0]
    nc = _CACHE["nc"]
    w = {n: np.ascontiguousarray(np.asarray(inputs[n], dtype=np.float32)) for n in WNAMES}
    in_maps = []
    for i in range(n_cores):
        m = {"x": x[2 * i:2 * i + 2], "pos": pos[2 * i:2 * i + 2], "blob": _CACHE["blob"]}
        m.update(w)
        in_maps.append(m)
    res = run_bass_kernel_spmd(nc, in_maps, core_ids=list(range(n_cores)))
    return np.concatenate([np.asarray(r["out"]) for r in res.results], axis=0).astype(np.float32)
```

```python
from contextlib import ExitStack
import numpy as np
import concourse.bass as bass
import concourse.mybir as mybir
from concourse.bass_utils import run_bass_kernel_spmd

F32 = mybir.dt.float32
BF16 = mybir.dt.bfloat16
I32 = mybir.dt.int32
AF = mybir.ActivationFunctionType
ALU = mybir.AluOpType
AX = mybir.AxisListType

D = 1024
S = 2048
DEPTH = 4
NCH = 4
NT = 16
N_IN = 2866
NEG = -30000.0
EPS = 1e-6
OFF = dict(q=0, kc=384, vc=512, ks=640, vs=768, kw=896, vw=1024, gate=1152, cq=1170, ckv=1426,
           kpe=1554, rq=1586, rk=1906, rv=2226, rg=2546)

ENGS = ("pe", "act", "dve", "pool", "sp")


class T:
    __slots__ = ("ap", "name", "lw", "rd")

    def __init__(self, ap, name=""):
        self.ap = ap
        self.name = name
        self.lw = {}
        self.rd = {}

    def __getitem__(self, idx):
        return self.ap[idx]


class DmaGroup:
    def __init__(self, mk, name):
        self.name = name
        self.total = 0
        self.sem = mk._newsem("g_" + name)


class MK:
    def __init__(self, nc, stack):
        self.nc = nc
        self.stack = stack
        self.q = {e: [] for e in ENGS}
        self.cnt = {e: 0 for e in ENGS}
        self.waited = {e: {} for e in ENGS}
        self.esem = {e: self._newsem("c_" + e) for e in ENGS}
        self.groups = {}

    def _newsem(self, name):
        return self.stack.enter_context(self.nc.semaphore(name))

    def group(self, name):
        g = self.groups.get(name)
        if g is None:
            g = DmaGroup(self, name)
            self.groups[name] = g
        return g

    def sb(self, name, shape, dt):
        return self.stack.enter_context(self.nc.sbuf_tensor(name, list(shape), dt))

    def ps(self, name, shape, dt):
        return self.stack.enter_context(self.nc.psum_tensor(name, list(shape), dt))

    def _collect(self, eng, reads, writes):
        need = {}

        def add(d, same_ok):
            for k, v in d.items():
                if k == eng and not same_ok:
                    continue
                if need.get(k, 0) < v:
                    need[k] = v
        raw_same = eng != "pe"
        for t in reads:
            add(t.lw, raw_same)
        for t in writes:
            add(t.lw, False)
            add(t.rd, False)
        out = []
        w = self.waited[eng]
        for k, v in need.items():
            if isinstance(k, DmaGroup):
                v = k.total
            if w.get(k, 0) >= v:
                continue
            w[k] = v
            out.append((k, v))
        return out

    def _mark(self, key, val, reads, writes):
        for t in reads:
            if t.rd.get(key, 0) < val:
                t.rd[key] = val
        for t in writes:
            t.lw = {key: val}
            t.rd = {}

    def op(self, eng, fn, reads=(), writes=()):
        waits = self._collect(eng, reads, writes)
        self.cnt[eng] += 1
        self.q[eng].append((fn, waits, None))
        self._mark(eng, self.cnt[eng], reads, writes)

    def dma(self, eng, out_ap, in_ap, group, reads=(), writes=(), **kw):
        waits = self._collect(eng, reads, writes)
        g = self.group(group) if isinstance(group, str) else group
        g.total += 16
        self.q[eng].append((lambda e: e.dma_start(out=out_ap, in_=in_ap, **kw), waits, g))
        self._mark(g, g.total, reads, writes)

    def claim(self, tiles, others):
        for t in tiles:
            for u in others:
                if u is t:
                    continue
                for d in (u.lw, u.rd):
                    for k, v in d.items():
                        if t.rd.get(k, 0) < v:
                            t.rd[k] = v

    def wait_all_dma(self, eng="sp"):
        waits = []
        for g in self.groups.values():
            if g.total and self.waited[eng].get(g, 0) < g.total:
                waits.append((g, g.total))
                self.waited[eng][g] = g.total
        self.q[eng].append((None, waits, None))

    def _sem(self, k):
        return k.sem if isinstance(k, DmaGroup) else self.esem[k]

    def emit(self):
        nc = self.nc
        mk = self
        with nc.Block() as block:
            def run(engname, e):
                for fn, waits, g in mk.q[engname]:
                    if fn is None:
                        for k, v in waits:
                            e.wait_ge(mk._sem(k), v)
                        continue
                    for k, v in waits[1:]:
                        e.wait_ge(mk._sem(k), v)
                    ins = fn(e)
                    if waits:
                        k, v = waits[0]
                        ins._wait_ge(mk._sem(k), v)
                    if g is not None:
                        ins.then_inc(g.sem, 16)
                    else:
                        ins.then_inc(mk.esem[engname], 1)

            @block.sync
            def _(e):
                run("sp", e)

            @block.scalar
            def _(e):
                run("act", e)

            @block.vector
            def _(e):
                run("dve", e)

            @block.gpsimd
            def _(e):
                run("pool", e)

            @block.tensor
            def _(e):
                run("pe", e)


def _blob_layout():
    secs = [("ident", 128), ("tri", 128), ("winb", 128), ("cmpb", 2048), ("E", 2048), ("ovl1", 33),
            ("keep", 256), ("addt", 256), ("intraT", 640), ("rd", 640), ("wd", 5), ("cols", 8),
            ("sel9", 576), ("ones", 128), ("sel18", 1152)]
    off = {}
    o = 0
    for n, w in secs:
        off[n] = (o, w)
        o += w
    return off, o


BLOB_OFF, BLOB_W = _blob_layout()


def make_blob():
    b = np.zeros((128, BLOB_W), np.float32)

    def sec(n):
        o, w = BLOB_OFF[n]
        return b[:, o:o + w]
    p = np.arange(128)
    sec("ident")[:] = np.eye(128, dtype=np.float32)
    sec("ones")[:] = 1.0
    sec("tri")[:] = np.where(p[:, None] <= p[None, :], 0.0, NEG)
    sec("winb")[:] = np.where(p[:, None] > p[None, :], 0.0, NEG)
    q = np.arange(S)
    cm = np.where((16 * p[:, None] + 31) <= q[None, :], 0.0, NEG)
    cm[127, :] = NEG
    sec("cmpb")[:] = cm
    e = np.zeros((128, S), np.float32)
    for r in range(32):
        e[64 + r, 64 * r:64 * r + 64] = -NEG
    sec("E")[:] = e
    n_cmp, n_sel = 127, 32
    cs = np.arange(n_cmp) * 16
    ce = cs + 32
    ss = np.arange(n_sel) * 64
    se = ss + 64
    ov = np.clip(np.minimum(ce[:, None], se[None, :]) - np.maximum(cs[:, None], ss[None, :]), 0, None) / 32.0
    o1 = sec("ovl1")
    o1[:127, :32] = ov
    o1[:127, 32] = 1.0
    keep = np.zeros((128, 8, 32), np.float32)
    addt = np.zeros((128, 8, 32), np.float32)
    blk = np.arange(32)
    for qt in range(8, 16):
        t = qt * 128 + p
        cur = t // 64
        valid = blk[None, :] <= cur[:, None]
        forced = (blk[None, :] == 0) | (blk[None, :] == cur[:, None]) | (blk[None, :] == cur[:, None] - 1)
        keep[:, qt - 8, :] = (valid & ~forced)
        addt[:, qt - 8, :] = np.where(valid & forced, 1e9, np.where(valid, 0.0, -1e30))
    sec("keep")[:] = keep.reshape(128, 256)
    sec("addt")[:] = addt.reshape(128, 256)
    lg = np.log(1.0 - 2.0 ** (-5.0 - np.arange(5, dtype=np.float64)))
    i = np.arange(128, dtype=np.float64)
    it = np.zeros((128, 5, 128), np.float64)
    for h in range(5):
        diff = i[None, :] - i[:, None]
        it[:, h, :] = np.where(diff >= 0, np.exp(np.maximum(diff, 0) * lg[h]), 0.0) * 0.125
    sec("intraT")[:] = it.reshape(128, 640)
    rd = np.zeros((128, 5, 128), np.float64)
    for h in range(5):
        rd[:, h, :] = np.exp((i + 1.0) * lg[h])[None, :]
    sec("rd")[:] = rd.reshape(128, 640)
    wd = sec("wd")
    for h in range(5):
        wd[:, h] = np.exp((127.0 - i) * lg[h]) * 0.125
    cols = sec("cols")
    inv_p = 1.0 / (500000.0 ** (np.arange(0, 16, 2, dtype=np.float32) / 16)).astype(np.float32)
    inv_m = 1.0 / (500000.0 ** (np.arange(0, 32, 2, dtype=np.float32) / 32)).astype(np.float32)
    inv_r = 1.0 / (10000.0 ** (np.arange(0, 64, 2, dtype=np.float32) / 64)).astype(np.float32)
    tp = 2.0 * np.pi
    cols[:, 1] = tp
    cols[:, 3] = tp
    for r in range(16):
        cols[r, 0] = np.float64(inv_p[r % 8]) / tp
    cols[0:8, 1] = -tp
    for r in range(64, 96):
        cols[r, 0] = np.float64(inv_m[(r - 64) % 16]) / tp
    cols[64:80, 1] = -tp
    for r in range(128):
        cols[r, 2] = np.float64(inv_r[(r % 64) % 32]) / tp
        if (r % 64) < 32:
            cols[r, 3] = -tp
    s9 = np.zeros((128, 9, 64), np.float32)
    for j in range(9):
        s9[j, j, :] = 1.0
    sec("sel9")[:] = s9.reshape(128, 576)
    s18 = np.zeros((128, 18, 64), np.float32)
    for j in range(18):
        s18[j, j, :] = 1.0
    sec("sel18")[:] = s18.reshape(128, 1152)
    return b, lg


WNAMES = ["ln1_gain", "w_in", "cmp_pos_k", "cmp_w1_k", "cmp_w2_k", "cmp_pos_v", "cmp_w1_v", "cmp_w2_v",
          "mla_q_norm", "mla_w_uq", "mla_kv_norm", "mla_w_ukv", "ret_gn_gain", "w_out", "ln2_gain",
          "w_up", "w_down", "final_gain"]
WSHAPES = dict(ln1_gain=[4, 1024], w_in=[4, 1024, 2866], cmp_pos_k=[4, 32, 64], cmp_w1_k=[4, 2048, 128],
               cmp_w2_k=[4, 128, 64], cmp_pos_v=[4, 32, 64], cmp_w1_v=[4, 2048, 128], cmp_w2_v=[4, 128, 64],
               mla_q_norm=[4, 256], mla_w_uq=[4, 256, 480], mla_kv_norm=[4, 128], mla_w_ukv=[4, 128, 640],
               ret_gn_gain=[4, 5, 64], w_out=[4, 1024, 1024], ln2_gain=[4, 1024], w_up=[4, 1024, 4096],
               w_down=[4, 4096, 1024], final_gain=[1024])


class Ctx:
    pass


def build(n_layers=DEPTH, n_seq=2, mixers=("nsa", "mla", "ret"), dbg=False):
    nc = bass.Bass("TRN2", target_bir_lowering=False)
    C = Ctx()
    C.nc = nc
    C.dbg = dbg
    C.dbg_outs = {}
    dr = {}
    dr["x"] = nc.dram_tensor("x", [n_seq, S, D], F32, kind="ExternalInput").ap()
    dr["pos"] = nc.dram_tensor("pos", [n_seq, S], I32, kind="ExternalInput").ap()
    dr["blob"] = nc.dram_tensor("blob", [128, BLOB_W], F32, kind="ExternalInput").ap()
    for n in WNAMES:
        dr[n] = nc.dram_tensor(n, WSHAPES[n], F32, kind="ExternalInput").ap()
    dr["out"] = nc.dram_tensor("out", [n_seq, S, D], F32, kind="ExternalOutput").ap()
    C.dr = dr
    with ExitStack() as st:
        mk = MK(nc, st)
        C.mk = mk
        _setup(C)
        if "ret" in mixers:
            _tok0_prepass(C, n_seq, n_layers)
            dump(C, "s00", C.Ts00, C.s00[:, :], [128, 40])
        for s in range(n_seq):
            C.cur_seq = s
            _load_x(C, s)
            if s == 0:
                dump(C, "xT", C.TX[0], C.xT[:, :, 0:512], [128, 8, 512])
                dump(C, "xT3", C.TX[3], C.xT[:, :, 1536:2048], [128, 8, 512])
            if mixers:
                _rope_tables(C, s)
            for l in range(n_layers):
                if mixers:
                    _rmsnorm_main(C, C.g1, l)
                    _mixer(C, l, mixers)
                _rmsnorm_main(C, C.g2, l)
                if s == 0 and l == 0:
                    dump(C, "h2T", C.TH[0], C.hT[:, :, 0:512], [128, 8, 512], BF16)
                _ffn(C, l)
                if s == 0 and l == 0:
                    dump(C, "aT", C.TA, C.aT[:, :, :], [128, 32, 512], BF16)
                    dump(C, "x2T", C.TX[3], C.xT[:, :, 1536:2048], [128, 8, 512])
            _final(C, s)
        mk.wait_all_dma("sp")
        mk.emit()
    return nc, C


def _bsec(C, name):
    o, w = BLOB_OFF[name]
    return C.dr["blob"][:, o:o + w]


def _setup(C):
    mk = C.mk
    cg = "const"

    def cf32(name, sec, w):
        t = mk.sb(name, [128, w], F32)
        tt = T(t, name)
        mk.dma("sp", t[:], _bsec(C, sec), cg, writes=[tt])
        return tt

    def cbf(name, sec, w):
        t = mk.sb(name, [128, w], BF16)
        tt = T(t, name)
        mk.dma("pool", t[:], _bsec(C, sec), cg, writes=[tt])
        return tt
    C.identF = cf32("identF", "ident", 128)
    C.identB = cbf("identB", "ident", 128)
    C.onesB = cbf("onesB", "ones", 128)
    C.triB = cbf("triB", "tri", 128)
    C.winbB = cbf("winbB", "winb", 128)
    C.cmpbB = cbf("cmpbB", "cmpb", 2048)
    C.ovl1B = cbf("ovl1B", "ovl1", 33)
    C.keepF = cf32("keepF", "keep", 256)
    C.addtF = cf32("addtF", "addt", 256)
    C.intraTF = cf32("intraTF", "intraT", 640)
    C.rdB = cbf("rdB", "rd", 640)
    C.wdF = cf32("wdF", "wd", 5)
    C.colsF = cf32("colsF", "cols", 8)
    C.sel9B = cbf("sel9B", "sel9", 576)
    C.onesF = cf32("onesF", "ones", 128)
    t0 = mk.sb("s00", [128, 40], F32)
    C.s00 = t0
    C.Ts00 = T(t0, "s00")
    t = mk.sb("bigF", [128, 32], F32)
    C.bigF = T(t, "bigF")
    MEMSET(C, "pool", t[:], 1.0e15, [C.bigF])
    dr = C.dr

    def gain_cols(name, src_ap, shape):
        t = mk.sb(name, shape, F32)
        tt = T(t, name)
        mk.dma("sp", t[:], src_ap, cg, writes=[tt], allow_slow_non_contiguous=True)
        return tt
    C.g1 = gain_cols("g1", dr["ln1_gain"].rearrange("l (c p) -> p l c", p=128), [128, 4, 8])
    C.g2 = gain_cols("g2", dr["ln2_gain"].rearrange("l (c p) -> p l c", p=128), [128, 4, 8])
    C.gf = gain_cols("gf", dr["final_gain"].rearrange("(c p) -> p c", p=128), [128, 8])
    C.gq = gain_cols("gq", dr["mla_q_norm"].rearrange("l (c p) -> p l c", p=128), [128, 4, 2])
    C.gkv = gain_cols("gkv", dr["mla_kv_norm"].rearrange("l p -> p l"), [128, 4])
    C.gnT = gain_cols("gnT", dr["ret_gn_gain"].rearrange("l h d -> d (l h)"), [64, 20])
    C.xT = mk.sb("xT", [128, 8, S], F32)
    C.TX = [T(C.xT, "xT%d" % c) for c in range(NCH)]
    C.hT = mk.sb("hT", [128, 8, S], BF16)
    C.TH = [T(C.hT, "hT%d" % c) for c in range(NCH)]
    AW = 27648
    C.arena = mk.sb("arena", [128, AW], BF16)
    C.arena_tiles = []
    C.mixT = C.arena[:, 0:4 * S].rearrange("p (c t) -> p c t", c=4)
    C.TMIX = [T(C.mixT, "mix%d" % c) for c in range(4)]
    C.aT = C.arena[:, 0:32 * 512].rearrange("p (f t) -> p f t", f=32)
    C.TA = T(C.aT, "aT")
    C.WORK0 = 4 * S
    C.sq = C.arena[:, AW - 4096:AW].rearrange("p (k t) -> p k t", k=8)
    C.TSQ = T(C.sq, "sq")
    stf = C.arena[:, AW - 8192:AW - 4096].bitcast(F32)
    C.stage = [stf[:, 0:1024], stf[:, 1024:2048]]
    C.TST = [T(C.stage[i], "stage%d" % i) for i in range(2)]
    C.arena_tiles += C.TMIX + [C.TA, C.TSQ] + C.TST
    C.NW = 2
    C.wsl = [mk.sb("wsl%d" % i, [128, 4096], BF16) for i in range(C.NW)]
    C.TW = [T(C.wsl[i], "wsl%d" % i) for i in range(C.NW)]
    C.wi = 0
    C.t32 = [mk.sb("t32_%d" % i, [128, 512], F32) for i in range(4)]
    C.TT32 = [T(C.t32[i], "t32_%d" % i) for i in range(4)]
    C.t32i = 0
    C.pb = [mk.ps("pb%d" % i, [128, 512], F32) for i in range(8)]
    C.TP = [T(C.pb[i], "pb%d" % i) for i in range(8)]
    C.pbi = {}


def bank(C, cls, choices):
    i = C.pbi.get(cls, 0)
    C.pbi[cls] = i + 1
    b = choices[i % len(choices)]
    return C.pb[b], C.TP[b]


def tmp32(C):
    i = C.t32i % 4
    C.t32i += 1
    return C.t32[i], C.TT32[i]


def wslot(C):
    i = C.wi % C.NW
    C.wi += 1
    return C.wsl[i], C.TW[i], "wsl%d" % i


def dump(C, name, t, ap, shape, dt=F32):
    if not C.dbg:
        return
    d = C.nc.dram_tensor("dbg_" + name, list(shape), dt, kind="ExternalOutput").ap()
    C.dbg_outs[name] = d
    C.mk.dma("sp", d, ap, "dbg", reads=[t])


def claim(C, tiles):
    C.mk.claim(tiles, C.arena_tiles)
    for t in tiles:
        if t not in C.arena_tiles:
            C.arena_tiles.append(t)


def MM(C, o, l, r, st, sp, reads, writes):
    C.mk.op("pe", lambda e: e.matmul(o, l, r, start=st, stop=sp, skip_group_check=True), reads, writes)


def TR(C, o, i, ident, reads, writes):
    C.mk.op("pe", lambda e: e.transpose(out=o, in_=i, identity=ident), reads, writes)


def ACTF(C, o, i, func, reads, writes, **kw):
    C.mk.op("act", lambda e: e.activation(out=o, in_=i, func=func, **kw), reads, writes)


def TT(C, eng, o, a, b, op, reads, writes):
    C.mk.op(eng, lambda e: e.tensor_tensor(out=o, in0=a, in1=b, op=op), reads, writes)


def TS(C, eng, o, a, s1, s2, op0, op1, reads, writes):
    if s2 is None:
        C.mk.op(eng, lambda e: e.tensor_scalar(out=o, in0=a, scalar1=s1, scalar2=None, op0=op0), reads, writes)
    else:
        C.mk.op(eng, lambda e: e.tensor_scalar(out=o, in0=a, scalar1=s1, scalar2=s2, op0=op0, op1=op1), reads, writes)


def STT(C, eng, o, a, sc, b, op0, op1, reads, writes):
    C.mk.op(eng, lambda e: e.scalar_tensor_tensor(out=o, in0=a, scalar=sc, in1=b, op0=op0, op1=op1), reads, writes)


def CP(C, eng, o, i, reads, writes):
    if eng == "act":
        C.mk.op(eng, lambda e: e.copy(out=o, in_=i), reads, writes)
    else:
        C.mk.op(eng, lambda e: e.tensor_copy(out=o, in_=i), reads, writes)


def RECIP(C, o, i, reads, writes):
    C.mk.op("dve", lambda e: e.reciprocal(out=o, in_=i), reads, writes)


def MEMSET(C, eng, o, v, writes):
    C.mk.op(eng, lambda e: e.memset(o, v), (), writes)


def LDW(C, dst_ap, src_ap, group, writes, eng="pool", **kw):
    C.mk.dma(eng, dst_ap, src_ap, group, writes=writes, **kw)


def claim(C, tiles):
    C.mk.claim(tiles, C.arena_tiles)
    for t in tiles:
        if t not in C.arena_tiles:
            C.arena_tiles.append(t)


def _load_x(C, s):
    mk = C.mk
    x = C.dr["x"]
    claim(C, C.TST)
    for t in range(NT):
        stg, tst = C.stage[t % 2], C.TST[t % 2]
        mk.dma("sp", stg, x[s, 128 * t:128 * t + 128, :], "stage%d" % (t % 2), writes=[tst])
        c = t // 4
        for half in range(2):
            pb, tp = bank(C, "tr", [0, 1])
            for j in range(4):
                dc = 4 * half + j
                MM(C, pb[:, 128 * j:128 * j + 128], stg[:, 128 * dc:128 * dc + 128], C.identF[:], True, True,
                   [tst, C.identF], [tp])
            CP(C, "dve", C.xT[:, 4 * half:4 * half + 4, 128 * t:128 * t + 128],
               pb[:].rearrange("p (a b) -> p a b", a=4), [tp], [C.TX[c]])


def _rms(C, src, srcT, sq, sqT, nk, rows, gain_ap_fn, dst_fn, dstT, inv_n):
    ACTF(C, sq[0:rows, 0:nk, :], src, AF.Square, [srcT], [sqT])
    pb, tp = bank(C, "rms", [2, 3])
    for k in range(nk):
        MM(C, pb[:], C.onesB[0:rows, :], sq[0:rows, k, :], k == 0, k == nk - 1, [sqT, C.onesB], [tp])
    rt, trt = tmp32(C)
    ACTF(C, rt[:], pb[:], AF.Ln, [tp], [trt], bias=EPS, scale=inv_n)
    ACTF(C, rt[:], rt[:], AF.Exp, [trt], [trt], scale=-0.5)
    for k in range(nk):
        STT(C, "dve", dst_fn(k), src[:, k, :], gain_ap_fn(k), rt[0:rows, :], ALU.mult, ALU.mult,
            [srcT, trt], [dstT])


def _rmsnorm_main(C, g, l):
    claim(C, [C.TSQ])
    for c in range(NCH):
        cs = slice(512 * c, 512 * c + 512)
        _rms(C, C.xT[:, :, cs], C.TX[c], C.sq, C.TSQ, 8, 128, lambda k, g=g, l=l: g[:, l, k:k + 1],
             lambda k, cs=cs: C.hT[:, k, cs], C.TH[c], 1.0 / D)


def _ffn(C, l):
    w_up = C.dr["w_up"]
    w_dn = C.dr["w_down"]
    if not hasattr(C, "ffn_tiles"):
        o = 0
        C.f_at, C.f_up, C.f_dn = [], [], []
        for nm, n, lst in (("fat", 2048, C.f_at), ("fat", 2048, C.f_at), ("fup", 4096, C.f_up), ("fup", 4096, C.f_up),
                           ("fdn", 4096, C.f_dn), ("fdn", 4096, C.f_dn)):
            ap = C.arena[:, o:o + n]
            o += n
            lst.append((ap, T(ap, "%s%d" % (nm, len(lst)))))
        C.ffn_tiles = [t for lst in (C.f_at, C.f_up, C.f_dn) for (_, t) in lst]
        C.fai = 0
    claim(C, C.ffn_tiles)
    pend = None

    def down(p):
        (c_, atv_, tat_, dnv_, tdn_) = p
        cs_ = slice(512 * c_, 512 * c_ + 512)
        for d in range(8):
            pb, tp = bank(C, "dn", [0, 1, 2, 3])
            for kk in range(4):
                MM(C, pb[:], dnv_[:, kk, 128 * d:128 * d + 128], atv_[:, kk, :], kk == 0, kk == 3, [tdn_, tat_], [tp])
            TT(C, "dve", C.xT[:, d, cs_], pb[:], C.xT[:, d, cs_], ALU.add, [tp, C.TX[c_]], [C.TX[c_]])

    for fb in range(8):
        up, tup = C.f_up[fb % 2]
        dn, tdn = C.f_dn[fb % 2]
        upv = up.rearrange("p (k n) -> p k n", k=8)
        dnv = dn.rearrange("p (k n) -> p k n", k=4)
        LDW(C, upv, w_up[l, :, 512 * fb:512 * fb + 512].rearrange("(k p) n -> p k n", p=128), "fup%d" % (fb % 2), [tup])
        LDW(C, dnv, w_dn[l, 512 * fb:512 * fb + 512, :].rearrange("(k p) n -> p k n", p=128), "fdn%d" % (fb % 2), [tdn])
        for c in range(NCH):
            cs = slice(512 * c, 512 * c + 512)
            at, tat = C.f_at[C.fai % 2]
            C.fai += 1
            atv = at.rearrange("p (k n) -> p k n", k=4)
            for fi in range(4):
                pb, tp = bank(C, "up", [4, 5])
                for k in range(8):
                    MM(C, pb[:], upv[:, k, 128 * fi:128 * fi + 128], C.hT[:, k, cs], k == 0, k == 7, [tup, C.TH[c]], [tp])
                r, tr = tmp32(C)
                ACTF(C, r[:], pb[:], AF.Relu, [tp], [tr])
                TT(C, "pool", atv[:, fi, :], r[:], r[:], ALU.mult, [tr], [tat])
            if pend is not None:
                down(pend)
            pend = (c, atv, tat, dnv, tdn)
    down(pend)


def _final(C, s):
    mk = C.mk
    out = C.dr["out"]
    claim(C, [C.TSQ] + C.TST)
    for c in range(NCH):
        cs = slice(512 * c, 512 * c + 512)
        ACTF(C, C.sq[:, :, :], C.xT[:, :, cs], AF.Square, [C.TX[c]], [C.TSQ])
        pb, tp = bank(C, "rms", [2, 3])
        for k in range(8):
            MM(C, pb[:], C.onesB[:, :], C.sq[:, k, :], k == 0, k == 7, [C.TSQ, C.onesB], [tp])
        rt, trt = tmp32(C)
        ACTF(C, rt[:], pb[:], AF.Ln, [tp], [trt], bias=EPS, scale=1.0 / D)
        ACTF(C, rt[:], rt[:], AF.Exp, [trt], [trt], scale=-0.5)
        for k in range(8):
            STT(C, "dve", C.xT[:, k, cs], C.xT[:, k, cs], C.gf[:, k:k + 1], rt[:], ALU.mult, ALU.mult,
                [C.TX[c], trt, C.gf], [C.TX[c]])
        for tl in range(4):
            t = 4 * c + tl
            stg, tst = C.stage[t % 2], C.TST[t % 2]
            for half in range(2):
                pb, tp = bank(C, "tr", [0, 1])
                for j in range(4):
                    dc = 4 * half + j
                    MM(C, pb[:, 128 * j:128 * j + 128], C.xT[:, dc, 128 * t:128 * t + 128], C.identF[:], True, True,
                       [C.TX[c], C.identF], [tp])
                CP(C, "act", stg[:, 512 * half:512 * half + 512], pb[:], [tp], [tst])
            mk.dma("sp", out[s, 128 * t:128 * t + 128, :], stg, "stage%d" % (t % 2), reads=[tst])


PI = float(np.pi)


class WA:
    def __init__(self, C, start=None):
        self.C = C
        self.o = C.WORK0 if start is None else start
        self.tiles = []

    def bf(self, n, name):
        ap = self.C.arena[:, self.o:self.o + n]
        self.o += n
        assert self.o <= 27648, (name, self.o)
        t = T(ap, name)
        self.tiles.append(t)
        return ap, t

    def f32(self, n, name):
        ap, t = self.bf(2 * n, name)
        ap = ap.bitcast(F32)
        t.ap = ap
        return ap, t


def _rope_tables(C, s):
    mk = C.mk
    if not hasattr(C, "CA"):
        C.CA = mk.sb("CA", [128, S], BF16)
        C.SA = mk.sb("SA", [128, S], BF16)
        C.CR = mk.sb("CR", [128, S], BF16)
        C.SR = mk.sb("SR", [128, S], BF16)
        C.TCA, C.TSA, C.TCR, C.TSR = T(C.CA, "CA"), T(C.SA, "SA"), T(C.CR, "CR"), T(C.SR, "SR")
    wa = WA(C, 0)
    posI, tpi = wa.f32(S, "posI")
    posI = posI.bitcast(I32)
    posF, tpf = wa.f32(S, "posF")
    ang, tang = wa.f32(S, "ang")
    tt, ttt = wa.f32(S, "ropetmp")
    claim(C, wa.tiles)
    mk.dma("sp", posI, C.dr["pos"][s, :].partition_broadcast(128), "const", writes=[tpi])
    CP(C, "dve", posF, posI, [tpi], [tpf])
    MAGIC = 12582912.0
    for (ic, isg, ctab, tct, stab, tst_) in ((0, 1, C.CA, C.TCA, C.SA, C.TSA), (2, 3, C.CR, C.TCR, C.SR, C.TSR)):
        for which in (0, 1):
            ACTF(C, ang, posF, AF.Identity, [tpf, C.colsF], [tang], scale=C.colsF[:, ic:ic + 1],
                 bias=(0.25 if which == 0 else 0.0))
            TS(C, "dve", tt, ang, MAGIC, None, ALU.add, None, [tang], [ttt])
            TS(C, "dve", tt, tt, -MAGIC, None, ALU.add, None, [ttt], [ttt])
            TT(C, "dve", tt, ang, tt, ALU.subtract, [tang, ttt], [ttt])
            if which == 0:
                ACTF(C, ctab[:], tt, AF.Sin, [ttt], [tct], scale=2 * PI)
            else:
                ACTF(C, stab[:], tt, AF.Sin, [ttt, C.colsF], [tst_], scale=C.colsF[:, isg:isg + 1])


def attn_chunk(C, c, kt_fn, TK, krows, QT, TQ, v_fn, TV, kind, scale, PT, TPT, st_banks=(4, 5), depth=1):
    if kind == "causal":
        kts = list(range(0, 4 * c + 4))
    else:
        first = 4 * c - 1 if c >= 1 else 0
        kts = [first] + [j for j in range(max(0, 4 * c - 4), 4 * c + 4) if j != first]
    oa, toa = bank(C, "oa", [6, 7])
    pend = []

    def pv(p):
        (j_, pi_, cs_, idx_) = p
        MM(C, oa[:, cs_], v_fn(j_), PT[pi_][:, cs_], idx_ == 0, idx_ == len(kts) - 1, [TV, TPT[pi_]], [toa])

    for idx, j in enumerate(kts):
        d = j - 4 * c
        lo = max(d, 0)
        hi = 4 if kind == "causal" else min(j + 4, 4 * c + 3) - 4 * c + 1
        st, tst = bank(C, "st%d" % len(st_banks), list(st_banks))
        segs = []
        plo, phi = lo, hi
        if d >= 0:
            segs.append((lo, lo + 1, C.triB))
            plo = lo + 1
        if kind == "window" and 0 <= j + 4 - 4 * c <= 3:
            segs.append((hi - 1, hi, C.winbB))
            phi = hi - 1
        if phi > plo:
            segs.append((plo, phi, None))
        for (a_, b_, bias) in segs:
            cs = slice(128 * a_, 128 * b_)
            MM(C, st[:, cs], kt_fn(j), QT[0:krows, cs], True, bias is None, [TK, TQ], [tst])
            if bias is not None:
                MM(C, st[:, cs], C.identB[:, :], bias[:, :], False, True, [C.identB, bias], [tst])
        cs = slice(128 * lo, 128 * hi)
        pi = C.pti % len(PT)
        C.pti += 1
        ACTF(C, PT[pi][:, cs], st[:, cs], AF.Exp, [tst], [TPT[pi]], scale=scale)
        pend.append((j, pi, cs, idx))
        if len(pend) > depth:
            pv(pend.pop(0))
    for p in pend:
        pv(p)
    return oa, toa


def _mix_dst(C, feat0):
    ch = (feat0 // 128) % 4
    po = feat0 % 128
    return ch, po


def _mla_pre(C, l):
    mk = C.mk
    wa = WA(C)
    M = Ctx()
    C.M = M
    M.cqn, M.tcqn = wa.bf(2 * S, "cqn")
    M.cqn = M.cqn.rearrange("p (k t) -> p k t", k=2)
    M.ckvn, M.tckvn = wa.bf(S, "ckvn")
    M.kpeT, M.tkpe = wa.bf(S, "kpeT")
    M.KT, M.tKT = wa.bf(S, "mlaKT")
    M.V, M.tV = wa.bf(S, "mlaV")
    M.V = M.V.rearrange("p (t n) -> p t n", t=16)
    M.QT, M.tQT = wa.bf(512, "mlaQT")
    M.PT, M.tPT = [], []
    for i in range(3):
        a, t = wa.bf(512, "mlaPT%d" % i)
        M.PT.append(a)
        M.tPT.append(t)
    M.sq, M.tsq = wa.bf(1024, "mlasq")
    M.sq = M.sq.rearrange("p (k t) -> p k t", k=2)
    M.wkp, M.twkp = wa.bf(768, "wkpeP")
    M.wkp = M.wkp.rearrange("p (k n) -> p k n", k=8)
    M.wuq, M.twuq = wa.bf(960, "wuq")
    M.wuq = M.wuq.rearrange("p (k n) -> p k n", k=2)
    M.wuqP, M.twuqP = wa.bf(960, "wuqP")
    M.wuqP = M.wuqP.rearrange("p (k n) -> p k n", k=2)
    M.wukv, M.twukv = wa.bf(640, "wukv")
    claim(C, wa.tiles)
    C.pti = 0
    ws, tw, gname = wslot(C)
    wv = ws[:, 0:8 * 416].rearrange("p (k n) -> p k n", k=8)
    LDW(C, wv, C.dr["w_in"][l, :, OFF["cq"]:OFF["cq"] + 416].rearrange("(k p) n -> p k n", p=128), gname, [tw])
    LDW(C, M.wuq, C.dr["mla_w_uq"][l].rearrange("(k p) n -> p k n", p=128), "mlaw", [M.twuq])
    LDW(C, M.wukv, C.dr["mla_w_ukv"][l], "mlaw", [M.twukv])
    CP(C, "pool", M.wkp, wv[:, :, 320:416], [tw], [M.twkp])
    CP(C, "pool", M.wkp[:, :, 64:80], wv[:, :, 400:416], [tw], [M.twkp])
    CP(C, "pool", M.wkp[:, :, 80:96], wv[:, :, 384:400], [tw], [M.twkp])
    CP(C, "pool", M.wuqP, M.wuq, [M.twuq], [M.twuqP])
    wq4 = M.wuq.rearrange("p k (h n) -> p k h n", h=5)
    wq4P = M.wuqP.rearrange("p k (h n) -> p k h n", h=5)
    CP(C, "pool", wq4P[:, :, :, 64:80], wq4[:, :, :, 80:96], [M.twuq], [M.twuqP])
    CP(C, "pool", wq4P[:, :, :, 80:96], wq4[:, :, :, 64:80], [M.twuq], [M.twuqP])
    MEMSET(C, "pool", M.V[:, :, 64:128], 1.0, [M.tV])
    for c in range(NCH):
        cs = slice(512 * c, 512 * c + 512)
        r0, tr0 = tmp32(C)
        r1, tr1 = tmp32(C)
        for m, (r, tr) in enumerate(((r0, tr0), (r1, tr1))):
            pb, tp = bank(C, "proj", [0, 1])
            for k in range(8):
                MM(C, pb[:], wv[:, k, 128 * m:128 * m + 128], C.hT[:, k, cs], k == 0, k == 7, [tw, C.TH[c]], [tp])
            CP(C, "act", r[:], pb[:], [tp], [tr])
            ACTF(C, M.sq[:, m, :], pb[:], AF.Square, [tp], [M.tsq])
        pb, tp = bank(C, "rms", [2, 3])
        for m in range(2):
            MM(C, pb[:], C.onesB[:, :], M.sq[:, m, :], m == 0, m == 1, [M.tsq, C.onesB], [tp])
        rt, trt = tmp32(C)
        ACTF(C, rt[:], pb[:], AF.Ln, [tp], [trt], bias=EPS, scale=1.0 / 256)
        ACTF(C, rt[:], rt[:], AF.Exp, [trt], [trt], scale=-0.5)
        for m, (r, tr) in enumerate(((r0, tr0), (r1, tr1))):
            STT(C, "dve", M.cqn[:, m, cs], r[:], C.gq[:, l, m:m + 1], rt[:], ALU.mult, ALU.mult,
                [tr, trt, C.gq], [M.tcqn])
        pb, tp = bank(C, "proj", [0, 1])
        for k in range(8):
            MM(C, pb[:], wv[:, k, 256:384], C.hT[:, k, cs], k == 0, k == 7, [tw, C.TH[c]], [tp])
        r, tr = tmp32(C)
        CP(C, "act", r[:], pb[:], [tp], [tr])
        ACTF(C, M.sq[:, 0, :], pb[:], AF.Square, [tp], [M.tsq])
        pb2, tp2 = bank(C, "rms", [2, 3])
        MM(C, pb2[:], C.onesB[:, :], M.sq[:, 0, :], True, True, [M.tsq, C.onesB], [tp2])
        rt, trt = tmp32(C)
        ACTF(C, rt[:], pb2[:], AF.Ln, [tp2], [trt], bias=EPS, scale=1.0 / 128)
        ACTF(C, rt[:], rt[:], AF.Exp, [trt], [trt], scale=-0.5)
        STT(C, "dve", M.ckvn[:, cs], r[:], C.gkv[:, l:l + 1], rt[:], ALU.mult, ALU.mult, [tr, trt, C.gkv], [M.tckvn])
        pa, tpa = bank(C, "proj", [0, 1])
        for k in range(8):
            MM(C, pa[0:96, :], wv[:, k, 320:416], C.hT[:, k, cs], k == 0, k == 7, [tw, C.TH[c]], [tpa])
        pbb, tpb = bank(C, "proj", [0, 1])
        for k in range(8):
            MM(C, pbb[0:96, :], M.wkp[:, k, :], C.hT[:, k, cs], k == 0, k == 7, [M.twkp, C.TH[c]], [tpb])
        ra, tra = tmp32(C)
        rb, trb = tmp32(C)
        TT(C, "dve", ra[64:96, :], pa[64:96, :], C.CA[64:96, cs], ALU.mult, [tpa, C.TCA], [tra])
        TT(C, "dve", rb[64:96, :], pbb[64:96, :], C.SA[64:96, cs], ALU.mult, [tpb, C.TSA], [trb])
        TT(C, "dve", M.kpeT[64:96, cs], ra[64:96, :], rb[64:96, :], ALU.add, [tra, trb], [M.tkpe])


def _mla_head(C, l, h):
    M = C.M
    for c in range(NCH):
        cs = slice(512 * c, 512 * c + 512)
        pb, tp = bank(C, "proj", [0, 1])
        MM(C, pb[0:64, :], M.wukv[:, 128 * h:128 * h + 64], M.ckvn[:, cs], True, True, [M.twukv, M.tckvn], [tp])
        CP(C, "act", M.KT[0:64, cs], pb[0:64, :], [tp], [M.tKT])
    CP(C, "pool", M.KT[64:96, :], M.kpeT[64:96, :], [M.tkpe], [M.tKT])
    for t4 in range(4):
        pb, tp = bank(C, "proj", [0, 1])
        for j in range(4):
            t = 4 * t4 + j
            MM(C, pb[:, 64 * j:64 * j + 64], M.ckvn[:, 128 * t:128 * t + 128], M.wukv[:, 128 * h + 64:128 * h + 128],
               True, True, [M.tckvn, M.twukv], [tp])
        CP(C, "dve", M.V[:, 4 * t4:4 * t4 + 4, 0:64], pb[:, 0:256].rearrange("p (a b) -> p a b", a=4), [tp], [M.tV])
    ch, po = _mix_dst(C, 384 + 64 * h)
    for c in range(NCH):
        cs = slice(512 * c, 512 * c + 512)
        pa, tpa = bank(C, "proj", [0, 1])
        for m in range(2):
            MM(C, pa[0:96, :], M.wuq[:, m, 96 * h:96 * h + 96], M.cqn[:, m, cs], m == 0, m == 1,
               [M.twuq, M.tcqn], [tpa])
        pbb, tpb = bank(C, "proj", [0, 1])
        for m in range(2):
            MM(C, pbb[0:96, :], M.wuqP[:, m, 96 * h:96 * h + 96], M.cqn[:, m, cs], m == 0, m == 1,
               [M.twuqP, M.tcqn], [tpb])
        CP(C, "act", M.QT[0:64, :], pa[0:64, :], [tpa], [M.tQT])
        ra, tra = tmp32(C)
        rb, trb = tmp32(C)
        TT(C, "dve", ra[64:96, :], pa[64:96, :], C.CA[64:96, cs], ALU.mult, [tpa, C.TCA], [tra])
        TT(C, "dve", rb[64:96, :], pbb[64:96, :], C.SA[64:96, cs], ALU.mult, [tpb, C.TSA], [trb])
        TT(C, "dve", M.QT[64:96, :], ra[64:96, :], rb[64:96, :], ALU.add, [tra, trb], [M.tQT])
        oa, toa = attn_chunk(C, c, lambda j: M.KT[0:96, 128 * j:128 * j + 128], M.tKT, 96, M.QT, M.tQT,
                             lambda j: M.V[:, j, :], M.tV, "causal", 96 ** -0.5, M.PT, M.tPT,
                             st_banks=(2, 3, 4, 5), depth=2)
        rc, trc = tmp32(C)
        ACTF(C, rc[0:64, :], oa[64:128, :], AF.Ln, [toa], [trc], bias=1e-30, scale=1.0)
        ACTF(C, rc[0:64, :], rc[0:64, :], AF.Exp, [trc], [trc], scale=-1.0)
        TT(C, "dve", C.mixT[po:po + 64, ch, cs], oa[0:64, :], rc[0:64, :], ALU.mult, [toa, trc], [C.TMIX[ch]])


def _wout(C, l, phase):
    w_out = C.dr["w_out"]
    for dq in range(4):
        ws, tw, gname = wslot(C)
        wv = ws[:, 0:1024].rearrange("p (k n) -> p k n", k=4)
        LDW(C, wv, w_out[l, 512 * phase:512 * phase + 512, 256 * dq:256 * dq + 256].rearrange(
            "(k p) n -> p k n", p=128), gname, [tw])
        for dd in range(2):
            d = 2 * dq + dd
            for c in range(NCH):
                cs = slice(512 * c, 512 * c + 512)
                pb, tp = bank(C, "proj", [0, 1])
                for k in range(4):
                    MM(C, pb[:], wv[:, k, 128 * dd:128 * dd + 128], C.mixT[:, k, cs], k == 0, k == 3,
                       [tw, C.TMIX[k]], [tp])
                TT(C, "dve", C.xT[:, d, cs], pb[:], C.xT[:, d, cs], ALU.add, [tp, C.TX[c]], [C.TX[c]])


def _tok0_prepass(C, n_seq, n_layers):
    mk = C.mk
    AR = C.arena[:].bitcast(F32)
    st = {"o": 0}
    tiles = []

    def alloc(n, name):
        ap = AR[:, st["o"]:st["o"] + n]
        st["o"] += n
        t = T(ap, name)
        tiles.append(t)
        return ap, t
    slot, tslot = [], []
    for i in range(2):
        a_, t_ = alloc(4096, "pslot%d" % i)
        slot.append(a_)
        tslot.append(t_)
    sel, tsel = alloc(1152, "sel18")
    X0, tX = alloc(16, "X0")
    X0 = X0.rearrange("p (k s) -> p k s", s=2)
    Hh, tH = alloc(16, "H0")
    Hh = Hh.rearrange("p (k s) -> p k s", s=2)
    SQ, tSQ = alloc(64, "SQ0")
    SQ = SQ.rearrange("p (k s) -> p k s", s=2)
    Aa, tA = alloc(64, "A0")
    Aa = Aa.rearrange("p (k s) -> p k s", s=2)
    PRn, tPRn = alloc(8, "PRn")
    PRn = PRn.rearrange("p (k s) -> p k s", s=2)
    SG, tSG = alloc(2, "SG")
    CK, tCK = alloc(2, "CK")
    CKN, tCKN = alloc(2, "CKN")
    R, tR = alloc(40, "R0")
    R = R.rearrange("p (k s) -> p k s", s=2)
    O, tO = alloc(10, "O0")
    O = O.rearrange("p (k s) -> p k s", s=2)
    CEN, tCEN = alloc(10, "CEN")
    CEN = CEN.rearrange("p (k s) -> p k s", s=2)
    RS_, tRS = alloc(10, "RSTD")
    RS_ = RS_.rearrange("p (k s) -> p k s", s=2)
    SGG, tSGG = alloc(10, "SGG")
    SGG = SGG.rearrange("p (k s) -> p k s", s=2)
    MIX, tMIX = alloc(32, "MIX0")
    MIX = MIX.rearrange("p (k s) -> p k s", s=2)
    sm, tsm = alloc(4, "psmall")
    tmp, ttmp = alloc(24, "ptmp")
    tmp3 = tmp.rearrange("p (k s) -> p k s", s=2)
    claim(C, tiles)
    C.wpi = 0
    dr = C.dr

    def wload(src_ap, view_fn):
        i = C.wpi % 2
        C.wpi += 1
        v = view_fn(slot[i])
        mk.dma("sp", v, src_ap, "pslot%d" % i, writes=[tslot[i]])
        return v, tslot[i]

    def mvg(lhs_fn, rhs_fn, nk, out, tout, reads):
        for k in range(nk):
            MM(C, out, lhs_fn(k), rhs_fn(k), k == 0, k == nk - 1, reads, [tout])

    def rms0(X, tXx, nk, rows, gain_fn, out, tout, n):
        TT(C, "dve", SQ[0:rows, 0:nk, :], X[0:rows, 0:nk, :], X[0:rows, 0:nk, :], ALU.mult, [tXx], [tSQ])
        mvg(lambda k: C.onesF[0:rows, 0:rows], lambda k: SQ[0:rows, k, :], nk, C.pb[7][0:rows, 0:2], C.TP[7],
            [C.onesF, tSQ])
        ACTF(C, sm[0:rows, 0:2], C.pb[7][0:rows, 0:2], AF.Sqrt, [C.TP[7]], [tsm], bias=EPS, scale=1.0 / n)
        RECIP(C, sm[0:rows, 0:2], sm[0:rows, 0:2], [tsm], [tsm])
        for k in range(nk):
            STT(C, "dve", out[0:rows, k, :], X[0:rows, k, :], gain_fn(k), sm[0:rows, 0:2], ALU.mult, ALU.mult,
                [tXx, tsm], [tout])

    MEMSET(C, "pool", AR[:, 8192 + 1152:st["o"]], 0.0, tiles[3:])
    o_, w_ = BLOB_OFF["sel18"]
    mk.dma("sp", sel, dr["blob"][:, o_:o_ + w_], "pconst", writes=[tsel])
    for s_ in range(n_seq):
        mk.dma("sp", X0[:, :, s_], dr["x"][s_, 0, :].rearrange("(c p) -> p c", p=128), "pconst", writes=[tX],
               allow_slow_non_contiguous=True)
    ps1, tp1 = C.pb[0], C.TP[0]
    ps2, tp2 = C.pb[1], C.TP[1]
    ps3, tp3 = C.pb[2], C.TP[2]
    psS, tpS = C.pb[3], C.TP[3]
    psD, tpD = C.pb[4], C.TP[4]
    psU, tpU = C.pb[5], C.TP[5]
    psV, tpV = C.pb[6], C.TP[6]
    win = dr["w_in"]
    for l in range(n_layers):
        rms0(X0, tX, 8, 128, lambda k: C.g1[:, l, k:k + 1], Hh, tH, float(D))
        hk = lambda k: Hh[:, k, :]
        last = (l == n_layers - 1)
        if not last:
            wv, tw = wload(win[l, :, OFF["vs"]:OFF["vs"] + 128].rearrange("(k p) n -> p k n", p=128),
                           lambda sl: sl[:, 0:1024].rearrange("p (k n) -> p k n", k=8))
            for g in range(2):
                mvg(lambda k, g=g: wv[:, k, 64 * g:64 * g + 64], hk, 8, ps1[0:64, 2 * g:2 * g + 2], tp1, [tw, tH])
            wv, tw = wload(win[l, :, OFF["vw"]:OFF["vw"] + 146].rearrange("(k p) n -> p k n", p=128),
                           lambda sl: sl[:, 0:8 * 146].rearrange("p (k n) -> p k n", k=8))
            for g in range(2):
                mvg(lambda k, g=g: wv[:, k, 64 * g:64 * g + 64], hk, 8, ps1[0:64, 4 + 2 * g:6 + 2 * g], tp1, [tw, tH])
            mvg(lambda k: wv[:, k, 128:146], hk, 8, ps1[0:18, 8:10], tp1, [tw, tH])
            CP(C, "dve", PRn[0:64, :, :], ps1[0:64, 0:8].rearrange("p (k s) -> p k s", s=2), [tp1], [tPRn])
            CP(C, "dve", SG[0:18, :], ps1[0:18, 8:10], [tp1], [tSG])
            ACTF(C, SG[0:18, :], SG[0:18, :], AF.Exp, [tSG], [tSG], scale=-1.0)
            TS(C, "dve", SG[0:18, :], SG[0:18, :], 1.0, None, ALU.add, None, [tSG], [tSG])
            RECIP(C, SG[0:18, :], SG[0:18, :], [tSG], [tSG])
            for h in range(6):
                for b in (1, 2):
                    j = 3 * h + b
                    ci = 2 * (2 * h + b - 1)
                    MM(C, ps2[0:64, ci:ci + 2], sel[0:18, 64 * j:64 * j + 64], SG[0:18, :], True, True, [tsel, tSG], [tp2])
            for h in range(6):
                g = h // 3
                c1 = 2 * (2 * h)
                TT(C, "dve", tmp3[0:64, 0, :], ps2[0:64, c1:c1 + 2], PRn[0:64, g, :], ALU.mult, [tp2, tPRn], [ttmp])
                TT(C, "dve", tmp3[0:64, 1, :], ps2[0:64, c1 + 2:c1 + 4], PRn[0:64, 2 + g, :], ALU.mult, [tp2, tPRn], [ttmp])
                TT(C, "dve", MIX[0:64, h, :], tmp3[0:64, 0, :], tmp3[0:64, 1, :], ALU.add, [ttmp], [tMIX])
            wv, tw = wload(win[l, :, OFF["ckv"]:OFF["ckv"] + 128].rearrange("(k p) n -> p k n", p=128),
                           lambda sl: sl[:, 0:1024].rearrange("p (k n) -> p k n", k=8))
            mvg(lambda k: wv[:, k, :], hk, 8, ps1[:, 10:12], tp1, [tw, tH])
            CP(C, "dve", CK[:, :], ps1[:, 10:12], [tp1], [tCK])
            CK3 = CK.rearrange("p (k s) -> p k s", s=2)
            CKN3 = CKN.rearrange("p (k s) -> p k s", s=2)
            rms0(CK3, tCK, 1, 128, lambda k: C.gkv[:, l:l + 1], CKN3, tCKN, 128.0)
            wv, tw = wload(dr["mla_w_ukv"][l], lambda sl: sl[:, 0:640])
            for h in range(5):
                MM(C, psV[0:64, 2 * h:2 * h + 2], wv[:, 128 * h + 64:128 * h + 128], CKN[:, 0:2], True, True, [tw, tCKN], [tpV])
            CP(C, "dve", MIX[0:64, 6:11, :], psV[0:64, 0:10].rearrange("p (k s) -> p k s", s=2), [tpV], [tMIX])
        for piece in range(3):
            n = 512 if piece < 2 else 256
            c0 = OFF["rq"] + 512 * piece
            wv, tw = wload(win[l, :, c0:c0 + n].rearrange("(k p) n -> p k n", p=128),
                           lambda sl, n=n: sl[:, 0:8 * n].rearrange("p (k n) -> p k n", k=8))
            for gi in range(8 * piece, 8 * piece + n // 64):
                off = 64 * gi - 512 * piece
                mvg(lambda k, off=off: wv[:, k, off:off + 64], hk, 8, ps3[0:64, 2 * gi:2 * gi + 2], tp3, [tw, tH])
        CP(C, "dve", R[0:64, :, :], ps3[0:64, 0:40].rearrange("p (k s) -> p k s", s=2), [tp3], [tR])
        TT(C, "dve", O[0:64, :, :], R[0:64, 0:5, :], R[0:64, 5:10, :], ALU.mult, [tR], [tO])
        MM(C, psS[0:64, 0:10], C.onesF[0:64, 0:64], O[0:64, :, :].rearrange("p k s -> p (k s)"), True, True,
           [C.onesF, tO], [tpS])
        CP(C, "dve", C.s00[0:64, 10 * l:10 * l + 10], psS[0:64, 0:10], [tpS], [C.Ts00])
        if last:
            continue
        TT(C, "dve", O[0:64, :, :], psS[0:64, 0:10].rearrange("p (k s) -> p k s", s=2), R[0:64, 10:15, :], ALU.mult,
           [tpS, tR], [tO])
        TS(C, "dve", O[0:64, :, :], O[0:64, :, :], 0.125, None, ALU.mult, None, [tO], [tO])
        MM(C, psS[0:64, 10:20], C.onesF[0:64, 0:64], O[0:64, :, :].rearrange("p k s -> p (k s)"), True, True,
           [C.onesF, tO], [tpS])
        STT(C, "dve", CEN[0:64, :, :], psS[0:64, 10:20].rearrange("p (k s) -> p k s", s=2), -1.0 / 64, O[0:64, :, :],
            ALU.mult, ALU.add, [tpS, tO], [tCEN])
        TT(C, "dve", O[0:64, :, :], CEN[0:64, :, :], CEN[0:64, :, :], ALU.mult, [tCEN], [tO])
        MM(C, C.pb[7][0:64, 2:12], C.onesF[0:64, 0:64], O[0:64, :, :].rearrange("p k s -> p (k s)"), True, True,
           [C.onesF, tO], [C.TP[7]])
        ACTF(C, RS_[0:64, :, :], C.pb[7][0:64, 2:12].rearrange("p (k s) -> p k s", s=2), AF.Sqrt, [C.TP[7]], [tRS],
             bias=EPS, scale=1.0 / 64)
        RECIP(C, RS_[0:64, :, :], RS_[0:64, :, :], [tRS], [tRS])
        ACTF(C, SGG[0:64, :, :], R[0:64, 15:20, :], AF.Exp, [tR], [tSGG], scale=-1.0)
        TS(C, "dve", SGG[0:64, :, :], SGG[0:64, :, :], 1.0, None, ALU.add, None, [tSGG], [tSGG])
        RECIP(C, SGG[0:64, :, :], SGG[0:64, :, :], [tSGG], [tSGG])
        TT(C, "dve", SGG[0:64, :, :], SGG[0:64, :, :], R[0:64, 15:20, :], ALU.mult, [tSGG, tR], [tSGG])
        for h in range(5):
            STT(C, "dve", CEN[0:64, h, :], CEN[0:64, h, :], C.gnT[:, 5 * l + h:5 * l + h + 1], RS_[0:64, h, :],
                ALU.mult, ALU.mult, [tCEN, tRS, C.gnT], [tCEN])
        TT(C, "dve", MIX[0:64, 11:16, :], CEN[0:64, :, :], SGG[0:64, :, :], ALU.mult, [tCEN, tSGG], [tMIX])
        for q4 in range(4):
            wv, tw = wload(dr["w_out"][l, :, 256 * q4:256 * q4 + 256].rearrange("(j p) n -> p j n", p=64),
                           lambda sl: sl[0:64, :].rearrange("p (j n) -> p j n", j=16))
            for oo in range(2):
                oc = 2 * q4 + oo
                mvg(lambda j, oo=oo: wv[:, j, 128 * oo:128 * oo + 128], lambda j: MIX[0:64, j, :], 16,
                    psD[:, 2 * oc:2 * oc + 2], tpD, [tw, tMIX])
        TT(C, "dve", X0, X0, psD[:, 0:16].rearrange("p (k s) -> p k s", s=2), ALU.add, [tpD, tX], [tX])
        rms0(X0, tX, 8, 128, lambda k: C.g2[:, l, k:k + 1], Hh, tH, float(D))
        for fb in range(8):
            wv, tw = wload(dr["w_up"][l, :, 512 * fb:512 * fb + 512].rearrange("(k p) n -> p k n", p=128),
                           lambda sl: sl[:, :].rearrange("p (k n) -> p k n", k=8))
            for fi in range(4):
                f = 4 * fb + fi
                mvg(lambda k, fi=fi: wv[:, k, 128 * fi:128 * fi + 128], hk, 8, psU[:, 2 * f:2 * f + 2], tpU, [tw, tH])
        ACTF(C, Aa, psU[:, 0:64].rearrange("p (k s) -> p k s", s=2), AF.Relu, [tpU], [tA])
        TT(C, "dve", Aa, Aa, Aa, ALU.mult, [tA], [tA])
        MEMSET(C, "dve", psD[:, 16:32], 0.0, [tpD])
        for fs in range(8):
            wv, tw = wload(dr["w_down"][l, 512 * fs:512 * fs + 512, :].rearrange("(k p) n -> p k n", p=128),
                           lambda sl: sl[:, :].rearrange("p (k n) -> p k n", k=4))
            for oc in range(8):
                for kk in range(4):
                    MM(C, psD[:, 16 + 2 * oc:18 + 2 * oc], wv[:, kk, 128 * oc:128 * oc + 128], Aa[:, 4 * fs + kk, :],
                       False, False, [tw, tA], [tpD])
        TT(C, "dve", X0, X0, psD[:, 16:32].rearrange("p (k s) -> p k s", s=2), ALU.add, [tpD, tX], [tX])


def _tok0_h(C, l):
    h0, th = C.h0, C.Th0
    x0 = C.xT[:, :, 0]
    ACTF(C, h0[:, 11:19], x0, AF.Square, [C.TX[0]], [th], accum_out=h0[:, 9:10])
    pb, tp = bank(C, "rms", [2, 3])
    MM(C, pb[:, 0:2], C.onesF[:, :], h0[:, 9:11], True, True, [C.onesF, th], [tp])
    ACTF(C, h0[:, 19:20], pb[:, 0:1], AF.Sqrt, [tp], [th], bias=EPS, scale=1.0 / D)
    RECIP(C, h0[:, 19:20], h0[:, 19:20], [th], [th])
    STT(C, "dve", h0[:, 0:8], x0, h0[:, 19:20], C.g1[:, l, :], ALU.mult, ALU.mult, [C.TX[0], th, C.g1], [th])


def _mixer(C, l, mixers):
    claim(C, C.TMIX)
    if "nsa" in mixers:
        _nsa(C, l, 0)
        _nsa(C, l, 1)
    else:
        for ch in range(3):
            MEMSET(C, "pool", C.mixT[:, ch, :], 0.0, [C.TMIX[ch]])
    if "mla" in mixers:
        _mla_pre(C, l)
        _mla_head(C, l, 0)
        _mla_head(C, l, 1)
    else:
        MEMSET(C, "pool", C.mixT[:, 3, :], 0.0, [C.TMIX[3]])
    _wout(C, l, 0)
    if "mla" in mixers:
        for h in (2, 3, 4):
            _mla_head(C, l, h)
    else:
        MEMSET(C, "pool", C.mixT[:, 0, :], 0.0, [C.TMIX[0]])
        MEMSET(C, "pool", C.mixT[0:64, 1, :], 0.0, [C.TMIX[1]])
    if "ret" in mixers:
        _ret(C, l)
    else:
        MEMSET(C, "pool", C.mixT[64:128, 1, :], 0.0, [C.TMIX[1]])
        for ch in (2, 3):
            MEMSET(C, "pool", C.mixT[:, ch, :], 0.0, [C.TMIX[ch]])
    _wout(C, l, 1)


def proj_rope(C, c, wa_fn, wb_fn, M, wts, dst, tdst, tabC, tCt, tabS, tSt, nr=16):
    cs = slice(512 * c, 512 * c + 512)
    pa, tpa = bank(C, "proj", [0, 1])
    for k in range(8):
        MM(C, pa[0:M, :], wa_fn(k), C.hT[:, k, cs], k == 0, k == 7, wts + [C.TH[c]], [tpa])
    pb, tpb = bank(C, "proj", [0, 1])
    for k in range(8):
        MM(C, pb[0:nr, :], wb_fn(k), C.hT[:, k, cs], k == 0, k == 7, wts + [C.TH[c]], [tpb])
    CP(C, "dve", dst[0:M, :], pa[0:M, :], [tpa], [tdst])
    ra, tra = tmp32(C)
    rb, trb = tmp32(C)
    TT(C, "dve", ra[0:nr, :], pa[0:nr, :], tabC[0:nr, cs], ALU.mult, [tpa, tCt], [tra])
    TT(C, "dve", rb[0:nr, :], pb[0:nr, :], tabS[0:nr, cs], ALU.mult, [tpb, tSt], [trb])
    TT(C, "pool", dst[0:nr, :], ra[0:nr, :], rb[0:nr, :], ALU.add, [tra, trb], [tdst])


def _nsa(C, l, g):
    mk = C.mk
    win = C.dr["w_in"]
    wa = WA(C)
    KVc, tKVc = wa.bf(S, "KVc")
    KTs, tKTs = wa.bf(S, "KTs")
    KTw, tKTw = wa.bf(S, "KTw")
    Vs, tVs = wa.bf(S, "Vs")
    Vs = Vs.rearrange("p (t n) -> p t n", t=16)
    Vw, tVw = wa.bf(S, "Vw")
    Vw = Vw.rearrange("p (t n) -> p t n", t=16)
    QT, tQT = [], []
    for r in range(3):
        a_, t_ = wa.bf(512, "nQT%d" % r)
        QT.append(a_)
        tQT.append(t_)
    PT, tPT = [], []
    for i in range(2):
        a_, t_ = wa.bf(512, "nPT%d" % i)
        PT.append(a_)
        tPT.append(t_)
    acc, tacc = [], []
    for r in range(3):
        a_, t_ = wa.f32(512, "nacc%d" % r)
        acc.append(a_)
        tacc.append(t_)
    kcT, tkcT = wa.bf(128, "kcT")
    Vc, tVc = wa.bf(128, "Vcaug")
    hid, thid = wa.bf(128, "hid")
    Wn2, tWn2 = wa.bf(640, "Wn2")
    Wn2 = Wn2.rearrange("p (k n) -> p k n", k=8)
    WP, tWP = wa.bf(768, "nWP")
    WP = WP.rearrange("p (k n) -> p k n", k=8)
    glog, tglog = wa.bf(512, "glog")
    selb, tselb = wa.bf(128, "selb")
    w2, tw2 = wa.bf(128, "w2kv")
    posT, tposT = wa.bf(64, "posT")
    zt, tzt = wa.f32(128, "gz")
    ut, tut = wa.f32(128, "gu")
    sm, tsm = wa.f32(128, "nsmall")
    bsb, tbsb = wa.f32(4, "nbias")
    claim(C, wa.tiles)
    C.pti = 0
    ws, tw, gname = wslot(C)
    Wn = ws[:].rearrange("p (k n) -> p k n", k=8)
    cols = [(OFF["q"] + 192 * g, 192, 0), (OFF["kc"] + 64 * g, 64, 192), (OFF["vc"] + 64 * g, 64, 256),
            (OFF["ks"] + 64 * g, 64, 320), (OFF["kw"] + 64 * g, 64, 384), (OFF["vs"] + 64 * g, 64, 448)]
    for (src, n, dst) in cols:
        LDW(C, Wn[:, :, dst:dst + n], win[l, :, src:src + n].rearrange("(k p) n -> p k n", p=128), gname, [tw])
    LDW(C, Wn2[:, :, 0:64], win[l, :, OFF["vw"] + 64 * g:OFF["vw"] + 64 * g + 64].rearrange("(k p) n -> p k n", p=128),
        "nsaw", [tWn2])
    LDW(C, Wn2[:, :, 64:73], win[l, :, OFF["gate"] + 9 * g:OFF["gate"] + 9 * g + 9].rearrange(
        "(k p) n -> p k n", p=128), "nsaw", [tWn2], allow_slow_non_contiguous=True)
    ws1, tw1, gname1 = wslot(C)
    w1 = ws1[:].rearrange("p (l h) -> p l h", l=32)
    LDW(C, w1[0:64], C.dr["cmp_w1_k"][l].rearrange("(l d) h -> d l h", d=64), gname1, [tw1])
    LDW(C, w1[64:128], C.dr["cmp_w1_v"][l].rearrange("(l d) h -> d l h", d=64), gname1, [tw1])
    LDW(C, w2[:, 0:64], C.dr["cmp_w2_k"][l], "nsaw", [tw2])
    LDW(C, w2[:, 64:128], C.dr["cmp_w2_v"][l], "nsaw", [tw2])
    MEMSET(C, "pool", posT, 0.0, [tposT])
    LDW(C, posT[0:64, 0:32], C.dr["cmp_pos_k"][l].rearrange("l d -> d l"), "nsaw", [tposT],
        allow_slow_non_contiguous=True)
    LDW(C, posT[64:128, 0:32], C.dr["cmp_pos_v"][l].rearrange("l d -> d l"), "nsaw", [tposT],
        allow_slow_non_contiguous=True)
    o, w_ = BLOB_OFF["E"]
    LDW(C, KTs[64:96, :], C.dr["blob"][64:96, o:o + w_], "nsaw", [tKTs])
    for i, off in enumerate((0, 64, 128, 192, 320, 384)):
        CP(C, "pool", WP[:, :, 16 * i:16 * i + 8], Wn[:, :, off + 8:off + 16], [tw], [tWP])
        CP(C, "pool", WP[:, :, 16 * i + 8:16 * i + 16], Wn[:, :, off:off + 8], [tw], [tWP])
    MEMSET(C, "pool", Vs[:, :, 64:128], 1.0, [tVs])
    MEMSET(C, "pool", Vw[:, :, 64:128], 1.0, [tVw])
    MEMSET(C, "pool", Vc[:, 64:128], 1.0, [tVc])
    MEMSET(C, "pool", Vc[:, 0:64], 0.0, [tVc])
    MEMSET(C, "pool", selb, 0.0, [tselb])
    for c in range(NCH):
        cs = slice(512 * c, 512 * c + 512)
        proj_rope(C, c, lambda k: Wn[:, k, 192:320], lambda k: WP[:, k, 48:64], 128, [tw, tWP], KVc[:, cs], tKVc,
                  C.CA, C.TCA, C.SA, C.TSA)
        proj_rope(C, c, lambda k: Wn[:, k, 320:384], lambda k: WP[:, k, 64:80], 64, [tw, tWP], KTs[:, cs], tKTs,
                  C.CA, C.TCA, C.SA, C.TSA)
        proj_rope(C, c, lambda k: Wn[:, k, 384:448], lambda k: WP[:, k, 80:96], 64, [tw, tWP], KTw[:, cs], tKTw,
                  C.CA, C.TCA, C.SA, C.TSA)
        pv, tpv = bank(C, "proj", [0, 1])
        for tl in range(4):
            t = 4 * c + tl
            for k in range(8):
                MM(C, pv[:, 128 * tl:128 * tl + 64], C.hT[:, k, 128 * t:128 * t + 128], Wn[:, k, 448:512], k == 0, k == 7,
                   [tw, C.TH[c]], [tpv])
            for k in range(8):
                MM(C, pv[:, 128 * tl + 64:128 * tl + 128], C.hT[:, k, 128 * t:128 * t + 128], Wn2[:, k, 0:64], k == 0,
                   k == 7, [tWn2, C.TH[c]], [tpv])
        pv3 = pv[:].rearrange("p (a b) -> p a b", a=4)
        CP(C, "act", Vs[:, 4 * c:4 * c + 4, 0:64], pv3[:, :, 0:64], [tpv], [tVs])
        CP(C, "act", Vw[:, 4 * c:4 * c + 4, 0:64], pv3[:, :, 64:128], [tpv], [tVw])
    for kind in (0, 1):
        r0 = 64 * kind
        pbias, tpbias = bank(C, "oa", [6, 7])
        for li in range(32):
            MM(C, pbias[:, 0:2], w1[r0:r0 + 64, li, :], posT[r0:r0 + 64, li:li + 2], li == 0, li == 31,
               [tw1, tposT], [tpbias])
        CP(C, "dve", bsb[:, 0:2], pbias[:, 0:2], [tpbias], [tbsb])
        ph, tph = bank(C, "st", [4, 5])
        for li in range(32):
            MM(C, ph[:, 0:127], w1[r0:r0 + 64, li, :], KVc[r0:r0 + 64, li:li + 2017:16], li == 0, li == 31,
               [tw1, tKVc], [tph])
        ACTF(C, zt[:, 0:127], ph[:, 0:127], AF.Identity, [tph, tbsb], [tzt], bias=bsb[:, 0:1], scale=1.0)
        TT(C, "pool", ut[:, 0:127], zt[:, 0:127], zt[:, 0:127], ALU.mult, [tzt], [tut])
        TS(C, "pool", ut[:, 0:127], ut[:, 0:127], 0.044715, None, ALU.mult, None, [tut], [tut])
        TS(C, "pool", ut[:, 0:127], ut[:, 0:127], 1.0, None, ALU.add, None, [tut], [tut])
        TT(C, "pool", ut[:, 0:127], ut[:, 0:127], zt[:, 0:127], ALU.mult, [tut, tzt], [tut])
        ACTF(C, ut[:, 0:127], ut[:, 0:127], AF.Exp, [tut], [tut], scale=-1.5957691216057308)
        TS(C, "pool", ut[:, 0:127], ut[:, 0:127], 1.0, None, ALU.add, None, [tut], [tut])
        RECIP(C, ut[:, 0:127], ut[:, 0:127], [tut], [tut])
        TT(C, "dve", hid[:, 0:127], zt[:, 0:127], ut[:, 0:127], ALU.mult, [tzt, tut], [thid])
        pk, tpk = bank(C, "oa", [6, 7])
        if kind == 0:
            MM(C, pk[0:64, 0:127], w2[:, 0:64], hid[:, 0:127], True, True, [tw2, thid], [tpk])
            CP(C, "act", kcT[0:64, 0:127], pk[0:64, 0:127], [tpk], [tkcT])
        else:
            MM(C, pk[0:127, 0:64], hid[:, 0:127], w2[:, 64:128], True, True, [tw2, thid], [tpk])
            CP(C, "act", Vc[0:127, 0:64], pk[0:127, 0:64], [tpk], [tVc])

    def combine(oa, toa, r, b, c, mode):
        cs = slice(512 * c, 512 * c + 512)
        rc, trc = tmp32(C)
        ACTF(C, rc[0:64, :], oa[64:128, :], AF.Ln, [toa], [trc], bias=1e-30, scale=1.0)
        j = 3 * r + b
        pg2, tpg2 = bank(C, "g", [2])
        MM(C, pg2[0:64, :], C.sel9B[0:9, 64 * j:64 * j + 64], glog[0:9, :], True, True, [C.sel9B, tglog], [tpg2])
        gs, tgs = tmp32(C)
        ACTF(C, gs[0:64, :], pg2[0:64, :], AF.Exp, [tpg2], [tgs], scale=-1.0)
        ACTF(C, gs[0:64, :], gs[0:64, :], AF.Ln, [tgs], [tgs], bias=1.0, scale=1.0)
        TT(C, "pool", rc[0:64, :], rc[0:64, :], gs[0:64, :], ALU.add, [trc, tgs], [trc])
        ACTF(C, rc[0:64, :], rc[0:64, :], AF.Exp, [trc], [trc], scale=-1.0)
        if mode == 0:
            TT(C, "dve", acc[r][0:64, :], oa[0:64, :], rc[0:64, :], ALU.mult, [toa, trc], [tacc[r]])
        elif mode == 1:
            TT(C, "dve", gs[0:64, :], oa[0:64, :], rc[0:64, :], ALU.mult, [toa, trc, tgs], [tgs])
            TT(C, "pool", acc[r][0:64, :], acc[r][0:64, :], gs[0:64, :], ALU.add, [tacc[r], tgs], [tacc[r]])
        else:
            ch, po = _mix_dst(C, 64 * (3 * g + r))
            TT(C, "dve", gs[0:64, :], oa[0:64, :], rc[0:64, :], ALU.mult, [toa, trc, tgs], [tgs])
            TT(C, "dve", C.mixT[po:po + 64, ch, cs], acc[r][0:64, :], gs[0:64, :], ALU.add, [tacc[r], tgs],
               [C.TMIX[ch]])

    for c in range(NCH):
        cs = slice(512 * c, 512 * c + 512)
        for r in range(3):
            proj_rope(C, c, lambda k, r=r: Wn[:, k, 64 * r:64 * r + 64], lambda k, r=r: WP[:, k, 16 * r:16 * r + 16], 64,
                      [tw, tWP], QT[r], tQT[r], C.CA, C.TCA, C.SA, C.TSA)
            if c < 2:
                MEMSET(C, "pool", QT[r][64:96, :], 0.0, [tQT[r]])
        pg, tpg = bank(C, "g", [2])
        for k in range(8):
            MM(C, pg[0:9, :], Wn2[:, k, 64:73], C.hT[:, k, cs], k == 0, k == 7, [tWn2, C.TH[c]], [tpg])
        CP(C, "act", glog[0:9, :], pg[0:9, :], [tpg], [tglog])
        impb, timp = C.pb[3], C.TP[3]
        for r in range(3):
            st, tst = bank(C, "st", [4, 5])
            MM(C, st[0:127, :], kcT[0:64, 0:127], QT[r][0:64, :], True, False, [tkcT, tQT[r]], [tst])
            MM(C, st[0:127, :], C.identB[0:127, 0:127], C.cmpbB[0:127, cs], False, True, [C.identB, C.cmpbB], [tst])
            pi = C.pti % 2
            C.pti += 1
            ACTF(C, PT[pi][0:127, :], st[0:127, :], AF.Exp, [tst], [tPT[pi]], scale=0.125)
            oa, toa = bank(C, "oa", [6, 7])
            MM(C, oa[:, :], Vc[0:127, :], PT[pi][0:127, :], True, True, [tVc, tPT[pi]], [toa])
            if c >= 2:
                for tl in range(4):
                    o0 = (tl * 3 + r) * 33
                    MM(C, impb[:, o0:o0 + 33], PT[pi][0:127, 128 * tl:128 * tl + 128], C.ovl1B[0:127, 0:33], True, True,
                       [tPT[pi], C.ovl1B], [timp])
            combine(oa, toa, r, 0, c, 0)
        if c >= 2:
            for tl in range(4):
                kq = 4 * c + tl - 8
                bb = sm[:, 0:32]
                t2 = sm[:, 32:64]
                m1 = sm[:, 64:72]
                m2 = sm[:, 72:80]
                dn = sm[:, 80:83]
                keep = C.keepF[:, 32 * kq:32 * kq + 32]
                addt = C.addtF[:, 32 * kq:32 * kq + 32]
                TS(C, "dve", dn, impb[:, tl * 99 + 32:tl * 99 + 99:33], 1e-30, None, ALU.max, None, [timp], [tsm])
                RECIP(C, dn, dn, [tsm], [tsm])
                for r in range(3):
                    o0 = (tl * 3 + r) * 33
                    if r == 0:
                        STT(C, "dve", bb, impb[:, o0:o0 + 32], dn[:, 0:1], keep, ALU.mult, ALU.mult,
                            [timp, tsm, C.keepF], [tsm])
                    else:
                        STT(C, "dve", t2, impb[:, o0:o0 + 32], dn[:, r:r + 1], keep, ALU.mult, ALU.mult,
                            [timp, tsm, C.keepF], [tsm])
                        TT(C, "dve", bb, bb, t2, ALU.add, [tsm], [tsm])
                TT(C, "dve", bb, bb, addt, ALU.add, [tsm, C.addtF], [tsm])
                mk.op("dve", lambda e, m1=m1, bb=bb: e.max(out=m1, in_=bb), [tsm], [tsm])
                mk.op("dve", lambda e, m1=m1, bb=bb, t2=t2: e.match_replace(out=t2, in_to_replace=m1, in_values=bb,
                                                                        imm_value=-3.0e38), [tsm], [tsm])
                mk.op("dve", lambda e, m2=m2, t2=t2: e.max(out=m2, in_=t2), [tsm], [tsm])
                STT(C, "dve", t2, bb, m2[:, 7:8], C.bigF[:, 0:32], ALU.subtract, ALU.mult, [tsm, C.bigF], [tsm])
                TS(C, "dve", t2, t2, 0.0, None, ALU.min, None, [tsm], [tsm])
                TS(C, "dve", selb[:, 64:96], t2, -1.0, None, ALU.max, None, [tsm], [tselb])
                ptr, tptr = bank(C, "proj", [0, 1])
                ptb = ptr[:].bitcast(BF16)
                TR(C, ptb[0:96, 0:128], selb[:, 0:96], C.identB[:, :], [tselb, C.identB], [tptr])
                for r in range(3):
                    CP(C, "act", QT[r][64:96, 128 * tl:128 * tl + 128], ptb[64:96, 0:128], [tptr], [tQT[r]])
        for r in range(3):
            oa, toa = attn_chunk(C, c, lambda j: KTs[0:96, 128 * j:128 * j + 128], tKTs, 96, QT[r], tQT[r],
                                 lambda j: Vs[:, j, :], tVs, "causal", 0.125, PT, tPT)
            combine(oa, toa, r, 1, c, 1)
            oa, toa = attn_chunk(C, c, lambda j: KTw[0:64, 128 * j:128 * j + 128], tKTw, 64, QT[r], tQT[r],
                                 lambda j: Vw[:, j, :], tVw, "window", 0.125, PT, tPT)
            combine(oa, toa, r, 2, c, 2)


def _ret(C, l):
    mk = C.mk
    RS = 3
    lg = np.log(1.0 - 2.0 ** (-5.0 - np.arange(5, dtype=np.float64)))
    for h in range(5):
        wa = WA(C)
        qT, tq = wa.bf(S, "rqT")
        kT, tk = wa.bf(S, "rkT")
        qd, tqd = wa.bf(S, "rqd")
        v, tv = wa.bf(1024, "rv")
        v = v.rearrange("p (t n) -> p t n", t=16)
        gt, tg = wa.bf(1024, "rg")
        gt = gt.rearrange("p (t n) -> p t n", t=16)
        W, tW = wa.bf(8 * 256, "rW")
        W = W.rearrange("p (i k n) -> p i k n", i=4, k=8)
        WP, tWP = wa.bf(2 * 8 * 128, "rWQ")
        WP = WP.rearrange("p (i k n) -> p i k n", i=2, k=8)
        scm, tscm = wa.bf(128, "rscm")
        kw, tkw = wa.bf(64, "rkw")
        sbf, tsbf = wa.bf(64, "rsbf")
        sf, tsf = wa.f32(64, "rsf")
        yt, tyt = wa.bf(128, "ryt")
        osb, tosb = wa.f32(64, "rosb")
        junk, tjunk = wa.f32(64, "rjunk")
        stt, tstt = wa.f32(16, "rstat")
        gn, tgn = wa.f32(64, "rgn")
        claim(C, wa.tiles)
        win = C.dr["w_in"]
        ch, po = _mix_dst(C, 704 + 64 * h)
        if RS <= -2:
            MEMSET(C, "pool", C.mixT[po:po + 64, ch, :], 0.0, [C.TMIX[ch]])
            continue
        for i, nm in enumerate(("rq", "rk", "rv", "rg")):
            LDW(C, W[:, i, :, :], win[l, :, OFF[nm] + 64 * h:OFF[nm] + 64 * h + 64].rearrange(
                "(k p) n -> p k n", p=128), "retw", [tW])
        mk.dma("sp", gn, C.dr["ret_gn_gain"][l, h, :].partition_broadcast(128), "const", writes=[tgn])
        for i in range(2):
            CP(C, "pool", WP[:, i, :, 0:64], W[:, i, :, :], [tW], [tWP])
            CP(C, "pool", WP[:, i, :, 64:96], W[:, i, :, 32:64], [tW], [tWP])
            CP(C, "pool", WP[:, i, :, 96:128], W[:, i, :, 0:32], [tW], [tWP])
        MEMSET(C, "pool", yt, 0.0, [tyt])
        if RS == -1:
            MEMSET(C, "pool", C.mixT[po:po + 64, ch, :], 0.0, [C.TMIX[ch]])
            continue
        for c in range(NCH):
            cs = slice(512 * c, 512 * c + 512)
            for i, (dst, tdst) in enumerate(((qT, tq), (kT, tk))):
                pa, tpa = bank(C, "proj", [0, 1])
                for k in range(8):
                    MM(C, pa[:, :], WP[:, i, k, :], C.hT[:, k, cs], k == 0, k == 7, [tWP, C.TH[c]], [tpa])
                ra, tra = tmp32(C)
                rb, trb = tmp32(C)
                TT(C, "dve", ra[0:64, :], pa[0:64, :], C.CR[0:64, cs], ALU.mult, [tpa, C.TCR], [tra])
                TT(C, "dve", rb[0:64, :], pa[64:128, :], C.SR[64:128, cs], ALU.mult, [tpa, C.TSR], [trb])
                TT(C, "pool", dst[0:64, cs], ra[0:64, :], rb[0:64, :], ALU.add, [tra, trb], [tdst])
            if RS != 0:
                pv, tpv = bank(C, "proj", [0, 1])
                for tl in range(4):
                    t = 4 * c + tl
                    for k in range(8):
                        MM(C, pv[:, 128 * tl:128 * tl + 64], C.hT[:, k, 128 * t:128 * t + 128], W[:, 2, k, :], k == 0,
                           k == 7, [tW, C.TH[c]], [tpv])
                    for k in range(8):
                        MM(C, pv[:, 128 * tl + 64:128 * tl + 128], C.hT[:, k, 128 * t:128 * t + 128], W[:, 3, k, :],
                           k == 0, k == 7, [tW, C.TH[c]], [tpv])
                pv3 = pv[:].rearrange("p (a b) -> p a b", a=4)
                CP(C, "act", v[:, 4 * c:4 * c + 4, :], pv3[:, :, 0:64], [tpv], [tv])
                ge, tge = tmp32(C)
                ge3 = ge[:, 0:256].rearrange("p (a b) -> p a b", a=4)
                ACTF(C, ge3, pv3[:, :, 64:128], AF.Exp, [tpv], [tge], scale=-1.0)
                ACTF(C, ge[:, 0:256], ge[:, 0:256], AF.Ln, [tge], [tge], bias=1.0, scale=1.0)
                ACTF(C, ge[:, 0:256], ge[:, 0:256], AF.Exp, [tge], [tge], scale=-1.0)
                if True:
                    TT(C, "dve", gt[:, 4 * c:4 * c + 4, :], pv3[:, :, 64:128], ge3, ALU.mult, [tpv, tge], [tg])
        if RS in (0, 5, 6, 7, 8, 9):
            MEMSET(C, "pool", C.mixT[po:po + 64, ch, :], 0.0, [C.TMIX[ch]])
            continue
        for n in range(16):
            ns = slice(128 * n, 128 * n + 128)
            TT(C, "pool", qd[0:64, ns], qT[0:64, ns], C.rdB[0:64, 128 * h:128 * h + 128], ALU.mult, [tq, C.rdB], [tqd])
        cd = float(np.exp(128.0 * lg[h]))
        if RS == 1:
            MEMSET(C, "pool", C.mixT[po:po + 64, ch, :], 0.0, [C.TMIX[ch]])
            continue
        for n in range(16):
            ns = slice(128 * n, 128 * n + 128)
            ps, tps = bank(C, "st", [4, 5])
            MM(C, ps[:, 0:128], kT[0:64, ns], qT[0:64, ns], True, True, [tk, tq], [tps])
            TT(C, "dve", scm, ps[:, 0:128], C.intraTF[:, 128 * h:128 * h + 128], ALU.mult, [tps, C.intraTF], [tscm])
            if n == 0:
                ix = (l * 5 + h) * 2 + C.cur_seq
                TS(C, "dve", scm[0:1, 0:1], C.s00[0:1, ix:ix + 1], 0.125, None, ALU.mult, None, [C.Ts00], [tscm])
            po_, tpo = bank(C, "oa", [6, 7])
            MM(C, po_[:, 0:64], scm, v[:, n, :], True, n == 0, [tscm, tv], [tpo])
            if n > 0:
                MM(C, po_[:, 0:64], qd[0:64, ns], sbf[0:64, :], False, True, [tqd, tsbf], [tpo])
            if n < 15:
                pk, tpk = bank(C, "st", [4, 5])
                MM(C, pk[:, 0:64], kT[0:64, ns], C.identB[0:64, 0:64], True, True, [tk, C.identB], [tpk])
                ACTF(C, kw, pk[:, 0:64], AF.Identity, [tpk, C.wdF], [tkw], scale=C.wdF[:, h:h + 1])
                pst, tpst = bank(C, "rms", [2, 3])
                MM(C, pst[0:64, 0:64], kw, v[:, n, :], True, True, [tkw, tv], [tpst])
                if n == 0:
                    CP(C, "dve", sf[0:64, :], pst[0:64, 0:64], [tpst], [tsf])
                else:
                    STT(C, "dve", sf[0:64, :], sf[0:64, :], cd, pst[0:64, 0:64], ALU.mult, ALU.add, [tsf, tpst], [tsf])
                CP(C, "pool", sbf[0:64, :], sf[0:64, :], [tsf], [tsbf])
            if RS == 2:
                CP(C, "act", yt[:, po:po + 64], po_[:, 0:64], [tpo], [tyt])
                ptr, tptr = bank(C, "proj", [0, 1])
                ptb = ptr[:].bitcast(BF16)
                TR(C, ptb[:, 0:128], yt, C.identB[:, :], [tyt, C.identB], [tptr])
                CP(C, "act", C.mixT[po:po + 64, ch, ns], ptb[po:po + 64, 0:128], [tptr], [C.TMIX[ch]])
                continue
            ACTF(C, osb, po_[:, 0:64], AF.Identity, [tpo], [tosb, tstt], accum_out=stt[:, 0:1])
            ACTF(C, junk, po_[:, 0:64], AF.Square, [tpo], [tjunk, tstt], accum_out=stt[:, 1:2])
            TS(C, "dve", stt[:, 2:3], stt[:, 0:1], 1.0 / 64, None, ALU.mult, None, [tstt], [tstt])
            TT(C, "dve", stt[:, 3:4], stt[:, 2:3], stt[:, 2:3], ALU.mult, [tstt], [tstt])
            STT(C, "dve", stt[:, 4:5], stt[:, 1:2], 1.0 / 64, stt[:, 3:4], ALU.mult, ALU.subtract, [tstt], [tstt])
            ACTF(C, stt[:, 5:6], stt[:, 4:5], AF.Ln, [tstt], [tstt], bias=EPS, scale=1.0)
            ACTF(C, stt[:, 6:7], stt[:, 5:6], AF.Exp, [tstt], [tstt], scale=-0.5)
            TT(C, "dve", stt[:, 7:8], stt[:, 2:3], stt[:, 6:7], ALU.mult, [tstt], [tstt])
            STT(C, "dve", osb, osb, stt[:, 6:7], stt[:, 7:8].to_broadcast([128, 64]), ALU.mult, ALU.subtract,
                [tosb, tstt], [tosb])
            TT(C, "pool", osb, osb, gn, ALU.mult, [tosb, tgn], [tosb])
            TT(C, "pool", yt[:, po:po + 64], osb, gt[:, n, :], ALU.mult, [tosb, tg], [tyt])
            ptr, tptr = bank(C, "proj", [0, 1])
            ptb = ptr[:].bitcast(BF16)
            TR(C, ptb[:, 0:128], yt, C.identB[:, :], [tyt, C.identB], [tptr])
            CP(C, "act", C.mixT[po:po + 64, ch, ns], ptb[po:po + 64, 0:128], [tptr], [C.TMIX[ch]])


_CACHE = {}


def kernel(**inputs):
    n_cores = 8
    x = np.ascontiguousarray(np.asarray(inputs["x"], dtype=np.float32))
    pos = np.ascontiguousarray(np.asarray(inputs["positions"], dtype=np.int32))
    if "nc" not in _CACHE:
        _CACHE["nc"] = build()[0]
        _CACHE["blob"] = make_blob()[0]
    nc = _CACHE["nc"]
    w = {n: np.ascontiguousarray(np.asarray(inputs[n], dtype=np.float32)) for n in WNAMES}
    in_maps = []
    for i in range(n_cores):
        m = {"x": x[2 * i:2 * i + 2], "pos": pos[2 * i:2 * i + 2], "blob": _CACHE["blob"]}
        m.update(w)
        in_maps.append(m)
    res = run_bass_kernel_spmd(nc, in_maps, core_ids=list(range(n_cores)))
    return np.concatenate([np.asarray(r["out"]) for r in res.results], axis=0).astype(np.float32)
```

```python
from contextlib import ExitStack
import numpy as np
import concourse.bass as bass
import concourse.mybir as mybir
from concourse.bass_utils import run_bass_kernel_spmd

F32 = mybir.dt.float32
BF16 = mybir.dt.bfloat16
I32 = mybir.dt.int32
AF = mybir.ActivationFunctionType
ALU = mybir.AluOpType
AX = mybir.AxisListType

D = 1024
S = 2048
DEPTH = 4
NCH = 4
NT = 16
N_IN = 2866
NEG = -30000.0
EPS = 1e-6
OFF = dict(q=0, kc=384, vc=512, ks=640, vs=768, kw=896, vw=1024, gate=1152, cq=1170, ckv=1426,
           kpe=1554, rq=1586, rk=1906, rv=2226, rg=2546)

ENGS = ("pe", "act", "dve", "pool", "sp")


class T:
    __slots__ = ("ap", "name", "lw", "rd")

    def __init__(self, ap, name=""):
        self.ap = ap
        self.name = name
        self.lw = {}
        self.rd = {}

    def __getitem__(self, idx):
        return self.ap[idx]


class DmaGroup:
    def __init__(self, mk, name):
        self.name = name
        self.total = 0
        self.sem = mk._newsem("g_" + name)


class MK:
    def __init__(self, nc, stack):
        self.nc = nc
        self.stack = stack
        self.q = {e: [] for e in ENGS}
        self.cnt = {e: 0 for e in ENGS}
        self.waited = {e: {} for e in ENGS}
        self.esem = {e: self._newsem("c_" + e) for e in ENGS}
        self.groups = {}

    def _newsem(self, name):
        return self.stack.enter_context(self.nc.semaphore(name))

    def group(self, name):
        g = self.groups.get(name)
        if g is None:
            g = DmaGroup(self, name)
            self.groups[name] = g
        return g

    def sb(self, name, shape, dt):
        return self.stack.enter_context(self.nc.sbuf_tensor(name, list(shape), dt))

    def ps(self, name, shape, dt):
        return self.stack.enter_context(self.nc.psum_tensor(name, list(shape), dt))

    def _collect(self, eng, reads, writes):
        need = {}

        def add(d, same_ok):
            for k, v in d.items():
                if k == eng and not same_ok:
                    continue
                if need.get(k, 0) < v:
                    need[k] = v
        raw_same = eng != "pe"
        for t in reads:
            add(t.lw, raw_same)
        for t in writes:
            add(t.lw, False)
            add(t.rd, False)
        out = []
        w = self.waited[eng]
        for k, v in need.items():
            if isinstance(k, DmaGroup):
                v = k.total
            if w.get(k, 0) >= v:
                continue
            w[k] = v
            out.append((k, v))
        return out

    def _mark(self, key, val, reads, writes):
        for t in reads:
            if t.rd.get(key, 0) < val:
                t.rd[key] = val
        for t in writes:
            t.lw = {key: val}
            t.rd = {}

    def op(self, eng, fn, reads=(), writes=()):
        waits = self._collect(eng, reads, writes)
        self.cnt[eng] += 1
        self.q[eng].append((fn, waits, None))
        self._mark(eng, self.cnt[eng], reads, writes)

    def dma(self, eng, out_ap, in_ap, group, reads=(), writes=(), **kw):
        waits = self._collect(eng, reads, writes)
        g = self.group(group) if isinstance(group, str) else group
        g.total += 16
        self.q[eng].append((lambda e: e.dma_start(out=out_ap, in_=in_ap, **kw), waits, g))
        self._mark(g, g.total, reads, writes)

    def claim(self, tiles, others):
        for t in tiles:
            for u in others:
                if u is t:
                    continue
                for d in (u.lw, u.rd):
                    for k, v in d.items():
                        if t.rd.get(k, 0) < v:
                            t.rd[k] = v

    def wait_all_dma(self, eng="sp"):
        waits = []
        for g in self.groups.values():
            if g.total and self.waited[eng].get(g, 0) < g.total:
                waits.append((g, g.total))
                self.waited[eng][g] = g.total
        self.q[eng].append((None, waits, None))

    def _sem(self, k):
        return k.sem if isinstance(k, DmaGroup) else self.esem[k]

    def emit(self):
        nc = self.nc
        mk = self
        with nc.Block() as block:
            def run(engname, e):
                for fn, waits, g in mk.q[engname]:
                    if fn is None:
                        for k, v in waits:
                            e.wait_ge(mk._sem(k), v)
                        continue
                    for k, v in waits[1:]:
                        e.wait_ge(mk._sem(k), v)
                    ins = fn(e)
                    if waits:
                        k, v = waits[0]
                        ins._wait_ge(mk._sem(k), v)
                    if g is not None:
                        ins.then_inc(g.sem, 16)
                    else:
                        ins.then_inc(mk.esem[engname], 1)

            @block.sync
            def _(e):
                run("sp", e)

            @block.scalar
            def _(e):
                run("act", e)

            @block.vector
            def _(e):
                run("dve", e)

            @block.gpsimd
            def _(e):
                run("pool", e)

            @block.tensor
            def _(e):
                run("pe", e)


def _blob_layout():
    secs = [("ident", 128), ("tri", 128), ("winb", 128), ("cmpb", 2048), ("E", 2048), ("ovl1", 33),
            ("keep", 256), ("addt", 256), ("intraT", 640), ("rd", 640), ("wd", 5), ("cols", 8),
            ("sel9", 576), ("ones", 128), ("sel18", 1152)]
    off = {}
    o = 0
    for n, w in secs:
        off[n] = (o, w)
        o += w
    return off, o


BLOB_OFF, BLOB_W = _blob_layout()


def make_blob():
    b = np.zeros((128, BLOB_W), np.float32)

    def sec(n):
        o, w = BLOB_OFF[n]
        return b[:, o:o + w]
    p = np.arange(128)
    sec("ident")[:] = np.eye(128, dtype=np.float32)
    sec("ones")[:] = 1.0
    sec("tri")[:] = np.where(p[:, None] <= p[None, :], 0.0, NEG)
    sec("winb")[:] = np.where(p[:, None] > p[None, :], 0.0, NEG)
    q = np.arange(S)
    cm = np.where((16 * p[:, None] + 31) <= q[None, :], 0.0, NEG)
    cm[127, :] = NEG
    sec("cmpb")[:] = cm
    e = np.zeros((128, S), np.float32)
    for r in range(32):
        e[64 + r, 64 * r:64 * r + 64] = -NEG
    sec("E")[:] = e
    n_cmp, n_sel = 127, 32
    cs = np.arange(n_cmp) * 16
    ce = cs + 32
    ss = np.arange(n_sel) * 64
    se = ss + 64
    ov = np.clip(np.minimum(ce[:, None], se[None, :]) - np.maximum(cs[:, None], ss[None, :]), 0, None) / 32.0
    o1 = sec("ovl1")
    o1[:127, :32] = ov
    o1[:127, 32] = 1.0
    keep = np.zeros((128, 8, 32), np.float32)
    addt = np.zeros((128, 8, 32), np.float32)
    blk = np.arange(32)
    for qt in range(8, 16):
        t = qt * 128 + p
        cur = t // 64
        valid = blk[None, :] <= cur[:, None]
        forced = (blk[None, :] == 0) | (blk[None, :] == cur[:, None]) | (blk[None, :] == cur[:, None] - 1)
        keep[:, qt - 8, :] = (valid & ~forced)
        addt[:, qt - 8, :] = np.where(valid & forced, 1e9, np.where(valid, 0.0, -1e30))
    sec("keep")[:] = keep.reshape(128, 256)
    sec("addt")[:] = addt.reshape(128, 256)
    lg = np.log(1.0 - 2.0 ** (-5.0 - np.arange(5, dtype=np.float64)))
    i = np.arange(128, dtype=np.float64)
    it = np.zeros((128, 5, 128), np.float64)
    for h in range(5):
        diff = i[None, :] - i[:, None]
        it[:, h, :] = np.where(diff >= 0, np.exp(np.maximum(diff, 0) * lg[h]), 0.0) * 0.125
    sec("intraT")[:] = it.reshape(128, 640)
    rd = np.zeros((128, 5, 128), np.float64)
    for h in range(5):
        rd[:, h, :] = np.exp((i + 1.0) * lg[h])[None, :]
    sec("rd")[:] = rd.reshape(128, 640)
    wd = sec("wd")
    for h in range(5):
        wd[:, h] = np.exp((127.0 - i) * lg[h]) * 0.125
    cols = sec("cols")
    inv_p = 1.0 / (500000.0 ** (np.arange(0, 16, 2, dtype=np.float32) / 16)).astype(np.float32)
    inv_m = 1.0 / (500000.0 ** (np.arange(0, 32, 2, dtype=np.float32) / 32)).astype(np.float32)
    inv_r = 1.0 / (10000.0 ** (np.arange(0, 64, 2, dtype=np.float32) / 64)).astype(np.float32)
    tp = 2.0 * np.pi
    cols[:, 1] = tp
    cols[:, 3] = tp
    for r in range(16):
        cols[r, 0] = np.float64(inv_p[r % 8]) / tp
    cols[0:8, 1] = -tp
    for r in range(64, 96):
        cols[r, 0] = np.float64(inv_m[(r - 64) % 16]) / tp
    cols[64:80, 1] = -tp
    for r in range(128):
        cols[r, 2] = np.float64(inv_r[(r % 64) % 32]) / tp
        if (r % 64) < 32:
            cols[r, 3] = -tp
    s9 = np.zeros((128, 9, 64), np.float32)
    for j in range(9):
        s9[j, j, :] = 1.0
    sec("sel9")[:] = s9.reshape(128, 576)
    s18 = np.zeros((128, 18, 64), np.float32)
    for j in range(18):
        s18[j, j, :] = 1.0
    sec("sel18")[:] = s18.reshape(128, 1152)
    return b, lg


WNAMES = ["ln1_gain", "w_in", "cmp_pos_k", "cmp_w1_k", "cmp_w2_k", "cmp_pos_v", "cmp_w1_v", "cmp_w2_v",
          "mla_q_norm", "mla_w_uq", "mla_kv_norm", "mla_w_ukv", "ret_gn_gain", "w_out", "ln2_gain",
          "w_up", "w_down", "final_gain"]
WSHAPES = dict(ln1_gain=[4, 1024], w_in=[4, 1024, 2866], cmp_pos_k=[4, 32, 64], cmp_w1_k=[4, 2048, 128],
               cmp_w2_k=[4, 128, 64], cmp_pos_v=[4, 32, 64], cmp_w1_v=[4, 2048, 128], cmp_w2_v=[4, 128, 64],
               mla_q_norm=[4, 256], mla_w_uq=[4, 256, 480], mla_kv_norm=[4, 128], mla_w_ukv=[4, 128, 640],
               ret_gn_gain=[4, 5, 64], w_out=[4, 1024, 1024], ln2_gain=[4, 1024], w_up=[4, 1024, 4096],
               w_down=[4, 4096, 1024], final_gain=[1024])


class Ctx:
    pass


def build(n_layers=DEPTH, n_seq=2, mixers=("nsa", "mla", "ret"), dbg=False):
    nc = bass.Bass("TRN2", target_bir_lowering=False)
    C = Ctx()
    C.nc = nc
    C.dbg = dbg
    C.dbg_outs = {}
    dr = {}
    dr["x"] = nc.dram_tensor("x", [n_seq, S, D], F32, kind="ExternalInput").ap()
    dr["pos"] = nc.dram_tensor("pos", [n_seq, S], I32, kind="ExternalInput").ap()
    dr["blob"] = nc.dram_tensor("blob", [128, BLOB_W], F32, kind="ExternalInput").ap()
    for n in WNAMES:
        dr[n] = nc.dram_tensor(n, WSHAPES[n], F32, kind="ExternalInput").ap()
    dr["out"] = nc.dram_tensor("out", [n_seq, S, D], F32, kind="ExternalOutput").ap()
    C.dr = dr
    with ExitStack() as st:
        mk = MK(nc, st)
        C.mk = mk
        _setup(C)
        if "ret" in mixers:
            _tok0_prepass(C, n_seq, n_layers)
            dump(C, "s00", C.Ts00, C.s00[:, :], [128, 40])
        for s in range(n_seq):
            C.cur_seq = s
            _load_x(C, s)
            if s == 0:
                dump(C, "xT", C.TX[0], C.xT[:, :, 0:512], [128, 8, 512])
                dump(C, "xT3", C.TX[3], C.xT[:, :, 1536:2048], [128, 8, 512])
            if mixers:
                _rope_tables(C, s)
            for l in range(n_layers):
                if mixers:
                    _rmsnorm_main(C, C.g1, l)
                    _mixer(C, l, mixers)
                _rmsnorm_main(C, C.g2, l)
                if s == 0 and l == 0:
                    dump(C, "h2T", C.TH[0], C.hT[:, :, 0:512], [128, 8, 512], BF16)
                _ffn(C, l)
                if s == 0 and l == 0:
                    dump(C, "aT", C.TA, C.aT[:, :, :], [128, 32, 512], BF16)
                    dump(C, "x2T", C.TX[3], C.xT[:, :, 1536:2048], [128, 8, 512])
            _final(C, s)
        mk.wait_all_dma("sp")
        mk.emit()
    return nc, C


def _bsec(C, name):
    o, w = BLOB_OFF[name]
    return C.dr["blob"][:, o:o + w]


def _setup(C):
    mk = C.mk
    cg = "const"

    def cf32(name, sec, w):
        t = mk.sb(name, [128, w], F32)
        tt = T(t, name)
        mk.dma("sp", t[:], _bsec(C, sec), cg, writes=[tt])
        return tt

    def cbf(name, sec, w):
        t = mk.sb(name, [128, w], BF16)
        tt = T(t, name)
        mk.dma("pool", t[:], _bsec(C, sec), cg, writes=[tt])
        return tt
    C.identF = cf32("identF", "ident", 128)
    C.identB = cbf("identB", "ident", 128)
    C.onesB = cbf("onesB", "ones", 128)
    C.triB = cbf("triB", "tri", 128)
    C.winbB = cbf("winbB", "winb", 128)
    C.cmpbB = cbf("cmpbB", "cmpb", 2048)
    C.ovl1B = cbf("ovl1B", "ovl1", 33)
    C.keepF = cf32("keepF", "keep", 256)
    C.addtF = cf32("addtF", "addt", 256)
    C.intraTF = cf32("intraTF", "intraT", 640)
    C.rdB = cbf("rdB", "rd", 640)
    C.wdF = cf32("wdF", "wd", 5)
    C.colsF = cf32("colsF", "cols", 8)
    C.sel9B = cbf("sel9B", "sel9", 576)
    C.onesF = cf32("onesF", "ones", 128)
    t0 = mk.sb("s00", [128, 40], F32)
    C.s00 = t0
    C.Ts00 = T(t0, "s00")
    t = mk.sb("bigF", [128, 32], F32)
    C.bigF = T(t, "bigF")
    MEMSET(C, "pool", t[:], 1.0e15, [C.bigF])
    dr = C.dr

    def gain_cols(name, src_ap, shape):
        t = mk.sb(name, shape, F32)
        tt = T(t, name)
        mk.dma("sp", t[:], src_ap, cg, writes=[tt], allow_slow_non_contiguous=True)
        return tt
    C.g1 = gain_cols("g1", dr["ln1_gain"].rearrange("l (c p) -> p l c", p=128), [128, 4, 8])
    C.g2 = gain_cols("g2", dr["ln2_gain"].rearrange("l (c p) -> p l c", p=128), [128, 4, 8])
    C.gf = gain_cols("gf", dr["final_gain"].rearrange("(c p) -> p c", p=128), [128, 8])
    C.gq = gain_cols("gq", dr["mla_q_norm"].rearrange("l (c p) -> p l c", p=128), [128, 4, 2])
    C.gkv = gain_cols("gkv", dr["mla_kv_norm"].rearrange("l p -> p l"), [128, 4])
    C.gnT = gain_cols("gnT", dr["ret_gn_gain"].rearrange("l h d -> d (l h)"), [64, 20])
    t2_ = mk.sb("gnT2", [128, 20], F32)
    C.gnT2 = T(t2_, "gnT2")
    for hf in range(2):
        mk.dma("sp", t2_[64 * hf:64 * hf + 64, :], dr["ret_gn_gain"].rearrange("l h d -> d (l h)"), cg, writes=[C.gnT2],
               allow_slow_non_contiguous=True)
    C.xT = mk.sb("xT", [128, 8, S], F32)
    C.TX = [T(C.xT, "xT%d" % c) for c in range(NCH)]
    C.hT = mk.sb("hT", [128, 8, S], BF16)
    C.TH = [T(C.hT, "hT%d" % c) for c in range(NCH)]
    AW = 27648
    C.arena = mk.sb("arena", [128, AW], BF16)
    C.arena_tiles = []
    C.mixT = C.arena[:, 0:4 * S].rearrange("p (c t) -> p c t", c=4)
    C.TMIX = [T(C.mixT, "mix%d" % c) for c in range(4)]
    C.aT = C.arena[:, 0:32 * 512].rearrange("p (f t) -> p f t", f=32)
    C.TA = T(C.aT, "aT")
    C.WORK0 = 4 * S
    C.sq = C.arena[:, AW - 4096:AW].rearrange("p (k t) -> p k t", k=8)
    C.TSQ = T(C.sq, "sq")
    stf = C.arena[:, AW - 8192:AW - 4096].bitcast(F32)
    C.stage = [stf[:, 0:1024], stf[:, 1024:2048]]
    C.TST = [T(C.stage[i], "stage%d" % i) for i in range(2)]
    C.arena_tiles += C.TMIX + [C.TA, C.TSQ] + C.TST
    C.NW = 2
    C.wsl = [mk.sb("wsl%d" % i, [128, 4096], BF16) for i in range(C.NW)]
    C.TW = [T(C.wsl[i], "wsl%d" % i) for i in range(C.NW)]
    C.wi = 0
    C.t32 = [mk.sb("t32_%d" % i, [128, 512], F32) for i in range(4)]
    C.TT32 = [T(C.t32[i], "t32_%d" % i) for i in range(4)]
    C.t32i = 0
    C.pb = [mk.ps("pb%d" % i, [128, 512], F32) for i in range(8)]
    C.TP = [T(C.pb[i], "pb%d" % i) for i in range(8)]
    C.pbi = {}


def bank(C, cls, choices):
    i = C.pbi.get(cls, 0)
    C.pbi[cls] = i + 1
    b = choices[i % len(choices)]
    return C.pb[b], C.TP[b]


def tmp32(C):
    i = C.t32i % 4
    C.t32i += 1
    return C.t32[i], C.TT32[i]


def wslot(C):
    i = C.wi % C.NW
    C.wi += 1
    return C.wsl[i], C.TW[i], "wsl%d" % i


def dump(C, name, t, ap, shape, dt=F32):
    if not C.dbg:
        return
    d = C.nc.dram_tensor("dbg_" + name, list(shape), dt, kind="ExternalOutput").ap()
    C.dbg_outs[name] = d
    C.mk.dma("sp", d, ap, "dbg", reads=[t])


def claim(C, tiles):
    C.mk.claim(tiles, C.arena_tiles)
    for t in tiles:
        if t not in C.arena_tiles:
            C.arena_tiles.append(t)


def MM(C, o, l, r, st, sp, reads, writes):
    C.mk.op("pe", lambda e: e.matmul(o, l, r, start=st, stop=sp, skip_group_check=True), reads, writes)


def TR(C, o, i, ident, reads, writes):
    C.mk.op("pe", lambda e: e.transpose(out=o, in_=i, identity=ident), reads, writes)


def ACTF(C, o, i, func, reads, writes, **kw):
    C.mk.op("act", lambda e: e.activation(out=o, in_=i, func=func, **kw), reads, writes)


def TT(C, eng, o, a, b, op, reads, writes):
    C.mk.op(eng, lambda e: e.tensor_tensor(out=o, in0=a, in1=b, op=op), reads, writes)


def TS(C, eng, o, a, s1, s2, op0, op1, reads, writes):
    if s2 is None:
        C.mk.op(eng, lambda e: e.tensor_scalar(out=o, in0=a, scalar1=s1, scalar2=None, op0=op0), reads, writes)
    else:
        C.mk.op(eng, lambda e: e.tensor_scalar(out=o, in0=a, scalar1=s1, scalar2=s2, op0=op0, op1=op1), reads, writes)


def STT(C, eng, o, a, sc, b, op0, op1, reads, writes):
    C.mk.op(eng, lambda e: e.scalar_tensor_tensor(out=o, in0=a, scalar=sc, in1=b, op0=op0, op1=op1), reads, writes)


def CP(C, eng, o, i, reads, writes):
    if eng == "act":
        C.mk.op(eng, lambda e: e.copy(out=o, in_=i), reads, writes)
    else:
        C.mk.op(eng, lambda e: e.tensor_copy(out=o, in_=i), reads, writes)


def RECIP(C, o, i, reads, writes):
    C.mk.op("dve", lambda e: e.reciprocal(out=o, in_=i), reads, writes)


def MEMSET(C, eng, o, v, writes):
    C.mk.op(eng, lambda e: e.memset(o, v), (), writes)


def LDW(C, dst_ap, src_ap, group, writes, eng="pool", **kw):
    C.mk.dma(eng, dst_ap, src_ap, group, writes=writes, **kw)


def claim(C, tiles):
    C.mk.claim(tiles, C.arena_tiles)
    for t in tiles:
        if t not in C.arena_tiles:
            C.arena_tiles.append(t)


def _load_x(C, s):
    mk = C.mk
    x = C.dr["x"]
    claim(C, C.TST)
    for t in range(NT):
        stg, tst = C.stage[t % 2], C.TST[t % 2]
        mk.dma("sp", stg, x[s, 128 * t:128 * t + 128, :], "stage%d" % (t % 2), writes=[tst])
        c = t // 4
        for half in range(2):
            pb, tp = bank(C, "tr", [0, 1])
            for j in range(4):
                dc = 4 * half + j
                MM(C, pb[:, 128 * j:128 * j + 128], stg[:, 128 * dc:128 * dc + 128], C.identF[:], True, True,
                   [tst, C.identF], [tp])
            CP(C, "dve", C.xT[:, 4 * half:4 * half + 4, 128 * t:128 * t + 128],
               pb[:].rearrange("p (a b) -> p a b", a=4), [tp], [C.TX[c]])


def _rms(C, src, srcT, sq, sqT, nk, rows, gain_ap_fn, dst_fn, dstT, inv_n):
    ACTF(C, sq[0:rows, 0:nk, :], src, AF.Square, [srcT], [sqT])
    pb, tp = bank(C, "rms", [2, 3])
    for k in range(nk):
        MM(C, pb[:], C.onesB[0:rows, :], sq[0:rows, k, :], k == 0, k == nk - 1, [sqT, C.onesB], [tp])
    rt, trt = tmp32(C)
    ACTF(C, rt[:], pb[:], AF.Ln, [tp], [trt], bias=EPS, scale=inv_n)
    ACTF(C, rt[:], rt[:], AF.Exp, [trt], [trt], scale=-0.5)
    for k in range(nk):
        STT(C, "dve", dst_fn(k), src[:, k, :], gain_ap_fn(k), rt[0:rows, :], ALU.mult, ALU.mult,
            [srcT, trt], [dstT])


def _rmsnorm_main(C, g, l):
    claim(C, [C.TSQ])
    for c in range(NCH):
        cs = slice(512 * c, 512 * c + 512)
        _rms(C, C.xT[:, :, cs], C.TX[c], C.sq, C.TSQ, 8, 128, lambda k, g=g, l=l: g[:, l, k:k + 1],
             lambda k, cs=cs: C.hT[:, k, cs], C.TH[c], 1.0 / D)


def _ffn(C, l):
    w_up = C.dr["w_up"]
    w_dn = C.dr["w_down"]
    if not hasattr(C, "ffn_tiles"):
        o = 0
        C.f_at, C.f_up, C.f_dn = [], [], []
        for nm, n, lst in (("fat", 2048, C.f_at), ("fat", 2048, C.f_at), ("fup", 4096, C.f_up), ("fup", 4096, C.f_up),
                           ("fdn", 4096, C.f_dn), ("fdn", 4096, C.f_dn)):
            ap = C.arena[:, o:o + n]
            o += n
            lst.append((ap, T(ap, "%s%d" % (nm, len(lst)))))
        C.ffn_tiles = [t for lst in (C.f_at, C.f_up, C.f_dn) for (_, t) in lst]
        C.fai = 0
    claim(C, C.ffn_tiles)
    pend = None

    def down(p):
        (c_, atv_, tat_, dnv_, tdn_) = p
        cs_ = slice(512 * c_, 512 * c_ + 512)
        for d in range(8):
            pb, tp = bank(C, "dn", [0, 1, 2, 3])
            for kk in range(4):
                MM(C, pb[:], dnv_[:, kk, 128 * d:128 * d + 128], atv_[:, kk, :], kk == 0, kk == 3, [tdn_, tat_], [tp])
            TT(C, "dve", C.xT[:, d, cs_], pb[:], C.xT[:, d, cs_], ALU.add, [tp, C.TX[c_]], [C.TX[c_]])

    for fb in range(8):
        up, tup = C.f_up[fb % 2]
        dn, tdn = C.f_dn[fb % 2]
        upv = up.rearrange("p (k n) -> p k n", k=8)
        dnv = dn.rearrange("p (k n) -> p k n", k=4)
        LDW(C, upv, w_up[l, :, 512 * fb:512 * fb + 512].rearrange("(k p) n -> p k n", p=128), "fup%d" % (fb % 2), [tup])
        LDW(C, dnv, w_dn[l, 512 * fb:512 * fb + 512, :].rearrange("(k p) n -> p k n", p=128), "fdn%d" % (fb % 2), [tdn])
        for c in range(NCH):
            cs = slice(512 * c, 512 * c + 512)
            at, tat = C.f_at[C.fai % 2]
            C.fai += 1
            atv = at.rearrange("p (k n) -> p k n", k=4)
            for fi in range(4):
                pb, tp = bank(C, "up", [4, 5])
                for k in range(8):
                    MM(C, pb[:], upv[:, k, 128 * fi:128 * fi + 128], C.hT[:, k, cs], k == 0, k == 7, [tup, C.TH[c]], [tp])
                r, tr = tmp32(C)
                ACTF(C, r[:], pb[:], AF.Relu, [tp], [tr])
                TT(C, "pool", atv[:, fi, :], r[:], r[:], ALU.mult, [tr], [tat])
            if pend is not None:
                down(pend)
            pend = (c, atv, tat, dnv, tdn)
    down(pend)


def _final(C, s):
    mk = C.mk
    out = C.dr["out"]
    claim(C, [C.TSQ] + C.TST)
    for c in range(NCH):
        cs = slice(512 * c, 512 * c + 512)
        ACTF(C, C.sq[:, :, :], C.xT[:, :, cs], AF.Square, [C.TX[c]], [C.TSQ])
        pb, tp = bank(C, "rms", [2, 3])
        for k in range(8):
            MM(C, pb[:], C.onesB[:, :], C.sq[:, k, :], k == 0, k == 7, [C.TSQ, C.onesB], [tp])
        rt, trt = tmp32(C)
        ACTF(C, rt[:], pb[:], AF.Ln, [tp], [trt], bias=EPS, scale=1.0 / D)
        ACTF(C, rt[:], rt[:], AF.Exp, [trt], [trt], scale=-0.5)
        for k in range(8):
            STT(C, "dve", C.xT[:, k, cs], C.xT[:, k, cs], C.gf[:, k:k + 1], rt[:], ALU.mult, ALU.mult,
                [C.TX[c], trt, C.gf], [C.TX[c]])
        for tl in range(4):
            t = 4 * c + tl
            stg, tst = C.stage[t % 2], C.TST[t % 2]
            for half in range(2):
                pb, tp = bank(C, "tr", [0, 1])
                for j in range(4):
                    dc = 4 * half + j
                    MM(C, pb[:, 128 * j:128 * j + 128], C.xT[:, dc, 128 * t:128 * t + 128], C.identF[:], True, True,
                       [C.TX[c], C.identF], [tp])
                CP(C, "act", stg[:, 512 * half:512 * half + 512], pb[:], [tp], [tst])
            mk.dma("sp", out[s, 128 * t:128 * t + 128, :], stg, "stage%d" % (t % 2), reads=[tst])


PI = float(np.pi)


class WA:
    def __init__(self, C, start=None):
        self.C = C
        self.o = C.WORK0 if start is None else start
        self.tiles = []

    def bf(self, n, name):
        ap = self.C.arena[:, self.o:self.o + n]
        self.o += n
        assert self.o <= 27648, (name, self.o)
        t = T(ap, name)
        self.tiles.append(t)
        return ap, t

    def f32(self, n, name):
        ap, t = self.bf(2 * n, name)
        ap = ap.bitcast(F32)
        t.ap = ap
        return ap, t


def _rope_tables(C, s):
    mk = C.mk
    if not hasattr(C, "CA"):
        C.CA = mk.sb("CA", [128, S], BF16)
        C.SA = mk.sb("SA", [128, S], BF16)
        C.CR = mk.sb("CR", [128, S], BF16)
        C.SR = mk.sb("SR", [128, S], BF16)
        C.TCA, C.TSA, C.TCR, C.TSR = T(C.CA, "CA"), T(C.SA, "SA"), T(C.CR, "CR"), T(C.SR, "SR")
    wa = WA(C, 0)
    posI, tpi = wa.f32(S, "posI")
    posI = posI.bitcast(I32)
    posF, tpf = wa.f32(S, "posF")
    ang, tang = wa.f32(S, "ang")
    tt, ttt = wa.f32(S, "ropetmp")
    claim(C, wa.tiles)
    mk.dma("sp", posI, C.dr["pos"][s, :].partition_broadcast(128), "const", writes=[tpi])
    CP(C, "dve", posF, posI, [tpi], [tpf])
    MAGIC = 12582912.0
    for (ic, isg, ctab, tct, stab, tst_) in ((0, 1, C.CA, C.TCA, C.SA, C.TSA), (2, 3, C.CR, C.TCR, C.SR, C.TSR)):
        for which in (0, 1):
            ACTF(C, ang, posF, AF.Identity, [tpf, C.colsF], [tang], scale=C.colsF[:, ic:ic + 1],
                 bias=(0.25 if which == 0 else 0.0))
            TS(C, "dve", tt, ang, MAGIC, None, ALU.add, None, [tang], [ttt])
            TS(C, "dve", tt, tt, -MAGIC, None, ALU.add, None, [ttt], [ttt])
            TT(C, "dve", tt, ang, tt, ALU.subtract, [tang, ttt], [ttt])
            if which == 0:
                ACTF(C, ctab[:], tt, AF.Sin, [ttt], [tct], scale=2 * PI)
            else:
                ACTF(C, stab[:], tt, AF.Sin, [ttt, C.colsF], [tst_], scale=C.colsF[:, isg:isg + 1])


def attn_chunk(C, c, kt_fn, TK, krows, QT, TQ, v_fn, TV, kind, scale, PT, TPT, st_banks=(4, 5), depth=1):
    if kind == "causal":
        kts = list(range(0, 4 * c + 4))
    else:
        first = 4 * c - 1 if c >= 1 else 0
        kts = [first] + [j for j in range(max(0, 4 * c - 4), 4 * c + 4) if j != first]
    oa, toa = bank(C, "oa", [6, 7])
    pend = []

    def pv(p):
        (j_, pi_, cs_, idx_) = p
        MM(C, oa[:, cs_], v_fn(j_), PT[pi_][:, cs_], idx_ == 0, idx_ == len(kts) - 1, [TV, TPT[pi_]], [toa])

    for idx, j in enumerate(kts):
        d = j - 4 * c
        lo = max(d, 0)
        hi = 4 if kind == "causal" else min(j + 4, 4 * c + 3) - 4 * c + 1
        st, tst = bank(C, "st%d" % len(st_banks), list(st_banks))
        segs = []
        plo, phi = lo, hi
        if d >= 0:
            segs.append((lo, lo + 1, C.triB))
            plo = lo + 1
        if kind == "window" and 0 <= j + 4 - 4 * c <= 3:
            segs.append((hi - 1, hi, C.winbB))
            phi = hi - 1
        if phi > plo:
            segs.append((plo, phi, None))
        for (a_, b_, bias) in segs:
            cs = slice(128 * a_, 128 * b_)
            MM(C, st[:, cs], kt_fn(j), QT[0:krows, cs], True, bias is None, [TK, TQ], [tst])
            if bias is not None:
                MM(C, st[:, cs], C.identB[:, :], bias[:, :], False, True, [C.identB, bias], [tst])
        cs = slice(128 * lo, 128 * hi)
        pi = C.pti % len(PT)
        C.pti += 1
        ACTF(C, PT[pi][:, cs], st[:, cs], AF.Exp, [tst], [TPT[pi]], scale=scale)
        pend.append((j, pi, cs, idx))
        if len(pend) > depth:
            pv(pend.pop(0))
    for p in pend:
        pv(p)
    return oa, toa


def _mix_dst(C, feat0):
    ch = (feat0 // 128) % 4
    po = feat0 % 128
    return ch, po


def _mla_pre(C, l):
    mk = C.mk
    wa = WA(C)
    M = Ctx()
    C.M = M
    M.cqn, M.tcqn = wa.bf(2 * S, "cqn")
    M.cqn = M.cqn.rearrange("p (k t) -> p k t", k=2)
    M.ckvn, M.tckvn = wa.bf(S, "ckvn")
    M.kpeT, M.tkpe = wa.bf(S, "kpeT")
    M.KT, M.tKT = wa.bf(S, "mlaKT")
    M.V, M.tV = wa.bf(S, "mlaV")
    M.V = M.V.rearrange("p (t n) -> p t n", t=16)
    M.QT, M.tQT = wa.bf(512, "mlaQT")
    M.PT, M.tPT = [], []
    for i in range(3):
        a, t = wa.bf(512, "mlaPT%d" % i)
        M.PT.append(a)
        M.tPT.append(t)
    M.sq, M.tsq = wa.bf(1024, "mlasq")
    M.sq = M.sq.rearrange("p (k t) -> p k t", k=2)
    M.wkp, M.twkp = wa.bf(768, "wkpeP")
    M.wkp = M.wkp.rearrange("p (k n) -> p k n", k=8)
    M.wuq, M.twuq = wa.bf(960, "wuq")
    M.wuq = M.wuq.rearrange("p (k n) -> p k n", k=2)
    M.wuqP, M.twuqP = wa.bf(960, "wuqP")
    M.wuqP = M.wuqP.rearrange("p (k n) -> p k n", k=2)
    M.wukv, M.twukv = wa.bf(640, "wukv")
    claim(C, wa.tiles)
    C.pti = 0
    ws, tw, gname = wslot(C)
    wv = ws[:, 0:8 * 416].rearrange("p (k n) -> p k n", k=8)
    LDW(C, wv, C.dr["w_in"][l, :, OFF["cq"]:OFF["cq"] + 416].rearrange("(k p) n -> p k n", p=128), gname, [tw])
    LDW(C, M.wuq, C.dr["mla_w_uq"][l].rearrange("(k p) n -> p k n", p=128), "mlaw", [M.twuq])
    LDW(C, M.wukv, C.dr["mla_w_ukv"][l], "mlaw", [M.twukv])
    CP(C, "pool", M.wkp, wv[:, :, 320:416], [tw], [M.twkp])
    CP(C, "pool", M.wkp[:, :, 64:80], wv[:, :, 400:416], [tw], [M.twkp])
    CP(C, "pool", M.wkp[:, :, 80:96], wv[:, :, 384:400], [tw], [M.twkp])
    CP(C, "pool", M.wuqP, M.wuq, [M.twuq], [M.twuqP])
    wq4 = M.wuq.rearrange("p k (h n) -> p k h n", h=5)
    wq4P = M.wuqP.rearrange("p k (h n) -> p k h n", h=5)
    CP(C, "pool", wq4P[:, :, :, 64:80], wq4[:, :, :, 80:96], [M.twuq], [M.twuqP])
    CP(C, "pool", wq4P[:, :, :, 80:96], wq4[:, :, :, 64:80], [M.twuq], [M.twuqP])
    MEMSET(C, "pool", M.V[:, :, 64:128], 1.0, [M.tV])
    for c in range(NCH):
        cs = slice(512 * c, 512 * c + 512)
        r0, tr0 = tmp32(C)
        r1, tr1 = tmp32(C)
        for m, (r, tr) in enumerate(((r0, tr0), (r1, tr1))):
            pb, tp = bank(C, "proj", [0, 1])
            for k in range(8):
                MM(C, pb[:], wv[:, k, 128 * m:128 * m + 128], C.hT[:, k, cs], k == 0, k == 7, [tw, C.TH[c]], [tp])
            CP(C, "act", r[:], pb[:], [tp], [tr])
            ACTF(C, M.sq[:, m, :], pb[:], AF.Square, [tp], [M.tsq])
        pb, tp = bank(C, "rms", [2, 3])
        for m in range(2):
            MM(C, pb[:], C.onesB[:, :], M.sq[:, m, :], m == 0, m == 1, [M.tsq, C.onesB], [tp])
        rt, trt = tmp32(C)
        ACTF(C, rt[:], pb[:], AF.Ln, [tp], [trt], bias=EPS, scale=1.0 / 256)
        ACTF(C, rt[:], rt[:], AF.Exp, [trt], [trt], scale=-0.5)
        for m, (r, tr) in enumerate(((r0, tr0), (r1, tr1))):
            STT(C, "dve", M.cqn[:, m, cs], r[:], C.gq[:, l, m:m + 1], rt[:], ALU.mult, ALU.mult,
                [tr, trt, C.gq], [M.tcqn])
        pb, tp = bank(C, "proj", [0, 1])
        for k in range(8):
            MM(C, pb[:], wv[:, k, 256:384], C.hT[:, k, cs], k == 0, k == 7, [tw, C.TH[c]], [tp])
        r, tr = tmp32(C)
        CP(C, "act", r[:], pb[:], [tp], [tr])
        ACTF(C, M.sq[:, 0, :], pb[:], AF.Square, [tp], [M.tsq])
        pb2, tp2 = bank(C, "rms", [2, 3])
        MM(C, pb2[:], C.onesB[:, :], M.sq[:, 0, :], True, True, [M.tsq, C.onesB], [tp2])
        rt, trt = tmp32(C)
        ACTF(C, rt[:], pb2[:], AF.Ln, [tp2], [trt], bias=EPS, scale=1.0 / 128)
        ACTF(C, rt[:], rt[:], AF.Exp, [trt], [trt], scale=-0.5)
        STT(C, "dve", M.ckvn[:, cs], r[:], C.gkv[:, l:l + 1], rt[:], ALU.mult, ALU.mult, [tr, trt, C.gkv], [M.tckvn])
        pa, tpa = bank(C, "proj", [0, 1])
        for k in range(8):
            MM(C, pa[0:96, :], wv[:, k, 320:416], C.hT[:, k, cs], k == 0, k == 7, [tw, C.TH[c]], [tpa])
        pbb, tpb = bank(C, "proj", [0, 1])
        for k in range(8):
            MM(C, pbb[0:96, :], M.wkp[:, k, :], C.hT[:, k, cs], k == 0, k == 7, [M.twkp, C.TH[c]], [tpb])
        ra, tra = tmp32(C)
        rb, trb = tmp32(C)
        TT(C, "dve", ra[64:96, :], pa[64:96, :], C.CA[64:96, cs], ALU.mult, [tpa, C.TCA], [tra])
        TT(C, "dve", rb[64:96, :], pbb[64:96, :], C.SA[64:96, cs], ALU.mult, [tpb, C.TSA], [trb])
        TT(C, "dve", M.kpeT[64:96, cs], ra[64:96, :], rb[64:96, :], ALU.add, [tra, trb], [M.tkpe])


def _mla_head(C, l, h):
    M = C.M
    for c in range(NCH):
        cs = slice(512 * c, 512 * c + 512)
        pb, tp = bank(C, "proj", [0, 1])
        MM(C, pb[0:64, :], M.wukv[:, 128 * h:128 * h + 64], M.ckvn[:, cs], True, True, [M.twukv, M.tckvn], [tp])
        CP(C, "act", M.KT[0:64, cs], pb[0:64, :], [tp], [M.tKT])
    CP(C, "pool", M.KT[64:96, :], M.kpeT[64:96, :], [M.tkpe], [M.tKT])
    for t4 in range(4):
        pb, tp = bank(C, "proj", [0, 1])
        for j in range(4):
            t = 4 * t4 + j
            MM(C, pb[:, 64 * j:64 * j + 64], M.ckvn[:, 128 * t:128 * t + 128], M.wukv[:, 128 * h + 64:128 * h + 128],
               True, True, [M.tckvn, M.twukv], [tp])
        CP(C, "dve", M.V[:, 4 * t4:4 * t4 + 4, 0:64], pb[:, 0:256].rearrange("p (a b) -> p a b", a=4), [tp], [M.tV])
    ch, po = _mix_dst(C, 384 + 64 * h)
    for c in range(NCH):
        cs = slice(512 * c, 512 * c + 512)
        pa, tpa = bank(C, "proj", [0, 1])
        for m in range(2):
            MM(C, pa[0:96, :], M.wuq[:, m, 96 * h:96 * h + 96], M.cqn[:, m, cs], m == 0, m == 1,
               [M.twuq, M.tcqn], [tpa])
        pbb, tpb = bank(C, "proj", [0, 1])
        for m in range(2):
            MM(C, pbb[0:96, :], M.wuqP[:, m, 96 * h:96 * h + 96], M.cqn[:, m, cs], m == 0, m == 1,
               [M.twuqP, M.tcqn], [tpb])
        CP(C, "act", M.QT[0:64, :], pa[0:64, :], [tpa], [M.tQT])
        ra, tra = tmp32(C)
        rb, trb = tmp32(C)
        TT(C, "dve", ra[64:96, :], pa[64:96, :], C.CA[64:96, cs], ALU.mult, [tpa, C.TCA], [tra])
        TT(C, "dve", rb[64:96, :], pbb[64:96, :], C.SA[64:96, cs], ALU.mult, [tpb, C.TSA], [trb])
        TT(C, "dve", M.QT[64:96, :], ra[64:96, :], rb[64:96, :], ALU.add, [tra, trb], [M.tQT])
        oa, toa = attn_chunk(C, c, lambda j: M.KT[0:96, 128 * j:128 * j + 128], M.tKT, 96, M.QT, M.tQT,
                             lambda j: M.V[:, j, :], M.tV, "causal", 96 ** -0.5, M.PT, M.tPT,
                             st_banks=(2, 3, 4, 5), depth=2)
        rc, trc = tmp32(C)
        ACTF(C, rc[0:64, :], oa[64:128, :], AF.Ln, [toa], [trc], bias=1e-30, scale=1.0)
        ACTF(C, rc[0:64, :], rc[0:64, :], AF.Exp, [trc], [trc], scale=-1.0)
        TT(C, "dve", C.mixT[po:po + 64, ch, cs], oa[0:64, :], rc[0:64, :], ALU.mult, [toa, trc], [C.TMIX[ch]])


def _wout(C, l, phase):
    w_out = C.dr["w_out"]
    for dq in range(4):
        ws, tw, gname = wslot(C)
        wv = ws[:, 0:1024].rearrange("p (k n) -> p k n", k=4)
        LDW(C, wv, w_out[l, 512 * phase:512 * phase + 512, 256 * dq:256 * dq + 256].rearrange(
            "(k p) n -> p k n", p=128), gname, [tw])
        for dd in range(2):
            d = 2 * dq + dd
            for c in range(NCH):
                cs = slice(512 * c, 512 * c + 512)
                pb, tp = bank(C, "proj", [0, 1])
                for k in range(4):
                    MM(C, pb[:], wv[:, k, 128 * dd:128 * dd + 128], C.mixT[:, k, cs], k == 0, k == 3,
                       [tw, C.TMIX[k]], [tp])
                TT(C, "dve", C.xT[:, d, cs], pb[:], C.xT[:, d, cs], ALU.add, [tp, C.TX[c]], [C.TX[c]])


def _tok0_prepass(C, n_seq, n_layers):
    mk = C.mk
    AR = C.arena[:].bitcast(F32)
    st = {"o": 0}
    tiles = []

    def alloc(n, name):
        ap = AR[:, st["o"]:st["o"] + n]
        st["o"] += n
        t = T(ap, name)
        tiles.append(t)
        return ap, t
    slot, tslot = [], []
    for i in range(2):
        a_, t_ = alloc(4096, "pslot%d" % i)
        slot.append(a_)
        tslot.append(t_)
    sel, tsel = alloc(1152, "sel18")
    X0, tX = alloc(16, "X0")
    X0 = X0.rearrange("p (k s) -> p k s", s=2)
    Hh, tH = alloc(16, "H0")
    Hh = Hh.rearrange("p (k s) -> p k s", s=2)
    SQ, tSQ = alloc(64, "SQ0")
    SQ = SQ.rearrange("p (k s) -> p k s", s=2)
    Aa, tA = alloc(64, "A0")
    Aa = Aa.rearrange("p (k s) -> p k s", s=2)
    PRn, tPRn = alloc(8, "PRn")
    PRn = PRn.rearrange("p (k s) -> p k s", s=2)
    SG, tSG = alloc(2, "SG")
    CK, tCK = alloc(2, "CK")
    CKN, tCKN = alloc(2, "CKN")
    R, tR = alloc(40, "R0")
    R = R.rearrange("p (k s) -> p k s", s=2)
    O, tO = alloc(10, "O0")
    O = O.rearrange("p (k s) -> p k s", s=2)
    CEN, tCEN = alloc(10, "CEN")
    CEN = CEN.rearrange("p (k s) -> p k s", s=2)
    RS_, tRS = alloc(10, "RSTD")
    RS_ = RS_.rearrange("p (k s) -> p k s", s=2)
    SGG, tSGG = alloc(10, "SGG")
    SGG = SGG.rearrange("p (k s) -> p k s", s=2)
    MIX, tMIX = alloc(32, "MIX0")
    MIX = MIX.rearrange("p (k s) -> p k s", s=2)
    sm, tsm = alloc(4, "psmall")
    tmp, ttmp = alloc(24, "ptmp")
    tmp3 = tmp.rearrange("p (k s) -> p k s", s=2)
    claim(C, tiles)
    C.wpi = 0
    dr = C.dr

    def wload(src_ap, view_fn):
        i = C.wpi % 2
        C.wpi += 1
        v = view_fn(slot[i])
        mk.dma("sp", v, src_ap, "pslot%d" % i, writes=[tslot[i]])
        return v, tslot[i]

    def mvg(lhs_fn, rhs_fn, nk, out, tout, reads):
        for k in range(nk):
            MM(C, out, lhs_fn(k), rhs_fn(k), k == 0, k == nk - 1, reads, [tout])

    def rms0(X, tXx, nk, rows, gain_fn, out, tout, n):
        TT(C, "dve", SQ[0:rows, 0:nk, :], X[0:rows, 0:nk, :], X[0:rows, 0:nk, :], ALU.mult, [tXx], [tSQ])
        mvg(lambda k: C.onesF[0:rows, 0:rows], lambda k: SQ[0:rows, k, :], nk, C.pb[7][0:rows, 0:2], C.TP[7],
            [C.onesF, tSQ])
        ACTF(C, sm[0:rows, 0:2], C.pb[7][0:rows, 0:2], AF.Sqrt, [C.TP[7]], [tsm], bias=EPS, scale=1.0 / n)
        RECIP(C, sm[0:rows, 0:2], sm[0:rows, 0:2], [tsm], [tsm])
        for k in range(nk):
            STT(C, "dve", out[0:rows, k, :], X[0:rows, k, :], gain_fn(k), sm[0:rows, 0:2], ALU.mult, ALU.mult,
                [tXx, tsm], [tout])

    MEMSET(C, "pool", AR[:, 8192 + 1152:st["o"]], 0.0, tiles[3:])
    o_, w_ = BLOB_OFF["sel18"]
    mk.dma("sp", sel, dr["blob"][:, o_:o_ + w_], "pconst", writes=[tsel])
    for s_ in range(n_seq):
        mk.dma("sp", X0[:, :, s_], dr["x"][s_, 0, :].rearrange("(c p) -> p c", p=128), "pconst", writes=[tX],
               allow_slow_non_contiguous=True)
    ps1, tp1 = C.pb[0], C.TP[0]
    ps2, tp2 = C.pb[1], C.TP[1]
    ps3, tp3 = C.pb[2], C.TP[2]
    psS, tpS = C.pb[3], C.TP[3]
    psD, tpD = C.pb[4], C.TP[4]
    psU, tpU = C.pb[5], C.TP[5]
    psV, tpV = C.pb[6], C.TP[6]
    win = dr["w_in"]
    for l in range(n_layers):
        rms0(X0, tX, 8, 128, lambda k: C.g1[:, l, k:k + 1], Hh, tH, float(D))
        hk = lambda k: Hh[:, k, :]
        last = (l == n_layers - 1)
        if not last:
            wv, tw = wload(win[l, :, OFF["vs"]:OFF["vs"] + 128].rearrange("(k p) n -> p k n", p=128),
                           lambda sl: sl[:, 0:1024].rearrange("p (k n) -> p k n", k=8))
            for g in range(2):
                mvg(lambda k, g=g: wv[:, k, 64 * g:64 * g + 64], hk, 8, ps1[0:64, 2 * g:2 * g + 2], tp1, [tw, tH])
            wv, tw = wload(win[l, :, OFF["vw"]:OFF["vw"] + 146].rearrange("(k p) n -> p k n", p=128),
                           lambda sl: sl[:, 0:8 * 146].rearrange("p (k n) -> p k n", k=8))
            for g in range(2):
                mvg(lambda k, g=g: wv[:, k, 64 * g:64 * g + 64], hk, 8, ps1[0:64, 4 + 2 * g:6 + 2 * g], tp1, [tw, tH])
            mvg(lambda k: wv[:, k, 128:146], hk, 8, ps1[0:18, 8:10], tp1, [tw, tH])
            CP(C, "dve", PRn[0:64, :, :], ps1[0:64, 0:8].rearrange("p (k s) -> p k s", s=2), [tp1], [tPRn])
            CP(C, "dve", SG[0:18, :], ps1[0:18, 8:10], [tp1], [tSG])
            ACTF(C, SG[0:18, :], SG[0:18, :], AF.Exp, [tSG], [tSG], scale=-1.0)
            TS(C, "dve", SG[0:18, :], SG[0:18, :], 1.0, None, ALU.add, None, [tSG], [tSG])
            RECIP(C, SG[0:18, :], SG[0:18, :], [tSG], [tSG])
            for h in range(6):
                for b in (1, 2):
                    j = 3 * h + b
                    ci = 2 * (2 * h + b - 1)
                    MM(C, ps2[0:64, ci:ci + 2], sel[0:18, 64 * j:64 * j + 64], SG[0:18, :], True, True, [tsel, tSG], [tp2])
            for h in range(6):
                g = h // 3
                c1 = 2 * (2 * h)
                TT(C, "dve", tmp3[0:64, 0, :], ps2[0:64, c1:c1 + 2], PRn[0:64, g, :], ALU.mult, [tp2, tPRn], [ttmp])
                TT(C, "dve", tmp3[0:64, 1, :], ps2[0:64, c1 + 2:c1 + 4], PRn[0:64, 2 + g, :], ALU.mult, [tp2, tPRn], [ttmp])
                TT(C, "dve", MIX[0:64, h, :], tmp3[0:64, 0, :], tmp3[0:64, 1, :], ALU.add, [ttmp], [tMIX])
            wv, tw = wload(win[l, :, OFF["ckv"]:OFF["ckv"] + 128].rearrange("(k p) n -> p k n", p=128),
                           lambda sl: sl[:, 0:1024].rearrange("p (k n) -> p k n", k=8))
            mvg(lambda k: wv[:, k, :], hk, 8, ps1[:, 10:12], tp1, [tw, tH])
            CP(C, "dve", CK[:, :], ps1[:, 10:12], [tp1], [tCK])
            CK3 = CK.rearrange("p (k s) -> p k s", s=2)
            CKN3 = CKN.rearrange("p (k s) -> p k s", s=2)
            rms0(CK3, tCK, 1, 128, lambda k: C.gkv[:, l:l + 1], CKN3, tCKN, 128.0)
            wv, tw = wload(dr["mla_w_ukv"][l], lambda sl: sl[:, 0:640])
            for h in range(5):
                MM(C, psV[0:64, 2 * h:2 * h + 2], wv[:, 128 * h + 64:128 * h + 128], CKN[:, 0:2], True, True, [tw, tCKN], [tpV])
            CP(C, "dve", MIX[0:64, 6:11, :], psV[0:64, 0:10].rearrange("p (k s) -> p k s", s=2), [tpV], [tMIX])
        for piece in range(3):
            n = 512 if piece < 2 else 256
            c0 = OFF["rq"] + 512 * piece
            wv, tw = wload(win[l, :, c0:c0 + n].rearrange("(k p) n -> p k n", p=128),
                           lambda sl, n=n: sl[:, 0:8 * n].rearrange("p (k n) -> p k n", k=8))
            for gi in range(8 * piece, 8 * piece + n // 64):
                off = 64 * gi - 512 * piece
                mvg(lambda k, off=off: wv[:, k, off:off + 64], hk, 8, ps3[0:64, 2 * gi:2 * gi + 2], tp3, [tw, tH])
        CP(C, "dve", R[0:64, :, :], ps3[0:64, 0:40].rearrange("p (k s) -> p k s", s=2), [tp3], [tR])
        TT(C, "dve", O[0:64, :, :], R[0:64, 0:5, :], R[0:64, 5:10, :], ALU.mult, [tR], [tO])
        MM(C, psS[0:64, 0:10], C.onesF[0:64, 0:64], O[0:64, :, :].rearrange("p k s -> p (k s)"), True, True,
           [C.onesF, tO], [tpS])
        CP(C, "dve", C.s00[0:64, 10 * l:10 * l + 10], psS[0:64, 0:10], [tpS], [C.Ts00])
        if last:
            continue
        TT(C, "dve", O[0:64, :, :], psS[0:64, 0:10].rearrange("p (k s) -> p k s", s=2), R[0:64, 10:15, :], ALU.mult,
           [tpS, tR], [tO])
        TS(C, "dve", O[0:64, :, :], O[0:64, :, :], 0.125, None, ALU.mult, None, [tO], [tO])
        MM(C, psS[0:64, 10:20], C.onesF[0:64, 0:64], O[0:64, :, :].rearrange("p k s -> p (k s)"), True, True,
           [C.onesF, tO], [tpS])
        STT(C, "dve", CEN[0:64, :, :], psS[0:64, 10:20].rearrange("p (k s) -> p k s", s=2), -1.0 / 64, O[0:64, :, :],
            ALU.mult, ALU.add, [tpS, tO], [tCEN])
        TT(C, "dve", O[0:64, :, :], CEN[0:64, :, :], CEN[0:64, :, :], ALU.mult, [tCEN], [tO])
        MM(C, C.pb[7][0:64, 2:12], C.onesF[0:64, 0:64], O[0:64, :, :].rearrange("p k s -> p (k s)"), True, True,
           [C.onesF, tO], [C.TP[7]])
        ACTF(C, RS_[0:64, :, :], C.pb[7][0:64, 2:12].rearrange("p (k s) -> p k s", s=2), AF.Sqrt, [C.TP[7]], [tRS],
             bias=EPS, scale=1.0 / 64)
        RECIP(C, RS_[0:64, :, :], RS_[0:64, :, :], [tRS], [tRS])
        ACTF(C, SGG[0:64, :, :], R[0:64, 15:20, :], AF.Exp, [tR], [tSGG], scale=-1.0)
        TS(C, "dve", SGG[0:64, :, :], SGG[0:64, :, :], 1.0, None, ALU.add, None, [tSGG], [tSGG])
        RECIP(C, SGG[0:64, :, :], SGG[0:64, :, :], [tSGG], [tSGG])
        TT(C, "dve", SGG[0:64, :, :], SGG[0:64, :, :], R[0:64, 15:20, :], ALU.mult, [tSGG, tR], [tSGG])
        for h in range(5):
            STT(C, "dve", CEN[0:64, h, :], CEN[0:64, h, :], C.gnT[:, 5 * l + h:5 * l + h + 1], RS_[0:64, h, :],
                ALU.mult, ALU.mult, [tCEN, tRS, C.gnT], [tCEN])
        TT(C, "dve", MIX[0:64, 11:16, :], CEN[0:64, :, :], SGG[0:64, :, :], ALU.mult, [tCEN, tSGG], [tMIX])
        for q4 in range(4):
            wv, tw = wload(dr["w_out"][l, :, 256 * q4:256 * q4 + 256].rearrange("(j p) n -> p j n", p=64),
                           lambda sl: sl[0:64, :].rearrange("p (j n) -> p j n", j=16))
            for oo in range(2):
                oc = 2 * q4 + oo
                mvg(lambda j, oo=oo: wv[:, j, 128 * oo:128 * oo + 128], lambda j: MIX[0:64, j, :], 16,
                    psD[:, 2 * oc:2 * oc + 2], tpD, [tw, tMIX])
        TT(C, "dve", X0, X0, psD[:, 0:16].rearrange("p (k s) -> p k s", s=2), ALU.add, [tpD, tX], [tX])
        rms0(X0, tX, 8, 128, lambda k: C.g2[:, l, k:k + 1], Hh, tH, float(D))
        for fb in range(8):
            wv, tw = wload(dr["w_up"][l, :, 512 * fb:512 * fb + 512].rearrange("(k p) n -> p k n", p=128),
                           lambda sl: sl[:, :].rearrange("p (k n) -> p k n", k=8))
            for fi in range(4):
                f = 4 * fb + fi
                mvg(lambda k, fi=fi: wv[:, k, 128 * fi:128 * fi + 128], hk, 8, psU[:, 2 * f:2 * f + 2], tpU, [tw, tH])
        ACTF(C, Aa, psU[:, 0:64].rearrange("p (k s) -> p k s", s=2), AF.Relu, [tpU], [tA])
        TT(C, "dve", Aa, Aa, Aa, ALU.mult, [tA], [tA])
        MEMSET(C, "dve", psD[:, 16:32], 0.0, [tpD])
        for fs in range(8):
            wv, tw = wload(dr["w_down"][l, 512 * fs:512 * fs + 512, :].rearrange("(k p) n -> p k n", p=128),
                           lambda sl: sl[:, :].rearrange("p (k n) -> p k n", k=4))
            for oc in range(8):
                for kk in range(4):
                    MM(C, psD[:, 16 + 2 * oc:18 + 2 * oc], wv[:, kk, 128 * oc:128 * oc + 128], Aa[:, 4 * fs + kk, :],
                       False, False, [tw, tA], [tpD])
        TT(C, "dve", X0, X0, psD[:, 16:32].rearrange("p (k s) -> p k s", s=2), ALU.add, [tpD, tX], [tX])


def _tok0_h(C, l):
    h0, th = C.h0, C.Th0
    x0 = C.xT[:, :, 0]
    ACTF(C, h0[:, 11:19], x0, AF.Square, [C.TX[0]], [th], accum_out=h0[:, 9:10])
    pb, tp = bank(C, "rms", [2, 3])
    MM(C, pb[:, 0:2], C.onesF[:, :], h0[:, 9:11], True, True, [C.onesF, th], [tp])
    ACTF(C, h0[:, 19:20], pb[:, 0:1], AF.Sqrt, [tp], [th], bias=EPS, scale=1.0 / D)
    RECIP(C, h0[:, 19:20], h0[:, 19:20], [th], [th])
    STT(C, "dve", h0[:, 0:8], x0, h0[:, 19:20], C.g1[:, l, :], ALU.mult, ALU.mult, [C.TX[0], th, C.g1], [th])


def _mixer(C, l, mixers):
    claim(C, C.TMIX)
    if "nsa" in mixers:
        _nsa(C, l, 0)
        _nsa(C, l, 1)
    else:
        for ch in range(3):
            MEMSET(C, "pool", C.mixT[:, ch, :], 0.0, [C.TMIX[ch]])
    if "mla" in mixers:
        _mla_pre(C, l)
        _mla_head(C, l, 0)
        _mla_head(C, l, 1)
    else:
        MEMSET(C, "pool", C.mixT[:, 3, :], 0.0, [C.TMIX[3]])
    _wout(C, l, 0)
    if "mla" in mixers:
        for h in (2, 3, 4):
            _mla_head(C, l, h)
    else:
        MEMSET(C, "pool", C.mixT[:, 0, :], 0.0, [C.TMIX[0]])
        MEMSET(C, "pool", C.mixT[0:64, 1, :], 0.0, [C.TMIX[1]])
    if "ret" in mixers:
        _ret(C, l)
    else:
        MEMSET(C, "pool", C.mixT[64:128, 1, :], 0.0, [C.TMIX[1]])
        for ch in (2, 3):
            MEMSET(C, "pool", C.mixT[:, ch, :], 0.0, [C.TMIX[ch]])
    _wout(C, l, 1)


def proj_rope(C, c, wa_fn, wb_fn, M, wts, dst, tdst, tabC, tCt, tabS, tSt, nr=16):
    cs = slice(512 * c, 512 * c + 512)
    pa, tpa = bank(C, "proj", [0, 1])
    for k in range(8):
        MM(C, pa[0:M, :], wa_fn(k), C.hT[:, k, cs], k == 0, k == 7, wts + [C.TH[c]], [tpa])
    pb, tpb = bank(C, "proj", [0, 1])
    for k in range(8):
        MM(C, pb[0:nr, :], wb_fn(k), C.hT[:, k, cs], k == 0, k == 7, wts + [C.TH[c]], [tpb])
    CP(C, "dve", dst[0:M, :], pa[0:M, :], [tpa], [tdst])
    ra, tra = tmp32(C)
    rb, trb = tmp32(C)
    TT(C, "dve", ra[0:nr, :], pa[0:nr, :], tabC[0:nr, cs], ALU.mult, [tpa, tCt], [tra])
    TT(C, "dve", rb[0:nr, :], pb[0:nr, :], tabS[0:nr, cs], ALU.mult, [tpb, tSt], [trb])
    TT(C, "pool", dst[0:nr, :], ra[0:nr, :], rb[0:nr, :], ALU.add, [tra, trb], [tdst])


def _nsa(C, l, g):
    mk = C.mk
    win = C.dr["w_in"]
    wa = WA(C)
    KVc, tKVc = wa.bf(S, "KVc")
    KTs, tKTs = wa.bf(S, "KTs")
    KTw, tKTw = wa.bf(S, "KTw")
    Vs, tVs = wa.bf(S, "Vs")
    Vs = Vs.rearrange("p (t n) -> p t n", t=16)
    Vw, tVw = wa.bf(S, "Vw")
    Vw = Vw.rearrange("p (t n) -> p t n", t=16)
    QT, tQT = [], []
    for r in range(3):
        a_, t_ = wa.bf(512, "nQT%d" % r)
        QT.append(a_)
        tQT.append(t_)
    PT, tPT = [], []
    for i in range(2):
        a_, t_ = wa.bf(512, "nPT%d" % i)
        PT.append(a_)
        tPT.append(t_)
    acc, tacc = [], []
    for r in range(3):
        a_, t_ = wa.f32(512, "nacc%d" % r)
        acc.append(a_)
        tacc.append(t_)
    kcT, tkcT = wa.bf(128, "kcT")
    Vc, tVc = wa.bf(128, "Vcaug")
    hid, thid = wa.bf(128, "hid")
    Wn2, tWn2 = wa.bf(640, "Wn2")
    Wn2 = Wn2.rearrange("p (k n) -> p k n", k=8)
    WP, tWP = wa.bf(768, "nWP")
    WP = WP.rearrange("p (k n) -> p k n", k=8)
    glog, tglog = wa.bf(512, "glog")
    selb, tselb = wa.bf(128, "selb")
    w2, tw2 = wa.bf(128, "w2kv")
    posT, tposT = wa.bf(64, "posT")
    zt, tzt = wa.f32(128, "gz")
    ut, tut = wa.f32(128, "gu")
    sm, tsm = wa.f32(128, "nsmall")
    bsb, tbsb = wa.f32(4, "nbias")
    claim(C, wa.tiles)
    C.pti = 0
    ws, tw, gname = wslot(C)
    Wn = ws[:].rearrange("p (k n) -> p k n", k=8)
    cols = [(OFF["q"] + 192 * g, 192, 0), (OFF["kc"] + 64 * g, 64, 192), (OFF["vc"] + 64 * g, 64, 256),
            (OFF["ks"] + 64 * g, 64, 320), (OFF["kw"] + 64 * g, 64, 384), (OFF["vs"] + 64 * g, 64, 448)]
    for (src, n, dst) in cols:
        LDW(C, Wn[:, :, dst:dst + n], win[l, :, src:src + n].rearrange("(k p) n -> p k n", p=128), gname, [tw])
    LDW(C, Wn2[:, :, 0:64], win[l, :, OFF["vw"] + 64 * g:OFF["vw"] + 64 * g + 64].rearrange("(k p) n -> p k n", p=128),
        "nsaw", [tWn2])
    LDW(C, Wn2[:, :, 64:73], win[l, :, OFF["gate"] + 9 * g:OFF["gate"] + 9 * g + 9].rearrange(
        "(k p) n -> p k n", p=128), "nsaw", [tWn2], allow_slow_non_contiguous=True)
    ws1, tw1, gname1 = wslot(C)
    w1 = ws1[:].rearrange("p (l h) -> p l h", l=32)
    LDW(C, w1[0:64], C.dr["cmp_w1_k"][l].rearrange("(l d) h -> d l h", d=64), gname1, [tw1])
    LDW(C, w1[64:128], C.dr["cmp_w1_v"][l].rearrange("(l d) h -> d l h", d=64), gname1, [tw1])
    LDW(C, w2[:, 0:64], C.dr["cmp_w2_k"][l], "nsaw", [tw2])
    LDW(C, w2[:, 64:128], C.dr["cmp_w2_v"][l], "nsaw", [tw2])
    MEMSET(C, "pool", posT, 0.0, [tposT])
    LDW(C, posT[0:64, 0:32], C.dr["cmp_pos_k"][l].rearrange("l d -> d l"), "nsaw", [tposT],
        allow_slow_non_contiguous=True)
    LDW(C, posT[64:128, 0:32], C.dr["cmp_pos_v"][l].rearrange("l d -> d l"), "nsaw", [tposT],
        allow_slow_non_contiguous=True)
    o, w_ = BLOB_OFF["E"]
    LDW(C, KTs[64:96, :], C.dr["blob"][64:96, o:o + w_], "nsaw", [tKTs])
    for i, off in enumerate((0, 64, 128, 192, 320, 384)):
        CP(C, "pool", WP[:, :, 16 * i:16 * i + 8], Wn[:, :, off + 8:off + 16], [tw], [tWP])
        CP(C, "pool", WP[:, :, 16 * i + 8:16 * i + 16], Wn[:, :, off:off + 8], [tw], [tWP])
    MEMSET(C, "pool", Vs[:, :, 64:128], 1.0, [tVs])
    MEMSET(C, "pool", Vw[:, :, 64:128], 1.0, [tVw])
    MEMSET(C, "pool", Vc[:, 64:128], 1.0, [tVc])
    MEMSET(C, "pool", Vc[:, 0:64], 0.0, [tVc])
    MEMSET(C, "pool", selb, 0.0, [tselb])
    for c in range(NCH):
        cs = slice(512 * c, 512 * c + 512)
        proj_rope(C, c, lambda k: Wn[:, k, 192:320], lambda k: WP[:, k, 48:64], 128, [tw, tWP], KVc[:, cs], tKVc,
                  C.CA, C.TCA, C.SA, C.TSA)
        proj_rope(C, c, lambda k: Wn[:, k, 320:384], lambda k: WP[:, k, 64:80], 64, [tw, tWP], KTs[:, cs], tKTs,
                  C.CA, C.TCA, C.SA, C.TSA)
        proj_rope(C, c, lambda k: Wn[:, k, 384:448], lambda k: WP[:, k, 80:96], 64, [tw, tWP], KTw[:, cs], tKTw,
                  C.CA, C.TCA, C.SA, C.TSA)
        pv, tpv = bank(C, "proj", [0, 1])
        for tl in range(4):
            t = 4 * c + tl
            for k in range(8):
                MM(C, pv[:, 128 * tl:128 * tl + 64], C.hT[:, k, 128 * t:128 * t + 128], Wn[:, k, 448:512], k == 0, k == 7,
                   [tw, C.TH[c]], [tpv])
            for k in range(8):
                MM(C, pv[:, 128 * tl + 64:128 * tl + 128], C.hT[:, k, 128 * t:128 * t + 128], Wn2[:, k, 0:64], k == 0,
                   k == 7, [tWn2, C.TH[c]], [tpv])
        pv3 = pv[:].rearrange("p (a b) -> p a b", a=4)
        CP(C, "act", Vs[:, 4 * c:4 * c + 4, 0:64], pv3[:, :, 0:64], [tpv], [tVs])
        CP(C, "act", Vw[:, 4 * c:4 * c + 4, 0:64], pv3[:, :, 64:128], [tpv], [tVw])
    for kind in (0, 1):
        r0 = 64 * kind
        pbias, tpbias = bank(C, "oa", [6, 7])
        for li in range(32):
            MM(C, pbias[:, 0:2], w1[r0:r0 + 64, li, :], posT[r0:r0 + 64, li:li + 2], li == 0, li == 31,
               [tw1, tposT], [tpbias])
        CP(C, "dve", bsb[:, 0:2], pbias[:, 0:2], [tpbias], [tbsb])
        ph, tph = bank(C, "st", [4, 5])
        for li in range(32):
            MM(C, ph[:, 0:127], w1[r0:r0 + 64, li, :], KVc[r0:r0 + 64, li:li + 2017:16], li == 0, li == 31,
               [tw1, tKVc], [tph])
        ACTF(C, zt[:, 0:127], ph[:, 0:127], AF.Identity, [tph, tbsb], [tzt], bias=bsb[:, 0:1], scale=1.0)
        TT(C, "pool", ut[:, 0:127], zt[:, 0:127], zt[:, 0:127], ALU.mult, [tzt], [tut])
        TS(C, "pool", ut[:, 0:127], ut[:, 0:127], 0.044715, None, ALU.mult, None, [tut], [tut])
        TS(C, "pool", ut[:, 0:127], ut[:, 0:127], 1.0, None, ALU.add, None, [tut], [tut])
        TT(C, "pool", ut[:, 0:127], ut[:, 0:127], zt[:, 0:127], ALU.mult, [tut, tzt], [tut])
        ACTF(C, ut[:, 0:127], ut[:, 0:127], AF.Exp, [tut], [tut], scale=-1.5957691216057308)
        TS(C, "pool", ut[:, 0:127], ut[:, 0:127], 1.0, None, ALU.add, None, [tut], [tut])
        RECIP(C, ut[:, 0:127], ut[:, 0:127], [tut], [tut])
        TT(C, "dve", hid[:, 0:127], zt[:, 0:127], ut[:, 0:127], ALU.mult, [tzt, tut], [thid])
        pk, tpk = bank(C, "oa", [6, 7])
        if kind == 0:
            MM(C, pk[0:64, 0:127], w2[:, 0:64], hid[:, 0:127], True, True, [tw2, thid], [tpk])
            CP(C, "act", kcT[0:64, 0:127], pk[0:64, 0:127], [tpk], [tkcT])
        else:
            MM(C, pk[0:127, 0:64], hid[:, 0:127], w2[:, 64:128], True, True, [tw2, thid], [tpk])
            CP(C, "act", Vc[0:127, 0:64], pk[0:127, 0:64], [tpk], [tVc])

    def combine(oa, toa, r, b, c, mode):
        cs = slice(512 * c, 512 * c + 512)
        rc, trc = tmp32(C)
        ACTF(C, rc[0:64, :], oa[64:128, :], AF.Ln, [toa], [trc], bias=1e-30, scale=1.0)
        j = 3 * r + b
        pg2, tpg2 = bank(C, "g", [2])
        MM(C, pg2[0:64, :], C.sel9B[0:9, 64 * j:64 * j + 64], glog[0:9, :], True, True, [C.sel9B, tglog], [tpg2])
        gs, tgs = tmp32(C)
        ACTF(C, gs[0:64, :], pg2[0:64, :], AF.Exp, [tpg2], [tgs], scale=-1.0)
        ACTF(C, gs[0:64, :], gs[0:64, :], AF.Ln, [tgs], [tgs], bias=1.0, scale=1.0)
        TT(C, "pool", rc[0:64, :], rc[0:64, :], gs[0:64, :], ALU.add, [trc, tgs], [trc])
        ACTF(C, rc[0:64, :], rc[0:64, :], AF.Exp, [trc], [trc], scale=-1.0)
        if mode == 0:
            TT(C, "dve", acc[r][0:64, :], oa[0:64, :], rc[0:64, :], ALU.mult, [toa, trc], [tacc[r]])
        elif mode == 1:
            TT(C, "dve", gs[0:64, :], oa[0:64, :], rc[0:64, :], ALU.mult, [toa, trc, tgs], [tgs])
            TT(C, "pool", acc[r][0:64, :], acc[r][0:64, :], gs[0:64, :], ALU.add, [tacc[r], tgs], [tacc[r]])
        else:
            ch, po = _mix_dst(C, 64 * (3 * g + r))
            TT(C, "dve", gs[0:64, :], oa[0:64, :], rc[0:64, :], ALU.mult, [toa, trc, tgs], [tgs])
            TT(C, "dve", C.mixT[po:po + 64, ch, cs], acc[r][0:64, :], gs[0:64, :], ALU.add, [tacc[r], tgs],
               [C.TMIX[ch]])

    for c in range(NCH):
        cs = slice(512 * c, 512 * c + 512)
        for r in range(3):
            proj_rope(C, c, lambda k, r=r: Wn[:, k, 64 * r:64 * r + 64], lambda k, r=r: WP[:, k, 16 * r:16 * r + 16], 64,
                      [tw, tWP], QT[r], tQT[r], C.CA, C.TCA, C.SA, C.TSA)
            if c < 2:
                MEMSET(C, "pool", QT[r][64:96, :], 0.0, [tQT[r]])
        pg, tpg = bank(C, "g", [2])
        for k in range(8):
            MM(C, pg[0:9, :], Wn2[:, k, 64:73], C.hT[:, k, cs], k == 0, k == 7, [tWn2, C.TH[c]], [tpg])
        CP(C, "act", glog[0:9, :], pg[0:9, :], [tpg], [tglog])
        impb, timp = C.pb[3], C.TP[3]
        for r in range(3):
            st, tst = bank(C, "st", [4, 5])
            MM(C, st[0:127, :], kcT[0:64, 0:127], QT[r][0:64, :], True, False, [tkcT, tQT[r]], [tst])
            MM(C, st[0:127, :], C.identB[0:127, 0:127], C.cmpbB[0:127, cs], False, True, [C.identB, C.cmpbB], [tst])
            pi = C.pti % 2
            C.pti += 1
            ACTF(C, PT[pi][0:127, :], st[0:127, :], AF.Exp, [tst], [tPT[pi]], scale=0.125)
            oa, toa = bank(C, "oa", [6, 7])
            MM(C, oa[:, :], Vc[0:127, :], PT[pi][0:127, :], True, True, [tVc, tPT[pi]], [toa])
            if c >= 2:
                for tl in range(4):
                    o0 = (tl * 3 + r) * 33
                    MM(C, impb[:, o0:o0 + 33], PT[pi][0:127, 128 * tl:128 * tl + 128], C.ovl1B[0:127, 0:33], True, True,
                       [tPT[pi], C.ovl1B], [timp])
            combine(oa, toa, r, 0, c, 0)
        if c >= 2:
            for tl in range(4):
                kq = 4 * c + tl - 8
                bb = sm[:, 0:32]
                t2 = sm[:, 32:64]
                m1 = sm[:, 64:72]
                m2 = sm[:, 72:80]
                dn = sm[:, 80:83]
                keep = C.keepF[:, 32 * kq:32 * kq + 32]
                addt = C.addtF[:, 32 * kq:32 * kq + 32]
                TS(C, "dve", dn, impb[:, tl * 99 + 32:tl * 99 + 99:33], 1e-30, None, ALU.max, None, [timp], [tsm])
                RECIP(C, dn, dn, [tsm], [tsm])
                for r in range(3):
                    o0 = (tl * 3 + r) * 33
                    if r == 0:
                        STT(C, "dve", bb, impb[:, o0:o0 + 32], dn[:, 0:1], keep, ALU.mult, ALU.mult,
                            [timp, tsm, C.keepF], [tsm])
                    else:
                        STT(C, "dve", t2, impb[:, o0:o0 + 32], dn[:, r:r + 1], keep, ALU.mult, ALU.mult,
                            [timp, tsm, C.keepF], [tsm])
                        TT(C, "dve", bb, bb, t2, ALU.add, [tsm], [tsm])
                TT(C, "dve", bb, bb, addt, ALU.add, [tsm, C.addtF], [tsm])
                mk.op("dve", lambda e, m1=m1, bb=bb: e.max(out=m1, in_=bb), [tsm], [tsm])
                mk.op("dve", lambda e, m1=m1, bb=bb, t2=t2: e.match_replace(out=t2, in_to_replace=m1, in_values=bb,
                                                                        imm_value=-3.0e38), [tsm], [tsm])
                mk.op("dve", lambda e, m2=m2, t2=t2: e.max(out=m2, in_=t2), [tsm], [tsm])
                STT(C, "dve", t2, bb, m2[:, 7:8], C.bigF[:, 0:32], ALU.subtract, ALU.mult, [tsm, C.bigF], [tsm])
                TS(C, "dve", t2, t2, 0.0, None, ALU.min, None, [tsm], [tsm])
                TS(C, "dve", selb[:, 64:96], t2, -1.0, None, ALU.max, None, [tsm], [tselb])
                ptr, tptr = bank(C, "proj", [0, 1])
                ptb = ptr[:].bitcast(BF16)
                TR(C, ptb[0:96, 0:128], selb[:, 0:96], C.identB[:, :], [tselb, C.identB], [tptr])
                for r in range(3):
                    CP(C, "act", QT[r][64:96, 128 * tl:128 * tl + 128], ptb[64:96, 0:128], [tptr], [tQT[r]])
        for r in range(3):
            oa, toa = attn_chunk(C, c, lambda j: KTs[0:96, 128 * j:128 * j + 128], tKTs, 96, QT[r], tQT[r],
                                 lambda j: Vs[:, j, :], tVs, "causal", 0.125, PT, tPT)
            combine(oa, toa, r, 1, c, 1)
            oa, toa = attn_chunk(C, c, lambda j: KTw[0:64, 128 * j:128 * j + 128], tKTw, 64, QT[r], tQT[r],
                                 lambda j: Vw[:, j, :], tVw, "window", 0.125, PT, tPT)
            combine(oa, toa, r, 2, c, 2)


def _ret(C, l):
    mk = C.mk
    RS = 3
    lg = np.log(1.0 - 2.0 ** (-5.0 - np.arange(5, dtype=np.float64)))
    for h in range(5):
        wa = WA(C)
        qT, tq = wa.bf(S, "rqT")
        kT, tk = wa.bf(S, "rkT")
        qdr = [wa.bf(128, "rqd0"), wa.bf(128, "rqd1")]
        v, tv = wa.bf(1024, "rv")
        v = v.rearrange("p (t n) -> p t n", t=16)
        gt, tg = wa.bf(1024, "rg")
        gt = gt.rearrange("p (t n) -> p t n", t=16)
        W, tW = wa.bf(8 * 256, "rW")
        W = W.rearrange("p (i k n) -> p i k n", i=4, k=8)
        WP, tWP = wa.bf(2 * 8 * 128, "rWQ")
        WP = WP.rearrange("p (i k n) -> p i k n", i=2, k=8)
        scmA, tscmA = wa.bf(2048, "rscmA")
        scmA = scmA.rearrange("p (n i) -> p n i", n=16)
        kwA, tkwA = wa.bf(1024, "rkwA")
        kwA = kwA.rearrange("p (n i) -> p n i", n=16)
        oA, toA = wa.f32(1024, "roA")
        oA = oA.rearrange("p (n i) -> p n i", n=16)
        ytA, tytA = wa.bf(2048, "rytA")
        ytA = ytA.rearrange("p (n i) -> p n i", n=16)
        sbf, tsbf = wa.bf(64, "rsbf")
        sf, tsf = wa.f32(64, "rsf")
        junk, tjunk = wa.f32(64, "rjunk")
        stt, tstt = wa.f32(64, "rstat")
        claim(C, wa.tiles)
        win = C.dr["w_in"]
        ch, po = _mix_dst(C, 704 + 64 * h)
        if RS <= -2:
            MEMSET(C, "pool", C.mixT[po:po + 64, ch, :], 0.0, [C.TMIX[ch]])
            continue
        for i, nm in enumerate(("rq", "rk", "rv", "rg")):
            LDW(C, W[:, i, :, :], win[l, :, OFF[nm] + 64 * h:OFF[nm] + 64 * h + 64].rearrange(
                "(k p) n -> p k n", p=128), "retw", [tW])
        for i in range(2):
            CP(C, "pool", WP[:, i, :, 0:64], W[:, i, :, :], [tW], [tWP])
            CP(C, "pool", WP[:, i, :, 64:96], W[:, i, :, 32:64], [tW], [tWP])
            CP(C, "pool", WP[:, i, :, 96:128], W[:, i, :, 0:32], [tW], [tWP])
        MEMSET(C, "pool", ytA, 0.0, [tytA])
        if RS == -1:
            MEMSET(C, "pool", C.mixT[po:po + 64, ch, :], 0.0, [C.TMIX[ch]])
            continue
        for c in range(NCH):
            cs = slice(512 * c, 512 * c + 512)
            for i, (dst, tdst) in enumerate(((qT, tq), (kT, tk))):
                pa, tpa = bank(C, "proj", [0, 1])
                for k in range(8):
                    MM(C, pa[:, :], WP[:, i, k, :], C.hT[:, k, cs], k == 0, k == 7, [tWP, C.TH[c]], [tpa])
                ra, tra = tmp32(C)
                rb, trb = tmp32(C)
                TT(C, "dve", ra[0:64, :], pa[0:64, :], C.CR[0:64, cs], ALU.mult, [tpa, C.TCR], [tra])
                TT(C, "dve", rb[0:64, :], pa[64:128, :], C.SR[64:128, cs], ALU.mult, [tpa, C.TSR], [trb])
                TT(C, "pool", dst[0:64, cs], ra[0:64, :], rb[0:64, :], ALU.add, [tra, trb], [tdst])
            if RS != 0:
                pv, tpv = bank(C, "proj", [0, 1])
                for tl in range(4):
                    t = 4 * c + tl
                    for k in range(8):
                        MM(C, pv[:, 128 * tl:128 * tl + 64], C.hT[:, k, 128 * t:128 * t + 128], W[:, 2, k, :], k == 0,
                           k == 7, [tW, C.TH[c]], [tpv])
                    for k in range(8):
                        MM(C, pv[:, 128 * tl + 64:128 * tl + 128], C.hT[:, k, 128 * t:128 * t + 128], W[:, 3, k, :],
                           k == 0, k == 7, [tW, C.TH[c]], [tpv])
                pv3 = pv[:].rearrange("p (a b) -> p a b", a=4)
                CP(C, "act", v[:, 4 * c:4 * c + 4, :], pv3[:, :, 0:64], [tpv], [tv])
                ge, tge = tmp32(C)
                ge3 = ge[:, 0:256].rearrange("p (a b) -> p a b", a=4)
                ACTF(C, ge3, pv3[:, :, 64:128], AF.Exp, [tpv], [tge], scale=-1.0)
                ACTF(C, ge[:, 0:256], ge[:, 0:256], AF.Ln, [tge], [tge], bias=1.0, scale=1.0)
                ACTF(C, ge[:, 0:256], ge[:, 0:256], AF.Exp, [tge], [tge], scale=-1.0)
                if True:
                    TT(C, "dve", gt[:, 4 * c:4 * c + 4, :], pv3[:, :, 64:128], ge3, ALU.mult, [tpv, tge], [tg])
        if RS in (0, 5, 6, 7, 8, 9):
            MEMSET(C, "pool", C.mixT[po:po + 64, ch, :], 0.0, [C.TMIX[ch]])
            continue
        cd = float(np.exp(128.0 * lg[h]))
        ix = (l * 5 + h) * 2 + C.cur_seq
        for n in range(16):
            ns = slice(128 * n, 128 * n + 128)
            ps, tps = bank(C, "st", [4, 5])
            MM(C, ps[:, 0:128], kT[0:64, ns], qT[0:64, ns], True, True, [tk, tq], [tps])
            TT(C, "dve", scmA[:, n, :], ps[:, 0:128], C.intraTF[:, 128 * h:128 * h + 128], ALU.mult, [tps, C.intraTF],
               [tscmA])
            if n == 0:
                TS(C, "dve", scmA[0:1, 0, 0:1], C.s00[0:1, ix:ix + 1], 0.125, None, ALU.mult, None, [C.Ts00], [tscmA])
            if n < 15:
                pk, tpk = bank(C, "st", [4, 5])
                MM(C, pk[:, 0:64], kT[0:64, ns], C.identB[0:64, 0:64], True, True, [tk, C.identB], [tpk])
                ACTF(C, kwA[:, n, :], pk[:, 0:64], AF.Identity, [tpk, C.wdF], [tkwA], scale=C.wdF[:, h:h + 1])
        for n in range(16):
            ns = slice(128 * n, 128 * n + 128)
            po_, tpo = bank(C, "oa", [6, 7])
            MM(C, po_[:, 0:64], scmA[:, n, :], v[:, n, :], True, n == 0, [tscmA, tv], [tpo])
            if n > 0:
                qdn, tqdn = qdr[n % 2]
                TT(C, "pool", qdn[0:64, :], qT[0:64, ns], C.rdB[0:64, 128 * h:128 * h + 128], ALU.mult, [tq, C.rdB], [tqdn])
                MM(C, po_[:, 0:64], qdn[0:64, :], sbf[0:64, :], False, True, [tqdn, tsbf], [tpo])
            if n < 15:
                pst, tpst = bank(C, "rms", [2, 3])
                MM(C, pst[0:64, 0:64], kwA[:, n, :], v[:, n, :], True, True, [tkwA, tv], [tpst])
                if n == 0:
                    CP(C, "dve", sf[0:64, :], pst[0:64, 0:64], [tpst], [tsf])
                else:
                    STT(C, "dve", sf[0:64, :], sf[0:64, :], cd, pst[0:64, 0:64], ALU.mult, ALU.add, [tsf, tpst], [tsf])
                CP(C, "dve", sbf[0:64, :], sf[0:64, :], [tsf], [tsbf])
            ACTF(C, oA[:, n, :], po_[:, 0:64], AF.Identity, [tpo], [toA, tstt], accum_out=stt[:, n:n + 1])
            ACTF(C, junk, po_[:, 0:64], AF.Square, [tpo], [tjunk, tstt], accum_out=stt[:, 16 + n:17 + n])
        mean = stt[:, 32:48]
        var = stt[:, 48:64]
        TS(C, "dve", mean, stt[:, 0:16], 1.0 / 64, None, ALU.mult, None, [tstt], [tstt])
        TT(C, "dve", var, mean, mean, ALU.mult, [tstt], [tstt])
        STT(C, "dve", var, stt[:, 16:32], 1.0 / 64, var, ALU.mult, ALU.subtract, [tstt], [tstt])
        ACTF(C, var, var, AF.Ln, [tstt], [tstt], bias=EPS, scale=1.0)
        ACTF(C, var, var, AF.Exp, [tstt], [tstt], scale=-0.5)
        TT(C, "dve", mean, mean, var, ALU.mult, [tstt], [tstt])
        TT(C, "dve", oA, oA, var.unsqueeze(2).to_broadcast([128, 16, 64]), ALU.mult, [toA, tstt], [toA])
        TT(C, "dve", oA, oA, mean.unsqueeze(2).to_broadcast([128, 16, 64]), ALU.subtract, [toA, tstt], [toA])
        TT(C, "pool", ytA[:, :, po:po + 64], oA, gt, ALU.mult, [toA, tg], [tytA])
        for n in range(16):
            ns = slice(128 * n, 128 * n + 128)
            ptr, tptr = bank(C, "proj", [0, 1])
            ptb = ptr[:].bitcast(BF16)
            TR(C, ptb[:, 0:128], ytA[:, n, :], C.identB[:, :], [tytA, C.identB], [tptr])
            ACTF(C, C.mixT[po:po + 64, ch, ns], ptb[po:po + 64, 0:128], AF.Identity, [tptr, C.gnT2], [C.TMIX[ch]],
                 scale=C.gnT2[po:po + 64, 5 * l + h:5 * l + h + 1])


_CACHE = {}


def kernel(**inputs):
    n_cores = 8
    x = np.ascontiguousarray(np.asarray(inputs["x"], dtype=np.float32))
    pos = np.ascontiguousarray(np.asarray(inputs["positions"], dtype=np.int32))
    if "nc" not in _CACHE:
        _CACHE["nc"] = build()[0]
        _CACHE["blob"] = make_blob()[0]
    nc = _CACHE["nc"]
    w = {n: np.ascontiguousarray(np.asarray(inputs[n], dtype=np.float32)) for n in WNAMES}
    in_maps = []
    for i in range(n_cores):
        m = {"x": x[2 * i:2 * i + 2], "pos": pos[2 * i:2 * i + 2], "blob": _CACHE["blob"]}
        m.update(w)
        in_maps.append(m)
    res = run_bass_kernel_spmd(nc, in_maps, core_ids=list(range(n_cores)))
    return np.concatenate([np.asarray(r["out"]) for r in res.results], axis=0).astype(np.float32)
```
